# Optimizing a Trainium2 kernel written in Bass

```python
import math
import jax, jax.numpy as jnp
from jax import lax
import numpy as np

D_MODEL = 1024
BATCH = 8
SEQ = 2048
DEPTH = 1

N_MEM = 256
EPS = 1e-6
DSA_HEADS = 8
DSA_LATENT = 128
DSA_VDIM = 64
IDX_HEADS = 8
IDX_DIM = 64
TOPK_MAX = 256
Q_BLOCK = 128
REL_BUCKETS = 32
REL_MAX_DIST = 128
S5_WIDTH = 512
S5_GROUP = 16
S5_GROUPS = S5_WIDTH // S5_GROUP
S5_STATE = 64
DT_MIN = 0.001
DT_MAX = 0.1
X_HEADS = 4
X_HEAD_DIM = 128
DSA_WIDTH = DSA_HEADS * DSA_VDIM
X_WIDTH = X_HEADS * X_HEAD_DIM
N_BRANCH = 3
D_FF = -(-(8 * D_MODEL) // (3 * 256)) * 256
IN_SPLITS = (DSA_HEADS * DSA_LATENT,
             DSA_LATENT,
             IDX_HEADS * IDX_DIM,
             IDX_DIM,
             IDX_HEADS,
             S5_WIDTH,
             X_WIDTH,
             N_BRANCH * D_MODEL)
D_IN = sum(IN_SPLITS)

kernel_name = 'hybrid_dsa_s5_memx_gated_block'


def rms_norm(x, g):
    x32 = x.astype(jnp.float32)
    y = x32 * lax.rsqrt(jnp.mean(x32 * x32, axis=-1, keepdims=True) + EPS)
    return (y * g.astype(jnp.float32)).astype(x.dtype)


def t5_bucket(n):
    max_exact = REL_BUCKETS // 2
    nf = jnp.maximum(n, 1).astype(jnp.float32)
    large = max_exact + (jnp.log(nf / max_exact) / math.log(REL_MAX_DIST / max_exact)
                         * (REL_BUCKETS - max_exact)).astype(jnp.int32)
    large = jnp.minimum(large, REL_BUCKETS - 1)
    return jnp.where(n < max_exact, n, large)


def dsa_attention(q, c, q_idx, k_idx, w_idx, rel_bias, w_uv):
    B, S = q.shape[0], q.shape[1]
    topk = min(TOPK_MAX, S // 4)
    n_blk = S // Q_BLOCK
    spos = jnp.arange(S)
    scale = DSA_LATENT ** -0.5

    def block(i):
        t0 = i * Q_BLOCK
        tpos = t0 + jnp.arange(Q_BLOCK)
        qb = lax.dynamic_slice_in_dim(q, t0, Q_BLOCK, axis=1)
        qib = lax.dynamic_slice_in_dim(q_idx, t0, Q_BLOCK, axis=1)
        wib = lax.dynamic_slice_in_dim(w_idx, t0, Q_BLOCK, axis=1)
        dots = jnp.einsum('bthd,bsd->bhts', qib, k_idx).astype(jnp.float32)
        score = jnp.einsum('bhts,bth->bts', jax.nn.relu(dots), wib.astype(jnp.float32))
        visible = spos[None, :] <= tpos[:, None]
        score = jnp.where(visible[None], score, -jnp.inf)
        _, idx = lax.top_k(score, topk)
        valid = idx <= tpos[None, :, None]
        c_sel = jax.vmap(lambda cb, ib: cb[ib])(c, idx)
        bucket = t5_bucket(jnp.maximum(tpos[None, :, None] - idx, 0))
        bias = jnp.moveaxis(rel_bias[bucket], -1, 1).astype(jnp.float32)
        logits = jnp.einsum('bthd,btkd->bhtk', qb, c_sel).astype(jnp.float32) * scale + bias
        logits = jnp.where(valid[:, None], logits, -jnp.inf)
        p = jax.nn.softmax(logits, axis=-1).astype(c.dtype)
        o = jnp.einsum('bhtk,btkd->bthd', p, c_sel)
        return jnp.einsum('bthd,hde->bthe', o, w_uv).reshape(B, Q_BLOCK, DSA_WIDTH)

    out = lax.map(block, jnp.arange(n_blk))
    return jnp.moveaxis(out, 0, 1).reshape(B, S, DSA_WIDTH)


def s5_mixer(u, a_re, a_im, log_dt, b_re, b_im, c_re, c_im, d_skip, w_glu):
    B, S = u.shape[0], u.shape[1]
    f32 = jnp.float32
    u32 = u.astype(f32).reshape(B, S, S5_GROUPS, S5_GROUP)
    lam = lax.complex(a_re.astype(f32), a_im.astype(f32))
    dt = jnp.exp(log_dt.astype(f32))[:, None]
    lam_bar = jnp.exp(lam * dt)
    b_bar = ((lam_bar - 1.0) / lam)[..., None] * lax.complex(b_re.astype(f32), b_im.astype(f32))
    bu = jnp.einsum('blgc,gnc->blgn', u32.astype(jnp.complex64), b_bar)
    a = jnp.broadcast_to(lam_bar, bu.shape)

    def combine(l, r):
        return (l[0] * r[0], r[0] * l[1] + r[1])

    _, states = lax.associative_scan(combine, (a, bu), axis=1)
    cmat = lax.complex(c_re.astype(f32), c_im.astype(f32))
    y = jnp.einsum('blgn,gcn->blgc', states, cmat).real + d_skip.astype(f32) * u32
    y = jax.nn.gelu(y.reshape(B, S, S5_WIDTH))
    y = y * jax.nn.sigmoid(y @ w_glu.astype(f32))
    return y.astype(u.dtype)


def cross_attention(q, mem_n, w_mem_kv, g_q, g_k):
    B, S = q.shape[0], q.shape[1]
    M = mem_n.shape[1]
    k, v = jnp.split(mem_n @ w_mem_kv, 2, axis=-1)
    qh = rms_norm(q.reshape(B, S, X_HEADS, X_HEAD_DIM), g_q)
    kh = rms_norm(k.reshape(B, M, X_HEADS, X_HEAD_DIM), g_k)
    vh = v.reshape(B, M, X_HEADS, X_HEAD_DIM)
    logits = jnp.einsum('bthd,bmhd->bhtm', qh, kh).astype(jnp.float32) * (X_HEAD_DIM ** -0.5)
    p = jax.nn.softmax(logits, axis=-1).astype(vh.dtype)
    return jnp.einsum('bhtm,bmhd->bthd', p, vh).reshape(B, S, X_WIDTH)


def hybrid_layer(x, mem, rel_bias, w_in, g_mix_norm, g_q_dsa, g_kv_dsa, w_uv_dsa,
                 a_re, a_im, log_dt, b_re, b_im, c_re, c_im, d_skip, w_glu,
                 g_mem_norm, w_mem_kv, g_q_cross, g_k_cross,
                 w_br_dsa, w_br_s5, w_br_cross, w_out,
                 g_ffn_norm, w_ffn_gate, w_ffn_up, w_ffn_down):
    B, S = x.shape[0], x.shape[1]
    h = rms_norm(x, g_mix_norm)
    offs = [int(o) for o in np.cumsum(IN_SPLITS)[:-1]]
    q_dsa, c_kv, q_idx, k_idx, w_idx, u_s5, q_x, gates = jnp.split(h @ w_in, offs, axis=-1)
    q_dsa = rms_norm(q_dsa.reshape(B, S, DSA_HEADS, DSA_LATENT), g_q_dsa)
    c_kv = rms_norm(c_kv, g_kv_dsa)
    o_dsa = dsa_attention(q_dsa, c_kv, q_idx.reshape(B, S, IDX_HEADS, IDX_DIM),
                          k_idx, w_idx, rel_bias, w_uv_dsa)
    o_s5 = s5_mixer(u_s5, a_re, a_im, log_dt, b_re, b_im, c_re, c_im, d_skip, w_glu)
    o_x = cross_attention(q_x, rms_norm(mem, g_mem_norm), w_mem_kv, g_q_cross, g_k_cross)
    g_dsa, g_s5, g_x = jnp.split(jax.nn.sigmoid(gates), N_BRANCH, axis=-1)
    merged = g_dsa * (o_dsa @ w_br_dsa) + g_s5 * (o_s5 @ w_br_s5) + g_x * (o_x @ w_br_cross)
    x = x + merged @ w_out
    hf = rms_norm(x, g_ffn_norm)
    return x + (jax.nn.silu(hf @ w_ffn_gate) * (hf @ w_ffn_up)) @ w_ffn_down


def setup_inputs(seed: int = 0) -> dict:
    key = jax.random.key(seed)
    ks = jax.random.split(key, 32)
    f32 = jnp.float32
    L = DEPTH

    def nrm(k, shape, fan):
        return jax.random.normal(k, shape, f32) * (fan ** -0.5)

    def gain(k, shape):
        return 1.0 + 0.01 * jax.random.normal(k, shape, f32)

    a_im = jnp.broadcast_to(math.pi * jnp.arange(S5_STATE, dtype=f32), (L, S5_GROUPS, S5_STATE))
    return {
        'x': jax.random.normal(ks[0], (BATCH, SEQ, D_MODEL), f32),
        'mem': jax.random.normal(ks[1], (BATCH, N_MEM, D_MODEL), f32),
        'rel_bias': 0.1 * jax.random.normal(ks[2], (REL_BUCKETS, DSA_HEADS), f32),
        'w_in': nrm(ks[3], (L, D_MODEL, D_IN), D_MODEL),
        'g_mix_norm': gain(ks[4], (L, D_MODEL)),
        'g_q_dsa': gain(ks[5], (L, DSA_LATENT)),
        'g_kv_dsa': gain(ks[6], (L, DSA_LATENT)),
        'w_uv_dsa': nrm(ks[7], (L, DSA_HEADS, DSA_LATENT, DSA_VDIM), DSA_LATENT),
        'a_re': -0.5 + 0.005 * jax.random.normal(ks[8], (L, S5_GROUPS, S5_STATE), f32),
        'a_im': a_im + 0.001 * jax.random.normal(ks[9], (L, S5_GROUPS, S5_STATE), f32),
        'log_dt': jax.random.uniform(ks[10], (L, S5_GROUPS), f32, math.log(DT_MIN), math.log(DT_MAX)),
        'b_re': nrm(ks[11], (L, S5_GROUPS, S5_STATE, S5_GROUP), 2 * S5_GROUP),
        'b_im': nrm(ks[12], (L, S5_GROUPS, S5_STATE, S5_GROUP), 2 * S5_GROUP),
        'c_re': nrm(ks[13], (L, S5_GROUPS, S5_GROUP, S5_STATE), 2 * S5_STATE),
        'c_im': nrm(ks[14], (L, S5_GROUPS, S5_GROUP, S5_STATE), 2 * S5_STATE),
        'd_skip': jax.random.normal(ks[15], (L, S5_GROUPS, S5_GROUP), f32),
        'w_glu': nrm(ks[16], (L, S5_WIDTH, S5_WIDTH), S5_WIDTH),
        'g_mem_norm': gain(ks[17], (L, D_MODEL)),
        'w_mem_kv': nrm(ks[18], (L, D_MODEL, 2 * X_WIDTH), D_MODEL),
        'g_q_cross': gain(ks[19], (L, X_HEAD_DIM)),
        'g_k_cross': gain(ks[20], (L, X_HEAD_DIM)),
        'w_br_dsa': nrm(ks[21], (L, DSA_WIDTH, D_MODEL), DSA_WIDTH),
        'w_br_s5': nrm(ks[22], (L, S5_WIDTH, D_MODEL), S5_WIDTH),
        'w_br_cross': nrm(ks[23], (L, X_WIDTH, D_MODEL), X_WIDTH),
        'w_out': nrm(ks[24], (L, D_MODEL, D_MODEL), D_MODEL),
        'g_ffn_norm': gain(ks[25], (L, D_MODEL)),
        'w_ffn_gate': nrm(ks[26], (L, D_MODEL, D_FF), D_MODEL),
        'w_ffn_up': nrm(ks[27], (L, D_MODEL, D_FF), D_MODEL),
        'w_ffn_down': nrm(ks[28], (L, D_FF, D_MODEL), D_FF),
    }


def reference(x, mem, rel_bias, w_in, g_mix_norm, g_q_dsa, g_kv_dsa, w_uv_dsa,
              a_re, a_im, log_dt, b_re, b_im, c_re, c_im, d_skip, w_glu,
              g_mem_norm, w_mem_kv, g_q_cross, g_k_cross,
              w_br_dsa, w_br_s5, w_br_cross, w_out,
              g_ffn_norm, w_ffn_gate, w_ffn_up, w_ffn_down):
    for l in range(DEPTH):
        x = hybrid_layer(x, mem, rel_bias, w_in[l], g_mix_norm[l], g_q_dsa[l], g_kv_dsa[l], w_uv_dsa[l],
                         a_re[l], a_im[l], log_dt[l], b_re[l], b_im[l], c_re[l], c_im[l], d_skip[l], w_glu[l],
                         g_mem_norm[l], w_mem_kv[l], g_q_cross[l], g_k_cross[l],
                         w_br_dsa[l], w_br_s5[l], w_br_cross[l], w_out[l],
                         g_ffn_norm[l], w_ffn_gate[l], w_ffn_up[l], w_ffn_down[l])
    return x
```

```python
import math
import numpy as np
import concourse.bass as bass
import concourse.mybir as mybir
from concourse.bass_utils import run_bass_kernel_spmd

F32 = mybir.dt.float32
BF16 = mybir.dt.bfloat16
AF = mybir.ActivationFunctionType
ALU = mybir.AluOpType

S = 2048
D = 1024
NB = 16
EPS = 1e-6
D_IN = 5832
D_FF = 2816
NFF = 22
NEG = -1.0e30
MBIAS = -30000.0


class Prog:
    ENGS = ("pe", "act", "dve", "pool", "sp")

    def __init__(self, nc, n_dma_sems=8):
        self.nc = nc
        self.streams = {e: [] for e in self.ENGS}
        self.sems = {e: nc.alloc_semaphore("s_" + e) for e in self.ENGS}
        self.cnt = {e: 0 for e in self.ENGS}
        self.known = {e: {} for e in self.ENGS}
        self.bufs = {}
        self.dma_sems = {}
        self.dma_rr = {}
        self.dma_val = {}
        self.semobj = {}
        for q in ("sp", "pool", "act"):
            self.dma_sems[q] = [nc.alloc_semaphore(f"d_{q}{i}") for i in range(n_dma_sems)]
            self.dma_rr[q] = 0
            for s in self.dma_sems[q]:
                self.semobj[s.name] = s
                self.dma_val[s.name] = 0
        for e in self.ENGS:
            self.semobj[self.sems[e].name] = self.sems[e]
        self.pending_pe = False

    def _need(self, eng, tok, waits):
        if tok is None:
            return
        sname, val = tok
        if self.known[eng].get(sname, 0) >= val:
            return
        if waits.get(sname, 0) < val:
            waits[sname] = val

    def _deps(self, eng, reads, writes, skip_writer=False):
        waits = {}
        for k in reads:
            st = self.bufs.get(k)
            if st is not None:
                self._need(eng, st[0], waits)
        for k in writes:
            st = self.bufs.get(k)
            if st is not None:
                if not skip_writer:
                    self._need(eng, st[0], waits)
                for tok in st[1].values():
                    self._need(eng, tok, waits)
        return waits

    def _commit(self, who, tok, reads, writes):
        for k in reads:
            st = self.bufs.setdefault(k, [None, {}])
            st[1][who + ":" + tok[0]] = tok
        for k in writes:
            self.bufs[k] = [tok, {}]

    def op(self, eng, fn, reads=(), writes=(), sig=True, accum=False):
        waits = self._deps(eng, reads, writes, skip_writer=(accum and eng == "pe"))
        if eng == "pe":
            waits.pop(self.sems["pe"].name, None)
        for s, v in waits.items():
            self.known[eng][s] = v
        if sig:
            self.cnt[eng] += 1
            tok = (self.sems[eng].name, self.cnt[eng])
            if eng == "pe":
                self.pending_pe = False
        else:
            assert eng == "pe"
            tok = (self.sems[eng].name, self.cnt[eng] + 1)
            self.pending_pe = True
        self.streams[eng].append((list(waits.items()), fn, (self.sems[eng], 1) if sig else None))
        self._commit(eng, tok, reads, writes)
        return tok

    def dma(self, q, fn, reads=(), writes=()):
        waits = self._deps(q, reads, writes)
        pool = self.dma_sems[q]
        s = pool[self.dma_rr[q] % len(pool)]
        self.dma_rr[q] += 1
        prev = self.dma_val[s.name]
        if prev > 0:
            self._need(q, (s.name, prev), waits)
        for sn, v in waits.items():
            self.known[q][sn] = v
        self.dma_val[s.name] = prev + 16
        tok = (s.name, prev + 16)
        self.streams[q].append((list(waits.items()), fn, (s, 16)))
        self._commit(q, tok, reads, writes)
        return tok

    def barrier(self):
        assert not self.pending_pe
        targets = [(self.sems[e].name, self.cnt[e]) for e in self.ENGS if self.cnt[e] > 0]
        targets += [(sn, v) for sn, v in self.dma_val.items() if v > 0]
        for e in self.ENGS:
            waits = {}
            for tok in targets:
                if e == "pe" and tok[0] == self.sems["pe"].name:
                    continue
                self._need(e, tok, waits)
            for sn, v in waits.items():
                self.known[e][sn] = v
            if waits:
                self.streams[e].append((list(waits.items()), None, None))
        self.bufs = {}

    def emit(self):
        assert not self.pending_pe
        nc = self.nc
        hmap = {"pe": "tensor", "act": "scalar", "dve": "vector", "pool": "gpsimd", "sp": "sync"}
        semobj = self.semobj
        with nc.Block() as block:
            for e in self.ENGS:
                def body(h, stream=self.streams[e]):
                    for waits, fn, inc in stream:
                        for sn, v in waits:
                            h.wait_ge(semobj[sn], v)
                        if fn is not None:
                            ins = fn(h)
                            if inc is not None:
                                ins.then_inc(inc[0], inc[1])
                getattr(block, hmap[e])(body)


class Arena:
    def __init__(self, nc, start=16640, limit=229376):
        self.nc = nc
        self.off = start
        self.limit = limit
        self.n = 0
        self.peak = 0

    def alloc(self, shape, dtype):
        nbytes = int(np.prod(shape[1:])) * (2 if dtype == BF16 else 4)
        nbytes = (nbytes + 63) // 64 * 64
        assert self.off + nbytes <= self.limit, ("SBUF overflow", self.off, nbytes)
        self.n += 1
        t = self.nc.alloc_sbuf_tensor_at(f"sb{self.n}", list(shape), dtype, offset=self.off)
        self.off += nbytes
        self.peak = max(self.peak, self.off)
        return t

    def mark(self):
        return self.off

    def release(self, m):
        if not getattr(self, "no_release", False):
            self.off = m


def _t5_bucket_table():
    d = np.arange(256)
    max_exact = 16
    nf = np.maximum(d, 1).astype(np.float32)
    large = max_exact + (np.log(nf / max_exact) / math.log(128 / max_exact) * (32 - max_exact)).astype(np.int32)
    large = np.minimum(large, 31)
    return np.where(d < max_exact, d, large)


def build(dbg=None):
    dbg = dbg or {}
    nc = bass.Bass("TRN2", target_bir_lowering=False)
    P = Prog(nc)
    A = Arena(nc)
    A.no_release = bool(dbg.get("no_release"))
    dbg_outs = []

    def din(name, shape):
        return nc.dram_tensor(name, list(shape), F32, kind="ExternalInput").ap()

    xT_d = din("xT", [D, S])
    memT_d = din("memT", [D, 256])
    w_in_d = din("w_in", [D, D_IN])
    w_uv_d = din("w_uv", [128, 8, 64])
    w_glu_d = din("w_glu", [512, 512])
    w_kv_d = din("w_mem_kv", [D, D])
    w_brd_d = din("w_br_dsa", [512, D])
    w_brs_d = din("w_br_s5", [512, D])
    w_brx_d = din("w_br_cross", [512, D])
    w_out_d = din("w_out", [D, D])
    w_g_d = din("w_ffn_gate", [D, D_FF])
    w_u_d = din("w_ffn_up", [D, D_FF])
    w_d_d = din("w_ffn_down", [D_FF, D])
    gvec_d = din("gvec", [128, 32])
    gkv_bc_d = din("gkv_bc", [128, 128])
    t5_d = din("t5", [128, 2, 8, 128])
    rb31_d = din("rb31", [128, 8])
    cm_d = din("cmats", [128, 5, 128])
    s5a_d = din("s5a", [128, 3, 32])
    s5b_d = din("s5b", [128, 4, 32, 16])
    dsk_d = din("dsk", [128, 32])
    outT_d = nc.dram_tensor("outT", [D, S], F32, kind="ExternalOutput").ap()
    scr_u = nc.dram_tensor("scr_u", [512, S], BF16).ap()
    scr_y = nc.dram_tensor("scr_y", [512, S], BF16).ap()

    ps = [nc.alloc_psum_tensor(f"ps{i}", [128, 512], F32) for i in range(6)]
    psTs = [nc.alloc_psum_tensor(f"psT{i}", [128, 1024], BF16) for i in range(2)]

    w_in_v = w_in_d.rearrange("(kc p) n -> p kc n", p=128)

    def dump(name, ap, shape, dt=F32):
        t = nc.dram_tensor("dbg_" + name, list(shape), dt, kind="ExternalOutput").ap()
        dbg_outs.append(("dbg_" + name))
        return t

    cm_f = A.alloc([128, 5, 128], F32)
    ident_f = cm_f[:, 0, :]
    causal_f = cm_f[:, 1, :]
    tmask_f = cm_f[:, 2, :]
    swap_f = cm_f[:, 3, :]
    cm_b = A.alloc([128, 5, 128], BF16)
    ident_b = cm_b[:, 0, :]
    ones_b = cm_b[:, 4, :]
    gvec = A.alloc([128, 32], F32)
    rx_bc = A.alloc([128, S], F32)
    rx_tok = A.alloc([128, NB], F32)
    off_o = A.mark()
    o_dsa = A.alloc([128, 4, S], BF16)

    P.dma("sp", lambda h: h.dma_start(out=cm_f[:], in_=cm_d), writes=["cm_f"])
    P.dma("pool", lambda h: h.dma_start(out=cm_b[:], in_=cm_d), writes=["cm_b"])
    P.dma("sp", lambda h: h.dma_start(out=gvec[:], in_=gvec_d), writes=["gvec"])
    CONST = ["cm_f", "cm_b", "gvec"]

    bank_rr = [0]

    def nb(lo=0, hi=6):
        b = lo + bank_rr[0] % (hi - lo)
        bank_rr[0] += 1
        return b

    def load_w(dst, src3, key, cols, q="pool"):
        kc_n = dst.shape[1]
        c0, c1 = cols
        for kc in range(kc_n):
            P.dma(q, (lambda h, kc=kc: h.dma_start(out=dst[:, kc, 0:c1 - c0], in_=src3[:, kc, c0:c1])),
                  writes=[(key, kc)])

    def build_xg(xg, with_stats, release=False):
        m = A.mark()
        xst = [A.alloc([128, S], F32) for _ in range(2)]
        sq = [A.alloc([128, S], BF16) for _ in range(2)]
        for kc in range(8):
            b = kc % 2
            P.dma("sp", (lambda h, kc=kc, b=b: h.dma_start(out=xst[b][:], in_=xT_d[kc * 128:(kc + 1) * 128, :])),
                  writes=[("xst", b)])
            if with_stats:
                P.op("act", (lambda h, b=b: h.activation(out=sq[b][:], in_=xst[b][:], func=AF.Square)),
                     reads=[("xst", b)], writes=[("sq", b)])
                for tc in range(4):
                    P.op("pe", (lambda h, b=b, tc=tc, kc=kc: h.matmul(ps[tc][:], lhsT=ones_b, rhs=sq[b][:, tc * 512:(tc + 1) * 512],
                                                                       start=(kc == 0), stop=(kc == 7))),
                         reads=[("sq", b), "cm_b"], writes=[("ps", tc)], sig=(tc == 3), accum=(kc > 0))
            P.op("dve", (lambda h, kc=kc, b=b: h.tensor_scalar(out=xg[:, kc, :], in0=xst[b][:], scalar1=gvec[:, kc:kc + 1],
                                                                scalar2=None, op0=ALU.mult)),
                 reads=[("xst", b), "gvec"], writes=[("xg", kc)])
        if with_stats:
            for tc in range(4):
                P.op("act", (lambda h, tc=tc: h.activation(out=rx_bc[:, tc * 512:(tc + 1) * 512], in_=ps[tc][:], func=AF.Sqrt,
                                                           scale=1.0 / D, bias=EPS)),
                     reads=[("ps", tc)], writes=[("rxs", tc)])
                P.op("dve", (lambda h, tc=tc: h.reciprocal(out=rx_bc[:, tc * 512:(tc + 1) * 512], in_=rx_bc[:, tc * 512:(tc + 1) * 512])),
                     reads=[("rxs", tc)], writes=[("rx_bc", tc)])
            for tt in range(NB):
                P.op("pe", (lambda h, tt=tt: h.matmul(ps[4][:, tt:tt + 1], lhsT=rx_bc[:, tt * 128:(tt + 1) * 128], rhs=ident_f[:, 0:1],
                                                      start=True, stop=True)),
                     reads=[("rx_bc", tt // 4), "cm_f"], writes=[("ps", 4)], sig=(tt == NB - 1))
            P.op("act", lambda h: h.copy(out=rx_tok[:], in_=ps[4][:, 0:NB]), reads=[("ps", 4)], writes=["rx_tok"])
        if release:
            A.release(m)

    XG = [("xg", kc) for kc in range(8)]
    RX = [("rx_bc", tc) for tc in range(4)]

    def proj_fm(xg, wb, wkey, c0, ncol, tc, bank, rhs_view=None):
        for kc in range(8):
            rhs = xg[:, kc, tc * 512:(tc + 1) * 512] if rhs_view is None else rhs_view(kc, tc)
            P.op("pe", (lambda h, kc=kc, rhs=rhs: h.matmul(ps[bank][0:ncol, :], lhsT=wb[:, kc, c0:c0 + ncol], rhs=rhs,
                                                            start=(kc == 0), stop=(kc == 7))),
                 reads=[(wkey, kc), ("xg", kc)], writes=[("ps", bank)], sig=(kc == 7), accum=(kc > 0))

    def head_norm(bank, bank2, gcol, dst, tmp_y, tmp_sq, tmp_sd, rx_ap, extra_reads, dst_key):
        n = dst.shape[-1]
        if rx_ap is None:
            P.op("act", lambda h: h.copy(out=tmp_y, in_=ps[bank][:, 0:n]), reads=[("ps", bank)] + extra_reads, writes=["hn_y"])
        else:
            P.op("dve", lambda h: h.tensor_tensor(out=tmp_y, in0=ps[bank][:, 0:n], in1=rx_ap, op=ALU.mult),
                 reads=[("ps", bank)] + extra_reads, writes=["hn_y"])
        P.op("act", lambda h: h.activation(out=tmp_sq, in_=tmp_y, func=AF.Square), reads=["hn_y"], writes=["hn_sq"])
        P.op("pe", lambda h: h.matmul(ps[bank2][:, 0:n], lhsT=ones_b, rhs=tmp_sq, start=True, stop=True),
             reads=["hn_sq", "cm_b"], writes=[("ps", bank2)])
        P.op("act", lambda h: h.activation(out=tmp_sd, in_=ps[bank2][:, 0:n], func=AF.Sqrt, scale=1.0 / 128, bias=EPS),
             reads=[("ps", bank2)], writes=["hn_sd"])
        P.op("dve", lambda h: h.reciprocal(out=tmp_sd, in_=tmp_sd), reads=["hn_sd"], writes=["hn_sd"])
        P.op("dve", lambda h: h.scalar_tensor_tensor(out=dst, in0=tmp_y, scalar=gvec[:, gcol:gcol + 1], in1=tmp_sd,
                                                     op0=ALU.mult, op1=ALU.mult),
             reads=["hn_y", "hn_sd", "gvec"], writes=[dst_key])

    base_mark = A.mark()

    xg = A.alloc([128, 8, S], BF16)
    build_xg(xg, True, release=True)
    P.op("dve", lambda h: h.tensor_scalar(out=gvec[:, 27:28], in0=gvec[:, 24:25], scalar1=128 ** -0.5, scalar2=None, op0=ALU.mult),
         reads=["gvec"], writes=["gvec"])
    P.op("dve", lambda h: h.tensor_scalar(out=gvec[:, 28:29], in0=gvec[:, 26:27], scalar1=128 ** -0.5, scalar2=None, op0=ALU.mult),
         reads=["gvec"], writes=["gvec"])

    def early(tag, tensors):
        if dbg.get("stop") != tag:
            return False
        for i, (t, shape, dt, keys) in enumerate(tensors):
            d = dump(f"{tag}{i}", None, shape, dt)
            P.dma("sp", (lambda h, d=d, t=t: h.dma_start(out=d, in_=t)), reads=keys, writes=[f"dbg_{tag}{i}"])
        P.barrier()
        P.emit()
        return True

    if early("xg", [(xg[:], [128, 8, S], BF16, XG), (rx_bc[:], [128, S], F32, RX), (rx_tok[:], [128, NB], F32, ["rx_tok"])]):
        return nc, dbg_outs

    def mb_base(m):
        return 128 * (m * (m + 1) // 2)

    MBT = A.alloc([128, 128 * 136], BF16)
    mA = A.mark()

    wbi = A.alloc([128, 8, 584], BF16)
    wki = A.alloc([128, 8, 128], BF16)
    qiT = A.alloc([128, 4, S], BF16)
    kiT = A.alloc([128, S], BF16)
    widx = A.alloc([128, NB, 8], F32)
    load_w(wbi, w_in_v, "wbi", (1152, 1736))
    for kc in range(8):
        P.dma("pool", (lambda h, kc=kc: h.dma_start(out=wki[:, kc, 0:64], in_=w_in_v[:, kc, 1664:1728])), writes=[("wki", kc)])
        P.dma("pool", (lambda h, kc=kc: h.dma_start(out=wki[:, kc, 64:128], in_=w_in_v[:, kc, 1664:1728])), writes=[("wki", kc)])
    for j in range(4):
        for tc in range(4):
            b = nb()
            proj_fm(xg, wbi, "wbi", j * 128, 128, tc, b)
            P.op("dve", (lambda h, b=b, j=j, tc=tc: h.tensor_tensor(out=qiT[:, j, tc * 512:(tc + 1) * 512], in0=ps[b][:],
                                                                      in1=rx_bc[:, tc * 512:(tc + 1) * 512], op=ALU.mult)),
                 reads=[("ps", b), ("rx_bc", tc)], writes=[("qiT", j, tc)])
    for tc in range(4):
        b = nb()
        proj_fm(xg, wki, "wki", 0, 128, tc, b)
        P.op("dve", (lambda h, b=b, tc=tc: h.tensor_tensor(out=kiT[:, tc * 512:(tc + 1) * 512], in0=ps[b][:],
                                                            in1=rx_bc[:, tc * 512:(tc + 1) * 512], op=ALU.mult)),
             reads=[("ps", b), ("rx_bc", tc)], writes=[("kiT", tc)])
    bw = nb()
    for tt in range(NB):
        for kc in range(8):
            P.op("pe", (lambda h, tt=tt, kc=kc: h.matmul(ps[bw][:, tt * 8:(tt + 1) * 8], lhsT=xg[:, kc, tt * 128:(tt + 1) * 128],
                                                         rhs=wbi[:, kc, 576:584], start=(kc == 0), stop=(kc == 7))),
                 reads=[("wbi", kc), ("xg", kc)], writes=[("ps", bw)], sig=(kc == 7 and tt == NB - 1), accum=not (kc == 0 and tt == 0))
    P.op("dve", lambda h: h.tensor_tensor(out=widx[:], in0=ps[bw][:, 0:NB * 8].rearrange("p (a b) -> p a b", b=8),
                                          in1=rx_tok[:].unsqueeze(2).to_broadcast([128, NB, 8]), op=ALU.mult),
         reads=[("ps", bw), "rx_tok"], writes=["widx"])

    if early("proj", [(qiT[:], [128, 4, S], BF16, [("qiT", j, tc) for j in range(4) for tc in range(4)]),
                      (kiT[:], [128, S], BF16, [("kiT", tc) for tc in range(4)]), (widx[:], [128, NB, 8], F32, ["widx"])]):
        return nc, dbg_outs
    acc = [A.alloc([128, S], F32) for _ in range(2)]
    work = A.alloc([128, S], F32)
    rl = [A.alloc([128, 512], F32) for _ in range(3)]
    mbt = [A.alloc([128, S], BF16) for _ in range(2)]
    m8 = A.alloc([128, 8], F32)
    thr0 = A.alloc([128, 1], F32)
    P.op("dve", lambda h: h.memset(thr0[:], -1.0e29), writes=["thr0"])
    rli = 0
    for m in dbg.get("m_list", range(dbg.get("m_max", NB))):
        ab = m % 2
        n = 128 * (m + 1)
        nsc = (n + 511) // 512
        for sc in range(nsc):
            w = min(512, n - 512 * sc)
            if dbg.get("skip_sc1") and sc == 1:
                continue
            for hh in range(8):
                j, half = hh // 2, hh % 2
                b = nb()
                r = rli % 3
                rli += 1
                P.op("pe", (lambda h, b=b, j=j, half=half, m=m, sc=sc, w=w: h.matmul(
                    ps[b][:, 0:w], lhsT=qiT[64 * half:64 * half + 64, j, m * 128:(m + 1) * 128],
                    rhs=kiT[64 * half:64 * half + 64, sc * 512:sc * 512 + w], start=True, stop=True)),
                    reads=[("qiT", j, m // 4), ("kiT", sc)], writes=[("ps", b)])
                P.op("act", (lambda h, b=b, r=r, w=w: h.activation(out=rl[r][:, 0:w], in_=ps[b][:, 0:w], func=AF.Relu)),
                     reads=[("ps", b)], writes=[("rl", r)])
                if hh == 0:
                    P.op("dve", (lambda h, r=r, w=w, ab=ab, sc=sc, m=m: h.tensor_scalar(
                        out=acc[ab][:, sc * 512:sc * 512 + w], in0=rl[r][:, 0:w], scalar1=widx[:, m, 0:1], scalar2=None, op0=ALU.mult)),
                        reads=[("rl", r), "widx"], writes=[("acc", ab)])
                else:
                    P.op("dve", (lambda h, r=r, w=w, ab=ab, sc=sc, m=m, hh=hh: h.scalar_tensor_tensor(
                        out=acc[ab][:, sc * 512:sc * 512 + w], in0=rl[r][:, 0:w], scalar=widx[:, m, hh:hh + 1],
                        in1=acc[ab][:, sc * 512:sc * 512 + w], op0=ALU.mult, op1=ALU.add)),
                        reads=[("rl", r), "widx", ("acc", ab)], writes=[("acc", ab)])
        if not dbg.get("skip_diag"):
            P.op("dve", (lambda h, ab=ab, m=m: h.tensor_tensor(out=acc[ab][:, m * 128:(m + 1) * 128], in0=acc[ab][:, m * 128:(m + 1) * 128],
                                                              in1=causal_f, op=ALU.add)),
                 reads=[("acc", ab), "cm_f"], writes=[("acc", ab)])
        if "score" in dbg and m == NB - 1:
            t = dump("score", None, [128, S])
            P.dma("sp", (lambda h, t=t, ab=ab: h.dma_start(out=t, in_=acc[ab][:])), reads=[("acc", ab)], writes=["dbg_score"])
        if m >= 2 and not dbg.get("no_topk"):
            for r_ in range(32):
                src = acc[ab] if r_ == 0 else work
                srck = ("acc", ab) if r_ == 0 else "work"
                P.op("dve", (lambda h, src=src, n=n: h.max(out=m8[:], in_=src[:, 0:n])), reads=[srck], writes=["m8"])
                if r_ < 31:
                    P.op("dve", (lambda h, src=src, n=n: h.match_replace(out=work[:, 0:n], in_to_replace=m8[:], in_values=src[:, 0:n],
                                                                         imm_value=NEG)),
                         reads=[srck, "m8"], writes=["work"])
            thr = m8[:, 7:8]
            thrk = "m8"
        else:
            thr = thr0[:]
            thrk = "thr0"
        if not dbg.get("skip_mb"):
            P.op("dve", (lambda h, ab=ab, n=n, thr=thr: h.tensor_scalar(out=mbt[ab][:, 0:n], in0=acc[ab][:, 0:n], scalar1=thr, scalar2=MBIAS,
                                                                        op0=ALU.is_lt, op1=ALU.mult)),
                 reads=[("acc", ab), thrk], writes=[("mbt", ab)])
        for j0 in ([] if dbg.get("no_tr") else range(0, m + 1, 4)):
            jn = min(4, m + 1 - j0)
            tb = (j0 // 4) % 2
            for jj in range(jn):
                j = j0 + jj
                P.op("pe", (lambda h, ab=ab, j=j, jj=jj, tb=tb: h.transpose(out=psTs[tb][:, jj * 128:(jj + 1) * 128],
                                                                         in_=mbt[ab][:, j * 128:(j + 1) * 128], identity=ident_b)),
                     reads=[("mbt", ab), "cm_b"], writes=[("psT", tb)], sig=(jj == jn - 1), accum=(jj > 0))
            P.op("act", (lambda h, m=m, j0=j0, jn=jn, tb=tb: h.copy(out=MBT[:, mb_base(m) + j0 * 128: mb_base(m) + (j0 + jn) * 128],
                                                                 in_=psTs[tb][:, 0:jn * 128])),
                 reads=[("psT", tb)], writes=[("MBT", m)])
    if early("scores", [(MBT[:], [128, 128 * 136], BF16, [("MBT", m) for m in dbg.get("m_list", range(dbg.get("m_max", NB)))]),
                        (acc[0][:], [128, S], F32, [("acc", 0)]), (acc[1][:], [128, S], F32, [("acc", 1)]), (m8[:], [128, 8], F32, ["m8"])]):
        return nc, dbg_outs
    if "mbt" in dbg:
        t = dump("mbt", None, [128, 128 * 136], BF16)
        P.dma("sp", (lambda h, t=t: h.dma_start(out=t, in_=MBT[:])), reads=[("MBT", m) for m in range(NB)], writes=["dbg_mbt"])

    P.barrier()
    A.release(mA)

    wbq = [A.alloc([128, 8, 512], BF16) for _ in range(2)]
    wbc = A.alloc([128, 8, 128], BF16)
    wuv = A.alloc([128, 8, 64], BF16)
    qT = A.alloc([128, 8, S], BF16)
    c_tok = A.alloc([128, NB, 128], BF16)
    cT = A.alloc([128, S], BF16)
    t5f = A.alloc([128, 2, 8, 128], F32)
    t5b = A.alloc([128, 2, 8, 128], BF16)
    rb31 = A.alloc([128, 8], F32)
    gkv_bc = A.alloc([128, 128], F32)
    tmp_y = A.alloc([128, 512], F32)
    tmp_sq = A.alloc([128, 512], BF16)
    tmp_sd = A.alloc([128, 512], F32)
    ss1 = A.alloc([128, 2], F32)
    for i in range(2):
        load_w(wbq[i], w_in_v, ("wbq", i), (512 * i, 512 * i + 512))
    load_w(wbc, w_in_v, "wbc", (1024, 1152))
    P.dma("pool", lambda h: h.dma_start(out=wuv[:], in_=w_uv_d), writes=["wuv"])
    P.dma("sp", lambda h: h.dma_start(out=t5f[:], in_=t5_d), writes=["t5f"])
    P.dma("sp", lambda h: h.dma_start(out=rb31[:], in_=rb31_d), writes=["rb31"])
    P.dma("sp", lambda h: h.dma_start(out=gkv_bc[:], in_=gkv_bc_d), writes=["gkv_bc"])
    P.op("dve", lambda h: h.tensor_tensor(out=t5b[:], in0=t5f[:], in1=rb31[:].unsqueeze(1).unsqueeze(3).to_broadcast([128, 2, 8, 128]),
                                          op=ALU.subtract),
         reads=["t5f", "rb31"], writes=["t5b"])
    for hh in range(8):
        for tc in range(4):
            b = nb(0, 3)
            proj_fm(xg, wbq[hh // 4], ("wbq", hh // 4), (hh % 4) * 128, 128, tc, b)
            head_norm(b, 3 + (tc % 2), 27, qT[:, hh, tc * 512:(tc + 1) * 512], tmp_y[:], tmp_sq[:], tmp_sd[:],
                      rx_bc[:, tc * 512:(tc + 1) * 512], [("rx_bc", tc)], ("qT", hh, tc))
    for tt in range(NB):
        b = nb(0, 3)
        for kc in range(8):
            P.op("pe", (lambda h, b=b, tt=tt, kc=kc: h.matmul(ps[b][:, 0:128], lhsT=xg[:, kc, tt * 128:(tt + 1) * 128], rhs=wbc[:, kc, :],
                                                              start=(kc == 0), stop=(kc == 7))),
                 reads=[("wbc", kc), ("xg", kc)], writes=[("ps", b)], sig=(kc == 7), accum=(kc > 0))
        P.op("act", (lambda h, b=b, tt=tt: h.activation(out=tmp_y[:, 0:128], in_=ps[b][:, 0:128], func=AF.Copy, scale=rx_tok[:, tt:tt + 1])),
             reads=[("ps", b), "rx_tok"], writes=["hn_y"])
        P.op("act", (lambda h: h.activation(out=tmp_y[:, 128:256], in_=tmp_y[:, 0:128], func=AF.Square, accum_out=ss1[:, 0:1])),
             reads=["hn_y"], writes=["c_ss", "hn_y"])
        P.op("act", (lambda h: h.activation(out=ss1[:, 1:2], in_=ss1[:, 0:1], func=AF.Sqrt, scale=1.0 / 128, bias=EPS)),
             reads=["c_ss"], writes=["c_sd"])
        P.op("dve", (lambda h: h.reciprocal(out=ss1[:, 1:2], in_=ss1[:, 1:2])), reads=["c_sd"], writes=["c_sd"])
        P.op("dve", (lambda h, tt=tt: h.scalar_tensor_tensor(out=c_tok[:, tt, :], in0=tmp_y[:, 0:128], scalar=ss1[:, 1:2], in1=gkv_bc[:],
                                                             op0=ALU.mult, op1=ALU.mult)),
             reads=["hn_y", "c_sd", "gkv_bc"], writes=[("c_tok", tt)])
    for t0 in range(0, NB, 4):
        tb = (t0 // 4) % 2
        for jj in range(4):
            tt = t0 + jj
            P.op("pe", (lambda h, tt=tt, jj=jj, tb=tb: h.transpose(out=psTs[tb][:, jj * 128:(jj + 1) * 128],
                                                                 in_=c_tok[:, tt, :], identity=ident_b)),
                 reads=[("c_tok", tt), "cm_b"], writes=[("psT", tb)], sig=(jj == 3), accum=(jj > 0))
        P.op("act", (lambda h, t0=t0, tb=tb: h.copy(out=cT[:, t0 * 128:(t0 + 4) * 128], in_=psTs[tb][:, 0:512])),
             reads=[("psT", tb)], writes=[("cT", t0 // 4)])
    if "qT" in dbg:
        t = dump("qT", None, [128, 8, S], BF16)
        P.dma("sp", (lambda h, t=t: h.dma_start(out=t, in_=qT[:])), reads=[("qT", a, b_) for a in range(8) for b_ in range(4)], writes=["dbg_qT"])
        t2 = dump("cT", None, [128, S], BF16)
        P.dma("sp", (lambda h, t2=t2: h.dma_start(out=t2, in_=cT[:])), reads=[("cT", i) for i in range(4)], writes=["dbg_cT"])

    PT = [A.alloc([128, 512], BF16) for _ in range(3)]
    rden = A.alloc([128, 512], F32)
    onT = [A.alloc([128, 512], BF16) for _ in range(2)]
    pti = 0
    BO, BD, BU = 3, 4, 5
    for c in range(4):
        for hh in range(8):
            nj = 4 * c + 4
            for j in range(nj):
                t_lo = max(512 * c, 128 * j)
                co = t_lo - 512 * c
                bs = nb(0, 3)
                pb = pti % 3
                pti += 1
                P.op("pe", (lambda h, bs=bs, j=j, hh=hh, t_lo=t_lo, co=co, c=c: h.matmul(
                    ps[bs][:, co:512], lhsT=cT[:, j * 128:(j + 1) * 128], rhs=qT[:, hh, t_lo:512 * c + 512], start=True, stop=False)),
                    reads=[("cT", j // 4), ("qT", hh, c)], writes=[("ps", bs)], sig=False)
                adds = []
                for m in range(4 * c, 4 * c + 4):
                    if m >= j:
                        adds.append(((m - 4 * c) * 128, MBT[:, mb_base(m) + j * 128: mb_base(m) + (j + 1) * 128], ("MBT", m)))
                for diff in (0, 1):
                    m = j + diff
                    if 4 * c <= m <= 4 * c + 3:
                        adds.append(((m - 4 * c) * 128, t5b[:, diff, hh, :], "t5b"))
                for ai, (col, rhs, key) in enumerate(adds):
                    last = ai == len(adds) - 1
                    P.op("pe", (lambda h, bs=bs, col=col, rhs=rhs, last=last: h.matmul(ps[bs][:, col:col + 128], lhsT=ident_b, rhs=rhs,
                                                                                      start=False, stop=last)),
                         reads=[key, "cm_b"], writes=[("ps", bs)], sig=last, accum=True)
                P.op("act", (lambda h, bs=bs, pb=pb, co=co, hh=hh: h.activation(out=PT[pb][:, 0:512 - co], in_=ps[bs][:, co:512], func=AF.Exp,
                                                                              bias=rb31[:, hh:hh + 1], scale=1.0)),
                     reads=[("ps", bs), "rb31"], writes=[("PT", pb)])
                P.op("pe", (lambda h, pb=pb, co=co, j=j, nj=nj: h.matmul(ps[BO][:, co:512], lhsT=c_tok[:, j, :], rhs=PT[pb][:, 0:512 - co],
                                                                         start=(j == 0), stop=(j == nj - 1))),
                     reads=[("c_tok", j), ("PT", pb)], writes=[("ps", BO)], sig=False, accum=(j > 0))
                P.op("pe", (lambda h, pb=pb, co=co, j=j, nj=nj: h.matmul(ps[BD][:, co:512], lhsT=ones_b, rhs=PT[pb][:, 0:512 - co],
                                                                         start=(j == 0), stop=(j == nj - 1))),
                     reads=["cm_b", ("PT", pb)], writes=[("ps", BD)], sig=True, accum=(j > 0))
            ob = hh % 2
            P.op("dve", lambda h: h.reciprocal(out=rden[:], in_=ps[BD][:]), reads=[("ps", BD)], writes=["rden"])
            P.op("dve", (lambda h, ob=ob: h.tensor_tensor(out=onT[ob][:], in0=ps[BO][:], in1=rden[:], op=ALU.mult)),
                 reads=[("ps", BO), "rden"], writes=[("onT", ob)])
            P.op("pe", (lambda h, ob=ob, hh=hh: h.matmul(ps[BU][64 * (hh % 2):64 * (hh % 2) + 64, :], lhsT=wuv[:, hh, :], rhs=onT[ob][:],
                                                         start=True, stop=True)),
                 reads=["wuv", ("onT", ob)], writes=[("ps", BU, hh % 2)])
            if hh % 2 == 1:
                P.op("act", (lambda h, hh=hh, c=c: h.copy(out=o_dsa[:, hh // 2, c * 512:(c + 1) * 512], in_=ps[BU][:])),
                     reads=[("ps", BU, 0), ("ps", BU, 1)], writes=[("o_dsa", hh // 2, c)])
    if "o_dsa" in dbg:
        t = dump("o_dsa", None, [128, 4, S], BF16)
        P.dma("sp", (lambda h, t=t: h.dma_start(out=t, in_=o_dsa[:])), reads=[("o_dsa", a, c) for a in range(4) for c in range(4)],
              writes=["dbg_o_dsa"])

    P.barrier()
    A.release(base_mark)
    o_s5 = A.alloc([128, 4, S], BF16)
    o_x = A.alloc([128, 4, S], BF16)
    base_mark = A.mark()
    if dbg.get("od_early"):
        t = dump("od0", None, [128, 4, S], BF16)
        P.dma("sp", (lambda h, t=t: h.dma_start(out=t, in_=o_dsa[:])), writes=["dbg_od0"])
    if dbg.get("stop") == "dsa":
        P.barrier()
        P.emit()
        return nc, dbg_outs


    xgX = A.alloc([128, 8, S], BF16)
    build_xg(xgX, False)
    wbx = A.alloc([128, 8, 512], BF16)
    wbu = A.alloc([128, 8, 512], BF16)
    wkv = A.alloc([128, 8, 1024], BF16)
    memf = A.alloc([128, 8, 256], F32)
    msq = A.alloc([128, 8, 256], BF16)
    memn = A.alloc([128, 8, 256], BF16)
    rm = A.alloc([128, 256], F32)
    khT = A.alloc([128, 4, 256], BF16)
    vtok = A.alloc([128, 2, 512], BF16)
    qxT = A.alloc([128, 4, S], BF16)
    ust = [A.alloc([128, 512], BF16) for _ in range(2)]
    tyX = A.alloc([128, 512], F32)
    tqX = A.alloc([128, 512], BF16)
    tdX = A.alloc([128, 512], F32)
    PTx = [A.alloc([128, 512], BF16) for _ in range(3)]
    rdx = A.alloc([128, 512], F32)
    w_kv_v = w_kv_d.rearrange("(kc p) n -> p kc n", p=128)
    if not dbg.get("no_wbx"):
        load_w(wbx, w_in_v, "wbx", (2248, 2760))
    if not dbg.get("no_wbu"):
        load_w(wbu, w_in_v, "wbu", (1736, 2248))
    for i in range(2):
        if not dbg.get("no_wkv"):
            load_w(wkv[:, :, i * 512:(i + 1) * 512], w_kv_v, ("wkv", i), (i * 512, (i + 1) * 512))
    P.dma("sp", lambda h: h.dma_start(out=memf[:], in_=memT_d.rearrange("(kc p) m -> p kc m", p=128)), writes=["memf"])
    if dbg.get("stop") == "x0":
        P.barrier()
        t = dump("od", None, [128, 4, S], BF16)
        P.dma("sp", (lambda h, t=t: h.dma_start(out=t, in_=o_dsa[:])), writes=["dbg_od"])
        P.barrier()
        P.emit()
        return nc, dbg_outs
    P.op("act", lambda h: h.activation(out=msq[:], in_=memf[:], func=AF.Square), reads=["memf"], writes=["msq"])
    bm = nb(0, 3)
    for kc in range(8):
        P.op("pe", (lambda h, kc=kc: h.matmul(ps[bm][:, 0:256], lhsT=ones_b, rhs=msq[:, kc, :], start=(kc == 0), stop=(kc == 7))),
             reads=["msq", "cm_b"], writes=[("ps", bm)], sig=(kc == 7), accum=(kc > 0))
    P.op("act", lambda h: h.activation(out=rm[:], in_=ps[bm][:, 0:256], func=AF.Sqrt, scale=1.0 / D, bias=EPS), reads=[("ps", bm)], writes=["rm"])
    P.op("dve", lambda h: h.reciprocal(out=rm[:], in_=rm[:]), reads=["rm"], writes=["rm"])
    for kc in range(8):
        P.op("dve", (lambda h, kc=kc: h.scalar_tensor_tensor(out=memn[:, kc, :], in0=memf[:, kc, :], scalar=gvec[:, 8 + kc:9 + kc], in1=rm[:],
                                                             op0=ALU.mult, op1=ALU.mult)),
             reads=["memf", "rm", "gvec"], writes=[("memn", kc)])
    for hh in range(4):
        b = nb(0, 3)
        for kc in range(8):
            P.op("pe", (lambda h, b=b, hh=hh, kc=kc: h.matmul(ps[b][:, 0:256], lhsT=wkv[:, kc, hh * 128:(hh + 1) * 128], rhs=memn[:, kc, :],
                                                              start=(kc == 0), stop=(kc == 7))),
                 reads=[(("wkv", 0), kc), ("memn", kc)], writes=[("ps", b)], sig=(kc == 7), accum=(kc > 0))
        head_norm(b, 3 + (hh % 2), 28, khT[:, hh, :], tyX[:, 0:256], tqX[:, 0:256], tdX[:, 0:256], None, [], ("khT", hh))
    for mb in range(2):
        b = nb(0, 3)
        for kc in range(8):
            P.op("pe", (lambda h, b=b, mb=mb, kc=kc: h.matmul(ps[b][:], lhsT=memn[:, kc, mb * 128:(mb + 1) * 128], rhs=wkv[:, kc, 512:1024],
                                                              start=(kc == 0), stop=(kc == 7))),
                 reads=[(("wkv", 1), kc), ("memn", kc)], writes=[("ps", b)], sig=(kc == 7), accum=(kc > 0))
        P.op("act", (lambda h, b=b, mb=mb: h.copy(out=vtok[:, mb, :], in_=ps[b][:])), reads=[("ps", b)], writes=[("vtok", mb)])
    for hh in range(4):
        for tc in range(4):
            b = nb(0, 3)
            proj_fm(xgX, wbx, "wbx", hh * 128, 128, tc, b)
            head_norm(b, 3 + (tc % 2), 25, qxT[:, hh, tc * 512:(tc + 1) * 512], tyX[:], tqX[:], tdX[:],
                      rx_bc[:, tc * 512:(tc + 1) * 512], [("rx_bc", tc)], ("qxT", hh, tc))
    ui = 0
    for ch in range(4):
        for tcp in range(4):
            b = nb(0, 3)
            ub = ui % 2
            ui += 1
            proj_fm(xgX, wbu, "wbu", ch * 128, 128, tcp, b,
                    rhs_view=(lambda kc, tcp: xgX[:, kc, :].rearrange("p (b j) -> p j b", j=8)[:, 2 * tcp:2 * tcp + 2, :]))
            P.op("dve", (lambda h, b=b, ub=ub, tcp=tcp: h.tensor_tensor(
                out=ust[ub][:].rearrange("p (j b) -> p j b", j=2), in0=ps[b][:].rearrange("p (j b) -> p j b", j=2),
                in1=rx_bc[:].rearrange("p (b j) -> p j b", j=8)[:, 2 * tcp:2 * tcp + 2, :], op=ALU.mult)),
                reads=[("ps", b)] + RX, writes=[("ust", ub)])
            P.dma("sp", (lambda h, ub=ub, ch=ch, tcp=tcp: h.dma_start(out=scr_u[ch * 128:(ch + 1) * 128, tcp * 512:(tcp + 1) * 512], in_=ust[ub][:])),
                  reads=[("ust", ub)], writes=[("scr_u", ch, tcp)])
    ptiX = 0
    BO, BD = 3, 4
    for c in range(4):
        for hh in range(4):
            for mb in range(2):
                bs = nb(0, 3)
                pb = ptiX % 3
                ptiX += 1
                P.op("pe", (lambda h, bs=bs, hh=hh, mb=mb, c=c: h.matmul(ps[bs][:], lhsT=khT[:, hh, mb * 128:(mb + 1) * 128],
                                                                         rhs=qxT[:, hh, c * 512:(c + 1) * 512], start=True, stop=True)),
                     reads=[("khT", hh), ("qxT", hh, c)], writes=[("ps", bs)])
                P.op("act", (lambda h, bs=bs, pb=pb: h.activation(out=PTx[pb][:], in_=ps[bs][:], func=AF.Exp)),
                     reads=[("ps", bs)], writes=[("PT", pb)])
                P.op("pe", (lambda h, pb=pb, mb=mb, hh=hh: h.matmul(ps[BO][:], lhsT=vtok[:, mb, hh * 128:(hh + 1) * 128], rhs=PTx[pb][:],
                                                                    start=(mb == 0), stop=(mb == 1))),
                     reads=[("vtok", mb), ("PT", pb)], writes=[("ps", BO)], sig=False, accum=(mb > 0))
                P.op("pe", (lambda h, pb=pb, mb=mb: h.matmul(ps[BD][:], lhsT=ones_b, rhs=PTx[pb][:], start=(mb == 0), stop=(mb == 1))),
                     reads=["cm_b", ("PT", pb)], writes=[("ps", BD)], sig=True, accum=(mb > 0))
            P.op("dve", lambda h: h.reciprocal(out=rdx[:], in_=ps[BD][:]), reads=[("ps", BD)], writes=["rden"])
            P.op("dve", (lambda h, hh=hh, c=c: h.tensor_tensor(out=o_x[:, hh, c * 512:(c + 1) * 512], in0=ps[BO][:], in1=rdx[:], op=ALU.mult)),
                 reads=[("ps", BO), "rden"], writes=[("o_x", hh, c)])
    if dbg.get("stop") == "cross":
        t = dump("od", None, [128, 4, S], BF16)
        P.dma("sp", (lambda h, t=t: h.dma_start(out=t, in_=o_dsa[:])), writes=["dbg_od"])
        t = dump("o_x", None, [128, 4, S], BF16)
        P.dma("sp", (lambda h, t=t: h.dma_start(out=t, in_=o_x[:])), reads=[("o_x", a, c) for a in range(4) for c in range(4)], writes=["dbg_o_x"])
        t2 = dump("scr_u", None, [512, S], BF16)
        P.dma("sp", (lambda h, t2=t2: h.dma_start(out=t2, in_=scr_u)), reads=[("scr_u", a, c) for a in range(4) for c in range(4)], writes=["dbg_scr_u"])
        P.barrier()
        P.emit()
        return nc, dbg_outs
    P.barrier()
    A.release(base_mark)

    s5a = A.alloc([128, 3, 32], F32)
    s5b = A.alloc([128, 4, 32, 16], F32)
    dsk = A.alloc([128, 32], F32)
    TB = [A.alloc([128, 2, 8, 32], F32) for _ in range(4)]
    TK = A.alloc([128, 8, 2, 32], F32)
    cw = A.alloc([128, 16, 2, 32], F32)
    bb = A.alloc([128, 2, 32, 16], F32)
    W1 = A.alloc([128, 32, 128], BF16)
    W2 = A.alloc([128, 32, 128], BF16)
    Tm = A.alloc([128, 32, 128], BF16)
    U8 = A.alloc([128, 32, 256], BF16)
    X = A.alloc([128, 32, 256], BF16)
    mS = A.alloc([128, 1], F32)
    mS1 = A.mark()
    W1T = A.alloc([128, 32, 128], BF16)
    Lm = A.alloc([128, 32, 128], BF16)
    Rm = A.alloc([128, 32, 128], BF16)
    tA = A.alloc([128, 32, 128], F32)
    tB = A.alloc([128, 32, 128], F32)
    P.dma("sp", lambda h: h.dma_start(out=s5a[:], in_=s5a_d), writes=["s5a"])
    P.dma("sp", lambda h: h.dma_start(out=s5b[:], in_=s5b_d), writes=["s5b"])
    P.dma("sp", lambda h: h.dma_start(out=dsk[:], in_=dsk_d), writes=["dsk"])
    for jl in range(8):
        P.dma("sp", (lambda h, jl=jl: h.dma_start(out=U8[jl * 16:(jl + 1) * 16, :, :],
                                                  in_=scr_u.rearrange("(g c) (j b) -> j c g b", c=16, j=8)[jl])),
              reads=[("scr_u", a, c) for a in range(4) for c in range(4)], writes=[("U8", jl)])
    U8K = [("U8", jl) for jl in range(8)]

    def V(i):
        return cw[:, i, :, :]

    def tt(out, a, b_, op, rk, wk):
        P.op("dve", lambda h: h.tensor_tensor(out=out, in0=a, in1=b_, op=op), reads=rk, writes=wk)

    def cmul(dst, a, b_, ka, kb, kd):
        t1, t2 = V(14), V(15)
        tt(t1[:, 0, :], a[:, 0, :], b_[:, 0, :], ALU.mult, [ka, kb], ["cw_t1a"])
        tt(t1[:, 1, :], a[:, 1, :], b_[:, 1, :], ALU.mult, [ka, kb], ["cw_t1b"])
        tt(t2[:, 0, :], a[:, 0, :], b_[:, 1, :], ALU.mult, [ka, kb], ["cw_t2a"])
        tt(t2[:, 1, :], a[:, 1, :], b_[:, 0, :], ALU.mult, [ka, kb], ["cw_t2b"])
        tt(dst[:, 0, :], t1[:, 0, :], t1[:, 1, :], ALU.subtract, ["cw_t1a", "cw_t1b"], [kd])
        tt(dst[:, 1, :], t2[:, 0, :], t2[:, 1, :], ALU.add, ["cw_t2a", "cw_t2b", kd], [kd])

    a_re, a_im, ldt = s5a[:, 0, :], s5a[:, 1, :], s5a[:, 2, :]
    dtv = V(0)[:, 0, :]
    adr = V(0)[:, 1, :]
    adi = V(1)[:, 0, :]
    P.op("act", lambda h: h.activation(out=dtv, in_=ldt, func=AF.Exp), reads=["s5a"], writes=["dtv"])
    tt(adr, a_re, dtv, ALU.mult, ["s5a", "dtv"], ["adr"])
    tt(adi, a_im, dtv, ALU.mult, ["s5a", "dtv"], ["adi"])
    mag, magn, cs, sn = V(2)[:, 0, :], V(2)[:, 1, :], V(3)[:, 0, :], V(3)[:, 1, :]
    P.op("dve", lambda h: h.memset(mS[:], math.pi / 2), writes=["mS"])
    P.op("act", lambda h: h.activation(out=mag, in_=adr, func=AF.Exp, scale=1.0 / 16), reads=["adr"], writes=["mag"])
    P.op("act", lambda h: h.activation(out=magn, in_=adr, func=AF.Exp, scale=-1.0 / 16), reads=["adr"], writes=["magn"])
    P.op("act", lambda h: h.activation(out=cs, in_=adi, func=AF.Sin, scale=1.0 / 16, bias=mS[:]), reads=["adi", "mS"], writes=["cs"])
    P.op("act", lambda h: h.activation(out=sn, in_=adi, func=AF.Sin, scale=1.0 / 16), reads=["adi"], writes=["sn"])
    mu, nu = V(4), V(5)
    tt(mu[:, 0, :], mag, cs, ALU.mult, ["mag", "cs"], ["mu"])
    tt(mu[:, 1, :], mag, sn, ALU.mult, ["mag", "sn", "mu"], ["mu"])
    tt(nu[:, 0, :], magn, cs, ALU.mult, ["magn", "cs"], ["nu"])
    P.op("dve", lambda h: h.scalar_tensor_tensor(out=nu[:, 1, :], in0=magn, scalar=-1.0, in1=sn, op0=ALU.mult, op1=ALU.mult),
         reads=["magn", "sn", "nu"], writes=["nu"])
    def pw_slot(t, slot):
        return TB[t][:, :, slot, :]
    TW1, TW2, TL, TR = 0, 1, 2, 3
    cur, ck = mu, "mu"
    for i in range(4):
        dst = V(6 + (i % 2)) if i < 3 else pw_slot(TR, 1)
        kd = f"sqp{i}" if i < 3 else ("P", 1)
        cmul(dst, cur, cur, ck, ck, kd)
        cur, ck = dst, kd
    cur, ck = nu, "nu"
    for i in range(4):
        dst = V(8 + (i % 2)) if i < 3 else pw_slot(TL, 1)
        kd = f"sqn{i}" if i < 3 else ("N", 1)
        cmul(dst, cur, cur, ck, ck, kd)
        cur, ck = dst, kd
    Pp = {1: pw_slot(TR, 1)}
    Np = {1: pw_slot(TL, 1)}
    for t_ in (TR, TL, TW1):
        sl = 7 if t_ == TW1 else 0
        P.op("dve", (lambda h, t_=t_, sl=sl: h.memset(TB[t_][:, 0, sl, :], 1.0)), writes=[("one", t_, 0)])
        P.op("dve", (lambda h, t_=t_, sl=sl: h.memset(TB[t_][:, 1, sl, :], 0.0)), writes=[("one", t_, 1)])
    for k, (a, b_) in ((2, (1, 1)), (3, (2, 1)), (4, (2, 2)), (5, (4, 1)), (6, (4, 2)), (7, (4, 3))):
        Pp[k] = pw_slot(TR, k)
        cmul(Pp[k], Pp[a], Pp[b_], ("P", a), ("P", b_), ("P", k))
        Np[k] = pw_slot(TL, k)
        cmul(Np[k], Np[a], Np[b_], ("N", a), ("N", b_), ("N", k))
    Pp[8] = pw_slot(TW2, 7)
    cmul(Pp[8], Pp[4], Pp[4], ("P", 4), ("P", 4), ("P", 8))
    PK = [("P", k) for k in range(1, 9)]
    NK = [("N", k) for k in range(1, 8)]
    P.op("dve", lambda h: h.tensor_copy(out=TB[TW2][:, :, 0:7, :], in_=TB[TR][:, :, 1:8, :]), reads=PK, writes=["TW2"])
    for jl in range(7):
        P.op("dve", (lambda h, jl=jl: h.tensor_copy(out=TB[TW1][:, :, jl, :], in_=TB[TR][:, :, 7 - jl, :])), reads=PK, writes=[("TW1", jl)])
    TW1K = [("TW1", jl) for jl in range(7)] + [("one", TW1, 0), ("one", TW1, 1)]
    TRK = PK + [("one", TR, 0), ("one", TR, 1)]
    TLK = NK + [("one", TL, 0), ("one", TL, 1)]
    TW2K = ["TW2", ("P", 8)]
    P.op("dve", lambda h: h.tensor_copy(out=TK[:, 0, :, :], in_=Pp[8]), reads=[("P", 8)], writes=[("TK", 0)])
    for l in range(1, 8):
        cmul(TK[:, l, :, :], TK[:, l - 1, :, :], TK[:, l - 1, :, :], ("TK", l - 1), ("TK", l - 1), ("TK", l))
    P.op("dve", lambda h: h.tensor_scalar(out=TK[64:128, :, 1, :], in0=TK[64:128, :, 1, :], scalar1=-1.0, scalar2=None, op0=ALU.mult),
         reads=[("TK", l) for l in range(8)], writes=["TKs"])
    num, qv = V(10), V(11)
    den = V(12)[:, 0, :]
    P.op("dve", lambda h: h.tensor_scalar(out=num[:, 0, :], in0=Pp[1][:, 0, :], scalar1=-1.0, scalar2=None, op0=ALU.add),
         reads=[("P", 1)], writes=["num"])
    P.op("dve", lambda h: h.tensor_copy(out=num[:, 1, :], in_=Pp[1][:, 1, :]), reads=[("P", 1), "num"], writes=["num"])
    tt(den, a_re, a_re, ALU.mult, ["s5a"], ["den"])
    tt(V(12)[:, 1, :], a_im, a_im, ALU.mult, ["s5a"], ["den2"])
    tt(den, den, V(12)[:, 1, :], ALU.add, ["den", "den2"], ["den"])
    P.op("dve", lambda h: h.reciprocal(out=den, in_=den), reads=["den"], writes=["den"])
    t13 = V(13)
    tt(t13[:, 0, :], num[:, 0, :], a_re, ALU.mult, ["num", "s5a"], ["t13a"])
    tt(t13[:, 1, :], num[:, 1, :], a_im, ALU.mult, ["num", "s5a"], ["t13b"])
    tt(qv[:, 0, :], t13[:, 0, :], t13[:, 1, :], ALU.add, ["t13a", "t13b"], ["qv0"])
    tt(qv[:, 0, :], qv[:, 0, :], den, ALU.mult, ["qv0", "den"], ["qv0"])
    tt(t13[:, 0, :], num[:, 1, :], a_re, ALU.mult, ["num", "s5a", "qv0"], ["t13a"])
    tt(t13[:, 1, :], num[:, 0, :], a_im, ALU.mult, ["num", "s5a", "qv0"], ["t13b"])
    tt(qv[:, 1, :], t13[:, 0, :], t13[:, 1, :], ALU.subtract, ["t13a", "t13b"], ["qv1"])
    tt(qv[:, 1, :], qv[:, 1, :], den, ALU.mult, ["qv1", "den"], ["qv1"])
    q_re = qv[:, 0, :].unsqueeze(2).to_broadcast([128, 32, 16])
    q_im = qv[:, 1, :].unsqueeze(2).to_broadcast([128, 32, 16])
    B_re, B_im, C_re, C_im = s5b[:, 0], s5b[:, 1], s5b[:, 2], s5b[:, 3]
    tAv = tA[:].rearrange("p g (j c) -> p g j c", c=16)
    tBv = tB[:].rearrange("p g (j c) -> p g j c", c=16)
    tt(tAv[:, :, 0, :], q_re, B_re, ALU.mult, ["qv0", "s5b"], ["tA"])
    tt(tBv[:, :, 0, :], q_im, B_im, ALU.mult, ["qv1", "s5b"], ["tB"])
    tt(bb[:, 0], tAv[:, :, 0, :], tBv[:, :, 0, :], ALU.subtract, ["tA", "tB"], ["bb0"])
    tt(tAv[:, :, 0, :], q_re, B_im, ALU.mult, ["qv0", "s5b", "bb0"], ["tA"])
    tt(tBv[:, :, 0, :], q_im, B_re, ALU.mult, ["qv1", "s5b", "bb0"], ["tB"])
    tt(bb[:, 1], tAv[:, :, 0, :], tBv[:, :, 0, :], ALU.add, ["tA", "tB"], ["bb1"])

    def build_mat(dst, tbl, tkeys, v_re, v_im, vkeys, mode, dkey):
        dv = dst[:].rearrange("p g (j c) -> p g j c", c=16)
        for half in range(2):
            pr = slice(64 * half, 64 * half + 64)
            Tre = TB[tbl][pr, 0, :, :].rearrange("p s g -> p g s").unsqueeze(3).to_broadcast([64, 32, 8, 16])
            Tim = TB[tbl][pr, 1, :, :].rearrange("p s g -> p g s").unsqueeze(3).to_broadcast([64, 32, 8, 16])
            va, vb = (v_re, v_im) if half == 0 else (v_im, v_re)
            Va = va[pr].unsqueeze(2).to_broadcast([64, 32, 8, 16])
            Vb = vb[pr].unsqueeze(2).to_broadcast([64, 32, 8, 16])
            tt(tAv[pr], Tre, Va, ALU.mult, tkeys + vkeys + [dkey], [("tA", half)])
            tt(tBv[pr], Tim, Vb, ALU.mult, tkeys + vkeys + [dkey], [("tB", half)])
            if half == 0:
                tt(dv[pr], tAv[pr], tBv[pr], ALU.subtract, [("tA", 0), ("tB", 0)], [(dkey, 0)])
            elif mode == "B":
                tt(dv[pr], tAv[pr], tBv[pr], ALU.add, [("tA", 1), ("tB", 1)], [(dkey, 1)])
            else:
                P.op("dve", (lambda h, pr=pr: h.scalar_tensor_tensor(out=dv[pr], in0=tAv[pr], scalar=-1.0, in1=tBv[pr],
                                                                      op0=ALU.mult, op1=ALU.subtract)),
                     reads=[("tA", 1), ("tB", 1)], writes=[(dkey, 1)])

    P.op("dve", lambda h: h.memset(mS[:], 0.0), reads=["tA", "tB", "mS"], writes=[("tA", 0), ("tA", 1), ("tB", 0), ("tB", 1), "mS"])
    build_mat(W1T, TW1, TW1K, bb[:, 0], bb[:, 1], ["bb0", "bb1"], "B", "W1T")
    build_mat(W2, TW2, TW2K, C_re, C_im, ["s5b"], "C", "W2")
    build_mat(Lm, TL, TLK, bb[:, 0], bb[:, 1], ["bb0", "bb1"], "B", "Lm")
    build_mat(Rm, TR, TRK, C_re, C_im, ["s5b"], "C", "Rm")
    for g0 in range(0, 32, 4):
        tb = (g0 // 4) % 2
        for jj in range(4):
            P.op("pe", (lambda h, g0=g0, jj=jj, tb=tb: h.transpose(out=psTs[tb][:, jj * 128:(jj + 1) * 128], in_=W1T[:, g0 + jj, :], identity=ident_b)),
                 reads=[("W1T", 0), ("W1T", 1), "cm_b"], writes=[("psT", tb)], sig=(jj == 3), accum=(jj > 0))
        P.op("act", (lambda h, g0=g0, tb=tb: h.copy(out=W1[:, g0:g0 + 4, :], in_=psTs[tb][:, 0:512].rearrange("p (a b) -> p a b", b=128))),
             reads=[("psT", tb)], writes=[("W1", g0 // 4)])
    for g0 in range(0, 32, 4):
        b = nb(0, 3)
        for jj in range(4):
            P.op("pe", (lambda h, g0=g0, jj=jj, b=b: h.matmul(ps[b][:, jj * 128:(jj + 1) * 128], lhsT=Lm[:, g0 + jj, :], rhs=Rm[:, g0 + jj, :],
                                                              start=True, stop=True)),
                 reads=[("Lm", 0), ("Lm", 1), ("Rm", 0), ("Rm", 1)], writes=[("ps", b)], sig=(jj == 3), accum=(jj > 0))
        P.op("dve", (lambda h, b=b, g0=g0: h.tensor_tensor(out=tA[:, g0:g0 + 4, :], in0=ps[b][:].rearrange("p (a b) -> p a b", b=128),
                                                           in1=tmask_f.unsqueeze(1).to_broadcast([128, 4, 128]), op=ALU.mult)),
             reads=[("ps", b), "cm_f", ("tA", 0), ("tA", 1)], writes=[("tAm", g0 // 4)])
        for jj in range(4):
            g = g0 + jj
            P.op("dve", (lambda h, g=g: h.scalar_tensor_tensor(out=Tm[:, g, :], in0=ident_f, scalar=dsk[:, g:g + 1], in1=tA[:, g, :],
                                                               op0=ALU.mult, op1=ALU.add)),
                 reads=[("tAm", g0 // 4), "dsk", "cm_f"], writes=[("Tm", g0 // 4)])
    if dbg.get("stop") == "s5pre":
        for nm, tns, keys in (("W1", W1, [("W1", i) for i in range(8)]), ("W2", W2, [("W2", 0), ("W2", 1)]), ("Tm", Tm, [("Tm", i) for i in range(8)])):
            t = dump(nm, None, [128, 32, 128], BF16)
            P.dma("sp", (lambda h, t=t, tns=tns: h.dma_start(out=t, in_=tns[:])), reads=keys, writes=["dbg_" + nm])
        t = dump("TK", None, [128, 8, 2, 32])
        P.dma("sp", (lambda h, t=t: h.dma_start(out=t, in_=TK[:])), reads=["TKs"], writes=["dbg_TK"])
        t = dump("TB", None, [128, 2, 8, 32])
        P.dma("sp", (lambda h, t=t: h.dma_start(out=t, in_=TB[TR][:])), reads=TRK, writes=["dbg_TB"])
        P.barrier()
        P.emit()
        return nc, dbg_outs
    P.barrier()
    A.release(mS1)

    Yst = A.alloc([128, 32, 256], BF16)
    gq = A.alloc([128, 512], F32)
    gz = A.alloc([128, 512], F32)
    gs = A.alloc([128, 512], F32)
    mS2 = A.mark()
    Rl = [A.alloc([128, 32, 128], BF16) for _ in range(2)]
    rt1 = A.alloc([128, 32, 128], BF16)
    rt2 = A.alloc([128, 32, 128], BF16)
    XK = lambda gp: ("X", gp)
    for gp in range(16):
        b = nb(0, 3)
        for gi in range(2):
            g = 2 * gp + gi
            P.op("pe", (lambda h, b=b, gi=gi, g=g: h.matmul(ps[b][:, gi * 256:(gi + 1) * 256], lhsT=W1[:, g, :], rhs=U8[:, g, :], start=True, stop=True)),
                 reads=[("W1", g // 4)] + U8K, writes=[("ps", b)], sig=(gi == 1), accum=(gi > 0))
        P.op("act", (lambda h, b=b, gp=gp: h.copy(out=X[:, 2 * gp:2 * gp + 2, :], in_=ps[b][:].rearrange("p (a b) -> p a b", b=256))),
             reads=[("ps", b)], writes=[XK(gp)])
    for l in range(8):
        d = 1 << l
        R = Rl[l % 2]
        P.op("dve", (lambda h, l=l: h.tensor_tensor(out=rt1[:], in0=ident_f.unsqueeze(1).to_broadcast([128, 32, 128]),
                                                    in1=TK[:, l, 0, :].unsqueeze(2).to_broadcast([128, 32, 128]), op=ALU.mult)),
             reads=["cm_f", "TKs"], writes=["rt1"])
        P.op("dve", (lambda h, l=l: h.tensor_tensor(out=rt2[:], in0=swap_f.unsqueeze(1).to_broadcast([128, 32, 128]),
                                                    in1=TK[:, l, 1, :].unsqueeze(2).to_broadcast([128, 32, 128]), op=ALU.mult)),
             reads=["cm_f", "TKs"], writes=["rt2"])
        P.op("dve", (lambda h, R=R: h.tensor_tensor(out=R[:], in0=rt1[:], in1=rt2[:], op=ALU.add)),
             reads=["rt1", "rt2"], writes=[("R", l % 2)])
        for gp in range(16):
            b = nb(0, 3)
            for gi in range(2):
                g = 2 * gp + gi
                P.op("pe", (lambda h, b=b, gi=gi, g=g, d=d, R=R: h.matmul(ps[b][:, gi * 256 + d:(gi + 1) * 256], lhsT=R[:, g, :], rhs=X[:, g, 0:256 - d],
                                                                          start=True, stop=True)),
                     reads=[("R", l % 2), XK(gp)], writes=[("ps", b)], sig=(gi == 1), accum=(gi > 0))
            P.op("dve", (lambda h, b=b, gp=gp, d=d: h.tensor_tensor(out=X[:, 2 * gp:2 * gp + 2, d:256], in0=X[:, 2 * gp:2 * gp + 2, d:256],
                                                                   in1=ps[b][:].rearrange("p (a b) -> p a b", b=256)[:, :, d:256], op=ALU.add)),
                 reads=[("ps", b), XK(gp)], writes=[XK(gp)])
    for gp in range(16):
        b = nb(0, 3)
        for gi in range(2):
            g = 2 * gp + gi
            P.op("pe", (lambda h, b=b, gi=gi, g=g: h.matmul(ps[b][:, gi * 256:(gi + 1) * 256], lhsT=Tm[:, g, :], rhs=U8[:, g, :], start=True, stop=False)),
                 reads=[("Tm", g // 4)] + U8K, writes=[("ps", b)], sig=False, accum=(gi > 0))
            P.op("pe", (lambda h, b=b, gi=gi, g=g: h.matmul(ps[b][:, gi * 256 + 1:(gi + 1) * 256], lhsT=W2[:, g, :], rhs=X[:, g, 0:255], start=False, stop=True)),
                 reads=[("W2", 0), ("W2", 1), XK(gp)], writes=[("ps", b)], sig=(gi == 1), accum=True)
        P.op("act", (lambda h, b=b: h.activation(out=gq[:], in_=ps[b][:], func=AF.Square)), reads=[("ps", b)], writes=["gq"])
        P.op("dve", lambda h: h.tensor_scalar(out=gz[:], in0=gq[:], scalar1=0.044715, scalar2=1.0, op0=ALU.mult, op1=ALU.add),
             reads=["gq"], writes=["gz"])
        P.op("dve", (lambda h, b=b: h.tensor_tensor(out=gz[:], in0=gz[:], in1=ps[b][:], op=ALU.mult)), reads=["gz", ("ps", b)], writes=["gz"])
        P.op("act", lambda h: h.activation(out=gs[:], in_=gz[:], func=AF.Sigmoid, scale=2.0 * math.sqrt(2.0 / math.pi)), reads=["gz"], writes=["gs"])
        P.op("dve", (lambda h, b=b, gp=gp: h.tensor_tensor(out=Yst[:, 2 * gp:2 * gp + 2, :], in0=gs[:].rearrange("p (a b) -> p a b", b=256),
                                                          in1=ps[b][:].rearrange("p (a b) -> p a b", b=256), op=ALU.mult)),
             reads=["gs", ("ps", b)], writes=[("Yst", gp)])
    for il in range(8):
        P.dma("sp", (lambda h, il=il: h.dma_start(out=scr_y.rearrange("(g c) (i b) -> i c g b", c=16, i=8)[il], in_=Yst[il * 16:(il + 1) * 16, :, :])),
              reads=[("Yst", gp) for gp in range(16)], writes=[("scr_y", il)])
    P.barrier()
    A.release(mS2)
    YT = A.alloc([128, 4, S], BF16)
    wglu = A.alloc([128, 4, 512], BF16)
    P.dma("pool", lambda h: h.dma_start(out=wglu[:], in_=w_glu_d.rearrange("(kc p) n -> p kc n", p=128)), writes=["wglu"])
    P.dma("sp", lambda h: h.dma_start(out=YT[:], in_=scr_y.rearrange("(ch p) t -> p ch t", p=128)), writes=["YT"])
    for chp in range(4):
        for tcp in range(4):
            b = nb(0, 3)
            for k in range(4):
                P.op("pe", (lambda h, b=b, k=k, chp=chp, tcp=tcp: h.matmul(ps[b][:], lhsT=wglu[:, k, chp * 128:(chp + 1) * 128],
                                                                           rhs=YT[:, k, tcp * 512:(tcp + 1) * 512], start=(k == 0), stop=(k == 3))),
                     reads=["wglu", "YT"], writes=[("ps", b)], sig=(k == 3), accum=(k > 0))
            P.op("act", (lambda h, b=b: h.activation(out=gs[:], in_=ps[b][:], func=AF.Sigmoid)), reads=[("ps", b)], writes=["gs"])
            P.op("dve", (lambda h, chp=chp, tcp=tcp: h.tensor_tensor(out=o_s5[:, chp, tcp * 512:(tcp + 1) * 512], in0=gs[:],
                                                                    in1=YT[:, chp, tcp * 512:(tcp + 1) * 512], op=ALU.mult)),
                 reads=["gs", "YT"], writes=[("o_s5", chp, tcp)])
    if dbg.get("stop") == "s5":
        t = dump("od", None, [128, 4, S], BF16)
        P.dma("sp", (lambda h, t=t: h.dma_start(out=t, in_=o_dsa[:])), writes=["dbg_od"])
        t = dump("o_s5", None, [128, 4, S], BF16)
        P.dma("sp", (lambda h, t=t: h.dma_start(out=t, in_=o_s5[:])), reads=[("o_s5", a, c) for a in range(4) for c in range(4)], writes=["dbg_o_s5"])
        t = dump("YT", None, [128, 4, S], BF16)
        P.dma("sp", (lambda h, t=t: h.dma_start(out=t, in_=YT[:])), reads=["YT"], writes=["dbg_YT"])
        P.barrier()
        P.emit()
        return nc, dbg_outs
    P.barrier()
    A.release(base_mark)

    off_merged = A.mark()
    merged = A.alloc([128, 8, S], BF16)
    mM1 = A.mark()
    xgM = A.alloc([128, 8, S], BF16)
    build_xg(xgM, False)
    wg = [A.alloc([128, 8, 384], BF16) for _ in range(2)]
    wbr = [A.alloc([128, 4, 384], BF16) for _ in range(2)]
    gt = A.alloc([128, 512], F32)
    sg = A.alloc([128, 512], F32)
    macc = A.alloc([128, 512], F32)
    mtmp = A.alloc([128, 512], F32)
    w_br_v = [w.rearrange("(kc p) n -> p kc n", p=128) for w in (w_brd_d, w_brs_d, w_brx_d)]
    obr = [o_dsa, o_s5, o_x]
    for fc in range(8):
        wb_ = fc % 2
        for br in range(3):
            c0 = 2760 + br * 1024 + fc * 128
            P.dma("pool", (lambda h, wb_=wb_, br=br, c0=c0: h.dma_start(out=wg[wb_][:, :, br * 128:(br + 1) * 128], in_=w_in_v[:, :, c0:c0 + 128])),
                  writes=[("wg", wb_, br)])
            P.dma("pool", (lambda h, wb_=wb_, br=br, fc=fc: h.dma_start(out=wbr[wb_][:, :, br * 128:(br + 1) * 128],
                                                                       in_=w_br_v[br][:, :, fc * 128:(fc + 1) * 128])),
                  writes=[("wbr", wb_, br)])
        for tc in range(4):
            for br in range(3):
                bG = nb(0, 6)
                for kc in range(8):
                    P.op("pe", (lambda h, bG=bG, kc=kc, wb_=wb_, br=br, tc=tc: h.matmul(ps[bG][:], lhsT=wg[wb_][:, kc, br * 128:(br + 1) * 128],
                                                                                        rhs=xgM[:, kc, tc * 512:(tc + 1) * 512], start=(kc == 0), stop=(kc == 7))),
                         reads=[("wg", wb_, br), ("xg", kc)], writes=[("ps", bG)], sig=(kc == 7), accum=(kc > 0))
                P.op("dve", (lambda h, bG=bG, tc=tc: h.tensor_tensor(out=gt[:], in0=ps[bG][:], in1=rx_bc[:, tc * 512:(tc + 1) * 512], op=ALU.mult)),
                     reads=[("ps", bG)], writes=["gt"])
                P.op("act", lambda h: h.activation(out=sg[:], in_=gt[:], func=AF.Sigmoid), reads=["gt"], writes=["sg"])
                bB = nb(0, 6)
                for k in range(4):
                    if br == 1:
                        rhs = o_s5[:, k, :].rearrange("p (j b) -> p b j", j=8)[:, 64 * tc:64 * tc + 64, :]
                    else:
                        rhs = obr[br][:, k, tc * 512:(tc + 1) * 512]
                    P.op("pe", (lambda h, bB=bB, k=k, wb_=wb_, br=br, rhs=rhs: h.matmul(ps[bB][:], lhsT=wbr[wb_][:, k, br * 128:(br + 1) * 128], rhs=rhs,
                                                                                        start=(k == 0), stop=(k == 3))),
                         reads=[("wbr", wb_, br)], writes=[("ps", bB)], sig=(k == 3), accum=(k > 0))
                if br == 0:
                    P.op("dve", (lambda h, bB=bB: h.tensor_tensor(out=macc[:], in0=ps[bB][:], in1=sg[:], op=ALU.mult)),
                         reads=[("ps", bB), "sg"], writes=["macc"])
                else:
                    P.op("dve", (lambda h, bB=bB: h.tensor_tensor(out=mtmp[:], in0=ps[bB][:], in1=sg[:], op=ALU.mult)),
                         reads=[("ps", bB), "sg"], writes=["mtmp"])
                    if br == 1:
                        P.op("dve", lambda h: h.tensor_tensor(out=macc[:], in0=macc[:], in1=mtmp[:], op=ALU.add), reads=["macc", "mtmp"], writes=["macc"])
                    else:
                        P.op("dve", (lambda h, fc=fc, tc=tc: h.tensor_tensor(out=merged[:, fc, tc * 512:(tc + 1) * 512], in0=macc[:], in1=mtmp[:], op=ALU.add)),
                             reads=["macc", "mtmp"], writes=[("merged", fc, tc)])
                if dbg.get("stop") == "merge1" and br == dbg.get("br", 0):
                    for nm, tns, shp, dt_, keys in (("wg", wg[0][:], [128, 8, 384], BF16, [("wg", 0, i) for i in range(3)]), ("gt", gt[:], [128, 512], F32, ["gt"]),
                                                    ("sg", sg[:], [128, 512], F32, ["sg"]), ("macc", macc[:], [128, 512], F32, ["macc"]),
                                                    ("mtmp", mtmp[:], [128, 512], F32, ["mtmp"]),
                                                    ("xg", xgM[:], [128, 8, S], BF16, XG), ("od", o_dsa[:], [128, 4, S], BF16, []), ("os", o_s5[:], [128, 4, S], BF16, []), ("ox", o_x[:], [128, 4, S], BF16, []), ("wbr", wbr[0][:], [128, 4, 384], BF16, [("wbr", 0, i) for i in range(3)])):
                        t = dump(nm, None, shp, dt_)
                        P.dma("sp", (lambda h, t=t, tns=tns: h.dma_start(out=t, in_=tns)), reads=keys, writes=["dbg_" + nm])
                    P.barrier()
                    P.emit()
                    return nc, dbg_outs
    if dbg.get("stop") == "merge":
        t = dump("merged", None, [128, 8, S], BF16)
        P.dma("sp", (lambda h, t=t: h.dma_start(out=t, in_=merged[:])), reads=[("merged", a, c) for a in range(8) for c in range(4)], writes=["dbg_merged"])
        P.barrier()
        P.emit()
        return nc, dbg_outs
    P.barrier()
    A.release(mM1)
    off_x1 = A.mark()
    x1T = A.alloc([128, 8, S], F32)
    mM0 = A.mark()
    wout = A.alloc([128, 8, 1024], BF16)
    xres = [A.alloc([128, 512], F32) for _ in range(2)]
    w_out_v = w_out_d.rearrange("(kc p) n -> p kc n", p=128)
    for i in range(2):
        load_w(wout[:, :, i * 512:(i + 1) * 512], w_out_v, ("wout", i), (i * 512, (i + 1) * 512))
    xi = 0
    for fc in range(8):
        for tc in range(4):
            xb = xi % 2
            xi += 1
            P.dma("sp", (lambda h, xb=xb, fc=fc, tc=tc: h.dma_start(out=xres[xb][:], in_=xT_d[fc * 128:(fc + 1) * 128, tc * 512:(tc + 1) * 512])),
                  writes=[("xres", xb)])
            b = nb(0, 6)
            for kc in range(8):
                P.op("pe", (lambda h, b=b, kc=kc, fc=fc, tc=tc: h.matmul(ps[b][:], lhsT=wout[:, kc, fc * 128:(fc + 1) * 128],
                                                                         rhs=merged[:, kc, tc * 512:(tc + 1) * 512], start=(kc == 0), stop=(kc == 7))),
                     reads=[(("wout", fc // 4), kc), ("merged", kc, tc)], writes=[("ps", b)], sig=(kc == 7), accum=(kc > 0))
            P.op("dve", (lambda h, b=b, xb=xb, fc=fc, tc=tc: h.tensor_tensor(out=x1T[:, fc, tc * 512:(tc + 1) * 512], in0=ps[b][:], in1=xres[xb][:], op=ALU.add)),
                 reads=[("ps", b), ("xres", xb)], writes=[("x1T", fc, tc)])
    if dbg.get("stop") == "x1":
        t = dump("x1T", None, [128, 8, S])
        P.dma("sp", (lambda h, t=t: h.dma_start(out=t, in_=x1T[:])), reads=[("x1T", a, c) for a in range(8) for c in range(4)], writes=["dbg_x1T"])
        P.barrier()
        P.emit()
        return nc, dbg_outs
    P.barrier()
    A.release(mM0)
    top = A.mark()
    A.off = off_o
    fsq = [A.alloc([128, 512], BF16) for _ in range(2)]
    rf = A.alloc([128, 512], F32)
    wgu = [[A.alloc([128, 8, 256], BF16) for _ in range(2)] for _ in range(2)]
    wdn = [A.alloc([128, NFF, 256], BF16) for _ in range(2)]
    fs = A.alloc([128, 512], F32)
    assert A.off <= off_merged
    A.off = off_merged
    hfT = A.alloc([128, 8, 512], BF16)
    hT = A.alloc([128, NFF, 512], BF16)
    assert A.off <= off_x1
    A.off = top
    fo = [A.alloc([128, 512], F32) for _ in range(2)]
    w_gu_v = [w.rearrange("(kc p) n -> p kc n", p=128) for w in (w_g_d, w_u_d)]
    w_d_v = w_d_d.rearrange("(fk p) n -> p fk n", p=128)
    wi = 0
    di = 0
    oi = 0
    for tq in range(4):
        tsl = slice(tq * 512, (tq + 1) * 512)
        bS = nb(0, 6)
        for kc in range(8):
            sb_ = kc % 2
            P.op("act", (lambda h, kc=kc, sb_=sb_, tsl=tsl: h.activation(out=fsq[sb_][:], in_=x1T[:, kc, tsl], func=AF.Square)),
                 reads=[("x1T", kc, tq)], writes=[("fsq", sb_)])
            P.op("pe", (lambda h, kc=kc, sb_=sb_, bS=bS: h.matmul(ps[bS][:], lhsT=ones_b, rhs=fsq[sb_][:], start=(kc == 0), stop=(kc == 7))),
                 reads=[("fsq", sb_), "cm_b"], writes=[("ps", bS)], sig=True, accum=(kc > 0))
        P.op("act", (lambda h, bS=bS: h.activation(out=rf[:], in_=ps[bS][:], func=AF.Sqrt, scale=1.0 / D, bias=EPS)), reads=[("ps", bS)], writes=["rf"])
        P.op("dve", lambda h: h.reciprocal(out=rf[:], in_=rf[:]), reads=["rf"], writes=["rf"])
        for kc in range(8):
            P.op("dve", (lambda h, kc=kc, tsl=tsl: h.scalar_tensor_tensor(out=hfT[:, kc, :], in0=x1T[:, kc, tsl], scalar=gvec[:, 16 + kc:17 + kc], in1=rf[:],
                                                                          op0=ALU.mult, op1=ALU.mult)),
                 reads=[("x1T", kc, tq), "rf", "gvec"], writes=[("hfT", kc)])
        for f0 in range(0, NFF, 2):
            wb_ = wi % 2
            wi += 1
            for gu in range(2):
                P.dma("pool", (lambda h, gu=gu, wb_=wb_, f0=f0: h.dma_start(out=wgu[gu][wb_][:], in_=w_gu_v[gu][:, :, f0 * 128:(f0 + 2) * 128])),
                      writes=[("wgu", gu, wb_)])
            for ff in range(f0, f0 + 2):
                bg, bu = nb(0, 6), nb(0, 6)
                for gu, bb_ in ((0, bg), (1, bu)):
                    for kc in range(8):
                        P.op("pe", (lambda h, gu=gu, bb_=bb_, kc=kc, wb_=wb_, ff=ff, f0=f0: h.matmul(
                            ps[bb_][:], lhsT=wgu[gu][wb_][:, kc, (ff - f0) * 128:(ff - f0 + 1) * 128], rhs=hfT[:, kc, :], start=(kc == 0), stop=(kc == 7))),
                            reads=[("wgu", gu, wb_), ("hfT", kc)], writes=[("ps", bb_)], sig=(kc == 7), accum=(kc > 0))
                P.op("act", (lambda h, bg=bg: h.activation(out=fs[:], in_=ps[bg][:], func=AF.Silu)), reads=[("ps", bg)], writes=["fs"])
                P.op("dve", (lambda h, bu=bu, ff=ff: h.tensor_tensor(out=hT[:, ff, :], in0=ps[bu][:], in1=fs[:], op=ALU.mult)),
                     reads=[("ps", bu), "fs"], writes=[("hT", ff)])
        for c0 in range(0, 8, 2):
            db = di % 2
            di += 1
            P.dma("pool", (lambda h, db=db, c0=c0: h.dma_start(out=wdn[db][:], in_=w_d_v[:, :, c0 * 128:(c0 + 2) * 128])), writes=[("wdn", db)])
            for fc in range(c0, c0 + 2):
                b = nb(0, 6)
                for ff in range(NFF):
                    P.op("pe", (lambda h, b=b, ff=ff, db=db, fc=fc, c0=c0: h.matmul(ps[b][:], lhsT=wdn[db][:, ff, (fc - c0) * 128:(fc - c0 + 1) * 128], rhs=hT[:, ff, :],
                                                                                    start=(ff == 0), stop=(ff == NFF - 1))),
                         reads=[("wdn", db), ("hT", ff)], writes=[("ps", b)], sig=(ff == NFF - 1), accum=(ff > 0))
                ob = oi % 2
                oi += 1
                P.op("dve", (lambda h, b=b, ob=ob, fc=fc, tsl=tsl: h.tensor_tensor(out=fo[ob][:], in0=ps[b][:], in1=x1T[:, fc, tsl], op=ALU.add)),
                     reads=[("ps", b), ("x1T", fc, tq)], writes=[("fo", ob)])
                P.dma("sp", (lambda h, ob=ob, fc=fc, tsl=tsl: h.dma_start(out=outT_d[fc * 128:(fc + 1) * 128, tsl], in_=fo[ob][:])),
                      reads=[("fo", ob)], writes=[("out", fc, tq)])
    P.barrier()
    P.emit()
    return nc, dbg_outs


def _host_inputs(inputs):
    f = lambda a: np.ascontiguousarray(np.asarray(a, dtype=np.float32))
    x = f(inputs["x"])
    mem = f(inputs["mem"])
    rel_bias = f(inputs["rel_bias"])
    shared = {}
    shared["w_in"] = f(inputs["w_in"][0])
    shared["w_uv"] = f(np.transpose(inputs["w_uv_dsa"][0], (1, 0, 2)))
    shared["w_glu"] = f(inputs["w_glu"][0])
    shared["w_mem_kv"] = f(inputs["w_mem_kv"][0])
    shared["w_br_dsa"] = f(inputs["w_br_dsa"][0])
    shared["w_br_s5"] = f(inputs["w_br_s5"][0])
    shared["w_br_cross"] = f(inputs["w_br_cross"][0])
    shared["w_out"] = f(inputs["w_out"][0])
    shared["w_ffn_gate"] = f(inputs["w_ffn_gate"][0])
    shared["w_ffn_up"] = f(inputs["w_ffn_up"][0])
    shared["w_ffn_down"] = f(inputs["w_ffn_down"][0])
    gvec = np.zeros((128, 32), np.float32)
    gvec[:, 0:8] = np.asarray(inputs["g_mix_norm"][0]).reshape(8, 128).T
    gvec[:, 8:16] = np.asarray(inputs["g_mem_norm"][0]).reshape(8, 128).T
    gvec[:, 16:24] = np.asarray(inputs["g_ffn_norm"][0]).reshape(8, 128).T
    gvec[:, 24] = np.asarray(inputs["g_q_dsa"][0])
    gvec[:, 25] = np.asarray(inputs["g_q_cross"][0])
    gvec[:, 26] = np.asarray(inputs["g_k_cross"][0])
    shared["gvec"] = gvec
    shared["gkv_bc"] = f(np.broadcast_to(np.asarray(inputs["g_kv_dsa"][0])[None, :], (128, 128)))
    bt = _t5_bucket_table()
    s_i = np.arange(128)[:, None]
    t_i = np.arange(128)[None, :]
    t5 = np.zeros((128, 2, 8, 128), np.float32)
    for diff in (0, 1):
        dist = np.maximum(t_i - s_i + 128 * diff, 0)
        t5[:, diff, :, :] = np.transpose(rel_bias[bt[dist]], (0, 2, 1))
    shared["t5"] = t5
    shared["rb31"] = f(np.broadcast_to(rel_bias[31][None, :], (128, 8)))
    cm = np.zeros((128, 5, 128), np.float32)
    cm[:, 0, :] = np.eye(128)
    cm[:, 1, :] = np.where(t_i.T >= s_i.T, 0.0, NEG)
    il = np.arange(128)[None, :] // 16
    jl = np.arange(128)[:, None] // 16
    cm[:, 2, :] = (il >= jl).astype(np.float32)
    sw = np.zeros((128, 128), np.float32)
    sw[np.arange(64), np.arange(64) + 64] = 1.0
    sw[np.arange(64) + 64, np.arange(64)] = 1.0
    cm[:, 3, :] = sw
    cm[:, 4, :] = 1.0
    shared["cmats"] = cm
    tile2 = lambda a: np.concatenate([a, a], axis=0)
    s5a = np.zeros((128, 3, 32), np.float32)
    s5a[:, 0, :] = tile2(np.asarray(inputs["a_re"][0]).T)
    s5a[:, 1, :] = tile2(np.asarray(inputs["a_im"][0]).T)
    s5a[:, 2, :] = np.broadcast_to(np.asarray(inputs["log_dt"][0])[None, :], (128, 32))
    shared["s5a"] = s5a
    s5b = np.zeros((128, 4, 32, 16), np.float32)
    s5b[:, 0] = tile2(np.transpose(inputs["b_re"][0], (1, 0, 2)))
    s5b[:, 1] = tile2(np.transpose(inputs["b_im"][0], (1, 0, 2)))
    s5b[:, 2] = tile2(np.transpose(inputs["c_re"][0], (2, 0, 1)))
    s5b[:, 3] = tile2(np.transpose(inputs["c_im"][0], (2, 0, 1)))
    shared["s5b"] = s5b
    shared["dsk"] = f(np.tile(np.asarray(inputs["d_skip"][0]).T, (8, 1)))
    in_maps = []
    for b in range(8):
        d = dict(shared)
        d["xT"] = f(x[b].T)
        d["memT"] = f(mem[b].T)
        in_maps.append(d)
    return in_maps


_CACHE = {}


def kernel(**inputs):
    in_maps = _host_inputs(inputs)
    if "nc" not in _CACHE:
        _CACHE["nc"] = build()[0]
    res = run_bass_kernel_spmd(_CACHE["nc"], in_maps, core_ids=list(range(8)))
    out = np.stack([np.ascontiguousarray(res.results[b]["outT"].T) for b in range(8)], axis=0)
    return out.astype(np.float32)
```

```python
import math
import numpy as np
import concourse.bass as bass
import concourse.mybir as mybir
from concourse.bass_utils import run_bass_kernel_spmd

F32 = mybir.dt.float32
BF16 = mybir.dt.bfloat16
AF = mybir.ActivationFunctionType
ALU = mybir.AluOpType

S = 2048
D = 1024
NB = 16
EPS = 1e-6
D_IN = 5832
D_FF = 2816
NFF = 22
NEG = -1.0e30
MBIAS = -30000.0


class Prog:
    ENGS = ("pe", "act", "dve", "pool", "sp")

    def __init__(self, nc, n_dma_sems=8):
        self.nc = nc
        self.streams = {e: [] for e in self.ENGS}
        self.sems = {e: nc.alloc_semaphore("s_" + e) for e in self.ENGS}
        self.cnt = {e: 0 for e in self.ENGS}
        self.known = {e: {} for e in self.ENGS}
        self.bufs = {}
        self.dma_sems = {}
        self.dma_rr = {}
        self.dma_val = {}
        self.semobj = {}
        for q in ("sp", "pool", "act"):
            self.dma_sems[q] = [nc.alloc_semaphore(f"d_{q}{i}") for i in range(n_dma_sems)]
            self.dma_rr[q] = 0
            for s in self.dma_sems[q]:
                self.semobj[s.name] = s
                self.dma_val[s.name] = 0
        for e in self.ENGS:
            self.semobj[self.sems[e].name] = self.sems[e]
        self.pending_pe = False

    def _need(self, eng, tok, waits):
        if tok is None:
            return
        sname, val = tok
        if self.known[eng].get(sname, 0) >= val:
            return
        if waits.get(sname, 0) < val:
            waits[sname] = val

    def _deps(self, eng, reads, writes, skip_writer=False):
        waits = {}
        for k in reads:
            st = self.bufs.get(k)
            if st is not None:
                self._need(eng, st[0], waits)
        for k in writes:
            st = self.bufs.get(k)
            if st is not None:
                if not skip_writer:
                    self._need(eng, st[0], waits)
                for tok in st[1].values():
                    self._need(eng, tok, waits)
        return waits

    def _commit(self, who, tok, reads, writes):
        for k in reads:
            st = self.bufs.setdefault(k, [None, {}])
            st[1][who + ":" + tok[0]] = tok
        for k in writes:
            self.bufs[k] = [tok, {}]

    def op(self, eng, fn, reads=(), writes=(), sig=True, accum=False):
        waits = self._deps(eng, reads, writes, skip_writer=(accum and eng == "pe"))
        if eng == "pe":
            waits.pop(self.sems["pe"].name, None)
        for s, v in waits.items():
            self.known[eng][s] = v
        if sig:
            self.cnt[eng] += 1
            tok = (self.sems[eng].name, self.cnt[eng])
            if eng == "pe":
                self.pending_pe = False
        else:
            assert eng == "pe"
            tok = (self.sems[eng].name, self.cnt[eng] + 1)
            self.pending_pe = True
        self.streams[eng].append((list(waits.items()), fn, (self.sems[eng], 1) if sig else None))
        self._commit(eng, tok, reads, writes)
        return tok

    def dma(self, q, fn, reads=(), writes=()):
        waits = self._deps(q, reads, writes)
        pool = self.dma_sems[q]
        s = pool[self.dma_rr[q] % len(pool)]
        self.dma_rr[q] += 1
        prev = self.dma_val[s.name]
        if prev > 0:
            self._need(q, (s.name, prev), waits)
        for sn, v in waits.items():
            self.known[q][sn] = v
        self.dma_val[s.name] = prev + 16
        tok = (s.name, prev + 16)
        self.streams[q].append((list(waits.items()), fn, (s, 16)))
        self._commit(q, tok, reads, writes)
        return tok

    def barrier(self):
        assert not self.pending_pe
        targets = [(self.sems[e].name, self.cnt[e]) for e in self.ENGS if self.cnt[e] > 0]
        targets += [(sn, v) for sn, v in self.dma_val.items() if v > 0]
        for e in self.ENGS:
            waits = {}
            for tok in targets:
                if e == "pe" and tok[0] == self.sems["pe"].name:
                    continue
                self._need(e, tok, waits)
            for sn, v in waits.items():
                self.known[e][sn] = v
            if waits:
                self.streams[e].append((list(waits.items()), None, None))
        self.bufs = {}

    def emit(self):
        assert not self.pending_pe
        nc = self.nc
        hmap = {"pe": "tensor", "act": "scalar", "dve": "vector", "pool": "gpsimd", "sp": "sync"}
        semobj = self.semobj
        with nc.Block() as block:
            for e in self.ENGS:
                def body(h, stream=self.streams[e]):
                    for waits, fn, inc in stream:
                        for sn, v in waits:
                            h.wait_ge(semobj[sn], v)
                        if fn is not None:
                            ins = fn(h)
                            if inc is not None:
                                ins.then_inc(inc[0], inc[1])
                getattr(block, hmap[e])(body)


class Arena:
    def __init__(self, nc, start=16640, limit=229376):
        self.nc = nc
        self.off = start
        self.limit = limit
        self.n = 0
        self.peak = 0

    def alloc(self, shape, dtype):
        nbytes = int(np.prod(shape[1:])) * (2 if dtype == BF16 else 4)
        nbytes = (nbytes + 63) // 64 * 64
        assert self.off + nbytes <= self.limit, ("SBUF overflow", self.off, nbytes)
        self.n += 1
        t = self.nc.alloc_sbuf_tensor_at(f"sb{self.n}", list(shape), dtype, offset=self.off)
        self.off += nbytes
        self.peak = max(self.peak, self.off)
        return t

    def mark(self):
        return self.off

    def release(self, m):
        if not getattr(self, "no_release", False):
            self.off = m


def _t5_bucket_table():
    d = np.arange(256)
    max_exact = 16
    nf = np.maximum(d, 1).astype(np.float32)
    large = max_exact + (np.log(nf / max_exact) / math.log(128 / max_exact) * (32 - max_exact)).astype(np.int32)
    large = np.minimum(large, 31)
    return np.where(d < max_exact, d, large)


def build(dbg=None):
    dbg = dbg or {}
    nc = bass.Bass("TRN2", target_bir_lowering=False)
    P = Prog(nc)
    A = Arena(nc)
    A.no_release = bool(dbg.get("no_release"))
    dbg_outs = []

    def din(name, shape):
        return nc.dram_tensor(name, list(shape), F32, kind="ExternalInput").ap()

    xT_d = din("xT", [D, S])
    memT_d = din("memT", [D, 256])
    w_in_d = din("w_in", [D, D_IN])
    w_uv_d = din("w_uv", [128, 8, 64])
    w_glu_d = din("w_glu", [512, 512])
    w_kv_d = din("w_mem_kv", [D, D])
    w_brd_d = din("w_br_dsa", [512, D])
    w_brs_d = din("w_br_s5", [512, D])
    w_brx_d = din("w_br_cross", [512, D])
    w_out_d = din("w_out", [D, D])
    w_g_d = din("w_ffn_gate", [D, D_FF])
    w_u_d = din("w_ffn_up", [D, D_FF])
    w_d_d = din("w_ffn_down", [D_FF, D])
    gvec_d = din("gvec", [128, 32])
    gkv_bc_d = din("gkv_bc", [128, 128])
    t5_d = din("t5", [128, 2, 8, 128])
    rb31_d = din("rb31", [128, 8])
    cm_d = din("cmats", [128, 5, 128])
    s5a_d = din("s5a", [128, 3, 32])
    s5b_d = din("s5b", [128, 4, 32, 16])
    dsk_d = din("dsk", [128, 32])
    outT_d = nc.dram_tensor("outT", [D, S], F32, kind="ExternalOutput").ap()
    scr_u = nc.dram_tensor("scr_u", [512, S], BF16).ap()
    scr_y = nc.dram_tensor("scr_y", [512, S], BF16).ap()

    ps = [nc.alloc_psum_tensor(f"ps{i}", [128, 512], F32) for i in range(6)]
    psTs = [nc.alloc_psum_tensor(f"psT{i}", [128, 1024], BF16) for i in range(2)]

    w_in_v = w_in_d.rearrange("(kc p) n -> p kc n", p=128)

    def dump(name, ap, shape, dt=F32):
        t = nc.dram_tensor("dbg_" + name, list(shape), dt, kind="ExternalOutput").ap()
        dbg_outs.append(("dbg_" + name))
        return t

    cm_f = A.alloc([128, 5, 128], F32)
    ident_f = cm_f[:, 0, :]
    causal_f = cm_f[:, 1, :]
    tmask_f = cm_f[:, 2, :]
    swap_f = cm_f[:, 3, :]
    cm_b = A.alloc([128, 5, 128], BF16)
    ident_b = cm_b[:, 0, :]
    ones_b = cm_b[:, 4, :]
    gvec = A.alloc([128, 32], F32)
    rx_bc = A.alloc([128, S], F32)
    rx_tok = A.alloc([128, NB], F32)
    off_o = A.mark()
    o_dsa = A.alloc([128, 4, S], BF16)

    P.dma("sp", lambda h: h.dma_start(out=cm_f[:], in_=cm_d), writes=["cm_f"])
    P.dma("pool", lambda h: h.dma_start(out=cm_b[:], in_=cm_d), writes=["cm_b"])
    P.dma("sp", lambda h: h.dma_start(out=gvec[:], in_=gvec_d), writes=["gvec"])
    CONST = ["cm_f", "cm_b", "gvec"]

    bank_rr = [0]

    def nb(lo=0, hi=6):
        b = lo + bank_rr[0] % (hi - lo)
        bank_rr[0] += 1
        return b

    def load_w(dst, src3, key, cols, q="pool"):
        kc_n = dst.shape[1]
        c0, c1 = cols
        for kc in range(kc_n):
            P.dma(q, (lambda h, kc=kc: h.dma_start(out=dst[:, kc, 0:c1 - c0], in_=src3[:, kc, c0:c1])),
                  writes=[(key, kc)])

    def build_xg(xg, with_stats, release=False):
        m = A.mark()
        xst = [A.alloc([128, S], F32) for _ in range(2)]
        sq = [A.alloc([128, S], BF16) for _ in range(2)]
        for kc in range(8):
            b = kc % 2
            P.dma("sp", (lambda h, kc=kc, b=b: h.dma_start(out=xst[b][:], in_=xT_d[kc * 128:(kc + 1) * 128, :])),
                  writes=[("xst", b)])
            if with_stats:
                P.op("act", (lambda h, b=b: h.activation(out=sq[b][:], in_=xst[b][:], func=AF.Square)),
                     reads=[("xst", b)], writes=[("sq", b)])
                for tc in range(4):
                    P.op("pe", (lambda h, b=b, tc=tc, kc=kc: h.matmul(ps[tc][:], lhsT=ones_b, rhs=sq[b][:, tc * 512:(tc + 1) * 512],
                                                                       start=(kc == 0), stop=(kc == 7))),
                         reads=[("sq", b), "cm_b"], writes=[("ps", tc)], sig=(tc == 3), accum=(kc > 0))
            P.op("dve", (lambda h, kc=kc, b=b: h.tensor_scalar(out=xg[:, kc, :], in0=xst[b][:], scalar1=gvec[:, kc:kc + 1],
                                                                scalar2=None, op0=ALU.mult)),
                 reads=[("xst", b), "gvec"], writes=[("xg", kc)])
        if with_stats:
            for tc in range(4):
                P.op("act", (lambda h, tc=tc: h.activation(out=rx_bc[:, tc * 512:(tc + 1) * 512], in_=ps[tc][:], func=AF.Sqrt,
                                                           scale=1.0 / D, bias=EPS)),
                     reads=[("ps", tc)], writes=[("rxs", tc)])
                P.op("dve", (lambda h, tc=tc: h.reciprocal(out=rx_bc[:, tc * 512:(tc + 1) * 512], in_=rx_bc[:, tc * 512:(tc + 1) * 512])),
                     reads=[("rxs", tc)], writes=[("rx_bc", tc)])
            for tt in range(NB):
                P.op("pe", (lambda h, tt=tt: h.matmul(ps[4][:, tt:tt + 1], lhsT=rx_bc[:, tt * 128:(tt + 1) * 128], rhs=ident_f[:, 0:1],
                                                      start=True, stop=True)),
                     reads=[("rx_bc", tt // 4), "cm_f"], writes=[("ps", 4)], sig=(tt == NB - 1))
            P.op("act", lambda h: h.copy(out=rx_tok[:], in_=ps[4][:, 0:NB]), reads=[("ps", 4)], writes=["rx_tok"])
        if release:
            A.release(m)

    XG = [("xg", kc) for kc in range(8)]
    RX = [("rx_bc", tc) for tc in range(4)]

    def proj_fm(xg, wb, wkey, c0, ncol, tc, bank, rhs_view=None):
        for kc in range(8):
            rhs = xg[:, kc, tc * 512:(tc + 1) * 512] if rhs_view is None else rhs_view(kc, tc)
            P.op("pe", (lambda h, kc=kc, rhs=rhs: h.matmul(ps[bank][0:ncol, :], lhsT=wb[:, kc, c0:c0 + ncol], rhs=rhs,
                                                            start=(kc == 0), stop=(kc == 7))),
                 reads=[(wkey, kc), ("xg", kc)], writes=[("ps", bank)], sig=(kc == 7), accum=(kc > 0))

    def head_norm(bank, bank2, gcol, dst, tmp_y, tmp_sq, tmp_sd, rx_ap, extra_reads, dst_key):
        n = dst.shape[-1]
        if rx_ap is None:
            P.op("act", lambda h: h.copy(out=tmp_y, in_=ps[bank][:, 0:n]), reads=[("ps", bank)] + extra_reads, writes=["hn_y"])
        else:
            P.op("dve", lambda h: h.tensor_tensor(out=tmp_y, in0=ps[bank][:, 0:n], in1=rx_ap, op=ALU.mult),
                 reads=[("ps", bank)] + extra_reads, writes=["hn_y"])
        P.op("act", lambda h: h.activation(out=tmp_sq, in_=tmp_y, func=AF.Square), reads=["hn_y"], writes=["hn_sq"])
        P.op("pe", lambda h: h.matmul(ps[bank2][:, 0:n], lhsT=ones_b, rhs=tmp_sq, start=True, stop=True),
             reads=["hn_sq", "cm_b"], writes=[("ps", bank2)])
        P.op("act", lambda h: h.activation(out=tmp_sd, in_=ps[bank2][:, 0:n], func=AF.Sqrt, scale=1.0 / 128, bias=EPS),
             reads=[("ps", bank2)], writes=["hn_sd"])
        P.op("dve", lambda h: h.reciprocal(out=tmp_sd, in_=tmp_sd), reads=["hn_sd"], writes=["hn_sd"])
        P.op("dve", lambda h: h.scalar_tensor_tensor(out=dst, in0=tmp_y, scalar=gvec[:, gcol:gcol + 1], in1=tmp_sd,
                                                     op0=ALU.mult, op1=ALU.mult),
             reads=["hn_y", "hn_sd", "gvec"], writes=[dst_key])

    base_mark = A.mark()

    xg = A.alloc([128, 8, S], BF16)
    build_xg(xg, True, release=True)
    P.op("dve", lambda h: h.tensor_scalar(out=gvec[:, 27:28], in0=gvec[:, 24:25], scalar1=128 ** -0.5, scalar2=None, op0=ALU.mult),
         reads=["gvec"], writes=["gvec"])
    P.op("dve", lambda h: h.tensor_scalar(out=gvec[:, 28:29], in0=gvec[:, 26:27], scalar1=128 ** -0.5, scalar2=None, op0=ALU.mult),
         reads=["gvec"], writes=["gvec"])

    def early(tag, tensors):
        if dbg.get("stop") != tag:
            return False
        for i, (t, shape, dt, keys) in enumerate(tensors):
            d = dump(f"{tag}{i}", None, shape, dt)
            P.dma("sp", (lambda h, d=d, t=t: h.dma_start(out=d, in_=t)), reads=keys, writes=[f"dbg_{tag}{i}"])
        P.barrier()
        P.emit()
        return True

    if early("xg", [(xg[:], [128, 8, S], BF16, XG), (rx_bc[:], [128, S], F32, RX), (rx_tok[:], [128, NB], F32, ["rx_tok"])]):
        return nc, dbg_outs

    def mb_base(m):
        return 128 * (m * (m + 1) // 2)

    MBT = A.alloc([128, 128 * 136], BF16)
    mA = A.mark()

    wbi = A.alloc([128, 8, 584], BF16)
    wki = A.alloc([128, 8, 128], BF16)
    qiT = A.alloc([128, 4, S], BF16)
    kiT = A.alloc([128, S], BF16)
    widx = A.alloc([128, NB, 8], F32)
    load_w(wbi, w_in_v, "wbi", (1152, 1736))
    for kc in range(8):
        P.dma("pool", (lambda h, kc=kc: h.dma_start(out=wki[:, kc, 0:64], in_=w_in_v[:, kc, 1664:1728])), writes=[("wki", kc)])
        P.dma("pool", (lambda h, kc=kc: h.dma_start(out=wki[:, kc, 64:128], in_=w_in_v[:, kc, 1664:1728])), writes=[("wki", kc)])
    for j in range(4):
        for tc in range(4):
            b = nb()
            proj_fm(xg, wbi, "wbi", j * 128, 128, tc, b)
            P.op("dve", (lambda h, b=b, j=j, tc=tc: h.tensor_tensor(out=qiT[:, j, tc * 512:(tc + 1) * 512], in0=ps[b][:],
                                                                      in1=rx_bc[:, tc * 512:(tc + 1) * 512], op=ALU.mult)),
                 reads=[("ps", b), ("rx_bc", tc)], writes=[("qiT", j, tc)])
    for tc in range(4):
        b = nb()
        proj_fm(xg, wki, "wki", 0, 128, tc, b)
        P.op("dve", (lambda h, b=b, tc=tc: h.tensor_tensor(out=kiT[:, tc * 512:(tc + 1) * 512], in0=ps[b][:],
                                                            in1=rx_bc[:, tc * 512:(tc + 1) * 512], op=ALU.mult)),
             reads=[("ps", b), ("rx_bc", tc)], writes=[("kiT", tc)])
    bw = nb()
    for tt in range(NB):
        for kc in range(8):
            P.op("pe", (lambda h, tt=tt, kc=kc: h.matmul(ps[bw][:, tt * 8:(tt + 1) * 8], lhsT=xg[:, kc, tt * 128:(tt + 1) * 128],
                                                         rhs=wbi[:, kc, 576:584], start=(kc == 0), stop=(kc == 7))),
                 reads=[("wbi", kc), ("xg", kc)], writes=[("ps", bw)], sig=(kc == 7 and tt == NB - 1), accum=not (kc == 0 and tt == 0))
    P.op("dve", lambda h: h.tensor_tensor(out=widx[:], in0=ps[bw][:, 0:NB * 8].rearrange("p (a b) -> p a b", b=8),
                                          in1=rx_tok[:].unsqueeze(2).to_broadcast([128, NB, 8]), op=ALU.mult),
         reads=[("ps", bw), "rx_tok"], writes=["widx"])

    if early("proj", [(qiT[:], [128, 4, S], BF16, [("qiT", j, tc) for j in range(4) for tc in range(4)]),
                      (kiT[:], [128, S], BF16, [("kiT", tc) for tc in range(4)]), (widx[:], [128, NB, 8], F32, ["widx"])]):
        return nc, dbg_outs
    acc = [A.alloc([128, S], F32) for _ in range(2)]
    work = A.alloc([128, S], F32)
    rl = [A.alloc([128, 512], F32) for _ in range(3)]
    mbt = [A.alloc([128, S], BF16) for _ in range(2)]
    m8 = A.alloc([128, 8], F32)
    thr0 = A.alloc([128, 1], F32)
    bs_lo = A.alloc([128, 1], F32)
    bs_w = A.alloc([128, 1], F32)
    bs_mid = A.alloc([128, 1], F32)
    bs_cnt = A.alloc([128, 1], F32)
    bs_t = A.alloc([128, 1], F32)
    bs_ck = A.alloc([128, 24], F32)
    bs_hk = A.alloc([128, 24], F32)
    for k_ in range(24):
        P.op("dve", (lambda h, k_=k_: h.memset(bs_ck[:, k_:k_ + 1], 2.0 ** (-(k_ + 1)))), writes=["bs_ck"])
    P.op("dve", lambda h: h.memset(thr0[:], -1.0e29), writes=["thr0"])
    rli = 0
    for m in dbg.get("m_list", range(dbg.get("m_max", NB))):
        ab = m % 2
        n = 128 * (m + 1)
        nsc = (n + 511) // 512
        for sc in range(nsc):
            w = min(512, n - 512 * sc)
            if dbg.get("skip_sc1") and sc == 1:
                continue
            for hh in range(8):
                j, half = hh // 2, hh % 2
                b = nb()
                r = rli % 3
                rli += 1
                P.op("pe", (lambda h, b=b, j=j, half=half, m=m, sc=sc, w=w: h.matmul(
                    ps[b][:, 0:w], lhsT=qiT[64 * half:64 * half + 64, j, m * 128:(m + 1) * 128],
                    rhs=kiT[64 * half:64 * half + 64, sc * 512:sc * 512 + w], start=True, stop=True)),
                    reads=[("qiT", j, m // 4), ("kiT", sc)], writes=[("ps", b)])
                P.op("act", (lambda h, b=b, r=r, w=w: h.activation(out=rl[r][:, 0:w], in_=ps[b][:, 0:w], func=AF.Relu)),
                     reads=[("ps", b)], writes=[("rl", r)])
                if hh == 0:
                    P.op("dve", (lambda h, r=r, w=w, ab=ab, sc=sc, m=m: h.tensor_scalar(
                        out=acc[ab][:, sc * 512:sc * 512 + w], in0=rl[r][:, 0:w], scalar1=widx[:, m, 0:1], scalar2=None, op0=ALU.mult)),
                        reads=[("rl", r), "widx"], writes=[("acc", ab)])
                else:
                    P.op("dve", (lambda h, r=r, w=w, ab=ab, sc=sc, m=m, hh=hh: h.scalar_tensor_tensor(
                        out=acc[ab][:, sc * 512:sc * 512 + w], in0=rl[r][:, 0:w], scalar=widx[:, m, hh:hh + 1],
                        in1=acc[ab][:, sc * 512:sc * 512 + w], op0=ALU.mult, op1=ALU.add)),
                        reads=[("rl", r), "widx", ("acc", ab)], writes=[("acc", ab)])
        if not dbg.get("skip_diag"):
            P.op("dve", (lambda h, ab=ab, m=m: h.tensor_tensor(out=acc[ab][:, m * 128:(m + 1) * 128], in0=acc[ab][:, m * 128:(m + 1) * 128],
                                                              in1=causal_f, op=ALU.add)),
                 reads=[("acc", ab), "cm_f"], writes=[("acc", ab)])
        if "score" in dbg and m == NB - 1:
            t = dump("score", None, [128, S])
            P.dma("sp", (lambda h, t=t, ab=ab: h.dma_start(out=t, in_=acc[ab][:])), reads=[("acc", ab)], writes=["dbg_score"])
        if m >= 2 and not dbg.get("no_topk"):
            KB = 20
            nv = m * 128
            P.op("dve", (lambda h, ab=ab, n=n: h.max(out=m8[:], in_=acc[ab][:, 0:n])), reads=[("acc", ab)], writes=["m8"])
            P.op("dve", (lambda h, ab=ab, nv=nv: h.tensor_reduce(out=bs_lo[:], in_=acc[ab][:, 0:nv], axis=mybir.AxisListType.X, op=ALU.min)),
                 reads=[("acc", ab)], writes=["bs_lo"])
            P.op("dve", lambda h: h.tensor_tensor(out=bs_w[:], in0=m8[:, 0:1], in1=bs_lo[:], op=ALU.subtract), reads=["m8", "bs_lo"], writes=["bs_w"])
            P.op("dve", lambda h: h.tensor_scalar(out=bs_hk[:], in0=bs_ck[:], scalar1=bs_w[:], scalar2=None, op0=ALU.mult),
                 reads=["bs_w", "bs_ck"], writes=["bs_hk"])
            P.op("dve", lambda h: h.tensor_tensor(out=bs_mid[:], in0=bs_lo[:], in1=bs_hk[:, 0:1], op=ALU.add), reads=["bs_lo", "bs_hk"], writes=["bs_mid"])
            for k_ in range(KB):
                P.op("dve", (lambda h, ab=ab, n=n: h.tensor_scalar(out=work[:, 0:n], in0=acc[ab][:, 0:n], scalar1=bs_mid[:], scalar2=None,
                                                                  op0=ALU.is_ge, op1=ALU.add, accum_out=bs_cnt[:])),
                     reads=[("acc", ab), "bs_mid"], writes=["work", "bs_cnt"])
                P.op("dve", lambda h: h.tensor_scalar(out=bs_t[:], in0=bs_cnt[:], scalar1=255.5, scalar2=-0.5, op0=ALU.is_ge, op1=ALU.add),
                     reads=["bs_cnt"], writes=["bs_t"])
                P.op("dve", (lambda h, k_=k_: h.scalar_tensor_tensor(out=bs_mid[:], in0=bs_t[:], scalar=bs_hk[:, k_:k_ + 1], in1=bs_mid[:],
                                                                     op0=ALU.mult, op1=ALU.add)),
                     reads=["bs_t", "bs_hk", "bs_mid"], writes=["bs_mid"])
            P.op("dve", (lambda h, KB=KB: h.tensor_tensor(out=m8[:, 7:8], in0=bs_mid[:], in1=bs_hk[:, KB:KB + 1], op=ALU.subtract)),
                 reads=["bs_mid", "bs_hk", "m8"], writes=["m8"])
            thr = m8[:, 7:8]
            thrk = "m8"
        else:
            thr = thr0[:]
            thrk = "thr0"
        if not dbg.get("skip_mb"):
            P.op("dve", (lambda h, ab=ab, n=n, thr=thr: h.tensor_scalar(out=mbt[ab][:, 0:n], in0=acc[ab][:, 0:n], scalar1=thr, scalar2=MBIAS,
                                                                        op0=ALU.is_lt, op1=ALU.mult)),
                 reads=[("acc", ab), thrk], writes=[("mbt", ab)])
        for j0 in ([] if dbg.get("no_tr") else range(0, m + 1, 4)):
            jn = min(4, m + 1 - j0)
            tb = (j0 // 4) % 2
            for jj in range(jn):
                j = j0 + jj
                P.op("pe", (lambda h, ab=ab, j=j, jj=jj, tb=tb: h.transpose(out=psTs[tb][:, jj * 128:(jj + 1) * 128],
                                                                         in_=mbt[ab][:, j * 128:(j + 1) * 128], identity=ident_b)),
                     reads=[("mbt", ab), "cm_b"], writes=[("psT", tb)], sig=(jj == jn - 1), accum=(jj > 0))
            P.op("act", (lambda h, m=m, j0=j0, jn=jn, tb=tb: h.copy(out=MBT[:, mb_base(m) + j0 * 128: mb_base(m) + (j0 + jn) * 128],
                                                                 in_=psTs[tb][:, 0:jn * 128])),
                 reads=[("psT", tb)], writes=[("MBT", m)])
    if early("scores", [(MBT[:], [128, 128 * 136], BF16, [("MBT", m) for m in dbg.get("m_list", range(dbg.get("m_max", NB)))]),
                        (acc[0][:], [128, S], F32, [("acc", 0)]), (acc[1][:], [128, S], F32, [("acc", 1)]), (m8[:], [128, 8], F32, ["m8"])]):
        return nc, dbg_outs
    if "mbt" in dbg:
        t = dump("mbt", None, [128, 128 * 136], BF16)
        P.dma("sp", (lambda h, t=t: h.dma_start(out=t, in_=MBT[:])), reads=[("MBT", m) for m in range(NB)], writes=["dbg_mbt"])

    P.barrier()
    A.release(mA)

    wbq = [A.alloc([128, 8, 512], BF16) for _ in range(2)]
    wbc = A.alloc([128, 8, 128], BF16)
    wuv = A.alloc([128, 8, 64], BF16)
    qT = A.alloc([128, 8, S], BF16)
    c_tok = A.alloc([128, NB, 128], BF16)
    cT = A.alloc([128, S], BF16)
    t5f = A.alloc([128, 2, 8, 128], F32)
    t5b = A.alloc([128, 2, 8, 128], BF16)
    rb31 = A.alloc([128, 8], F32)
    gkv_bc = A.alloc([128, 128], F32)
    tmp_y = A.alloc([128, 512], F32)
    tmp_sq = A.alloc([128, 512], BF16)
    tmp_sd = A.alloc([128, 512], F32)
    ss1 = A.alloc([128, 2], F32)
    for i in range(2):
        load_w(wbq[i], w_in_v, ("wbq", i), (512 * i, 512 * i + 512))
    load_w(wbc, w_in_v, "wbc", (1024, 1152))
    P.dma("pool", lambda h: h.dma_start(out=wuv[:], in_=w_uv_d), writes=["wuv"])
    P.dma("sp", lambda h: h.dma_start(out=t5f[:], in_=t5_d), writes=["t5f"])
    P.dma("sp", lambda h: h.dma_start(out=rb31[:], in_=rb31_d), writes=["rb31"])
    P.dma("sp", lambda h: h.dma_start(out=gkv_bc[:], in_=gkv_bc_d), writes=["gkv_bc"])
    P.op("dve", lambda h: h.tensor_tensor(out=t5b[:], in0=t5f[:], in1=rb31[:].unsqueeze(1).unsqueeze(3).to_broadcast([128, 2, 8, 128]),
                                          op=ALU.subtract),
         reads=["t5f", "rb31"], writes=["t5b"])
    for hh in range(8):
        for tc in range(4):
            b = nb(0, 3)
            proj_fm(xg, wbq[hh // 4], ("wbq", hh // 4), (hh % 4) * 128, 128, tc, b)
            head_norm(b, 3 + (tc % 2), 27, qT[:, hh, tc * 512:(tc + 1) * 512], tmp_y[:], tmp_sq[:], tmp_sd[:],
                      rx_bc[:, tc * 512:(tc + 1) * 512], [("rx_bc", tc)], ("qT", hh, tc))
    for tt in range(NB):
        b = nb(0, 3)
        for kc in range(8):
            P.op("pe", (lambda h, b=b, tt=tt, kc=kc: h.matmul(ps[b][:, 0:128], lhsT=xg[:, kc, tt * 128:(tt + 1) * 128], rhs=wbc[:, kc, :],
                                                              start=(kc == 0), stop=(kc == 7))),
                 reads=[("wbc", kc), ("xg", kc)], writes=[("ps", b)], sig=(kc == 7), accum=(kc > 0))
        P.op("act", (lambda h, b=b, tt=tt: h.activation(out=tmp_y[:, 0:128], in_=ps[b][:, 0:128], func=AF.Copy, scale=rx_tok[:, tt:tt + 1])),
             reads=[("ps", b), "rx_tok"], writes=["hn_y"])
        P.op("act", (lambda h: h.activation(out=tmp_y[:, 128:256], in_=tmp_y[:, 0:128], func=AF.Square, accum_out=ss1[:, 0:1])),
             reads=["hn_y"], writes=["c_ss", "hn_y"])
        P.op("act", (lambda h: h.activation(out=ss1[:, 1:2], in_=ss1[:, 0:1], func=AF.Sqrt, scale=1.0 / 128, bias=EPS)),
             reads=["c_ss"], writes=["c_sd"])
        P.op("dve", (lambda h: h.reciprocal(out=ss1[:, 1:2], in_=ss1[:, 1:2])), reads=["c_sd"], writes=["c_sd"])
        P.op("dve", (lambda h, tt=tt: h.scalar_tensor_tensor(out=c_tok[:, tt, :], in0=tmp_y[:, 0:128], scalar=ss1[:, 1:2], in1=gkv_bc[:],
                                                             op0=ALU.mult, op1=ALU.mult)),
             reads=["hn_y", "c_sd", "gkv_bc"], writes=[("c_tok", tt)])
    for t0 in range(0, NB, 4):
        tb = (t0 // 4) % 2
        for jj in range(4):
            tt = t0 + jj
            P.op("pe", (lambda h, tt=tt, jj=jj, tb=tb: h.transpose(out=psTs[tb][:, jj * 128:(jj + 1) * 128],
                                                                 in_=c_tok[:, tt, :], identity=ident_b)),
                 reads=[("c_tok", tt), "cm_b"], writes=[("psT", tb)], sig=(jj == 3), accum=(jj > 0))
        P.op("act", (lambda h, t0=t0, tb=tb: h.copy(out=cT[:, t0 * 128:(t0 + 4) * 128], in_=psTs[tb][:, 0:512])),
             reads=[("psT", tb)], writes=[("cT", t0 // 4)])
    if "qT" in dbg:
        t = dump("qT", None, [128, 8, S], BF16)
        P.dma("sp", (lambda h, t=t: h.dma_start(out=t, in_=qT[:])), reads=[("qT", a, b_) for a in range(8) for b_ in range(4)], writes=["dbg_qT"])
        t2 = dump("cT", None, [128, S], BF16)
        P.dma("sp", (lambda h, t2=t2: h.dma_start(out=t2, in_=cT[:])), reads=[("cT", i) for i in range(4)], writes=["dbg_cT"])

    PT = [A.alloc([128, 512], BF16) for _ in range(4)]
    rden = A.alloc([128, 512], F32)
    onT = [A.alloc([128, 512], BF16) for _ in range(2)]
    BU = 5
    accO = [ps[3][:], ps[3][:]]
    accD = [ps[4][:], ps[4][:]]
    accK = [(("ps", 3), ("ps", 4)), (("ps", 3), ("ps", 4))]
    its = []
    for c in range(4):
        for hh in range(8):
            nj = 4 * c + 4
            for j in range(nj):
                its.append((c, hh, j, nj))
    LAG = 2
    deferred = []

    def front(i):
        c, hh, j, nj = its[i]
        t_lo = max(512 * c, 128 * j)
        co = t_lo - 512 * c
        bs = i % 3
        pb = i % 4
        P.op("pe", (lambda h: h.matmul(ps[bs][:, co:512], lhsT=cT[:, j * 128:(j + 1) * 128], rhs=qT[:, hh, t_lo:512 * c + 512], start=True, stop=False)),
             reads=[("cT", j // 4), ("qT", hh, c)], writes=[("ps", bs)], sig=False)
        adds = []
        for m in range(4 * c, 4 * c + 4):
            if m >= j:
                adds.append(((m - 4 * c) * 128, MBT[:, mb_base(m) + j * 128: mb_base(m) + (j + 1) * 128], ("MBT", m)))
        for diff in (0, 1):
            m = j + diff
            if 4 * c <= m <= 4 * c + 3:
                adds.append(((m - 4 * c) * 128, t5b[:, diff, hh, :], "t5b"))
        for ai, (col, rhs, key) in enumerate(adds):
            last = ai == len(adds) - 1
            P.op("pe", (lambda h, col=col, rhs=rhs, last=last: h.matmul(ps[bs][:, col:col + 128], lhsT=ident_b, rhs=rhs, start=False, stop=last)),
                 reads=[key, "cm_b"], writes=[("ps", bs)], sig=last, accum=True)
        P.op("act", (lambda h: h.activation(out=PT[pb][:, 0:512 - co], in_=ps[bs][:, co:512], func=AF.Exp, bias=rb31[:, hh:hh + 1], scale=1.0)),
             reads=[("ps", bs), "rb31"], writes=[("PT", pb)])

    def back(i):
        c, hh, j, nj = its[i]
        t_lo = max(512 * c, 128 * j)
        co = t_lo - 512 * c
        pb = i % 4
        hidx = c * 8 + hh
        ab_ = hidx % 2
        aO, aD = accO[ab_], accD[ab_]
        kO, kD = accK[ab_]
        P.op("pe", (lambda h: h.matmul(aO[:, co:512], lhsT=c_tok[:, j, :], rhs=PT[pb][:, 0:512 - co], start=(j == 0), stop=(j == nj - 1))),
             reads=[("c_tok", j), ("PT", pb)], writes=[kO], sig=False, accum=(j > 0))
        P.op("pe", (lambda h: h.matmul(aD[:, co:512], lhsT=ones_b, rhs=PT[pb][:, 0:512 - co], start=(j == 0), stop=(j == nj - 1))),
             reads=["cm_b", ("PT", pb)], writes=[kD], sig=True, accum=(j > 0))
        if j == nj - 1:
            ob = hh % 2
            P.op("dve", lambda h: h.reciprocal(out=rden[:], in_=aD), reads=[kD], writes=["rden"])
            P.op("dve", (lambda h: h.tensor_tensor(out=onT[ob][:], in0=aO, in1=rden[:], op=ALU.mult)),
                 reads=[kO, "rden"], writes=[("onT", ob)])

            def epilogue():
                P.op("pe", (lambda h: h.matmul(ps[BU][64 * (hh % 2):64 * (hh % 2) + 64, :], lhsT=wuv[:, hh, :], rhs=onT[ob][:], start=True, stop=True)),
                     reads=["wuv", ("onT", ob)], writes=[("ps", BU, hh % 2)])
                if hh % 2 == 1:
                    P.op("act", (lambda h: h.copy(out=o_dsa[:, hh // 2, c * 512:(c + 1) * 512], in_=ps[BU][:])),
                         reads=[("ps", BU, 0), ("ps", BU, 1)], writes=[("o_dsa", hh // 2, c)])
            deferred.append([2, epilogue])

    for i in range(len(its) + LAG):
        if i < len(its):
            front(i)
        if i - LAG >= 0:
            back(i - LAG)
            for dct in list(deferred):
                dct[0] -= 1
                if dct[0] <= 0:
                    dct[1]()
                    deferred.remove(dct)
    for dct in deferred:
        dct[1]()
    if "o_dsa" in dbg:
        t = dump("o_dsa", None, [128, 4, S], BF16)
        P.dma("sp", (lambda h, t=t: h.dma_start(out=t, in_=o_dsa[:])), reads=[("o_dsa", a, c) for a in range(4) for c in range(4)],
              writes=["dbg_o_dsa"])

    P.barrier()
    A.release(base_mark)
    o_s5 = A.alloc([128, 4, S], BF16)
    o_x = A.alloc([128, 4, S], BF16)
    base_mark = A.mark()
    if dbg.get("od_early"):
        t = dump("od0", None, [128, 4, S], BF16)
        P.dma("sp", (lambda h, t=t: h.dma_start(out=t, in_=o_dsa[:])), writes=["dbg_od0"])
    if dbg.get("stop") == "dsa":
        P.barrier()
        P.emit()
        return nc, dbg_outs


    xgX = A.alloc([128, 8, S], BF16)
    build_xg(xgX, False)
    wbx = A.alloc([128, 8, 512], BF16)
    wbu = A.alloc([128, 8, 512], BF16)
    wkv = A.alloc([128, 8, 1024], BF16)
    memf = A.alloc([128, 8, 256], F32)
    msq = A.alloc([128, 8, 256], BF16)
    memn = A.alloc([128, 8, 256], BF16)
    rm = A.alloc([128, 256], F32)
    khT = A.alloc([128, 4, 256], BF16)
    vtok = A.alloc([128, 2, 512], BF16)
    qxT = A.alloc([128, 4, S], BF16)
    ust = [A.alloc([128, 512], BF16) for _ in range(2)]
    tyX = A.alloc([128, 512], F32)
    tqX = A.alloc([128, 512], BF16)
    tdX = A.alloc([128, 512], F32)
    PTx = [A.alloc([128, 512], BF16) for _ in range(3)]
    rdx = A.alloc([128, 512], F32)
    w_kv_v = w_kv_d.rearrange("(kc p) n -> p kc n", p=128)
    if not dbg.get("no_wbx"):
        load_w(wbx, w_in_v, "wbx", (2248, 2760))
    if not dbg.get("no_wbu"):
        load_w(wbu, w_in_v, "wbu", (1736, 2248))
    for i in range(2):
        if not dbg.get("no_wkv"):
            load_w(wkv[:, :, i * 512:(i + 1) * 512], w_kv_v, ("wkv", i), (i * 512, (i + 1) * 512))
    P.dma("sp", lambda h: h.dma_start(out=memf[:], in_=memT_d.rearrange("(kc p) m -> p kc m", p=128)), writes=["memf"])
    if dbg.get("stop") == "x0":
        P.barrier()
        t = dump("od", None, [128, 4, S], BF16)
        P.dma("sp", (lambda h, t=t: h.dma_start(out=t, in_=o_dsa[:])), writes=["dbg_od"])
        P.barrier()
        P.emit()
        return nc, dbg_outs
    P.op("act", lambda h: h.activation(out=msq[:], in_=memf[:], func=AF.Square), reads=["memf"], writes=["msq"])
    bm = nb(0, 3)
    for kc in range(8):
        P.op("pe", (lambda h, kc=kc: h.matmul(ps[bm][:, 0:256], lhsT=ones_b, rhs=msq[:, kc, :], start=(kc == 0), stop=(kc == 7))),
             reads=["msq", "cm_b"], writes=[("ps", bm)], sig=(kc == 7), accum=(kc > 0))
    P.op("act", lambda h: h.activation(out=rm[:], in_=ps[bm][:, 0:256], func=AF.Sqrt, scale=1.0 / D, bias=EPS), reads=[("ps", bm)], writes=["rm"])
    P.op("dve", lambda h: h.reciprocal(out=rm[:], in_=rm[:]), reads=["rm"], writes=["rm"])
    for kc in range(8):
        P.op("dve", (lambda h, kc=kc: h.scalar_tensor_tensor(out=memn[:, kc, :], in0=memf[:, kc, :], scalar=gvec[:, 8 + kc:9 + kc], in1=rm[:],
                                                             op0=ALU.mult, op1=ALU.mult)),
             reads=["memf", "rm", "gvec"], writes=[("memn", kc)])
    for hh in range(4):
        b = nb(0, 3)
        for kc in range(8):
            P.op("pe", (lambda h, b=b, hh=hh, kc=kc: h.matmul(ps[b][:, 0:256], lhsT=wkv[:, kc, hh * 128:(hh + 1) * 128], rhs=memn[:, kc, :],
                                                              start=(kc == 0), stop=(kc == 7))),
                 reads=[(("wkv", 0), kc), ("memn", kc)], writes=[("ps", b)], sig=(kc == 7), accum=(kc > 0))
        head_norm(b, 3 + (hh % 2), 28, khT[:, hh, :], tyX[:, 0:256], tqX[:, 0:256], tdX[:, 0:256], None, [], ("khT", hh))
    for mb in range(2):
        b = nb(0, 3)
        for kc in range(8):
            P.op("pe", (lambda h, b=b, mb=mb, kc=kc: h.matmul(ps[b][:], lhsT=memn[:, kc, mb * 128:(mb + 1) * 128], rhs=wkv[:, kc, 512:1024],
                                                              start=(kc == 0), stop=(kc == 7))),
                 reads=[(("wkv", 1), kc), ("memn", kc)], writes=[("ps", b)], sig=(kc == 7), accum=(kc > 0))
        P.op("act", (lambda h, b=b, mb=mb: h.copy(out=vtok[:, mb, :], in_=ps[b][:])), reads=[("ps", b)], writes=[("vtok", mb)])
    for hh in range(4):
        for tc in range(4):
            b = nb(0, 3)
            proj_fm(xgX, wbx, "wbx", hh * 128, 128, tc, b)
            head_norm(b, 3 + (tc % 2), 25, qxT[:, hh, tc * 512:(tc + 1) * 512], tyX[:], tqX[:], tdX[:],
                      rx_bc[:, tc * 512:(tc + 1) * 512], [("rx_bc", tc)], ("qxT", hh, tc))
    ui = 0
    for ch in range(4):
        for tcp in range(4):
            b = nb(0, 3)
            ub = ui % 2
            ui += 1
            proj_fm(xgX, wbu, "wbu", ch * 128, 128, tcp, b,
                    rhs_view=(lambda kc, tcp: xgX[:, kc, :].rearrange("p (b j) -> p j b", j=8)[:, 2 * tcp:2 * tcp + 2, :]))
            P.op("dve", (lambda h, b=b, ub=ub, tcp=tcp: h.tensor_tensor(
                out=ust[ub][:].rearrange("p (j b) -> p j b", j=2), in0=ps[b][:].rearrange("p (j b) -> p j b", j=2),
                in1=rx_bc[:].rearrange("p (b j) -> p j b", j=8)[:, 2 * tcp:2 * tcp + 2, :], op=ALU.mult)),
                reads=[("ps", b)] + RX, writes=[("ust", ub)])
            P.dma("sp", (lambda h, ub=ub, ch=ch, tcp=tcp: h.dma_start(out=scr_u[ch * 128:(ch + 1) * 128, tcp * 512:(tcp + 1) * 512], in_=ust[ub][:])),
                  reads=[("ust", ub)], writes=[("scr_u", ch, tcp)])
    ptiX = 0
    BO, BD = 3, 4
    for c in range(4):
        for hh in range(4):
            for mb in range(2):
                bs = nb(0, 3)
                pb = ptiX % 3
                ptiX += 1
                P.op("pe", (lambda h, bs=bs, hh=hh, mb=mb, c=c: h.matmul(ps[bs][:], lhsT=khT[:, hh, mb * 128:(mb + 1) * 128],
                                                                         rhs=qxT[:, hh, c * 512:(c + 1) * 512], start=True, stop=True)),
                     reads=[("khT", hh), ("qxT", hh, c)], writes=[("ps", bs)])
                P.op("act", (lambda h, bs=bs, pb=pb: h.activation(out=PTx[pb][:], in_=ps[bs][:], func=AF.Exp)),
                     reads=[("ps", bs)], writes=[("PT", pb)])
                P.op("pe", (lambda h, pb=pb, mb=mb, hh=hh: h.matmul(ps[BO][:], lhsT=vtok[:, mb, hh * 128:(hh + 1) * 128], rhs=PTx[pb][:],
                                                                    start=(mb == 0), stop=(mb == 1))),
                     reads=[("vtok", mb), ("PT", pb)], writes=[("ps", BO)], sig=False, accum=(mb > 0))
                P.op("pe", (lambda h, pb=pb, mb=mb: h.matmul(ps[BD][:], lhsT=ones_b, rhs=PTx[pb][:], start=(mb == 0), stop=(mb == 1))),
                     reads=["cm_b", ("PT", pb)], writes=[("ps", BD)], sig=True, accum=(mb > 0))
            P.op("dve", lambda h: h.reciprocal(out=rdx[:], in_=ps[BD][:]), reads=[("ps", BD)], writes=["rden"])
            P.op("dve", (lambda h, hh=hh, c=c: h.tensor_tensor(out=o_x[:, hh, c * 512:(c + 1) * 512], in0=ps[BO][:], in1=rdx[:], op=ALU.mult)),
                 reads=[("ps", BO), "rden"], writes=[("o_x", hh, c)])
    if dbg.get("stop") == "cross":
        t = dump("od", None, [128, 4, S], BF16)
        P.dma("sp", (lambda h, t=t: h.dma_start(out=t, in_=o_dsa[:])), writes=["dbg_od"])
        t = dump("o_x", None, [128, 4, S], BF16)
        P.dma("sp", (lambda h, t=t: h.dma_start(out=t, in_=o_x[:])), reads=[("o_x", a, c) for a in range(4) for c in range(4)], writes=["dbg_o_x"])
        t2 = dump("scr_u", None, [512, S], BF16)
        P.dma("sp", (lambda h, t2=t2: h.dma_start(out=t2, in_=scr_u)), reads=[("scr_u", a, c) for a in range(4) for c in range(4)], writes=["dbg_scr_u"])
        P.barrier()
        P.emit()
        return nc, dbg_outs
    P.barrier()
    A.release(base_mark)

    s5a = A.alloc([128, 3, 32], F32)
    s5b = A.alloc([128, 4, 32, 16], F32)
    dsk = A.alloc([128, 32], F32)
    TB = [A.alloc([128, 2, 8, 32], F32) for _ in range(4)]
    TK = A.alloc([128, 8, 2, 32], F32)
    cw = A.alloc([128, 16, 2, 32], F32)
    bb = A.alloc([128, 2, 32, 16], F32)
    W1 = A.alloc([128, 32, 128], BF16)
    W2 = A.alloc([128, 32, 128], BF16)
    Tm = A.alloc([128, 32, 128], BF16)
    U8 = A.alloc([128, 32, 256], BF16)
    X = A.alloc([128, 32, 256], BF16)
    mS = A.alloc([128, 1], F32)
    mS1 = A.mark()
    W1T = A.alloc([128, 32, 128], BF16)
    Lm = A.alloc([128, 32, 128], BF16)
    Rm = A.alloc([128, 32, 128], BF16)
    tA = A.alloc([128, 32, 128], F32)
    tB = A.alloc([128, 32, 128], F32)
    P.dma("sp", lambda h: h.dma_start(out=s5a[:], in_=s5a_d), writes=["s5a"])
    P.dma("sp", lambda h: h.dma_start(out=s5b[:], in_=s5b_d), writes=["s5b"])
    P.dma("sp", lambda h: h.dma_start(out=dsk[:], in_=dsk_d), writes=["dsk"])
    for jl in range(8):
        P.dma("sp", (lambda h, jl=jl: h.dma_start(out=U8[jl * 16:(jl + 1) * 16, :, :],
                                                  in_=scr_u.rearrange("(g c) (j b) -> j c g b", c=16, j=8)[jl])),
              reads=[("scr_u", a, c) for a in range(4) for c in range(4)], writes=[("U8", jl)])
    U8K = [("U8", jl) for jl in range(8)]

    def V(i):
        return cw[:, i, :, :]

    def tt(out, a, b_, op, rk, wk):
        P.op("dve", lambda h: h.tensor_tensor(out=out, in0=a, in1=b_, op=op), reads=rk, writes=wk)

    def cmul(dst, a, b_, ka, kb, kd):
        t1, t2 = V(14), V(15)
        tt(t1[:, 0, :], a[:, 0, :], b_[:, 0, :], ALU.mult, [ka, kb], ["cw_t1a"])
        tt(t1[:, 1, :], a[:, 1, :], b_[:, 1, :], ALU.mult, [ka, kb], ["cw_t1b"])
        tt(t2[:, 0, :], a[:, 0, :], b_[:, 1, :], ALU.mult, [ka, kb], ["cw_t2a"])
        tt(t2[:, 1, :], a[:, 1, :], b_[:, 0, :], ALU.mult, [ka, kb], ["cw_t2b"])
        tt(dst[:, 0, :], t1[:, 0, :], t1[:, 1, :], ALU.subtract, ["cw_t1a", "cw_t1b"], [kd])
        tt(dst[:, 1, :], t2[:, 0, :], t2[:, 1, :], ALU.add, ["cw_t2a", "cw_t2b", kd], [kd])

    a_re, a_im, ldt = s5a[:, 0, :], s5a[:, 1, :], s5a[:, 2, :]
    dtv = V(0)[:, 0, :]
    adr = V(0)[:, 1, :]
    adi = V(1)[:, 0, :]
    P.op("act", lambda h: h.activation(out=dtv, in_=ldt, func=AF.Exp), reads=["s5a"], writes=["dtv"])
    tt(adr, a_re, dtv, ALU.mult, ["s5a", "dtv"], ["adr"])
    tt(adi, a_im, dtv, ALU.mult, ["s5a", "dtv"], ["adi"])
    mag, magn, cs, sn = V(2)[:, 0, :], V(2)[:, 1, :], V(3)[:, 0, :], V(3)[:, 1, :]
    P.op("dve", lambda h: h.memset(mS[:], math.pi / 2), writes=["mS"])
    P.op("act", lambda h: h.activation(out=mag, in_=adr, func=AF.Exp, scale=1.0 / 16), reads=["adr"], writes=["mag"])
    P.op("act", lambda h: h.activation(out=magn, in_=adr, func=AF.Exp, scale=-1.0 / 16), reads=["adr"], writes=["magn"])
    P.op("act", lambda h: h.activation(out=cs, in_=adi, func=AF.Sin, scale=1.0 / 16, bias=mS[:]), reads=["adi", "mS"], writes=["cs"])
    P.op("act", lambda h: h.activation(out=sn, in_=adi, func=AF.Sin, scale=1.0 / 16), reads=["adi"], writes=["sn"])
    mu, nu = V(4), V(5)
    tt(mu[:, 0, :], mag, cs, ALU.mult, ["mag", "cs"], ["mu"])
    tt(mu[:, 1, :], mag, sn, ALU.mult, ["mag", "sn", "mu"], ["mu"])
    tt(nu[:, 0, :], magn, cs, ALU.mult, ["magn", "cs"], ["nu"])
    P.op("dve", lambda h: h.scalar_tensor_tensor(out=nu[:, 1, :], in0=magn, scalar=-1.0, in1=sn, op0=ALU.mult, op1=ALU.mult),
         reads=["magn", "sn", "nu"], writes=["nu"])
    def pw_slot(t, slot):
        return TB[t][:, :, slot, :]
    TW1, TW2, TL, TR = 0, 1, 2, 3
    cur, ck = mu, "mu"
    for i in range(4):
        dst = V(6 + (i % 2)) if i < 3 else pw_slot(TR, 1)
        kd = f"sqp{i}" if i < 3 else ("P", 1)
        cmul(dst, cur, cur, ck, ck, kd)
        cur, ck = dst, kd
    cur, ck = nu, "nu"
    for i in range(4):
        dst = V(8 + (i % 2)) if i < 3 else pw_slot(TL, 1)
        kd = f"sqn{i}" if i < 3 else ("N", 1)
        cmul(dst, cur, cur, ck, ck, kd)
        cur, ck = dst, kd
    Pp = {1: pw_slot(TR, 1)}
    Np = {1: pw_slot(TL, 1)}
    for t_ in (TR, TL, TW1):
        sl = 7 if t_ == TW1 else 0
        P.op("dve", (lambda h, t_=t_, sl=sl: h.memset(TB[t_][:, 0, sl, :], 1.0)), writes=[("one", t_, 0)])
        P.op("dve", (lambda h, t_=t_, sl=sl: h.memset(TB[t_][:, 1, sl, :], 0.0)), writes=[("one", t_, 1)])
    for k, (a, b_) in ((2, (1, 1)), (3, (2, 1)), (4, (2, 2)), (5, (4, 1)), (6, (4, 2)), (7, (4, 3))):
        Pp[k] = pw_slot(TR, k)
        cmul(Pp[k], Pp[a], Pp[b_], ("P", a), ("P", b_), ("P", k))
        Np[k] = pw_slot(TL, k)
        cmul(Np[k], Np[a], Np[b_], ("N", a), ("N", b_), ("N", k))
    Pp[8] = pw_slot(TW2, 7)
    cmul(Pp[8], Pp[4], Pp[4], ("P", 4), ("P", 4), ("P", 8))
    PK = [("P", k) for k in range(1, 9)]
    NK = [("N", k) for k in range(1, 8)]
    P.op("dve", lambda h: h.tensor_copy(out=TB[TW2][:, :, 0:7, :], in_=TB[TR][:, :, 1:8, :]), reads=PK, writes=["TW2"])
    for jl in range(7):
        P.op("dve", (lambda h, jl=jl: h.tensor_copy(out=TB[TW1][:, :, jl, :], in_=TB[TR][:, :, 7 - jl, :])), reads=PK, writes=[("TW1", jl)])
    TW1K = [("TW1", jl) for jl in range(7)] + [("one", TW1, 0), ("one", TW1, 1)]
    TRK = PK + [("one", TR, 0), ("one", TR, 1)]
    TLK = NK + [("one", TL, 0), ("one", TL, 1)]
    TW2K = ["TW2", ("P", 8)]
    P.op("dve", lambda h: h.tensor_copy(out=TK[:, 0, :, :], in_=Pp[8]), reads=[("P", 8)], writes=[("TK", 0)])
    for l in range(1, 8):
        cmul(TK[:, l, :, :], TK[:, l - 1, :, :], TK[:, l - 1, :, :], ("TK", l - 1), ("TK", l - 1), ("TK", l))
    P.op("dve", lambda h: h.tensor_scalar(out=TK[64:128, :, 1, :], in0=TK[64:128, :, 1, :], scalar1=-1.0, scalar2=None, op0=ALU.mult),
         reads=[("TK", l) for l in range(8)], writes=["TKs"])
    num, qv = V(10), V(11)
    den = V(12)[:, 0, :]
    P.op("dve", lambda h: h.tensor_scalar(out=num[:, 0, :], in0=Pp[1][:, 0, :], scalar1=-1.0, scalar2=None, op0=ALU.add),
         reads=[("P", 1)], writes=["num"])
    P.op("dve", lambda h: h.tensor_copy(out=num[:, 1, :], in_=Pp[1][:, 1, :]), reads=[("P", 1), "num"], writes=["num"])
    tt(den, a_re, a_re, ALU.mult, ["s5a"], ["den"])
    tt(V(12)[:, 1, :], a_im, a_im, ALU.mult, ["s5a"], ["den2"])
    tt(den, den, V(12)[:, 1, :], ALU.add, ["den", "den2"], ["den"])
    P.op("dve", lambda h: h.reciprocal(out=den, in_=den), reads=["den"], writes=["den"])
    t13 = V(13)
    tt(t13[:, 0, :], num[:, 0, :], a_re, ALU.mult, ["num", "s5a"], ["t13a"])
    tt(t13[:, 1, :], num[:, 1, :], a_im, ALU.mult, ["num", "s5a"], ["t13b"])
    tt(qv[:, 0, :], t13[:, 0, :], t13[:, 1, :], ALU.add, ["t13a", "t13b"], ["qv0"])
    tt(qv[:, 0, :], qv[:, 0, :], den, ALU.mult, ["qv0", "den"], ["qv0"])
    tt(t13[:, 0, :], num[:, 1, :], a_re, ALU.mult, ["num", "s5a", "qv0"], ["t13a"])
    tt(t13[:, 1, :], num[:, 0, :], a_im, ALU.mult, ["num", "s5a", "qv0"], ["t13b"])
    tt(qv[:, 1, :], t13[:, 0, :], t13[:, 1, :], ALU.subtract, ["t13a", "t13b"], ["qv1"])
    tt(qv[:, 1, :], qv[:, 1, :], den, ALU.mult, ["qv1", "den"], ["qv1"])
    q_re = qv[:, 0, :].unsqueeze(2).to_broadcast([128, 32, 16])
    q_im = qv[:, 1, :].unsqueeze(2).to_broadcast([128, 32, 16])
    B_re, B_im, C_re, C_im = s5b[:, 0], s5b[:, 1], s5b[:, 2], s5b[:, 3]
    tAv = tA[:].rearrange("p g (j c) -> p g j c", c=16)
    tBv = tB[:].rearrange("p g (j c) -> p g j c", c=16)
    tt(tAv[:, :, 0, :], q_re, B_re, ALU.mult, ["qv0", "s5b"], ["tA"])
    tt(tBv[:, :, 0, :], q_im, B_im, ALU.mult, ["qv1", "s5b"], ["tB"])
    tt(bb[:, 0], tAv[:, :, 0, :], tBv[:, :, 0, :], ALU.subtract, ["tA", "tB"], ["bb0"])
    tt(tAv[:, :, 0, :], q_re, B_im, ALU.mult, ["qv0", "s5b", "bb0"], ["tA"])
    tt(tBv[:, :, 0, :], q_im, B_re, ALU.mult, ["qv1", "s5b", "bb0"], ["tB"])
    tt(bb[:, 1], tAv[:, :, 0, :], tBv[:, :, 0, :], ALU.add, ["tA", "tB"], ["bb1"])

    def build_mat(dst, tbl, tkeys, v_re, v_im, vkeys, mode, dkey):
        dv = dst[:].rearrange("p g (j c) -> p g j c", c=16)
        for half in range(2):
            pr = slice(64 * half, 64 * half + 64)
            Tre = TB[tbl][pr, 0, :, :].rearrange("p s g -> p g s").unsqueeze(3).to_broadcast([64, 32, 8, 16])
            Tim = TB[tbl][pr, 1, :, :].rearrange("p s g -> p g s").unsqueeze(3).to_broadcast([64, 32, 8, 16])
            va, vb = (v_re, v_im) if half == 0 else (v_im, v_re)
            Va = va[pr].unsqueeze(2).to_broadcast([64, 32, 8, 16])
            Vb = vb[pr].unsqueeze(2).to_broadcast([64, 32, 8, 16])
            tt(tAv[pr], Tre, Va, ALU.mult, tkeys + vkeys + [dkey], [("tA", half)])
            tt(tBv[pr], Tim, Vb, ALU.mult, tkeys + vkeys + [dkey], [("tB", half)])
            if half == 0:
                tt(dv[pr], tAv[pr], tBv[pr], ALU.subtract, [("tA", 0), ("tB", 0)], [(dkey, 0)])
            elif mode == "B":
                tt(dv[pr], tAv[pr], tBv[pr], ALU.add, [("tA", 1), ("tB", 1)], [(dkey, 1)])
            else:
                P.op("dve", (lambda h, pr=pr: h.scalar_tensor_tensor(out=dv[pr], in0=tAv[pr], scalar=-1.0, in1=tBv[pr],
                                                                      op0=ALU.mult, op1=ALU.subtract)),
                     reads=[("tA", 1), ("tB", 1)], writes=[(dkey, 1)])

    P.op("dve", lambda h: h.memset(mS[:], 0.0), reads=["tA", "tB", "mS"], writes=[("tA", 0), ("tA", 1), ("tB", 0), ("tB", 1), "mS"])
    build_mat(W1T, TW1, TW1K, bb[:, 0], bb[:, 1], ["bb0", "bb1"], "B", "W1T")
    build_mat(W2, TW2, TW2K, C_re, C_im, ["s5b"], "C", "W2")
    build_mat(Lm, TL, TLK, bb[:, 0], bb[:, 1], ["bb0", "bb1"], "B", "Lm")
    build_mat(Rm, TR, TRK, C_re, C_im, ["s5b"], "C", "Rm")
    for g0 in range(0, 32, 4):
        tb = (g0 // 4) % 2
        for jj in range(4):
            P.op("pe", (lambda h, g0=g0, jj=jj, tb=tb: h.transpose(out=psTs[tb][:, jj * 128:(jj + 1) * 128], in_=W1T[:, g0 + jj, :], identity=ident_b)),
                 reads=[("W1T", 0), ("W1T", 1), "cm_b"], writes=[("psT", tb)], sig=(jj == 3), accum=(jj > 0))
        P.op("act", (lambda h, g0=g0, tb=tb: h.copy(out=W1[:, g0:g0 + 4, :], in_=psTs[tb][:, 0:512].rearrange("p (a b) -> p a b", b=128))),
             reads=[("psT", tb)], writes=[("W1", g0 // 4)])
    for g0 in range(0, 32, 4):
        b = nb(0, 3)
        for jj in range(4):
            P.op("pe", (lambda h, g0=g0, jj=jj, b=b: h.matmul(ps[b][:, jj * 128:(jj + 1) * 128], lhsT=Lm[:, g0 + jj, :], rhs=Rm[:, g0 + jj, :],
                                                              start=True, stop=True)),
                 reads=[("Lm", 0), ("Lm", 1), ("Rm", 0), ("Rm", 1)], writes=[("ps", b)], sig=(jj == 3), accum=(jj > 0))
        P.op("dve", (lambda h, b=b, g0=g0: h.tensor_tensor(out=tA[:, g0:g0 + 4, :], in0=ps[b][:].rearrange("p (a b) -> p a b", b=128),
                                                           in1=tmask_f.unsqueeze(1).to_broadcast([128, 4, 128]), op=ALU.mult)),
             reads=[("ps", b), "cm_f", ("tA", 0), ("tA", 1)], writes=[("tAm", g0 // 4)])
        for jj in range(4):
            g = g0 + jj
            P.op("dve", (lambda h, g=g: h.scalar_tensor_tensor(out=Tm[:, g, :], in0=ident_f, scalar=dsk[:, g:g + 1], in1=tA[:, g, :],
                                                               op0=ALU.mult, op1=ALU.add)),
                 reads=[("tAm", g0 // 4), "dsk", "cm_f"], writes=[("Tm", g0 // 4)])
    if dbg.get("stop") == "s5pre":
        for nm, tns, keys in (("W1", W1, [("W1", i) for i in range(8)]), ("W2", W2, [("W2", 0), ("W2", 1)]), ("Tm", Tm, [("Tm", i) for i in range(8)])):
            t = dump(nm, None, [128, 32, 128], BF16)
            P.dma("sp", (lambda h, t=t, tns=tns: h.dma_start(out=t, in_=tns[:])), reads=keys, writes=["dbg_" + nm])
        t = dump("TK", None, [128, 8, 2, 32])
        P.dma("sp", (lambda h, t=t: h.dma_start(out=t, in_=TK[:])), reads=["TKs"], writes=["dbg_TK"])
        t = dump("TB", None, [128, 2, 8, 32])
        P.dma("sp", (lambda h, t=t: h.dma_start(out=t, in_=TB[TR][:])), reads=TRK, writes=["dbg_TB"])
        P.barrier()
        P.emit()
        return nc, dbg_outs
    P.barrier()
    A.release(mS1)

    Yst = A.alloc([128, 32, 256], BF16)
    gq = A.alloc([128, 512], F32)
    gz = A.alloc([128, 512], F32)
    gs = A.alloc([128, 512], F32)
    mS2 = A.mark()
    Rl = [A.alloc([128, 32, 128], BF16) for _ in range(2)]
    rt1 = A.alloc([128, 32, 128], BF16)
    rt2 = A.alloc([128, 32, 128], BF16)
    XK = lambda gp: ("X", gp)
    for gp in range(16):
        b = nb(0, 3)
        for gi in range(2):
            g = 2 * gp + gi
            P.op("pe", (lambda h, b=b, gi=gi, g=g: h.matmul(ps[b][:, gi * 256:(gi + 1) * 256], lhsT=W1[:, g, :], rhs=U8[:, g, :], start=True, stop=True)),
                 reads=[("W1", g // 4)] + U8K, writes=[("ps", b)], sig=(gi == 1), accum=(gi > 0))
        P.op("act", (lambda h, b=b, gp=gp: h.copy(out=X[:, 2 * gp:2 * gp + 2, :], in_=ps[b][:].rearrange("p (a b) -> p a b", b=256))),
             reads=[("ps", b)], writes=[XK(gp)])
    for l in range(8):
        d = 1 << l
        R = Rl[l % 2]
        P.op("dve", (lambda h, l=l: h.tensor_tensor(out=rt1[:], in0=ident_f.unsqueeze(1).to_broadcast([128, 32, 128]),
                                                    in1=TK[:, l, 0, :].unsqueeze(2).to_broadcast([128, 32, 128]), op=ALU.mult)),
             reads=["cm_f", "TKs"], writes=["rt1"])
        P.op("dve", (lambda h, l=l: h.tensor_tensor(out=rt2[:], in0=swap_f.unsqueeze(1).to_broadcast([128, 32, 128]),
                                                    in1=TK[:, l, 1, :].unsqueeze(2).to_broadcast([128, 32, 128]), op=ALU.mult)),
             reads=["cm_f", "TKs"], writes=["rt2"])
        P.op("dve", (lambda h, R=R: h.tensor_tensor(out=R[:], in0=rt1[:], in1=rt2[:], op=ALU.add)),
             reads=["rt1", "rt2"], writes=[("R", l % 2)])
        for gp in range(16):
            b = nb(0, 3)
            for gi in range(2):
                g = 2 * gp + gi
                P.op("pe", (lambda h, b=b, gi=gi, g=g, d=d, R=R: h.matmul(ps[b][:, gi * 256 + d:(gi + 1) * 256], lhsT=R[:, g, :], rhs=X[:, g, 0:256 - d],
                                                                          start=True, stop=True)),
                     reads=[("R", l % 2), XK(gp)], writes=[("ps", b)], sig=(gi == 1), accum=(gi > 0))
            P.op("dve", (lambda h, b=b, gp=gp, d=d: h.tensor_tensor(out=X[:, 2 * gp:2 * gp + 2, d:256], in0=X[:, 2 * gp:2 * gp + 2, d:256],
                                                                   in1=ps[b][:].rearrange("p (a b) -> p a b", b=256)[:, :, d:256], op=ALU.add)),
                 reads=[("ps", b), XK(gp)], writes=[XK(gp)])
    for gp in range(16):
        b = nb(0, 3)
        for gi in range(2):
            g = 2 * gp + gi
            P.op("pe", (lambda h, b=b, gi=gi, g=g: h.matmul(ps[b][:, gi * 256:(gi + 1) * 256], lhsT=Tm[:, g, :], rhs=U8[:, g, :], start=True, stop=False)),
                 reads=[("Tm", g // 4)] + U8K, writes=[("ps", b)], sig=False, accum=(gi > 0))
            P.op("pe", (lambda h, b=b, gi=gi, g=g: h.matmul(ps[b][:, gi * 256 + 1:(gi + 1) * 256], lhsT=W2[:, g, :], rhs=X[:, g, 0:255], start=False, stop=True)),
                 reads=[("W2", 0), ("W2", 1), XK(gp)], writes=[("ps", b)], sig=(gi == 1), accum=True)
        P.op("act", (lambda h, b=b: h.activation(out=gq[:], in_=ps[b][:], func=AF.Square)), reads=[("ps", b)], writes=["gq"])
        P.op("dve", lambda h: h.tensor_scalar(out=gz[:], in0=gq[:], scalar1=0.044715, scalar2=1.0, op0=ALU.mult, op1=ALU.add),
             reads=["gq"], writes=["gz"])
        P.op("dve", (lambda h, b=b: h.tensor_tensor(out=gz[:], in0=gz[:], in1=ps[b][:], op=ALU.mult)), reads=["gz", ("ps", b)], writes=["gz"])
        P.op("act", lambda h: h.activation(out=gs[:], in_=gz[:], func=AF.Sigmoid, scale=2.0 * math.sqrt(2.0 / math.pi)), reads=["gz"], writes=["gs"])
        P.op("dve", (lambda h, b=b, gp=gp: h.tensor_tensor(out=Yst[:, 2 * gp:2 * gp + 2, :], in0=gs[:].rearrange("p (a b) -> p a b", b=256),
                                                          in1=ps[b][:].rearrange("p (a b) -> p a b", b=256), op=ALU.mult)),
             reads=["gs", ("ps", b)], writes=[("Yst", gp)])
    for il in range(8):
        P.dma("sp", (lambda h, il=il: h.dma_start(out=scr_y.rearrange("(g c) (i b) -> i c g b", c=16, i=8)[il], in_=Yst[il * 16:(il + 1) * 16, :, :])),
              reads=[("Yst", gp) for gp in range(16)], writes=[("scr_y", il)])
    P.barrier()
    A.release(mS2)
    YT = A.alloc([128, 4, S], BF16)
    wglu = A.alloc([128, 4, 512], BF16)
    P.dma("pool", lambda h: h.dma_start(out=wglu[:], in_=w_glu_d.rearrange("(kc p) n -> p kc n", p=128)), writes=["wglu"])
    P.dma("sp", lambda h: h.dma_start(out=YT[:], in_=scr_y.rearrange("(ch p) t -> p ch t", p=128)), writes=["YT"])
    for chp in range(4):
        for tcp in range(4):
            b = nb(0, 3)
            for k in range(4):
                P.op("pe", (lambda h, b=b, k=k, chp=chp, tcp=tcp: h.matmul(ps[b][:], lhsT=wglu[:, k, chp * 128:(chp + 1) * 128],
                                                                           rhs=YT[:, k, tcp * 512:(tcp + 1) * 512], start=(k == 0), stop=(k == 3))),
                     reads=["wglu", "YT"], writes=[("ps", b)], sig=(k == 3), accum=(k > 0))
            P.op("act", (lambda h, b=b: h.activation(out=gs[:], in_=ps[b][:], func=AF.Sigmoid)), reads=[("ps", b)], writes=["gs"])
            P.op("dve", (lambda h, chp=chp, tcp=tcp: h.tensor_tensor(out=o_s5[:, chp, tcp * 512:(tcp + 1) * 512], in0=gs[:],
                                                                    in1=YT[:, chp, tcp * 512:(tcp + 1) * 512], op=ALU.mult)),
                 reads=["gs", "YT"], writes=[("o_s5", chp, tcp)])
    if dbg.get("stop") == "s5":
        t = dump("od", None, [128, 4, S], BF16)
        P.dma("sp", (lambda h, t=t: h.dma_start(out=t, in_=o_dsa[:])), writes=["dbg_od"])
        t = dump("o_s5", None, [128, 4, S], BF16)
        P.dma("sp", (lambda h, t=t: h.dma_start(out=t, in_=o_s5[:])), reads=[("o_s5", a, c) for a in range(4) for c in range(4)], writes=["dbg_o_s5"])
        t = dump("YT", None, [128, 4, S], BF16)
        P.dma("sp", (lambda h, t=t: h.dma_start(out=t, in_=YT[:])), reads=["YT"], writes=["dbg_YT"])
        P.barrier()
        P.emit()
        return nc, dbg_outs
    P.barrier()
    A.release(base_mark)

    off_merged = A.mark()
    merged = A.alloc([128, 8, S], BF16)
    mM1 = A.mark()
    xgM = A.alloc([128, 8, S], BF16)
    build_xg(xgM, False)
    wg = [A.alloc([128, 8, 384], BF16) for _ in range(2)]
    wbr = [A.alloc([128, 4, 384], BF16) for _ in range(2)]
    gt = A.alloc([128, 512], F32)
    sg = A.alloc([128, 512], F32)
    macc = A.alloc([128, 512], F32)
    mtmp = A.alloc([128, 512], F32)
    w_br_v = [w.rearrange("(kc p) n -> p kc n", p=128) for w in (w_brd_d, w_brs_d, w_brx_d)]
    obr = [o_dsa, o_s5, o_x]
    for fc in range(8):
        wb_ = fc % 2
        for br in range(3):
            c0 = 2760 + br * 1024 + fc * 128
            P.dma("pool", (lambda h, wb_=wb_, br=br, c0=c0: h.dma_start(out=wg[wb_][:, :, br * 128:(br + 1) * 128], in_=w_in_v[:, :, c0:c0 + 128])),
                  writes=[("wg", wb_, br)])
            P.dma("pool", (lambda h, wb_=wb_, br=br, fc=fc: h.dma_start(out=wbr[wb_][:, :, br * 128:(br + 1) * 128],
                                                                       in_=w_br_v[br][:, :, fc * 128:(fc + 1) * 128])),
                  writes=[("wbr", wb_, br)])
        for tc in range(4):
            for br in range(3):
                bG = nb(0, 6)
                for kc in range(8):
                    P.op("pe", (lambda h, bG=bG, kc=kc, wb_=wb_, br=br, tc=tc: h.matmul(ps[bG][:], lhsT=wg[wb_][:, kc, br * 128:(br + 1) * 128],
                                                                                        rhs=xgM[:, kc, tc * 512:(tc + 1) * 512], start=(kc == 0), stop=(kc == 7))),
                         reads=[("wg", wb_, br), ("xg", kc)], writes=[("ps", bG)], sig=(kc == 7), accum=(kc > 0))
                P.op("dve", (lambda h, bG=bG, tc=tc: h.tensor_tensor(out=gt[:], in0=ps[bG][:], in1=rx_bc[:, tc * 512:(tc + 1) * 512], op=ALU.mult)),
                     reads=[("ps", bG)], writes=["gt"])
                P.op("act", lambda h: h.activation(out=sg[:], in_=gt[:], func=AF.Sigmoid), reads=["gt"], writes=["sg"])
                bB = nb(0, 6)
                for k in range(4):
                    if br == 1:
                        rhs = o_s5[:, k, :].rearrange("p (j b) -> p b j", j=8)[:, 64 * tc:64 * tc + 64, :]
                    else:
                        rhs = obr[br][:, k, tc * 512:(tc + 1) * 512]
                    P.op("pe", (lambda h, bB=bB, k=k, wb_=wb_, br=br, rhs=rhs: h.matmul(ps[bB][:], lhsT=wbr[wb_][:, k, br * 128:(br + 1) * 128], rhs=rhs,
                                                                                        start=(k == 0), stop=(k == 3))),
                         reads=[("wbr", wb_, br)], writes=[("ps", bB)], sig=(k == 3), accum=(k > 0))
                if br == 0:
                    P.op("dve", (lambda h, bB=bB: h.tensor_tensor(out=macc[:], in0=ps[bB][:], in1=sg[:], op=ALU.mult)),
                         reads=[("ps", bB), "sg"], writes=["macc"])
                else:
                    P.op("dve", (lambda h, bB=bB: h.tensor_tensor(out=mtmp[:], in0=ps[bB][:], in1=sg[:], op=ALU.mult)),
                         reads=[("ps", bB), "sg"], writes=["mtmp"])
                    if br == 1:
                        P.op("dve", lambda h: h.tensor_tensor(out=macc[:], in0=macc[:], in1=mtmp[:], op=ALU.add), reads=["macc", "mtmp"], writes=["macc"])
                    else:
                        P.op("dve", (lambda h, fc=fc, tc=tc: h.tensor_tensor(out=merged[:, fc, tc * 512:(tc + 1) * 512], in0=macc[:], in1=mtmp[:], op=ALU.add)),
                             reads=["macc", "mtmp"], writes=[("merged", fc, tc)])
                if dbg.get("stop") == "merge1" and br == dbg.get("br", 0):
                    for nm, tns, shp, dt_, keys in (("wg", wg[0][:], [128, 8, 384], BF16, [("wg", 0, i) for i in range(3)]), ("gt", gt[:], [128, 512], F32, ["gt"]),
                                                    ("sg", sg[:], [128, 512], F32, ["sg"]), ("macc", macc[:], [128, 512], F32, ["macc"]),
                                                    ("mtmp", mtmp[:], [128, 512], F32, ["mtmp"]),
                                                    ("xg", xgM[:], [128, 8, S], BF16, XG), ("od", o_dsa[:], [128, 4, S], BF16, []), ("os", o_s5[:], [128, 4, S], BF16, []), ("ox", o_x[:], [128, 4, S], BF16, []), ("wbr", wbr[0][:], [128, 4, 384], BF16, [("wbr", 0, i) for i in range(3)])):
                        t = dump(nm, None, shp, dt_)
                        P.dma("sp", (lambda h, t=t, tns=tns: h.dma_start(out=t, in_=tns)), reads=keys, writes=["dbg_" + nm])
                    P.barrier()
                    P.emit()
                    return nc, dbg_outs
    if dbg.get("stop") == "merge":
        t = dump("merged", None, [128, 8, S], BF16)
        P.dma("sp", (lambda h, t=t: h.dma_start(out=t, in_=merged[:])), reads=[("merged", a, c) for a in range(8) for c in range(4)], writes=["dbg_merged"])
        P.barrier()
        P.emit()
        return nc, dbg_outs
    P.barrier()
    A.release(mM1)
    off_x1 = A.mark()
    x1T = A.alloc([128, 8, S], F32)
    mM0 = A.mark()
    wout = A.alloc([128, 8, 1024], BF16)
    xres = [A.alloc([128, 512], F32) for _ in range(2)]
    w_out_v = w_out_d.rearrange("(kc p) n -> p kc n", p=128)
    for i in range(2):
        load_w(wout[:, :, i * 512:(i + 1) * 512], w_out_v, ("wout", i), (i * 512, (i + 1) * 512))
    xi = 0
    for fc in range(8):
        for tc in range(4):
            xb = xi % 2
            xi += 1
            P.dma("sp", (lambda h, xb=xb, fc=fc, tc=tc: h.dma_start(out=xres[xb][:], in_=xT_d[fc * 128:(fc + 1) * 128, tc * 512:(tc + 1) * 512])),
                  writes=[("xres", xb)])
            b = nb(0, 6)
            for kc in range(8):
                P.op("pe", (lambda h, b=b, kc=kc, fc=fc, tc=tc: h.matmul(ps[b][:], lhsT=wout[:, kc, fc * 128:(fc + 1) * 128],
                                                                         rhs=merged[:, kc, tc * 512:(tc + 1) * 512], start=(kc == 0), stop=(kc == 7))),
                     reads=[(("wout", fc // 4), kc), ("merged", kc, tc)], writes=[("ps", b)], sig=(kc == 7), accum=(kc > 0))
            P.op("dve", (lambda h, b=b, xb=xb, fc=fc, tc=tc: h.tensor_tensor(out=x1T[:, fc, tc * 512:(tc + 1) * 512], in0=ps[b][:], in1=xres[xb][:], op=ALU.add)),
                 reads=[("ps", b), ("xres", xb)], writes=[("x1T", fc, tc)])
    if dbg.get("stop") == "x1":
        t = dump("x1T", None, [128, 8, S])
        P.dma("sp", (lambda h, t=t: h.dma_start(out=t, in_=x1T[:])), reads=[("x1T", a, c) for a in range(8) for c in range(4)], writes=["dbg_x1T"])
        P.barrier()
        P.emit()
        return nc, dbg_outs
    P.barrier()
    A.release(mM0)
    top = A.mark()
    A.off = off_o
    fsq = [A.alloc([128, 512], BF16) for _ in range(2)]
    rf = A.alloc([128, 512], F32)
    wgu = [[A.alloc([128, 8, 256], BF16) for _ in range(2)] for _ in range(2)]
    wdn = [A.alloc([128, NFF, 256], BF16) for _ in range(2)]
    fs = A.alloc([128, 512], F32)
    assert A.off <= off_merged
    A.off = off_merged
    hfT = A.alloc([128, 8, 512], BF16)
    hT = A.alloc([128, NFF, 512], BF16)
    assert A.off <= off_x1
    A.off = top
    fo = [A.alloc([128, 512], F32) for _ in range(2)]
    w_gu_v = [w.rearrange("(kc p) n -> p kc n", p=128) for w in (w_g_d, w_u_d)]
    w_d_v = w_d_d.rearrange("(fk p) n -> p fk n", p=128)
    wi = 0
    di = 0
    oi = 0
    for tq in range(4):
        tsl = slice(tq * 512, (tq + 1) * 512)
        bS = nb(0, 6)
        for kc in range(8):
            sb_ = kc % 2
            P.op("act", (lambda h, kc=kc, sb_=sb_, tsl=tsl: h.activation(out=fsq[sb_][:], in_=x1T[:, kc, tsl], func=AF.Square)),
                 reads=[("x1T", kc, tq)], writes=[("fsq", sb_)])
            P.op("pe", (lambda h, kc=kc, sb_=sb_, bS=bS: h.matmul(ps[bS][:], lhsT=ones_b, rhs=fsq[sb_][:], start=(kc == 0), stop=(kc == 7))),
                 reads=[("fsq", sb_), "cm_b"], writes=[("ps", bS)], sig=True, accum=(kc > 0))
        P.op("act", (lambda h, bS=bS: h.activation(out=rf[:], in_=ps[bS][:], func=AF.Sqrt, scale=1.0 / D, bias=EPS)), reads=[("ps", bS)], writes=["rf"])
        P.op("dve", lambda h: h.reciprocal(out=rf[:], in_=rf[:]), reads=["rf"], writes=["rf"])
        for kc in range(8):
            P.op("dve", (lambda h, kc=kc, tsl=tsl: h.scalar_tensor_tensor(out=hfT[:, kc, :], in0=x1T[:, kc, tsl], scalar=gvec[:, 16 + kc:17 + kc], in1=rf[:],
                                                                          op0=ALU.mult, op1=ALU.mult)),
                 reads=[("x1T", kc, tq), "rf", "gvec"], writes=[("hfT", kc)])
        for f0 in range(0, NFF, 2):
            wb_ = wi % 2
            wi += 1
            for gu in range(2):
                P.dma("pool", (lambda h, gu=gu, wb_=wb_, f0=f0: h.dma_start(out=wgu[gu][wb_][:], in_=w_gu_v[gu][:, :, f0 * 128:(f0 + 2) * 128])),
                      writes=[("wgu", gu, wb_)])
            for ff in range(f0, f0 + 2):
                bg, bu = nb(0, 6), nb(0, 6)
                for gu, bb_ in ((0, bg), (1, bu)):
                    for kc in range(8):
                        P.op("pe", (lambda h, gu=gu, bb_=bb_, kc=kc, wb_=wb_, ff=ff, f0=f0: h.matmul(
                            ps[bb_][:], lhsT=wgu[gu][wb_][:, kc, (ff - f0) * 128:(ff - f0 + 1) * 128], rhs=hfT[:, kc, :], start=(kc == 0), stop=(kc == 7))),
                            reads=[("wgu", gu, wb_), ("hfT", kc)], writes=[("ps", bb_)], sig=(kc == 7), accum=(kc > 0))
                P.op("act", (lambda h, bg=bg: h.activation(out=fs[:], in_=ps[bg][:], func=AF.Silu)), reads=[("ps", bg)], writes=["fs"])
                P.op("dve", (lambda h, bu=bu, ff=ff: h.tensor_tensor(out=hT[:, ff, :], in0=ps[bu][:], in1=fs[:], op=ALU.mult)),
                     reads=[("ps", bu), "fs"], writes=[("hT", ff)])
        for c0 in range(0, 8, 2):
            db = di % 2
            di += 1
            P.dma("pool", (lambda h, db=db, c0=c0: h.dma_start(out=wdn[db][:], in_=w_d_v[:, :, c0 * 128:(c0 + 2) * 128])), writes=[("wdn", db)])
            for fc in range(c0, c0 + 2):
                b = nb(0, 6)
                for ff in range(NFF):
                    P.op("pe", (lambda h, b=b, ff=ff, db=db, fc=fc, c0=c0: h.matmul(ps[b][:], lhsT=wdn[db][:, ff, (fc - c0) * 128:(fc - c0 + 1) * 128], rhs=hT[:, ff, :],
                                                                                    start=(ff == 0), stop=(ff == NFF - 1))),
                         reads=[("wdn", db), ("hT", ff)], writes=[("ps", b)], sig=(ff == NFF - 1), accum=(ff > 0))
                ob = oi % 2
                oi += 1
                P.op("dve", (lambda h, b=b, ob=ob, fc=fc, tsl=tsl: h.tensor_tensor(out=fo[ob][:], in0=ps[b][:], in1=x1T[:, fc, tsl], op=ALU.add)),
                     reads=[("ps", b), ("x1T", fc, tq)], writes=[("fo", ob)])
                P.dma("sp", (lambda h, ob=ob, fc=fc, tsl=tsl: h.dma_start(out=outT_d[fc * 128:(fc + 1) * 128, tsl], in_=fo[ob][:])),
                      reads=[("fo", ob)], writes=[("out", fc, tq)])
    P.barrier()
    P.emit()
    return nc, dbg_outs


def _host_inputs(inputs):
    f = lambda a: np.ascontiguousarray(np.asarray(a, dtype=np.float32))
    x = f(inputs["x"])
    mem = f(inputs["mem"])
    rel_bias = f(inputs["rel_bias"])
    shared = {}
    shared["w_in"] = f(inputs["w_in"][0])
    shared["w_uv"] = f(np.transpose(inputs["w_uv_dsa"][0], (1, 0, 2)))
    shared["w_glu"] = f(inputs["w_glu"][0])
    shared["w_mem_kv"] = f(inputs["w_mem_kv"][0])
    shared["w_br_dsa"] = f(inputs["w_br_dsa"][0])
    shared["w_br_s5"] = f(inputs["w_br_s5"][0])
    shared["w_br_cross"] = f(inputs["w_br_cross"][0])
    shared["w_out"] = f(inputs["w_out"][0])
    shared["w_ffn_gate"] = f(inputs["w_ffn_gate"][0])
    shared["w_ffn_up"] = f(inputs["w_ffn_up"][0])
    shared["w_ffn_down"] = f(inputs["w_ffn_down"][0])
    gvec = np.zeros((128, 32), np.float32)
    gvec[:, 0:8] = np.asarray(inputs["g_mix_norm"][0]).reshape(8, 128).T
    gvec[:, 8:16] = np.asarray(inputs["g_mem_norm"][0]).reshape(8, 128).T
    gvec[:, 16:24] = np.asarray(inputs["g_ffn_norm"][0]).reshape(8, 128).T
    gvec[:, 24] = np.asarray(inputs["g_q_dsa"][0])
    gvec[:, 25] = np.asarray(inputs["g_q_cross"][0])
    gvec[:, 26] = np.asarray(inputs["g_k_cross"][0])
    shared["gvec"] = gvec
    shared["gkv_bc"] = f(np.broadcast_to(np.asarray(inputs["g_kv_dsa"][0])[None, :], (128, 128)))
    bt = _t5_bucket_table()
    s_i = np.arange(128)[:, None]
    t_i = np.arange(128)[None, :]
    t5 = np.zeros((128, 2, 8, 128), np.float32)
    for diff in (0, 1):
        dist = np.maximum(t_i - s_i + 128 * diff, 0)
        t5[:, diff, :, :] = np.transpose(rel_bias[bt[dist]], (0, 2, 1))
    shared["t5"] = t5
    shared["rb31"] = f(np.broadcast_to(rel_bias[31][None, :], (128, 8)))
    cm = np.zeros((128, 5, 128), np.float32)
    cm[:, 0, :] = np.eye(128)
    cm[:, 1, :] = np.where(t_i.T >= s_i.T, 0.0, NEG)
    il = np.arange(128)[None, :] // 16
    jl = np.arange(128)[:, None] // 16
    cm[:, 2, :] = (il >= jl).astype(np.float32)
    sw = np.zeros((128, 128), np.float32)
    sw[np.arange(64), np.arange(64) + 64] = 1.0
    sw[np.arange(64) + 64, np.arange(64)] = 1.0
    cm[:, 3, :] = sw
    cm[:, 4, :] = 1.0
    shared["cmats"] = cm
    tile2 = lambda a: np.concatenate([a, a], axis=0)
    s5a = np.zeros((128, 3, 32), np.float32)
    s5a[:, 0, :] = tile2(np.asarray(inputs["a_re"][0]).T)
    s5a[:, 1, :] = tile2(np.asarray(inputs["a_im"][0]).T)
    s5a[:, 2, :] = np.broadcast_to(np.asarray(inputs["log_dt"][0])[None, :], (128, 32))
    shared["s5a"] = s5a
    s5b = np.zeros((128, 4, 32, 16), np.float32)
    s5b[:, 0] = tile2(np.transpose(inputs["b_re"][0], (1, 0, 2)))
    s5b[:, 1] = tile2(np.transpose(inputs["b_im"][0], (1, 0, 2)))
    s5b[:, 2] = tile2(np.transpose(inputs["c_re"][0], (2, 0, 1)))
    s5b[:, 3] = tile2(np.transpose(inputs["c_im"][0], (2, 0, 1)))
    shared["s5b"] = s5b
    shared["dsk"] = f(np.tile(np.asarray(inputs["d_skip"][0]).T, (8, 1)))
    in_maps = []
    for b in range(8):
        d = dict(shared)
        d["xT"] = f(x[b].T)
        d["memT"] = f(mem[b].T)
        in_maps.append(d)
    return in_maps


_CACHE = {}


def kernel(**inputs):
    in_maps = _host_inputs(inputs)
    if "nc" not in _CACHE:
        _CACHE["nc"] = build()[0]
    res = run_bass_kernel_spmd(_CACHE["nc"], in_maps, core_ids=list(range(8)))
    out = np.stack([np.ascontiguousarray(res.results[b]["outT"].T) for b in range(8)], axis=0)
    return out.astype(np.float32)
```

```python
import math
import numpy as np
import concourse.bass as bass
import concourse.mybir as mybir
from concourse.bass_utils import run_bass_kernel_spmd

F32 = mybir.dt.float32
BF16 = mybir.dt.bfloat16
AF = mybir.ActivationFunctionType
ALU = mybir.AluOpType

S = 2048
D = 1024
NB = 16
EPS = 1e-6
D_IN = 5832
D_FF = 2816
NFF = 22
NEG = -1.0e30
MBIAS = -30000.0


class Prog:
    ENGS = ("pe", "act", "dve", "pool", "sp")

    def __init__(self, nc, n_dma_sems=8):
        self.nc = nc
        self.streams = {e: [] for e in self.ENGS}
        self.sems = {e: nc.alloc_semaphore("s_" + e) for e in self.ENGS}
        self.cnt = {e: 0 for e in self.ENGS}
        self.known = {e: {} for e in self.ENGS}
        self.bufs = {}
        self.dma_sems = {}
        self.dma_rr = {}
        self.dma_val = {}
        self.semobj = {}
        for q in ("sp", "pool", "act"):
            self.dma_sems[q] = [nc.alloc_semaphore(f"d_{q}{i}") for i in range(n_dma_sems)]
            self.dma_rr[q] = 0
            for s in self.dma_sems[q]:
                self.semobj[s.name] = s
                self.dma_val[s.name] = 0
        for e in self.ENGS:
            self.semobj[self.sems[e].name] = self.sems[e]
        self.pending_pe = False

    def _need(self, eng, tok, waits):
        if tok is None:
            return
        sname, val = tok
        if self.known[eng].get(sname, 0) >= val:
            return
        if waits.get(sname, 0) < val:
            waits[sname] = val

    def _deps(self, eng, reads, writes, skip_writer=False):
        waits = {}
        for k in reads:
            st = self.bufs.get(k)
            if st is not None:
                self._need(eng, st[0], waits)
        for k in writes:
            st = self.bufs.get(k)
            if st is not None:
                if not skip_writer:
                    self._need(eng, st[0], waits)
                for tok in st[1].values():
                    self._need(eng, tok, waits)
        return waits

    def _commit(self, who, tok, reads, writes):
        for k in reads:
            st = self.bufs.setdefault(k, [None, {}])
            st[1][who + ":" + tok[0]] = tok
        for k in writes:
            self.bufs[k] = [tok, {}]

    def op(self, eng, fn, reads=(), writes=(), sig=True, accum=False):
        waits = self._deps(eng, reads, writes, skip_writer=(accum and eng == "pe"))
        if eng == "pe":
            waits.pop(self.sems["pe"].name, None)
        for s, v in waits.items():
            self.known[eng][s] = v
        if sig:
            self.cnt[eng] += 1
            tok = (self.sems[eng].name, self.cnt[eng])
            if eng == "pe":
                self.pending_pe = False
        else:
            assert eng == "pe"
            tok = (self.sems[eng].name, self.cnt[eng] + 1)
            self.pending_pe = True
        self.streams[eng].append((list(waits.items()), fn, (self.sems[eng], 1) if sig else None))
        self._commit(eng, tok, reads, writes)
        return tok

    def dma(self, q, fn, reads=(), writes=()):
        waits = self._deps(q, reads, writes)
        pool = self.dma_sems[q]
        s = pool[self.dma_rr[q] % len(pool)]
        self.dma_rr[q] += 1
        prev = self.dma_val[s.name]
        if prev > 0:
            self._need(q, (s.name, prev), waits)
        for sn, v in waits.items():
            self.known[q][sn] = v
        self.dma_val[s.name] = prev + 16
        tok = (s.name, prev + 16)
        self.streams[q].append((list(waits.items()), fn, (s, 16)))
        self._commit(q, tok, reads, writes)
        return tok

    def barrier(self):
        assert not self.pending_pe
        targets = [(self.sems[e].name, self.cnt[e]) for e in self.ENGS if self.cnt[e] > 0]
        targets += [(sn, v) for sn, v in self.dma_val.items() if v > 0]
        for e in self.ENGS:
            waits = {}
            for tok in targets:
                if e == "pe" and tok[0] == self.sems["pe"].name:
                    continue
                self._need(e, tok, waits)
            for sn, v in waits.items():
                self.known[e][sn] = v
            if waits:
                self.streams[e].append((list(waits.items()), None, None))
        self.bufs = {}

    def emit(self):
        assert not self.pending_pe
        nc = self.nc
        hmap = {"pe": "tensor", "act": "scalar", "dve": "vector", "pool": "gpsimd", "sp": "sync"}
        semobj = self.semobj
        with nc.Block() as block:
            for e in self.ENGS:
                def body(h, stream=self.streams[e]):
                    for waits, fn, inc in stream:
                        for sn, v in waits:
                            h.wait_ge(semobj[sn], v)
                        if fn is not None:
                            ins = fn(h)
                            if inc is not None:
                                ins.then_inc(inc[0], inc[1])
                getattr(block, hmap[e])(body)


class Arena:
    def __init__(self, nc, start=16640, limit=229376):
        self.nc = nc
        self.off = start
        self.limit = limit
        self.n = 0
        self.peak = 0

    def alloc(self, shape, dtype):
        nbytes = int(np.prod(shape[1:])) * (2 if dtype == BF16 else 4)
        nbytes = (nbytes + 63) // 64 * 64
        assert self.off + nbytes <= self.limit, ("SBUF overflow", self.off, nbytes)
        self.n += 1
        t = self.nc.alloc_sbuf_tensor_at(f"sb{self.n}", list(shape), dtype, offset=self.off)
        self.off += nbytes
        self.peak = max(self.peak, self.off)
        return t

    def mark(self):
        return self.off

    def release(self, m):
        if not getattr(self, "no_release", False):
            self.off = m


def _t5_bucket_table():
    d = np.arange(256)
    max_exact = 16
    nf = np.maximum(d, 1).astype(np.float32)
    large = max_exact + (np.log(nf / max_exact) / math.log(128 / max_exact) * (32 - max_exact)).astype(np.int32)
    large = np.minimum(large, 31)
    return np.where(d < max_exact, d, large)


def build(dbg=None):
    dbg = dbg or {}
    nc = bass.Bass("TRN2", target_bir_lowering=False)
    P = Prog(nc)
    A = Arena(nc)
    A.no_release = bool(dbg.get("no_release"))
    dbg_outs = []

    def din(name, shape):
        return nc.dram_tensor(name, list(shape), F32, kind="ExternalInput").ap()

    xT_d = din("xT", [D, S])
    memT_d = din("memT", [D, 256])
    w_in_d = din("w_in", [D, D_IN])
    w_uv_d = din("w_uv", [128, 8, 64])
    w_glu_d = din("w_glu", [512, 512])
    w_kv_d = din("w_mem_kv", [D, D])
    w_brd_d = din("w_br_dsa", [512, D])
    w_brs_d = din("w_br_s5", [512, D])
    w_brx_d = din("w_br_cross", [512, D])
    w_out_d = din("w_out", [D, D])
    w_g_d = din("w_ffn_gate", [D, D_FF])
    w_u_d = din("w_ffn_up", [D, D_FF])
    w_d_d = din("w_ffn_down", [D_FF, D])
    gvec_d = din("gvec", [128, 32])
    gkv_bc_d = din("gkv_bc", [128, 128])
    t5_d = din("t5", [128, 2, 8, 128])
    rb31_d = din("rb31", [128, 8])
    cm_d = din("cmats", [128, 5, 128])
    s5a_d = din("s5a", [128, 3, 32])
    s5b_d = din("s5b", [128, 4, 32, 16])
    dsk_d = din("dsk", [128, 32])
    outT_d = nc.dram_tensor("outT", [D, S], F32, kind="ExternalOutput").ap()
    scr_u = nc.dram_tensor("scr_u", [512, S], BF16).ap()
    scr_y = nc.dram_tensor("scr_y", [512, S], BF16).ap()

    ps = [nc.alloc_psum_tensor(f"ps{i}", [128, 512], F32) for i in range(6)]
    psTs = [nc.alloc_psum_tensor(f"psT{i}", [128, 1024], BF16) for i in range(2)]

    w_in_v = w_in_d.rearrange("(kc p) n -> p kc n", p=128)

    def dump(name, ap, shape, dt=F32):
        t = nc.dram_tensor("dbg_" + name, list(shape), dt, kind="ExternalOutput").ap()
        dbg_outs.append(("dbg_" + name))
        return t

    cm_f = A.alloc([128, 5, 128], F32)
    ident_f = cm_f[:, 0, :]
    causal_f = cm_f[:, 1, :]
    tmask_f = cm_f[:, 2, :]
    swap_f = cm_f[:, 3, :]
    cm_b = A.alloc([128, 5, 128], BF16)
    ident_b = cm_b[:, 0, :]
    ones_b = cm_b[:, 4, :]
    gvec = A.alloc([128, 32], F32)
    rx_bc = A.alloc([128, S], F32)
    rx_tok = A.alloc([128, NB], F32)
    off_o = A.mark()
    o_dsa = A.alloc([128, 4, S], BF16)

    P.dma("sp", lambda h: h.dma_start(out=cm_f[:], in_=cm_d), writes=["cm_f"])
    P.dma("pool", lambda h: h.dma_start(out=cm_b[:], in_=cm_d), writes=["cm_b"])
    P.dma("sp", lambda h: h.dma_start(out=gvec[:], in_=gvec_d), writes=["gvec"])
    CONST = ["cm_f", "cm_b", "gvec"]

    bank_rr = [0]

    def nb(lo=0, hi=6):
        b = lo + bank_rr[0] % (hi - lo)
        bank_rr[0] += 1
        return b

    def load_w(dst, src3, key, cols, q="pool"):
        kc_n = dst.shape[1]
        c0, c1 = cols
        for kc in range(kc_n):
            P.dma(q, (lambda h, kc=kc: h.dma_start(out=dst[:, kc, 0:c1 - c0], in_=src3[:, kc, c0:c1])),
                  writes=[(key, kc)])

    def build_xg(xg, with_stats, release=False):
        m = A.mark()
        xst = [A.alloc([128, S], F32) for _ in range(2)]
        sq = [A.alloc([128, S], BF16) for _ in range(2)]
        for kc in range(8):
            b = kc % 2
            P.dma("sp", (lambda h, kc=kc, b=b: h.dma_start(out=xst[b][:], in_=xT_d[kc * 128:(kc + 1) * 128, :])),
                  writes=[("xst", b)])
            if with_stats:
                P.op("act", (lambda h, b=b: h.activation(out=sq[b][:], in_=xst[b][:], func=AF.Square)),
                     reads=[("xst", b)], writes=[("sq", b)])
                for tc in range(4):
                    P.op("pe", (lambda h, b=b, tc=tc, kc=kc: h.matmul(ps[tc][:], lhsT=ones_b, rhs=sq[b][:, tc * 512:(tc + 1) * 512],
                                                                       start=(kc == 0), stop=(kc == 7))),
                         reads=[("sq", b), "cm_b"], writes=[("ps", tc)], sig=(tc == 3), accum=(kc > 0))
            P.op("dve", (lambda h, kc=kc, b=b: h.tensor_scalar(out=xg[:, kc, :], in0=xst[b][:], scalar1=gvec[:, kc:kc + 1],
                                                                scalar2=None, op0=ALU.mult)),
                 reads=[("xst", b), "gvec"], writes=[("xg", kc)])
        if with_stats:
            for tc in range(4):
                P.op("act", (lambda h, tc=tc: h.activation(out=rx_bc[:, tc * 512:(tc + 1) * 512], in_=ps[tc][:], func=AF.Sqrt,
                                                           scale=1.0 / D, bias=EPS)),
                     reads=[("ps", tc)], writes=[("rxs", tc)])
                P.op("dve", (lambda h, tc=tc: h.reciprocal(out=rx_bc[:, tc * 512:(tc + 1) * 512], in_=rx_bc[:, tc * 512:(tc + 1) * 512])),
                     reads=[("rxs", tc)], writes=[("rx_bc", tc)])
            for tt in range(NB):
                P.op("pe", (lambda h, tt=tt: h.matmul(ps[4][:, tt:tt + 1], lhsT=rx_bc[:, tt * 128:(tt + 1) * 128], rhs=ident_f[:, 0:1],
                                                      start=True, stop=True)),
                     reads=[("rx_bc", tt // 4), "cm_f"], writes=[("ps", 4)], sig=(tt == NB - 1))
            P.op("act", lambda h: h.copy(out=rx_tok[:], in_=ps[4][:, 0:NB]), reads=[("ps", 4)], writes=["rx_tok"])
        if release:
            A.release(m)

    XG = [("xg", kc) for kc in range(8)]
    RX = [("rx_bc", tc) for tc in range(4)]

    def proj_fm(xg, wb, wkey, c0, ncol, tc, bank, rhs_view=None):
        for kc in range(8):
            rhs = xg[:, kc, tc * 512:(tc + 1) * 512] if rhs_view is None else rhs_view(kc, tc)
            P.op("pe", (lambda h, kc=kc, rhs=rhs: h.matmul(ps[bank][0:ncol, :], lhsT=wb[:, kc, c0:c0 + ncol], rhs=rhs,
                                                            start=(kc == 0), stop=(kc == 7))),
                 reads=[(wkey, kc), ("xg", kc)], writes=[("ps", bank)], sig=(kc == 7), accum=(kc > 0))

    def head_norm(bank, bank2, gcol, dst, tmp_y, tmp_sq, tmp_sd, rx_ap, extra_reads, dst_key):
        n = dst.shape[-1]
        if rx_ap is None:
            P.op("act", lambda h: h.copy(out=tmp_y, in_=ps[bank][:, 0:n]), reads=[("ps", bank)] + extra_reads, writes=["hn_y"])
        else:
            P.op("dve", lambda h: h.tensor_tensor(out=tmp_y, in0=ps[bank][:, 0:n], in1=rx_ap, op=ALU.mult),
                 reads=[("ps", bank)] + extra_reads, writes=["hn_y"])
        P.op("act", lambda h: h.activation(out=tmp_sq, in_=tmp_y, func=AF.Square), reads=["hn_y"], writes=["hn_sq"])
        P.op("pe", lambda h: h.matmul(ps[bank2][:, 0:n], lhsT=ones_b, rhs=tmp_sq, start=True, stop=True),
             reads=["hn_sq", "cm_b"], writes=[("ps", bank2)])
        P.op("act", lambda h: h.activation(out=tmp_sd, in_=ps[bank2][:, 0:n], func=AF.Sqrt, scale=1.0 / 128, bias=EPS),
             reads=[("ps", bank2)], writes=["hn_sd"])
        P.op("dve", lambda h: h.reciprocal(out=tmp_sd, in_=tmp_sd), reads=["hn_sd"], writes=["hn_sd"])
        P.op("dve", lambda h: h.scalar_tensor_tensor(out=dst, in0=tmp_y, scalar=gvec[:, gcol:gcol + 1], in1=tmp_sd,
                                                     op0=ALU.mult, op1=ALU.mult),
             reads=["hn_y", "hn_sd", "gvec"], writes=[dst_key])

    base_mark = A.mark()

    xg = A.alloc([128, 8, S], BF16)
    build_xg(xg, True, release=True)
    P.op("dve", lambda h: h.tensor_scalar(out=gvec[:, 27:28], in0=gvec[:, 24:25], scalar1=128 ** -0.5, scalar2=None, op0=ALU.mult),
         reads=["gvec"], writes=["gvec"])
    P.op("dve", lambda h: h.tensor_scalar(out=gvec[:, 28:29], in0=gvec[:, 26:27], scalar1=128 ** -0.5, scalar2=None, op0=ALU.mult),
         reads=["gvec"], writes=["gvec"])

    def early(tag, tensors):
        if dbg.get("stop") != tag:
            return False
        for i, (t, shape, dt, keys) in enumerate(tensors):
            d = dump(f"{tag}{i}", None, shape, dt)
            P.dma("sp", (lambda h, d=d, t=t: h.dma_start(out=d, in_=t)), reads=keys, writes=[f"dbg_{tag}{i}"])
        P.barrier()
        P.emit()
        return True

    if early("xg", [(xg[:], [128, 8, S], BF16, XG), (rx_bc[:], [128, S], F32, RX), (rx_tok[:], [128, NB], F32, ["rx_tok"])]):
        return nc, dbg_outs

    def mb_base(m):
        return 128 * (m * (m + 1) // 2)

    def mk_off(c, j):
        return 512 * (2 * c * c + 2 * c + j)

    MBT = A.alloc([128, 512 * 40], BF16)
    mA = A.mark()

    wbi = A.alloc([128, 8, 584], BF16)
    wki = A.alloc([128, 8, 128], BF16)
    qiT = A.alloc([128, 4, S], BF16)
    kiT = A.alloc([128, S], BF16)
    widx = A.alloc([128, NB, 8], F32)
    load_w(wbi, w_in_v, "wbi", (1152, 1736))
    for kc in range(8):
        P.dma("pool", (lambda h, kc=kc: h.dma_start(out=wki[:, kc, 0:64], in_=w_in_v[:, kc, 1664:1728])), writes=[("wki", kc)])
        P.dma("pool", (lambda h, kc=kc: h.dma_start(out=wki[:, kc, 64:128], in_=w_in_v[:, kc, 1664:1728])), writes=[("wki", kc)])
    for j in range(4):
        for tc in range(4):
            b = nb()
            proj_fm(xg, wbi, "wbi", j * 128, 128, tc, b)
            P.op("dve", (lambda h, b=b, j=j, tc=tc: h.tensor_tensor(out=qiT[:, j, tc * 512:(tc + 1) * 512], in0=ps[b][:],
                                                                      in1=rx_bc[:, tc * 512:(tc + 1) * 512], op=ALU.mult)),
                 reads=[("ps", b), ("rx_bc", tc)], writes=[("qiT", j, tc)])
    for tc in range(4):
        b = nb()
        proj_fm(xg, wki, "wki", 0, 128, tc, b)
        P.op("dve", (lambda h, b=b, tc=tc: h.tensor_tensor(out=kiT[:, tc * 512:(tc + 1) * 512], in0=ps[b][:],
                                                            in1=rx_bc[:, tc * 512:(tc + 1) * 512], op=ALU.mult)),
             reads=[("ps", b), ("rx_bc", tc)], writes=[("kiT", tc)])
    bw = nb()
    for tt in range(NB):
        for kc in range(8):
            P.op("pe", (lambda h, tt=tt, kc=kc: h.matmul(ps[bw][:, tt * 8:(tt + 1) * 8], lhsT=xg[:, kc, tt * 128:(tt + 1) * 128],
                                                         rhs=wbi[:, kc, 576:584], start=(kc == 0), stop=(kc == 7))),
                 reads=[("wbi", kc), ("xg", kc)], writes=[("ps", bw)], sig=(kc == 7 and tt == NB - 1), accum=not (kc == 0 and tt == 0))
    P.op("dve", lambda h: h.tensor_tensor(out=widx[:], in0=ps[bw][:, 0:NB * 8].rearrange("p (a b) -> p a b", b=8),
                                          in1=rx_tok[:].unsqueeze(2).to_broadcast([128, NB, 8]), op=ALU.mult),
         reads=[("ps", bw), "rx_tok"], writes=["widx"])

    if early("proj", [(qiT[:], [128, 4, S], BF16, [("qiT", j, tc) for j in range(4) for tc in range(4)]),
                      (kiT[:], [128, S], BF16, [("kiT", tc) for tc in range(4)]), (widx[:], [128, NB, 8], F32, ["widx"])]):
        return nc, dbg_outs
    acc = [A.alloc([128, S], F32) for _ in range(2)]
    work = A.alloc([128, S], F32)
    rl = [A.alloc([128, 512], F32) for _ in range(3)]
    mbt = [A.alloc([128, S], BF16) for _ in range(2)]
    m8 = A.alloc([128, 8], F32)
    thr0 = A.alloc([128, 1], F32)
    bs_lo = A.alloc([128, 1], F32)
    bs_w = A.alloc([128, 1], F32)
    bs_mid = A.alloc([128, 1], F32)
    bs_cnt = A.alloc([128, 1], F32)
    bs_t = A.alloc([128, 1], F32)
    bs_ck = A.alloc([128, 24], F32)
    bs_hk = A.alloc([128, 24], F32)
    for k_ in range(24):
        P.op("dve", (lambda h, k_=k_: h.memset(bs_ck[:, k_:k_ + 1], 2.0 ** (-(k_ + 1)))), writes=["bs_ck"])
    P.op("dve", lambda h: h.memset(thr0[:], -1.0e29), writes=["thr0"])
    a_lo = A.alloc([128, 1], F32)
    a_w = A.alloc([128, 1], F32)
    a_hk = A.alloc([128, 24], F32)
    a_nh2 = A.alloc([128, 24], F32)
    a_nm = A.alloc([128, 2], F32)
    a_cnt = A.alloc([128, 1], F32)
    a_t = A.alloc([128, 1], F32)
    a_m8 = A.alloc([128, 8], F32)
    a_thr = A.alloc([128, 1], F32)
    workA = A.alloc([128, S], BF16)
    KB = 20
    rl_i = [0]

    def scores(m):
        ab = m % 2
        n = 128 * (m + 1)
        nsc = (n + 511) // 512
        for sc in range(nsc):
            w = min(512, n - 512 * sc)
            for hh in range(8):
                j, half = hh // 2, hh % 2
                b = nb()
                r = rl_i[0] % 3
                rl_i[0] += 1
                P.op("pe", (lambda h, b=b, j=j, half=half, sc=sc, w=w: h.matmul(
                    ps[b][:, 0:w], lhsT=qiT[64 * half:64 * half + 64, j, m * 128:(m + 1) * 128],
                    rhs=kiT[64 * half:64 * half + 64, sc * 512:sc * 512 + w], start=True, stop=True)),
                    reads=[("qiT", j, m // 4), ("kiT", sc)], writes=[("ps", b)])
                P.op("act", (lambda h, b=b, r=r, w=w: h.activation(out=rl[r][:, 0:w], in_=ps[b][:, 0:w], func=AF.Relu)),
                     reads=[("ps", b)], writes=[("rl", r)])
                if hh == 0:
                    P.op("dve", (lambda h, r=r, w=w, sc=sc: h.tensor_scalar(
                        out=acc[ab][:, sc * 512:sc * 512 + w], in0=rl[r][:, 0:w], scalar1=widx[:, m, 0:1], scalar2=None, op0=ALU.mult)),
                        reads=[("rl", r), "widx"], writes=[("acc", ab)])
                else:
                    P.op("dve", (lambda h, r=r, w=w, sc=sc, hh=hh: h.scalar_tensor_tensor(
                        out=acc[ab][:, sc * 512:sc * 512 + w], in0=rl[r][:, 0:w], scalar=widx[:, m, hh:hh + 1],
                        in1=acc[ab][:, sc * 512:sc * 512 + w], op0=ALU.mult, op1=ALU.add)),
                        reads=[("rl", r), "widx", ("acc", ab)], writes=[("acc", ab)])
        P.op("dve", (lambda h: h.tensor_tensor(out=acc[ab][:, m * 128:(m + 1) * 128], in0=acc[ab][:, m * 128:(m + 1) * 128],
                                               in1=causal_f, op=ALU.add)),
             reads=[("acc", ab), "cm_f"], writes=[("acc", ab)])

    def bisect_init(m, mx, lo, w_, hk, keyp, on_act):
        ab = m % 2
        n = 128 * (m + 1)
        nv = m * 128
        P.op("dve", (lambda h: h.max(out=mx[:], in_=acc[ab][:, 0:n])), reads=[("acc", ab)], writes=[keyp + "m8"])
        P.op("dve", (lambda h: h.tensor_reduce(out=lo[:], in_=acc[ab][:, 0:nv], axis=mybir.AxisListType.X, op=ALU.min)),
             reads=[("acc", ab)], writes=[keyp + "lo"])
        P.op("dve", lambda h: h.tensor_tensor(out=w_[:], in0=mx[:, 0:1], in1=lo[:], op=ALU.subtract), reads=[keyp + "m8", keyp + "lo"], writes=[keyp + "w"])
        P.op("dve", lambda h: h.tensor_scalar(out=hk[:], in0=bs_ck[:], scalar1=w_[:], scalar2=None, op0=ALU.mult),
             reads=[keyp + "w", "bs_ck"], writes=[keyp + "hk"])
        if on_act:
            P.op("dve", lambda h: h.tensor_scalar(out=a_nh2[:], in0=hk[:], scalar1=-0.5, scalar2=None, op0=ALU.mult), reads=[keyp + "hk"], writes=["a_nh2"])
            P.op("dve", lambda h: h.scalar_tensor_tensor(out=a_nm[:, 0:1], in0=lo[:], scalar=-1.0, in1=hk[:, 0:1], op0=ALU.mult, op1=ALU.subtract),
                 reads=[keyp + "lo", keyp + "hk"], writes=[("a_nm", 0)])
        else:
            P.op("dve", lambda h: h.tensor_tensor(out=bs_mid[:], in0=lo[:], in1=hk[:, 0:1], op=ALU.add), reads=[keyp + "lo", keyp + "hk"], writes=["bs_mid"])

    def bisect_dve(m):
        ab = m % 2
        n = 128 * (m + 1)
        for k_ in range(KB):
            P.op("dve", (lambda h: h.tensor_scalar(out=work[:, 0:n], in0=acc[ab][:, 0:n], scalar1=bs_mid[:], scalar2=None,
                                                   op0=ALU.is_ge, op1=ALU.add, accum_out=bs_cnt[:])),
                 reads=[("acc", ab), "bs_mid"], writes=["work", "bs_cnt"])
            P.op("dve", lambda h: h.tensor_scalar(out=bs_t[:], in0=bs_cnt[:], scalar1=255.5, scalar2=-0.5, op0=ALU.is_ge, op1=ALU.add),
                 reads=["bs_cnt"], writes=["bs_t"])
            P.op("dve", (lambda h, k_=k_: h.scalar_tensor_tensor(out=bs_mid[:], in0=bs_t[:], scalar=bs_hk[:, k_:k_ + 1], in1=bs_mid[:],
                                                                 op0=ALU.mult, op1=ALU.add)),
                 reads=["bs_t", "d_hk", "bs_mid"], writes=["bs_mid"])
        P.op("dve", (lambda h: h.tensor_tensor(out=m8[:, 7:8], in0=bs_mid[:], in1=bs_hk[:, KB:KB + 1], op=ALU.subtract)),
             reads=["bs_mid", "d_hk", "d_m8"], writes=["d_m8"])

    def bisect_act(m):
        ab = m % 2
        n = 128 * (m + 1)
        for k_ in range(KB):
            cur, nxt = k_ % 2, (k_ + 1) % 2
            P.op("act", (lambda h, cur=cur: h.activation(out=workA[:, 0:n], in_=acc[ab][:, 0:n], func=AF.Sign, bias=a_nm[:, cur:cur + 1], scale=1.0,
                                                         accum_out=a_cnt[:])),
                 reads=[("acc", ab), ("a_nm", cur)], writes=["workA", "a_cnt"])
            P.op("act", (lambda h: h.activation(out=a_t[:], in_=a_cnt[:], func=AF.Sign, bias=float(n) - 511.5, scale=1.0)),
                 reads=["a_cnt"], writes=["a_t"])
            P.op("act", (lambda h, cur=cur, nxt=nxt, k_=k_: h.activation(out=a_nm[:, nxt:nxt + 1], in_=a_t[:], func=AF.Identity,
                                                                        scale=a_nh2[:, k_:k_ + 1], bias=a_nm[:, cur:cur + 1])),
                 reads=["a_t", "a_nh2", ("a_nm", cur)], writes=[("a_nm", nxt)])
        fin = KB % 2
        P.op("dve", (lambda h: h.scalar_tensor_tensor(out=a_thr[:], in0=a_nm[:, fin:fin + 1], scalar=-1.0, in1=a_hk[:, KB:KB + 1],
                                                      op0=ALU.mult, op1=ALU.subtract)),
             reads=[("a_nm", fin), "a_hk"], writes=["a_thr"])

    def finish(m, thr, thrk):
        ab = m % 2
        n = 128 * (m + 1)
        P.op("dve", (lambda h: h.tensor_scalar(out=mbt[ab][:, 0:n], in0=acc[ab][:, 0:n], scalar1=thr, scalar2=None, op0=ALU.is_ge)),
             reads=[("acc", ab), thrk], writes=[("mbt", ab)])
        for j0 in range(0, m + 1, 4):
            jn = min(4, m + 1 - j0)
            tb = (j0 // 4) % 2
            for jj in range(jn):
                j = j0 + jj
                P.op("pe", (lambda h, j=j, jj=jj, tb=tb: h.transpose(out=psTs[tb][:, jj * 128:(jj + 1) * 128],
                                                                      in_=mbt[ab][:, j * 128:(j + 1) * 128], identity=ident_b)),
                     reads=[("mbt", ab), "cm_b"], writes=[("psT", tb)], sig=(jj == jn - 1), accum=(jj > 0))
            P.op("act", (lambda h, j0=j0, jn=jn, tb=tb: h.copy(
                out=MBT[:, mk_off(m // 4, j0): mk_off(m // 4, j0) + jn * 512].rearrange("p (a b) -> p a b", b=512)[:, :, (m % 4) * 128:(m % 4) * 128 + 128],
                in_=psTs[tb][:, 0:jn * 128].rearrange("p (a b) -> p a b", b=128))),
                 reads=[("psT", tb)], writes=[("MBT", m)])

    for pr_ in range(NB // 2):
        m0, m1 = 2 * pr_, 2 * pr_ + 1
        scores(m0)
        scores(m1)
        if m0 >= 2:
            bisect_init(m1, a_m8, a_lo, a_w, a_hk, "a_", True)
            bisect_init(m0, m8, bs_lo, bs_w, bs_hk, "d_", False)
            bisect_act(m1)
            bisect_dve(m0)
            finish(m0, m8[:, 7:8], "d_m8")
            finish(m1, a_thr[:], "a_thr")
        else:
            finish(m0, thr0[:], "thr0")
            finish(m1, thr0[:], "thr0")
    if early("scores", [(MBT[:], [128, 512 * 40], BF16, [("MBT", m) for m in dbg.get("m_list", range(dbg.get("m_max", NB)))]),
                        (acc[0][:], [128, S], F32, [("acc", 0)]), (acc[1][:], [128, S], F32, [("acc", 1)]), (m8[:], [128, 8], F32, ["d_m8"])]):
        return nc, dbg_outs
    if "mbt" in dbg:
        t = dump("mbt", None, [128, 512 * 40], BF16)
        P.dma("sp", (lambda h, t=t: h.dma_start(out=t, in_=MBT[:])), reads=[("MBT", m) for m in range(NB)], writes=["dbg_mbt"])

    P.barrier()
    A.release(mA)

    wbq = [A.alloc([128, 8, 512], BF16) for _ in range(2)]
    wbc = A.alloc([128, 8, 128], BF16)
    wuv = A.alloc([128, 8, 64], BF16)
    qT = A.alloc([128, 8, S], BF16)
    c_tok = A.alloc([128, NB, 128], BF16)
    cT = A.alloc([128, S], BF16)
    t5f = A.alloc([128, 2, 8, 128], F32)
    t5b = A.alloc([128, 2, 8, 128], BF16)
    rb31 = A.alloc([128, 8], F32)
    gkv_bc = A.alloc([128, 128], F32)
    tmp_y = A.alloc([128, 512], F32)
    tmp_sq = A.alloc([128, 512], BF16)
    tmp_sd = A.alloc([128, 512], F32)
    ss1 = A.alloc([128, 2], F32)
    for i in range(2):
        load_w(wbq[i], w_in_v, ("wbq", i), (512 * i, 512 * i + 512))
    load_w(wbc, w_in_v, "wbc", (1024, 1152))
    P.dma("pool", lambda h: h.dma_start(out=wuv[:], in_=w_uv_d), writes=["wuv"])
    P.dma("sp", lambda h: h.dma_start(out=t5f[:], in_=t5_d), writes=["t5f"])
    P.dma("sp", lambda h: h.dma_start(out=rb31[:], in_=rb31_d), writes=["rb31"])
    P.dma("sp", lambda h: h.dma_start(out=gkv_bc[:], in_=gkv_bc_d), writes=["gkv_bc"])
    P.op("dve", lambda h: h.tensor_tensor(out=t5b[:], in0=t5f[:], in1=rb31[:].unsqueeze(1).unsqueeze(3).to_broadcast([128, 2, 8, 128]),
                                          op=ALU.subtract),
         reads=["t5f", "rb31"], writes=["t5b"])
    for hh in range(8):
        for tc in range(4):
            b = nb(0, 3)
            proj_fm(xg, wbq[hh // 4], ("wbq", hh // 4), (hh % 4) * 128, 128, tc, b)
            head_norm(b, 3 + (tc % 2), 27, qT[:, hh, tc * 512:(tc + 1) * 512], tmp_y[:], tmp_sq[:], tmp_sd[:],
                      rx_bc[:, tc * 512:(tc + 1) * 512], [("rx_bc", tc)], ("qT", hh, tc))
    for tt in range(NB):
        b = nb(0, 3)
        for kc in range(8):
            P.op("pe", (lambda h, b=b, tt=tt, kc=kc: h.matmul(ps[b][:, 0:128], lhsT=xg[:, kc, tt * 128:(tt + 1) * 128], rhs=wbc[:, kc, :],
                                                              start=(kc == 0), stop=(kc == 7))),
                 reads=[("wbc", kc), ("xg", kc)], writes=[("ps", b)], sig=(kc == 7), accum=(kc > 0))
        P.op("act", (lambda h, b=b, tt=tt: h.activation(out=tmp_y[:, 0:128], in_=ps[b][:, 0:128], func=AF.Copy, scale=rx_tok[:, tt:tt + 1])),
             reads=[("ps", b), "rx_tok"], writes=["hn_y"])
        P.op("act", (lambda h: h.activation(out=tmp_y[:, 128:256], in_=tmp_y[:, 0:128], func=AF.Square, accum_out=ss1[:, 0:1])),
             reads=["hn_y"], writes=["c_ss", "hn_y"])
        P.op("act", (lambda h: h.activation(out=ss1[:, 1:2], in_=ss1[:, 0:1], func=AF.Sqrt, scale=1.0 / 128, bias=EPS)),
             reads=["c_ss"], writes=["c_sd"])
        P.op("dve", (lambda h: h.reciprocal(out=ss1[:, 1:2], in_=ss1[:, 1:2])), reads=["c_sd"], writes=["c_sd"])
        P.op("dve", (lambda h, tt=tt: h.scalar_tensor_tensor(out=c_tok[:, tt, :], in0=tmp_y[:, 0:128], scalar=ss1[:, 1:2], in1=gkv_bc[:],
                                                             op0=ALU.mult, op1=ALU.mult)),
             reads=["hn_y", "c_sd", "gkv_bc"], writes=[("c_tok", tt)])
    for t0 in range(0, NB, 4):
        tb = (t0 // 4) % 2
        for jj in range(4):
            tt = t0 + jj
            P.op("pe", (lambda h, tt=tt, jj=jj, tb=tb: h.transpose(out=psTs[tb][:, jj * 128:(jj + 1) * 128],
                                                                 in_=c_tok[:, tt, :], identity=ident_b)),
                 reads=[("c_tok", tt), "cm_b"], writes=[("psT", tb)], sig=(jj == 3), accum=(jj > 0))
        P.op("act", (lambda h, t0=t0, tb=tb: h.copy(out=cT[:, t0 * 128:(t0 + 4) * 128], in_=psTs[tb][:, 0:512])),
             reads=[("psT", tb)], writes=[("cT", t0 // 4)])
    if "qT" in dbg:
        t = dump("qT", None, [128, 8, S], BF16)
        P.dma("sp", (lambda h, t=t: h.dma_start(out=t, in_=qT[:])), reads=[("qT", a, b_) for a in range(8) for b_ in range(4)], writes=["dbg_qT"])
        t2 = dump("cT", None, [128, S], BF16)
        P.dma("sp", (lambda h, t2=t2: h.dma_start(out=t2, in_=cT[:])), reads=[("cT", i) for i in range(4)], writes=["dbg_cT"])

    PT = [A.alloc([128, 512], BF16) for _ in range(4)]
    PE_ = [A.alloc([128, 512], BF16) for _ in range(4)]
    rden = A.alloc([128, 512], F32)
    onT = [A.alloc([128, 512], BF16) for _ in range(2)]
    BU = 5
    accO = [ps[3][:], ps[3][:]]
    accD = [ps[4][:], ps[4][:]]
    accK = [(("ps", 3), ("ps", 4)), (("ps", 3), ("ps", 4))]
    its = []
    for c in range(4):
        for hh in range(8):
            nj = 4 * c + 4
            for j in range(nj):
                its.append((c, hh, j, nj))
    LAG = 2
    deferred = []

    def front(i):
        c, hh, j, nj = its[i]
        t_lo = max(512 * c, 128 * j)
        co = t_lo - 512 * c
        bs = i % 3
        pb = i % 4
        adds = []
        for diff in (0, 1):
            m = j + diff
            if 4 * c <= m <= 4 * c + 3:
                adds.append(((m - 4 * c) * 128, t5b[:, diff, hh, :], "t5b"))
        P.op("pe", (lambda h: h.matmul(ps[bs][:, co:512], lhsT=cT[:, j * 128:(j + 1) * 128], rhs=qT[:, hh, t_lo:512 * c + 512],
                                       start=True, stop=(len(adds) == 0))),
             reads=[("cT", j // 4), ("qT", hh, c)], writes=[("ps", bs)], sig=(len(adds) == 0))
        for ai, (col, rhs, key) in enumerate(adds):
            last = ai == len(adds) - 1
            P.op("pe", (lambda h, col=col, rhs=rhs, last=last: h.matmul(ps[bs][:, col:col + 128], lhsT=ident_b, rhs=rhs, start=False, stop=last)),
                 reads=[key, "cm_b"], writes=[("ps", bs)], sig=last, accum=True)
        P.op("act", (lambda h: h.activation(out=PE_[pb][:, 0:512 - co], in_=ps[bs][:, co:512], func=AF.Exp, bias=rb31[:, hh:hh + 1], scale=1.0)),
             reads=[("ps", bs), "rb31"], writes=[("PE_", pb)])
        P.op("dve", (lambda h: h.tensor_tensor(out=PT[pb][:, 0:512 - co], in0=PE_[pb][:, 0:512 - co],
                                               in1=MBT[:, mk_off(c, j) + co: mk_off(c, j) + 512], op=ALU.mult)),
             reads=[("PE_", pb)] + [("MBT", m) for m in range(4 * c, 4 * c + 4)], writes=[("PT", pb)])

    def back(i):
        c, hh, j, nj = its[i]
        t_lo = max(512 * c, 128 * j)
        co = t_lo - 512 * c
        pb = i % 4
        hidx = c * 8 + hh
        ab_ = hidx % 2
        aO, aD = accO[ab_], accD[ab_]
        kO, kD = accK[ab_]
        P.op("pe", (lambda h: h.matmul(aO[:, co:512], lhsT=c_tok[:, j, :], rhs=PT[pb][:, 0:512 - co], start=(j == 0), stop=(j == nj - 1))),
             reads=[("c_tok", j), ("PT", pb)], writes=[kO], sig=False, accum=(j > 0))
        P.op("pe", (lambda h: h.matmul(aD[:, co:512], lhsT=ones_b, rhs=PT[pb][:, 0:512 - co], start=(j == 0), stop=(j == nj - 1))),
             reads=["cm_b", ("PT", pb)], writes=[kD], sig=True, accum=(j > 0))
        if j == nj - 1:
            ob = hh % 2
            P.op("dve", lambda h: h.reciprocal(out=rden[:], in_=aD), reads=[kD], writes=["rden"])
            P.op("dve", (lambda h: h.tensor_tensor(out=onT[ob][:], in0=aO, in1=rden[:], op=ALU.mult)),
                 reads=[kO, "rden"], writes=[("onT", ob)])

            def epilogue():
                P.op("pe", (lambda h: h.matmul(ps[BU][64 * (hh % 2):64 * (hh % 2) + 64, :], lhsT=wuv[:, hh, :], rhs=onT[ob][:], start=True, stop=True)),
                     reads=["wuv", ("onT", ob)], writes=[("ps", BU, hh % 2)])
                if hh % 2 == 1:
                    P.op("act", (lambda h: h.copy(out=o_dsa[:, hh // 2, c * 512:(c + 1) * 512], in_=ps[BU][:])),
                         reads=[("ps", BU, 0), ("ps", BU, 1)], writes=[("o_dsa", hh // 2, c)])
            deferred.append([2, epilogue])

    for i in range(len(its) + LAG):
        if i < len(its):
            front(i)
        if i - LAG >= 0:
            back(i - LAG)
            for dct in list(deferred):
                dct[0] -= 1
                if dct[0] <= 0:
                    dct[1]()
                    deferred.remove(dct)
    for dct in deferred:
        dct[1]()
    if "o_dsa" in dbg:
        t = dump("o_dsa", None, [128, 4, S], BF16)
        P.dma("sp", (lambda h, t=t: h.dma_start(out=t, in_=o_dsa[:])), reads=[("o_dsa", a, c) for a in range(4) for c in range(4)],
              writes=["dbg_o_dsa"])

    P.barrier()
    A.release(base_mark)
    o_s5 = A.alloc([128, 4, S], BF16)
    o_x = A.alloc([128, 4, S], BF16)
    base_mark = A.mark()
    if dbg.get("od_early"):
        t = dump("od0", None, [128, 4, S], BF16)
        P.dma("sp", (lambda h, t=t: h.dma_start(out=t, in_=o_dsa[:])), writes=["dbg_od0"])
    if dbg.get("stop") == "dsa":
        P.barrier()
        P.emit()
        return nc, dbg_outs


    xgX = A.alloc([128, 8, S], BF16)
    build_xg(xgX, False)
    wbx = A.alloc([128, 8, 512], BF16)
    wbu = A.alloc([128, 8, 512], BF16)
    wkv = A.alloc([128, 8, 1024], BF16)
    memf = A.alloc([128, 8, 256], F32)
    msq = A.alloc([128, 8, 256], BF16)
    memn = A.alloc([128, 8, 256], BF16)
    rm = A.alloc([128, 256], F32)
    khT = A.alloc([128, 4, 256], BF16)
    vtok = A.alloc([128, 2, 512], BF16)
    qxT = A.alloc([128, 4, S], BF16)
    ust = [A.alloc([128, 512], BF16) for _ in range(2)]
    tyX = A.alloc([128, 512], F32)
    tqX = A.alloc([128, 512], BF16)
    tdX = A.alloc([128, 512], F32)
    PTx = [A.alloc([128, 512], BF16) for _ in range(3)]
    rdx = A.alloc([128, 512], F32)
    w_kv_v = w_kv_d.rearrange("(kc p) n -> p kc n", p=128)
    if not dbg.get("no_wbx"):
        load_w(wbx, w_in_v, "wbx", (2248, 2760))
    if not dbg.get("no_wbu"):
        load_w(wbu, w_in_v, "wbu", (1736, 2248))
    for i in range(2):
        if not dbg.get("no_wkv"):
            load_w(wkv[:, :, i * 512:(i + 1) * 512], w_kv_v, ("wkv", i), (i * 512, (i + 1) * 512))
    P.dma("sp", lambda h: h.dma_start(out=memf[:], in_=memT_d.rearrange("(kc p) m -> p kc m", p=128)), writes=["memf"])
    if dbg.get("stop") == "x0":
        P.barrier()
        t = dump("od", None, [128, 4, S], BF16)
        P.dma("sp", (lambda h, t=t: h.dma_start(out=t, in_=o_dsa[:])), writes=["dbg_od"])
        P.barrier()
        P.emit()
        return nc, dbg_outs
    P.op("act", lambda h: h.activation(out=msq[:], in_=memf[:], func=AF.Square), reads=["memf"], writes=["msq"])
    bm = nb(0, 3)
    for kc in range(8):
        P.op("pe", (lambda h, kc=kc: h.matmul(ps[bm][:, 0:256], lhsT=ones_b, rhs=msq[:, kc, :], start=(kc == 0), stop=(kc == 7))),
             reads=["msq", "cm_b"], writes=[("ps", bm)], sig=(kc == 7), accum=(kc > 0))
    P.op("act", lambda h: h.activation(out=rm[:], in_=ps[bm][:, 0:256], func=AF.Sqrt, scale=1.0 / D, bias=EPS), reads=[("ps", bm)], writes=["rm"])
    P.op("dve", lambda h: h.reciprocal(out=rm[:], in_=rm[:]), reads=["rm"], writes=["rm"])
    for kc in range(8):
        P.op("dve", (lambda h, kc=kc: h.scalar_tensor_tensor(out=memn[:, kc, :], in0=memf[:, kc, :], scalar=gvec[:, 8 + kc:9 + kc], in1=rm[:],
                                                             op0=ALU.mult, op1=ALU.mult)),
             reads=["memf", "rm", "gvec"], writes=[("memn", kc)])
    for hh in range(4):
        b = nb(0, 3)
        for kc in range(8):
            P.op("pe", (lambda h, b=b, hh=hh, kc=kc: h.matmul(ps[b][:, 0:256], lhsT=wkv[:, kc, hh * 128:(hh + 1) * 128], rhs=memn[:, kc, :],
                                                              start=(kc == 0), stop=(kc == 7))),
                 reads=[(("wkv", 0), kc), ("memn", kc)], writes=[("ps", b)], sig=(kc == 7), accum=(kc > 0))
        head_norm(b, 3 + (hh % 2), 28, khT[:, hh, :], tyX[:, 0:256], tqX[:, 0:256], tdX[:, 0:256], None, [], ("khT", hh))
    for mb in range(2):
        b = nb(0, 3)
        for kc in range(8):
            P.op("pe", (lambda h, b=b, mb=mb, kc=kc: h.matmul(ps[b][:], lhsT=memn[:, kc, mb * 128:(mb + 1) * 128], rhs=wkv[:, kc, 512:1024],
                                                              start=(kc == 0), stop=(kc == 7))),
                 reads=[(("wkv", 1), kc), ("memn", kc)], writes=[("ps", b)], sig=(kc == 7), accum=(kc > 0))
        P.op("act", (lambda h, b=b, mb=mb: h.copy(out=vtok[:, mb, :], in_=ps[b][:])), reads=[("ps", b)], writes=[("vtok", mb)])
    for hh in range(4):
        for tc in range(4):
            b = nb(0, 3)
            proj_fm(xgX, wbx, "wbx", hh * 128, 128, tc, b)
            head_norm(b, 3 + (tc % 2), 25, qxT[:, hh, tc * 512:(tc + 1) * 512], tyX[:], tqX[:], tdX[:],
                      rx_bc[:, tc * 512:(tc + 1) * 512], [("rx_bc", tc)], ("qxT", hh, tc))
    ui = 0
    for ch in range(4):
        for tcp in range(4):
            b = nb(0, 3)
            ub = ui % 2
            ui += 1
            proj_fm(xgX, wbu, "wbu", ch * 128, 128, tcp, b,
                    rhs_view=(lambda kc, tcp: xgX[:, kc, :].rearrange("p (b j) -> p j b", j=8)[:, 2 * tcp:2 * tcp + 2, :]))
            P.op("dve", (lambda h, b=b, ub=ub, tcp=tcp: h.tensor_tensor(
                out=ust[ub][:].rearrange("p (j b) -> p j b", j=2), in0=ps[b][:].rearrange("p (j b) -> p j b", j=2),
                in1=rx_bc[:].rearrange("p (b j) -> p j b", j=8)[:, 2 * tcp:2 * tcp + 2, :], op=ALU.mult)),
                reads=[("ps", b)] + RX, writes=[("ust", ub)])
            P.dma("sp", (lambda h, ub=ub, ch=ch, tcp=tcp: h.dma_start(out=scr_u[ch * 128:(ch + 1) * 128, tcp * 512:(tcp + 1) * 512], in_=ust[ub][:])),
                  reads=[("ust", ub)], writes=[("scr_u", ch, tcp)])
    ptiX = 0
    BO, BD = 3, 4
    for c in range(4):
        for hh in range(4):
            for mb in range(2):
                bs = nb(0, 3)
                pb = ptiX % 3
                ptiX += 1
                P.op("pe", (lambda h, bs=bs, hh=hh, mb=mb, c=c: h.matmul(ps[bs][:], lhsT=khT[:, hh, mb * 128:(mb + 1) * 128],
                                                                         rhs=qxT[:, hh, c * 512:(c + 1) * 512], start=True, stop=True)),
                     reads=[("khT", hh), ("qxT", hh, c)], writes=[("ps", bs)])
                P.op("act", (lambda h, bs=bs, pb=pb: h.activation(out=PTx[pb][:], in_=ps[bs][:], func=AF.Exp)),
                     reads=[("ps", bs)], writes=[("PT", pb)])
                P.op("pe", (lambda h, pb=pb, mb=mb, hh=hh: h.matmul(ps[BO][:], lhsT=vtok[:, mb, hh * 128:(hh + 1) * 128], rhs=PTx[pb][:],
                                                                    start=(mb == 0), stop=(mb == 1))),
                     reads=[("vtok", mb), ("PT", pb)], writes=[("ps", BO)], sig=False, accum=(mb > 0))
                P.op("pe", (lambda h, pb=pb, mb=mb: h.matmul(ps[BD][:], lhsT=ones_b, rhs=PTx[pb][:], start=(mb == 0), stop=(mb == 1))),
                     reads=["cm_b", ("PT", pb)], writes=[("ps", BD)], sig=True, accum=(mb > 0))
            P.op("dve", lambda h: h.reciprocal(out=rdx[:], in_=ps[BD][:]), reads=[("ps", BD)], writes=["rden"])
            P.op("dve", (lambda h, hh=hh, c=c: h.tensor_tensor(out=o_x[:, hh, c * 512:(c + 1) * 512], in0=ps[BO][:], in1=rdx[:], op=ALU.mult)),
                 reads=[("ps", BO), "rden"], writes=[("o_x", hh, c)])
    if dbg.get("stop") == "cross":
        t = dump("od", None, [128, 4, S], BF16)
        P.dma("sp", (lambda h, t=t: h.dma_start(out=t, in_=o_dsa[:])), writes=["dbg_od"])
        t = dump("o_x", None, [128, 4, S], BF16)
        P.dma("sp", (lambda h, t=t: h.dma_start(out=t, in_=o_x[:])), reads=[("o_x", a, c) for a in range(4) for c in range(4)], writes=["dbg_o_x"])
        t2 = dump("scr_u", None, [512, S], BF16)
        P.dma("sp", (lambda h, t2=t2: h.dma_start(out=t2, in_=scr_u)), reads=[("scr_u", a, c) for a in range(4) for c in range(4)], writes=["dbg_scr_u"])
        P.barrier()
        P.emit()
        return nc, dbg_outs
    P.barrier()
    A.release(base_mark)

    s5a = A.alloc([128, 3, 32], F32)
    s5b = A.alloc([128, 4, 32, 16], F32)
    dsk = A.alloc([128, 32], F32)
    TB = [A.alloc([128, 2, 8, 32], F32) for _ in range(4)]
    TK = A.alloc([128, 8, 2, 32], F32)
    cw = A.alloc([128, 16, 2, 32], F32)
    bb = A.alloc([128, 2, 32, 16], F32)
    W1 = A.alloc([128, 32, 128], BF16)
    W2 = A.alloc([128, 32, 128], BF16)
    Tm = A.alloc([128, 32, 128], BF16)
    U8 = A.alloc([128, 32, 256], BF16)
    X = A.alloc([128, 32, 256], BF16)
    mS = A.alloc([128, 1], F32)
    mS1 = A.mark()
    W1T = A.alloc([128, 32, 128], BF16)
    Lm = A.alloc([128, 32, 128], BF16)
    Rm = A.alloc([128, 32, 128], BF16)
    tA = A.alloc([128, 32, 128], F32)
    tB = A.alloc([128, 32, 128], F32)
    P.dma("sp", lambda h: h.dma_start(out=s5a[:], in_=s5a_d), writes=["s5a"])
    P.dma("sp", lambda h: h.dma_start(out=s5b[:], in_=s5b_d), writes=["s5b"])
    P.dma("sp", lambda h: h.dma_start(out=dsk[:], in_=dsk_d), writes=["dsk"])
    for jl in range(8):
        P.dma("sp", (lambda h, jl=jl: h.dma_start(out=U8[jl * 16:(jl + 1) * 16, :, :],
                                                  in_=scr_u.rearrange("(g c) (j b) -> j c g b", c=16, j=8)[jl])),
              reads=[("scr_u", a, c) for a in range(4) for c in range(4)], writes=[("U8", jl)])
    U8K = [("U8", jl) for jl in range(8)]

    def V(i):
        return cw[:, i, :, :]

    def tt(out, a, b_, op, rk, wk):
        P.op("dve", lambda h: h.tensor_tensor(out=out, in0=a, in1=b_, op=op), reads=rk, writes=wk)

    def cmul(dst, a, b_, ka, kb, kd):
        t1, t2 = V(14), V(15)
        tt(t1[:, 0, :], a[:, 0, :], b_[:, 0, :], ALU.mult, [ka, kb], ["cw_t1a"])
        tt(t1[:, 1, :], a[:, 1, :], b_[:, 1, :], ALU.mult, [ka, kb], ["cw_t1b"])
        tt(t2[:, 0, :], a[:, 0, :], b_[:, 1, :], ALU.mult, [ka, kb], ["cw_t2a"])
        tt(t2[:, 1, :], a[:, 1, :], b_[:, 0, :], ALU.mult, [ka, kb], ["cw_t2b"])
        tt(dst[:, 0, :], t1[:, 0, :], t1[:, 1, :], ALU.subtract, ["cw_t1a", "cw_t1b"], [kd])
        tt(dst[:, 1, :], t2[:, 0, :], t2[:, 1, :], ALU.add, ["cw_t2a", "cw_t2b", kd], [kd])

    a_re, a_im, ldt = s5a[:, 0, :], s5a[:, 1, :], s5a[:, 2, :]
    dtv = V(0)[:, 0, :]
    adr = V(0)[:, 1, :]
    adi = V(1)[:, 0, :]
    P.op("act", lambda h: h.activation(out=dtv, in_=ldt, func=AF.Exp), reads=["s5a"], writes=["dtv"])
    tt(adr, a_re, dtv, ALU.mult, ["s5a", "dtv"], ["adr"])
    tt(adi, a_im, dtv, ALU.mult, ["s5a", "dtv"], ["adi"])
    mag, magn, cs, sn = V(2)[:, 0, :], V(2)[:, 1, :], V(3)[:, 0, :], V(3)[:, 1, :]
    P.op("dve", lambda h: h.memset(mS[:], math.pi / 2), writes=["mS"])
    P.op("act", lambda h: h.activation(out=mag, in_=adr, func=AF.Exp, scale=1.0 / 16), reads=["adr"], writes=["mag"])
    P.op("act", lambda h: h.activation(out=magn, in_=adr, func=AF.Exp, scale=-1.0 / 16), reads=["adr"], writes=["magn"])
    P.op("act", lambda h: h.activation(out=cs, in_=adi, func=AF.Sin, scale=1.0 / 16, bias=mS[:]), reads=["adi", "mS"], writes=["cs"])
    P.op("act", lambda h: h.activation(out=sn, in_=adi, func=AF.Sin, scale=1.0 / 16), reads=["adi"], writes=["sn"])
    mu, nu = V(4), V(5)
    tt(mu[:, 0, :], mag, cs, ALU.mult, ["mag", "cs"], ["mu"])
    tt(mu[:, 1, :], mag, sn, ALU.mult, ["mag", "sn", "mu"], ["mu"])
    tt(nu[:, 0, :], magn, cs, ALU.mult, ["magn", "cs"], ["nu"])
    P.op("dve", lambda h: h.scalar_tensor_tensor(out=nu[:, 1, :], in0=magn, scalar=-1.0, in1=sn, op0=ALU.mult, op1=ALU.mult),
         reads=["magn", "sn", "nu"], writes=["nu"])
    def pw_slot(t, slot):
        return TB[t][:, :, slot, :]
    TW1, TW2, TL, TR = 0, 1, 2, 3
    cur, ck = mu, "mu"
    for i in range(4):
        dst = V(6 + (i % 2)) if i < 3 else pw_slot(TR, 1)
        kd = f"sqp{i}" if i < 3 else ("P", 1)
        cmul(dst, cur, cur, ck, ck, kd)
        cur, ck = dst, kd
    cur, ck = nu, "nu"
    for i in range(4):
        dst = V(8 + (i % 2)) if i < 3 else pw_slot(TL, 1)
        kd = f"sqn{i}" if i < 3 else ("N", 1)
        cmul(dst, cur, cur, ck, ck, kd)
        cur, ck = dst, kd
    Pp = {1: pw_slot(TR, 1)}
    Np = {1: pw_slot(TL, 1)}
    for t_ in (TR, TL, TW1):
        sl = 7 if t_ == TW1 else 0
        P.op("dve", (lambda h, t_=t_, sl=sl: h.memset(TB[t_][:, 0, sl, :], 1.0)), writes=[("one", t_, 0)])
        P.op("dve", (lambda h, t_=t_, sl=sl: h.memset(TB[t_][:, 1, sl, :], 0.0)), writes=[("one", t_, 1)])
    for k, (a, b_) in ((2, (1, 1)), (3, (2, 1)), (4, (2, 2)), (5, (4, 1)), (6, (4, 2)), (7, (4, 3))):
        Pp[k] = pw_slot(TR, k)
        cmul(Pp[k], Pp[a], Pp[b_], ("P", a), ("P", b_), ("P", k))
        Np[k] = pw_slot(TL, k)
        cmul(Np[k], Np[a], Np[b_], ("N", a), ("N", b_), ("N", k))
    Pp[8] = pw_slot(TW2, 7)
    cmul(Pp[8], Pp[4], Pp[4], ("P", 4), ("P", 4), ("P", 8))
    PK = [("P", k) for k in range(1, 9)]
    NK = [("N", k) for k in range(1, 8)]
    P.op("dve", lambda h: h.tensor_copy(out=TB[TW2][:, :, 0:7, :], in_=TB[TR][:, :, 1:8, :]), reads=PK, writes=["TW2"])
    for jl in range(7):
        P.op("dve", (lambda h, jl=jl: h.tensor_copy(out=TB[TW1][:, :, jl, :], in_=TB[TR][:, :, 7 - jl, :])), reads=PK, writes=[("TW1", jl)])
    TW1K = [("TW1", jl) for jl in range(7)] + [("one", TW1, 0), ("one", TW1, 1)]
    TRK = PK + [("one", TR, 0), ("one", TR, 1)]
    TLK = NK + [("one", TL, 0), ("one", TL, 1)]
    TW2K = ["TW2", ("P", 8)]
    P.op("dve", lambda h: h.tensor_copy(out=TK[:, 0, :, :], in_=Pp[8]), reads=[("P", 8)], writes=[("TK", 0)])
    for l in range(1, 8):
        cmul(TK[:, l, :, :], TK[:, l - 1, :, :], TK[:, l - 1, :, :], ("TK", l - 1), ("TK", l - 1), ("TK", l))
    P.op("dve", lambda h: h.tensor_scalar(out=TK[64:128, :, 1, :], in0=TK[64:128, :, 1, :], scalar1=-1.0, scalar2=None, op0=ALU.mult),
         reads=[("TK", l) for l in range(8)], writes=["TKs"])
    num, qv = V(10), V(11)
    den = V(12)[:, 0, :]
    P.op("dve", lambda h: h.tensor_scalar(out=num[:, 0, :], in0=Pp[1][:, 0, :], scalar1=-1.0, scalar2=None, op0=ALU.add),
         reads=[("P", 1)], writes=["num"])
    P.op("dve", lambda h: h.tensor_copy(out=num[:, 1, :], in_=Pp[1][:, 1, :]), reads=[("P", 1), "num"], writes=["num"])
    tt(den, a_re, a_re, ALU.mult, ["s5a"], ["den"])
    tt(V(12)[:, 1, :], a_im, a_im, ALU.mult, ["s5a"], ["den2"])
    tt(den, den, V(12)[:, 1, :], ALU.add, ["den", "den2"], ["den"])
    P.op("dve", lambda h: h.reciprocal(out=den, in_=den), reads=["den"], writes=["den"])
    t13 = V(13)
    tt(t13[:, 0, :], num[:, 0, :], a_re, ALU.mult, ["num", "s5a"], ["t13a"])
    tt(t13[:, 1, :], num[:, 1, :], a_im, ALU.mult, ["num", "s5a"], ["t13b"])
    tt(qv[:, 0, :], t13[:, 0, :], t13[:, 1, :], ALU.add, ["t13a", "t13b"], ["qv0"])
    tt(qv[:, 0, :], qv[:, 0, :], den, ALU.mult, ["qv0", "den"], ["qv0"])
    tt(t13[:, 0, :], num[:, 1, :], a_re, ALU.mult, ["num", "s5a", "qv0"], ["t13a"])
    tt(t13[:, 1, :], num[:, 0, :], a_im, ALU.mult, ["num", "s5a", "qv0"], ["t13b"])
    tt(qv[:, 1, :], t13[:, 0, :], t13[:, 1, :], ALU.subtract, ["t13a", "t13b"], ["qv1"])
    tt(qv[:, 1, :], qv[:, 1, :], den, ALU.mult, ["qv1", "den"], ["qv1"])
    q_re = qv[:, 0, :].unsqueeze(2).to_broadcast([128, 32, 16])
    q_im = qv[:, 1, :].unsqueeze(2).to_broadcast([128, 32, 16])
    B_re, B_im, C_re, C_im = s5b[:, 0], s5b[:, 1], s5b[:, 2], s5b[:, 3]
    tAv = tA[:].rearrange("p g (j c) -> p g j c", c=16)
    tBv = tB[:].rearrange("p g (j c) -> p g j c", c=16)
    tt(tAv[:, :, 0, :], q_re, B_re, ALU.mult, ["qv0", "s5b"], ["tA"])
    tt(tBv[:, :, 0, :], q_im, B_im, ALU.mult, ["qv1", "s5b"], ["tB"])
    tt(bb[:, 0], tAv[:, :, 0, :], tBv[:, :, 0, :], ALU.subtract, ["tA", "tB"], ["bb0"])
    tt(tAv[:, :, 0, :], q_re, B_im, ALU.mult, ["qv0", "s5b", "bb0"], ["tA"])
    tt(tBv[:, :, 0, :], q_im, B_re, ALU.mult, ["qv1", "s5b", "bb0"], ["tB"])
    tt(bb[:, 1], tAv[:, :, 0, :], tBv[:, :, 0, :], ALU.add, ["tA", "tB"], ["bb1"])

    def build_mat(dst, tbl, tkeys, v_re, v_im, vkeys, mode, dkey):
        dv = dst[:].rearrange("p g (j c) -> p g j c", c=16)
        for half in range(2):
            pr = slice(64 * half, 64 * half + 64)
            Tre = TB[tbl][pr, 0, :, :].rearrange("p s g -> p g s").unsqueeze(3).to_broadcast([64, 32, 8, 16])
            Tim = TB[tbl][pr, 1, :, :].rearrange("p s g -> p g s").unsqueeze(3).to_broadcast([64, 32, 8, 16])
            va, vb = (v_re, v_im) if half == 0 else (v_im, v_re)
            Va = va[pr].unsqueeze(2).to_broadcast([64, 32, 8, 16])
            Vb = vb[pr].unsqueeze(2).to_broadcast([64, 32, 8, 16])
            tt(tAv[pr], Tre, Va, ALU.mult, tkeys + vkeys + [dkey], [("tA", half)])
            tt(tBv[pr], Tim, Vb, ALU.mult, tkeys + vkeys + [dkey], [("tB", half)])
            if half == 0:
                tt(dv[pr], tAv[pr], tBv[pr], ALU.subtract, [("tA", 0), ("tB", 0)], [(dkey, 0)])
            elif mode == "B":
                tt(dv[pr], tAv[pr], tBv[pr], ALU.add, [("tA", 1), ("tB", 1)], [(dkey, 1)])
            else:
                P.op("dve", (lambda h, pr=pr: h.scalar_tensor_tensor(out=dv[pr], in0=tAv[pr], scalar=-1.0, in1=tBv[pr],
                                                                      op0=ALU.mult, op1=ALU.subtract)),
                     reads=[("tA", 1), ("tB", 1)], writes=[(dkey, 1)])

    P.op("dve", lambda h: h.memset(mS[:], 0.0), reads=["tA", "tB", "mS"], writes=[("tA", 0), ("tA", 1), ("tB", 0), ("tB", 1), "mS"])
    build_mat(W1T, TW1, TW1K, bb[:, 0], bb[:, 1], ["bb0", "bb1"], "B", "W1T")
    build_mat(W2, TW2, TW2K, C_re, C_im, ["s5b"], "C", "W2")
    build_mat(Lm, TL, TLK, bb[:, 0], bb[:, 1], ["bb0", "bb1"], "B", "Lm")
    build_mat(Rm, TR, TRK, C_re, C_im, ["s5b"], "C", "Rm")
    for g0 in range(0, 32, 4):
        tb = (g0 // 4) % 2
        for jj in range(4):
            P.op("pe", (lambda h, g0=g0, jj=jj, tb=tb: h.transpose(out=psTs[tb][:, jj * 128:(jj + 1) * 128], in_=W1T[:, g0 + jj, :], identity=ident_b)),
                 reads=[("W1T", 0), ("W1T", 1), "cm_b"], writes=[("psT", tb)], sig=(jj == 3), accum=(jj > 0))
        P.op("act", (lambda h, g0=g0, tb=tb: h.copy(out=W1[:, g0:g0 + 4, :], in_=psTs[tb][:, 0:512].rearrange("p (a b) -> p a b", b=128))),
             reads=[("psT", tb)], writes=[("W1", g0 // 4)])
    for g0 in range(0, 32, 4):
        b = nb(0, 3)
        for jj in range(4):
            P.op("pe", (lambda h, g0=g0, jj=jj, b=b: h.matmul(ps[b][:, jj * 128:(jj + 1) * 128], lhsT=Lm[:, g0 + jj, :], rhs=Rm[:, g0 + jj, :],
                                                              start=True, stop=True)),
                 reads=[("Lm", 0), ("Lm", 1), ("Rm", 0), ("Rm", 1)], writes=[("ps", b)], sig=(jj == 3), accum=(jj > 0))
        P.op("dve", (lambda h, b=b, g0=g0: h.tensor_tensor(out=tA[:, g0:g0 + 4, :], in0=ps[b][:].rearrange("p (a b) -> p a b", b=128),
                                                           in1=tmask_f.unsqueeze(1).to_broadcast([128, 4, 128]), op=ALU.mult)),
             reads=[("ps", b), "cm_f", ("tA", 0), ("tA", 1)], writes=[("tAm", g0 // 4)])
        for jj in range(4):
            g = g0 + jj
            P.op("dve", (lambda h, g=g: h.scalar_tensor_tensor(out=Tm[:, g, :], in0=ident_f, scalar=dsk[:, g:g + 1], in1=tA[:, g, :],
                                                               op0=ALU.mult, op1=ALU.add)),
                 reads=[("tAm", g0 // 4), "dsk", "cm_f"], writes=[("Tm", g0 // 4)])
    if dbg.get("stop") == "s5pre":
        for nm, tns, keys in (("W1", W1, [("W1", i) for i in range(8)]), ("W2", W2, [("W2", 0), ("W2", 1)]), ("Tm", Tm, [("Tm", i) for i in range(8)])):
            t = dump(nm, None, [128, 32, 128], BF16)
            P.dma("sp", (lambda h, t=t, tns=tns: h.dma_start(out=t, in_=tns[:])), reads=keys, writes=["dbg_" + nm])
        t = dump("TK", None, [128, 8, 2, 32])
        P.dma("sp", (lambda h, t=t: h.dma_start(out=t, in_=TK[:])), reads=["TKs"], writes=["dbg_TK"])
        t = dump("TB", None, [128, 2, 8, 32])
        P.dma("sp", (lambda h, t=t: h.dma_start(out=t, in_=TB[TR][:])), reads=TRK, writes=["dbg_TB"])
        P.barrier()
        P.emit()
        return nc, dbg_outs
    P.barrier()
    A.release(mS1)

    Yst = A.alloc([128, 32, 256], BF16)
    gq = A.alloc([128, 512], F32)
    gz = A.alloc([128, 512], F32)
    gs = A.alloc([128, 512], F32)
    mS2 = A.mark()
    Rl = [A.alloc([128, 32, 128], BF16) for _ in range(2)]
    rt1 = A.alloc([128, 32, 128], BF16)
    rt2 = A.alloc([128, 32, 128], BF16)
    XK = lambda gp: ("X", gp)
    for gp in range(16):
        b = nb(0, 3)
        for gi in range(2):
            g = 2 * gp + gi
            P.op("pe", (lambda h, b=b, gi=gi, g=g: h.matmul(ps[b][:, gi * 256:(gi + 1) * 256], lhsT=W1[:, g, :], rhs=U8[:, g, :], start=True, stop=True)),
                 reads=[("W1", g // 4)] + U8K, writes=[("ps", b)], sig=(gi == 1), accum=(gi > 0))
        P.op("act", (lambda h, b=b, gp=gp: h.copy(out=X[:, 2 * gp:2 * gp + 2, :], in_=ps[b][:].rearrange("p (a b) -> p a b", b=256))),
             reads=[("ps", b)], writes=[XK(gp)])
    for l in range(8):
        d = 1 << l
        R = Rl[l % 2]
        P.op("dve", (lambda h, l=l: h.tensor_tensor(out=rt1[:], in0=ident_f.unsqueeze(1).to_broadcast([128, 32, 128]),
                                                    in1=TK[:, l, 0, :].unsqueeze(2).to_broadcast([128, 32, 128]), op=ALU.mult)),
             reads=["cm_f", "TKs"], writes=["rt1"])
        P.op("dve", (lambda h, l=l: h.tensor_tensor(out=rt2[:], in0=swap_f.unsqueeze(1).to_broadcast([128, 32, 128]),
                                                    in1=TK[:, l, 1, :].unsqueeze(2).to_broadcast([128, 32, 128]), op=ALU.mult)),
             reads=["cm_f", "TKs"], writes=["rt2"])
        P.op("dve", (lambda h, R=R: h.tensor_tensor(out=R[:], in0=rt1[:], in1=rt2[:], op=ALU.add)),
             reads=["rt1", "rt2"], writes=[("R", l % 2)])
        for gp in range(16):
            b = nb(0, 3)
            for gi in range(2):
                g = 2 * gp + gi
                P.op("pe", (lambda h, b=b, gi=gi, g=g, d=d, R=R: h.matmul(ps[b][:, gi * 256 + d:(gi + 1) * 256], lhsT=R[:, g, :], rhs=X[:, g, 0:256 - d],
                                                                          start=True, stop=True)),
                     reads=[("R", l % 2), XK(gp)], writes=[("ps", b)], sig=(gi == 1), accum=(gi > 0))
            P.op("dve", (lambda h, b=b, gp=gp, d=d: h.tensor_tensor(out=X[:, 2 * gp:2 * gp + 2, d:256], in0=X[:, 2 * gp:2 * gp + 2, d:256],
                                                                   in1=ps[b][:].rearrange("p (a b) -> p a b", b=256)[:, :, d:256], op=ALU.add)),
                 reads=[("ps", b), XK(gp)], writes=[XK(gp)])
    for gp in range(16):
        b = nb(0, 3)
        for gi in range(2):
            g = 2 * gp + gi
            P.op("pe", (lambda h, b=b, gi=gi, g=g: h.matmul(ps[b][:, gi * 256:(gi + 1) * 256], lhsT=Tm[:, g, :], rhs=U8[:, g, :], start=True, stop=False)),
                 reads=[("Tm", g // 4)] + U8K, writes=[("ps", b)], sig=False, accum=(gi > 0))
            P.op("pe", (lambda h, b=b, gi=gi, g=g: h.matmul(ps[b][:, gi * 256 + 1:(gi + 1) * 256], lhsT=W2[:, g, :], rhs=X[:, g, 0:255], start=False, stop=True)),
                 reads=[("W2", 0), ("W2", 1), XK(gp)], writes=[("ps", b)], sig=(gi == 1), accum=True)
        P.op("act", (lambda h, b=b: h.activation(out=gq[:], in_=ps[b][:], func=AF.Square)), reads=[("ps", b)], writes=["gq"])
        P.op("dve", lambda h: h.tensor_scalar(out=gz[:], in0=gq[:], scalar1=0.044715, scalar2=1.0, op0=ALU.mult, op1=ALU.add),
             reads=["gq"], writes=["gz"])
        P.op("dve", (lambda h, b=b: h.tensor_tensor(out=gz[:], in0=gz[:], in1=ps[b][:], op=ALU.mult)), reads=["gz", ("ps", b)], writes=["gz"])
        P.op("act", lambda h: h.activation(out=gs[:], in_=gz[:], func=AF.Sigmoid, scale=2.0 * math.sqrt(2.0 / math.pi)), reads=["gz"], writes=["gs"])
        P.op("dve", (lambda h, b=b, gp=gp: h.tensor_tensor(out=Yst[:, 2 * gp:2 * gp + 2, :], in0=gs[:].rearrange("p (a b) -> p a b", b=256),
                                                          in1=ps[b][:].rearrange("p (a b) -> p a b", b=256), op=ALU.mult)),
             reads=["gs", ("ps", b)], writes=[("Yst", gp)])
    for il in range(8):
        P.dma("sp", (lambda h, il=il: h.dma_start(out=scr_y.rearrange("(g c) (i b) -> i c g b", c=16, i=8)[il], in_=Yst[il * 16:(il + 1) * 16, :, :])),
              reads=[("Yst", gp) for gp in range(16)], writes=[("scr_y", il)])
    P.barrier()
    A.release(mS2)
    YT = A.alloc([128, 4, S], BF16)
    wglu = A.alloc([128, 4, 512], BF16)
    P.dma("pool", lambda h: h.dma_start(out=wglu[:], in_=w_glu_d.rearrange("(kc p) n -> p kc n", p=128)), writes=["wglu"])
    P.dma("sp", lambda h: h.dma_start(out=YT[:], in_=scr_y.rearrange("(ch p) t -> p ch t", p=128)), writes=["YT"])
    for chp in range(4):
        for tcp in range(4):
            b = nb(0, 3)
            for k in range(4):
                P.op("pe", (lambda h, b=b, k=k, chp=chp, tcp=tcp: h.matmul(ps[b][:], lhsT=wglu[:, k, chp * 128:(chp + 1) * 128],
                                                                           rhs=YT[:, k, tcp * 512:(tcp + 1) * 512], start=(k == 0), stop=(k == 3))),
                     reads=["wglu", "YT"], writes=[("ps", b)], sig=(k == 3), accum=(k > 0))
            P.op("act", (lambda h, b=b: h.activation(out=gs[:], in_=ps[b][:], func=AF.Sigmoid)), reads=[("ps", b)], writes=["gs"])
            P.op("dve", (lambda h, chp=chp, tcp=tcp: h.tensor_tensor(out=o_s5[:, chp, tcp * 512:(tcp + 1) * 512], in0=gs[:],
                                                                    in1=YT[:, chp, tcp * 512:(tcp + 1) * 512], op=ALU.mult)),
                 reads=["gs", "YT"], writes=[("o_s5", chp, tcp)])
    if dbg.get("stop") == "s5":
        t = dump("od", None, [128, 4, S], BF16)
        P.dma("sp", (lambda h, t=t: h.dma_start(out=t, in_=o_dsa[:])), writes=["dbg_od"])
        t = dump("o_s5", None, [128, 4, S], BF16)
        P.dma("sp", (lambda h, t=t: h.dma_start(out=t, in_=o_s5[:])), reads=[("o_s5", a, c) for a in range(4) for c in range(4)], writes=["dbg_o_s5"])
        t = dump("YT", None, [128, 4, S], BF16)
        P.dma("sp", (lambda h, t=t: h.dma_start(out=t, in_=YT[:])), reads=["YT"], writes=["dbg_YT"])
        P.barrier()
        P.emit()
        return nc, dbg_outs
    P.barrier()
    A.release(base_mark)

    off_merged = A.mark()
    merged = A.alloc([128, 8, S], BF16)
    mM1 = A.mark()
    xgM = A.alloc([128, 8, S], BF16)
    build_xg(xgM, False)
    wg = [A.alloc([128, 8, 384], BF16) for _ in range(2)]
    wbr = [A.alloc([128, 4, 384], BF16) for _ in range(2)]
    gt = A.alloc([128, 512], F32)
    sg = A.alloc([128, 512], F32)
    macc = A.alloc([128, 512], F32)
    mtmp = A.alloc([128, 512], F32)
    w_br_v = [w.rearrange("(kc p) n -> p kc n", p=128) for w in (w_brd_d, w_brs_d, w_brx_d)]
    obr = [o_dsa, o_s5, o_x]
    for fc in range(8):
        wb_ = fc % 2
        for br in range(3):
            c0 = 2760 + br * 1024 + fc * 128
            P.dma("pool", (lambda h, wb_=wb_, br=br, c0=c0: h.dma_start(out=wg[wb_][:, :, br * 128:(br + 1) * 128], in_=w_in_v[:, :, c0:c0 + 128])),
                  writes=[("wg", wb_, br)])
            P.dma("pool", (lambda h, wb_=wb_, br=br, fc=fc: h.dma_start(out=wbr[wb_][:, :, br * 128:(br + 1) * 128],
                                                                       in_=w_br_v[br][:, :, fc * 128:(fc + 1) * 128])),
                  writes=[("wbr", wb_, br)])
        for tc in range(4):
            for br in range(3):
                bG = nb(0, 6)
                for kc in range(8):
                    P.op("pe", (lambda h, bG=bG, kc=kc, wb_=wb_, br=br, tc=tc: h.matmul(ps[bG][:], lhsT=wg[wb_][:, kc, br * 128:(br + 1) * 128],
                                                                                        rhs=xgM[:, kc, tc * 512:(tc + 1) * 512], start=(kc == 0), stop=(kc == 7))),
                         reads=[("wg", wb_, br), ("xg", kc)], writes=[("ps", bG)], sig=(kc == 7), accum=(kc > 0))
                P.op("dve", (lambda h, bG=bG, tc=tc: h.tensor_tensor(out=gt[:], in0=ps[bG][:], in1=rx_bc[:, tc * 512:(tc + 1) * 512], op=ALU.mult)),
                     reads=[("ps", bG)], writes=["gt"])
                P.op("act", lambda h: h.activation(out=sg[:], in_=gt[:], func=AF.Sigmoid), reads=["gt"], writes=["sg"])
                bB = nb(0, 6)
                for k in range(4):
                    if br == 1:
                        rhs = o_s5[:, k, :].rearrange("p (j b) -> p b j", j=8)[:, 64 * tc:64 * tc + 64, :]
                    else:
                        rhs = obr[br][:, k, tc * 512:(tc + 1) * 512]
                    P.op("pe", (lambda h, bB=bB, k=k, wb_=wb_, br=br, rhs=rhs: h.matmul(ps[bB][:], lhsT=wbr[wb_][:, k, br * 128:(br + 1) * 128], rhs=rhs,
                                                                                        start=(k == 0), stop=(k == 3))),
                         reads=[("wbr", wb_, br)], writes=[("ps", bB)], sig=(k == 3), accum=(k > 0))
                if br == 0:
                    P.op("dve", (lambda h, bB=bB: h.tensor_tensor(out=macc[:], in0=ps[bB][:], in1=sg[:], op=ALU.mult)),
                         reads=[("ps", bB), "sg"], writes=["macc"])
                else:
                    P.op("dve", (lambda h, bB=bB: h.tensor_tensor(out=mtmp[:], in0=ps[bB][:], in1=sg[:], op=ALU.mult)),
                         reads=[("ps", bB), "sg"], writes=["mtmp"])
                    if br == 1:
                        P.op("dve", lambda h: h.tensor_tensor(out=macc[:], in0=macc[:], in1=mtmp[:], op=ALU.add), reads=["macc", "mtmp"], writes=["macc"])
                    else:
                        P.op("dve", (lambda h, fc=fc, tc=tc: h.tensor_tensor(out=merged[:, fc, tc * 512:(tc + 1) * 512], in0=macc[:], in1=mtmp[:], op=ALU.add)),
                             reads=["macc", "mtmp"], writes=[("merged", fc, tc)])
                if dbg.get("stop") == "merge1" and br == dbg.get("br", 0):
                    for nm, tns, shp, dt_, keys in (("wg", wg[0][:], [128, 8, 384], BF16, [("wg", 0, i) for i in range(3)]), ("gt", gt[:], [128, 512], F32, ["gt"]),
                                                    ("sg", sg[:], [128, 512], F32, ["sg"]), ("macc", macc[:], [128, 512], F32, ["macc"]),
                                                    ("mtmp", mtmp[:], [128, 512], F32, ["mtmp"]),
                                                    ("xg", xgM[:], [128, 8, S], BF16, XG), ("od", o_dsa[:], [128, 4, S], BF16, []), ("os", o_s5[:], [128, 4, S], BF16, []), ("ox", o_x[:], [128, 4, S], BF16, []), ("wbr", wbr[0][:], [128, 4, 384], BF16, [("wbr", 0, i) for i in range(3)])):
                        t = dump(nm, None, shp, dt_)
                        P.dma("sp", (lambda h, t=t, tns=tns: h.dma_start(out=t, in_=tns)), reads=keys, writes=["dbg_" + nm])
                    P.barrier()
                    P.emit()
                    return nc, dbg_outs
    if dbg.get("stop") == "merge":
        t = dump("merged", None, [128, 8, S], BF16)
        P.dma("sp", (lambda h, t=t: h.dma_start(out=t, in_=merged[:])), reads=[("merged", a, c) for a in range(8) for c in range(4)], writes=["dbg_merged"])
        P.barrier()
        P.emit()
        return nc, dbg_outs
    P.barrier()
    A.release(mM1)
    off_x1 = A.mark()
    x1T = A.alloc([128, 8, S], F32)
    mM0 = A.mark()
    wout = A.alloc([128, 8, 1024], BF16)
    xres = [A.alloc([128, 512], F32) for _ in range(2)]
    w_out_v = w_out_d.rearrange("(kc p) n -> p kc n", p=128)
    for i in range(2):
        load_w(wout[:, :, i * 512:(i + 1) * 512], w_out_v, ("wout", i), (i * 512, (i + 1) * 512))
    xi = 0
    for fc in range(8):
        for tc in range(4):
            xb = xi % 2
            xi += 1
            P.dma("sp", (lambda h, xb=xb, fc=fc, tc=tc: h.dma_start(out=xres[xb][:], in_=xT_d[fc * 128:(fc + 1) * 128, tc * 512:(tc + 1) * 512])),
                  writes=[("xres", xb)])
            b = nb(0, 6)
            for kc in range(8):
                P.op("pe", (lambda h, b=b, kc=kc, fc=fc, tc=tc: h.matmul(ps[b][:], lhsT=wout[:, kc, fc * 128:(fc + 1) * 128],
                                                                         rhs=merged[:, kc, tc * 512:(tc + 1) * 512], start=(kc == 0), stop=(kc == 7))),
                     reads=[(("wout", fc // 4), kc), ("merged", kc, tc)], writes=[("ps", b)], sig=(kc == 7), accum=(kc > 0))
            P.op("dve", (lambda h, b=b, xb=xb, fc=fc, tc=tc: h.tensor_tensor(out=x1T[:, fc, tc * 512:(tc + 1) * 512], in0=ps[b][:], in1=xres[xb][:], op=ALU.add)),
                 reads=[("ps", b), ("xres", xb)], writes=[("x1T", fc, tc)])
    if dbg.get("stop") == "x1":
        t = dump("x1T", None, [128, 8, S])
        P.dma("sp", (lambda h, t=t: h.dma_start(out=t, in_=x1T[:])), reads=[("x1T", a, c) for a in range(8) for c in range(4)], writes=["dbg_x1T"])
        P.barrier()
        P.emit()
        return nc, dbg_outs
    P.barrier()
    A.release(mM0)
    top = A.mark()
    A.off = off_o
    fsq = [A.alloc([128, 512], BF16) for _ in range(2)]
    rf = A.alloc([128, 512], F32)
    wgu = [[A.alloc([128, 8, 256], BF16) for _ in range(2)] for _ in range(2)]
    wdn = [A.alloc([128, NFF, 256], BF16) for _ in range(2)]
    fs = A.alloc([128, 512], F32)
    assert A.off <= off_merged
    A.off = off_merged
    hfT = A.alloc([128, 8, 512], BF16)
    hT = A.alloc([128, NFF, 512], BF16)
    assert A.off <= off_x1
    A.off = top
    fo = [A.alloc([128, 512], F32) for _ in range(2)]
    w_gu_v = [w.rearrange("(kc p) n -> p kc n", p=128) for w in (w_g_d, w_u_d)]
    w_d_v = w_d_d.rearrange("(fk p) n -> p fk n", p=128)
    wi = 0
    di = 0
    oi = 0
    for tq in range(4):
        tsl = slice(tq * 512, (tq + 1) * 512)
        bS = nb(0, 6)
        for kc in range(8):
            sb_ = kc % 2
            P.op("act", (lambda h, kc=kc, sb_=sb_, tsl=tsl: h.activation(out=fsq[sb_][:], in_=x1T[:, kc, tsl], func=AF.Square)),
                 reads=[("x1T", kc, tq)], writes=[("fsq", sb_)])
            P.op("pe", (lambda h, kc=kc, sb_=sb_, bS=bS: h.matmul(ps[bS][:], lhsT=ones_b, rhs=fsq[sb_][:], start=(kc == 0), stop=(kc == 7))),
                 reads=[("fsq", sb_), "cm_b"], writes=[("ps", bS)], sig=True, accum=(kc > 0))
        P.op("act", (lambda h, bS=bS: h.activation(out=rf[:], in_=ps[bS][:], func=AF.Sqrt, scale=1.0 / D, bias=EPS)), reads=[("ps", bS)], writes=["rf"])
        P.op("dve", lambda h: h.reciprocal(out=rf[:], in_=rf[:]), reads=["rf"], writes=["rf"])
        for kc in range(8):
            P.op("dve", (lambda h, kc=kc, tsl=tsl: h.scalar_tensor_tensor(out=hfT[:, kc, :], in0=x1T[:, kc, tsl], scalar=gvec[:, 16 + kc:17 + kc], in1=rf[:],
                                                                          op0=ALU.mult, op1=ALU.mult)),
                 reads=[("x1T", kc, tq), "rf", "gvec"], writes=[("hfT", kc)])
        for f0 in range(0, NFF, 2):
            wb_ = wi % 2
            wi += 1
            for gu in range(2):
                P.dma("pool", (lambda h, gu=gu, wb_=wb_, f0=f0: h.dma_start(out=wgu[gu][wb_][:], in_=w_gu_v[gu][:, :, f0 * 128:(f0 + 2) * 128])),
                      writes=[("wgu", gu, wb_)])
            for ff in range(f0, f0 + 2):
                bg, bu = nb(0, 6), nb(0, 6)
                for gu, bb_ in ((0, bg), (1, bu)):
                    for kc in range(8):
                        P.op("pe", (lambda h, gu=gu, bb_=bb_, kc=kc, wb_=wb_, ff=ff, f0=f0: h.matmul(
                            ps[bb_][:], lhsT=wgu[gu][wb_][:, kc, (ff - f0) * 128:(ff - f0 + 1) * 128], rhs=hfT[:, kc, :], start=(kc == 0), stop=(kc == 7))),
                            reads=[("wgu", gu, wb_), ("hfT", kc)], writes=[("ps", bb_)], sig=(kc == 7), accum=(kc > 0))
                P.op("act", (lambda h, bg=bg: h.activation(out=fs[:], in_=ps[bg][:], func=AF.Silu)), reads=[("ps", bg)], writes=["fs"])
                P.op("dve", (lambda h, bu=bu, ff=ff: h.tensor_tensor(out=hT[:, ff, :], in0=ps[bu][:], in1=fs[:], op=ALU.mult)),
                     reads=[("ps", bu), "fs"], writes=[("hT", ff)])
        for c0 in range(0, 8, 2):
            db = di % 2
            di += 1
            P.dma("pool", (lambda h, db=db, c0=c0: h.dma_start(out=wdn[db][:], in_=w_d_v[:, :, c0 * 128:(c0 + 2) * 128])), writes=[("wdn", db)])
            for fc in range(c0, c0 + 2):
                b = nb(0, 6)
                for ff in range(NFF):
                    P.op("pe", (lambda h, b=b, ff=ff, db=db, fc=fc, c0=c0: h.matmul(ps[b][:], lhsT=wdn[db][:, ff, (fc - c0) * 128:(fc - c0 + 1) * 128], rhs=hT[:, ff, :],
                                                                                    start=(ff == 0), stop=(ff == NFF - 1))),
                         reads=[("wdn", db), ("hT", ff)], writes=[("ps", b)], sig=(ff == NFF - 1), accum=(ff > 0))
                ob = oi % 2
                oi += 1
                P.op("dve", (lambda h, b=b, ob=ob, fc=fc, tsl=tsl: h.tensor_tensor(out=fo[ob][:], in0=ps[b][:], in1=x1T[:, fc, tsl], op=ALU.add)),
                     reads=[("ps", b), ("x1T", fc, tq)], writes=[("fo", ob)])
                P.dma("sp", (lambda h, ob=ob, fc=fc, tsl=tsl: h.dma_start(out=outT_d[fc * 128:(fc + 1) * 128, tsl], in_=fo[ob][:])),
                      reads=[("fo", ob)], writes=[("out", fc, tq)])
    P.barrier()
    P.emit()
    return nc, dbg_outs


def _host_inputs(inputs):
    f = lambda a: np.ascontiguousarray(np.asarray(a, dtype=np.float32))
    x = f(inputs["x"])
    mem = f(inputs["mem"])
    rel_bias = f(inputs["rel_bias"])
    shared = {}
    shared["w_in"] = f(inputs["w_in"][0])
    shared["w_uv"] = f(np.transpose(inputs["w_uv_dsa"][0], (1, 0, 2)))
    shared["w_glu"] = f(inputs["w_glu"][0])
    shared["w_mem_kv"] = f(inputs["w_mem_kv"][0])
    shared["w_br_dsa"] = f(inputs["w_br_dsa"][0])
    shared["w_br_s5"] = f(inputs["w_br_s5"][0])
    shared["w_br_cross"] = f(inputs["w_br_cross"][0])
    shared["w_out"] = f(inputs["w_out"][0])
    shared["w_ffn_gate"] = f(inputs["w_ffn_gate"][0])
    shared["w_ffn_up"] = f(inputs["w_ffn_up"][0])
    shared["w_ffn_down"] = f(inputs["w_ffn_down"][0])
    gvec = np.zeros((128, 32), np.float32)
    gvec[:, 0:8] = np.asarray(inputs["g_mix_norm"][0]).reshape(8, 128).T
    gvec[:, 8:16] = np.asarray(inputs["g_mem_norm"][0]).reshape(8, 128).T
    gvec[:, 16:24] = np.asarray(inputs["g_ffn_norm"][0]).reshape(8, 128).T
    gvec[:, 24] = np.asarray(inputs["g_q_dsa"][0])
    gvec[:, 25] = np.asarray(inputs["g_q_cross"][0])
    gvec[:, 26] = np.asarray(inputs["g_k_cross"][0])
    shared["gvec"] = gvec
    shared["gkv_bc"] = f(np.broadcast_to(np.asarray(inputs["g_kv_dsa"][0])[None, :], (128, 128)))
    bt = _t5_bucket_table()
    s_i = np.arange(128)[:, None]
    t_i = np.arange(128)[None, :]
    t5 = np.zeros((128, 2, 8, 128), np.float32)
    for diff in (0, 1):
        dist = np.maximum(t_i - s_i + 128 * diff, 0)
        t5[:, diff, :, :] = np.transpose(rel_bias[bt[dist]], (0, 2, 1))
    shared["t5"] = t5
    shared["rb31"] = f(np.broadcast_to(rel_bias[31][None, :], (128, 8)))
    cm = np.zeros((128, 5, 128), np.float32)
    cm[:, 0, :] = np.eye(128)
    cm[:, 1, :] = np.where(t_i.T >= s_i.T, 0.0, NEG)
    il = np.arange(128)[None, :] // 16
    jl = np.arange(128)[:, None] // 16
    cm[:, 2, :] = (il >= jl).astype(np.float32)
    sw = np.zeros((128, 128), np.float32)
    sw[np.arange(64), np.arange(64) + 64] = 1.0
    sw[np.arange(64) + 64, np.arange(64)] = 1.0
    cm[:, 3, :] = sw
    cm[:, 4, :] = 1.0
    shared["cmats"] = cm
    tile2 = lambda a: np.concatenate([a, a], axis=0)
    s5a = np.zeros((128, 3, 32), np.float32)
    s5a[:, 0, :] = tile2(np.asarray(inputs["a_re"][0]).T)
    s5a[:, 1, :] = tile2(np.asarray(inputs["a_im"][0]).T)
    s5a[:, 2, :] = np.broadcast_to(np.asarray(inputs["log_dt"][0])[None, :], (128, 32))
    shared["s5a"] = s5a
    s5b = np.zeros((128, 4, 32, 16), np.float32)
    s5b[:, 0] = tile2(np.transpose(inputs["b_re"][0], (1, 0, 2)))
    s5b[:, 1] = tile2(np.transpose(inputs["b_im"][0], (1, 0, 2)))
    s5b[:, 2] = tile2(np.transpose(inputs["c_re"][0], (2, 0, 1)))
    s5b[:, 3] = tile2(np.transpose(inputs["c_im"][0], (2, 0, 1)))
    shared["s5b"] = s5b
    shared["dsk"] = f(np.tile(np.asarray(inputs["d_skip"][0]).T, (8, 1)))
    in_maps = []
    for b in range(8):
        d = dict(shared)
        d["xT"] = f(x[b].T)
        d["memT"] = f(mem[b].T)
        in_maps.append(d)
    return in_maps


_CACHE = {}


def kernel(**inputs):
    in_maps = _host_inputs(inputs)
    if "nc" not in _CACHE:
        _CACHE["nc"] = build()[0]
    res = run_bass_kernel_spmd(_CACHE["nc"], in_maps, core_ids=list(range(8)))
    out = np.stack([np.ascontiguousarray(res.results[b]["outT"].T) for b in range(8)], axis=0)
    return out.astype(np.float32)
```

```python
import math
import numpy as np
import concourse.bass as bass
import concourse.mybir as mybir
from concourse.bass_utils import run_bass_kernel_spmd

F32 = mybir.dt.float32
BF16 = mybir.dt.bfloat16
AF = mybir.ActivationFunctionType
ALU = mybir.AluOpType

S = 2048
D = 1024
NB = 16
EPS = 1e-6
D_IN = 5832
D_FF = 2816
NFF = 22
NEG = -1.0e30
MBIAS = -30000.0


class Prog:
    ENGS = ("pe", "act", "dve", "pool", "sp")

    def __init__(self, nc, n_dma_sems=8):
        self.nc = nc
        self.streams = {e: [] for e in self.ENGS}
        self.sems = {e: nc.alloc_semaphore("s_" + e) for e in self.ENGS}
        self.cnt = {e: 0 for e in self.ENGS}
        self.known = {e: {} for e in self.ENGS}
        self.bufs = {}
        self.dma_sems = {}
        self.dma_rr = {}
        self.dma_val = {}
        self.semobj = {}
        for q in ("sp", "pool", "act"):
            self.dma_sems[q] = [nc.alloc_semaphore(f"d_{q}{i}") for i in range(n_dma_sems)]
            self.dma_rr[q] = 0
            for s in self.dma_sems[q]:
                self.semobj[s.name] = s
                self.dma_val[s.name] = 0
        for e in self.ENGS:
            self.semobj[self.sems[e].name] = self.sems[e]
        self.pending_pe = False

    def _need(self, eng, tok, waits):
        if tok is None:
            return
        sname, val = tok
        if self.known[eng].get(sname, 0) >= val:
            return
        if waits.get(sname, 0) < val:
            waits[sname] = val

    def _deps(self, eng, reads, writes, skip_writer=False):
        waits = {}
        for k in reads:
            st = self.bufs.get(k)
            if st is not None:
                self._need(eng, st[0], waits)
        for k in writes:
            st = self.bufs.get(k)
            if st is not None:
                if not skip_writer:
                    self._need(eng, st[0], waits)
                for tok in st[1].values():
                    self._need(eng, tok, waits)
        return waits

    def _commit(self, who, tok, reads, writes):
        for k in reads:
            st = self.bufs.setdefault(k, [None, {}])
            st[1][who + ":" + tok[0]] = tok
        for k in writes:
            self.bufs[k] = [tok, {}]

    def op(self, eng, fn, reads=(), writes=(), sig=True, accum=False):
        waits = self._deps(eng, reads, writes, skip_writer=(accum and eng == "pe"))
        if eng == "pe":
            waits.pop(self.sems["pe"].name, None)
        for s, v in waits.items():
            self.known[eng][s] = v
        if sig:
            self.cnt[eng] += 1
            tok = (self.sems[eng].name, self.cnt[eng])
            if eng == "pe":
                self.pending_pe = False
        else:
            assert eng == "pe"
            tok = (self.sems[eng].name, self.cnt[eng] + 1)
            self.pending_pe = True
        self.streams[eng].append((list(waits.items()), fn, (self.sems[eng], 1) if sig else None))
        self._commit(eng, tok, reads, writes)
        return tok

    def dma(self, q, fn, reads=(), writes=()):
        waits = self._deps(q, reads, writes)
        pool = self.dma_sems[q]
        s = pool[self.dma_rr[q] % len(pool)]
        self.dma_rr[q] += 1
        prev = self.dma_val[s.name]
        if prev > 0:
            self._need(q, (s.name, prev), waits)
        for sn, v in waits.items():
            self.known[q][sn] = v
        self.dma_val[s.name] = prev + 16
        tok = (s.name, prev + 16)
        self.streams[q].append((list(waits.items()), fn, (s, 16)))
        self._commit(q, tok, reads, writes)
        return tok

    def barrier(self):
        assert not self.pending_pe
        targets = [(self.sems[e].name, self.cnt[e]) for e in self.ENGS if self.cnt[e] > 0]
        targets += [(sn, v) for sn, v in self.dma_val.items() if v > 0]
        for e in self.ENGS:
            waits = {}
            for tok in targets:
                if e == "pe" and tok[0] == self.sems["pe"].name:
                    continue
                self._need(e, tok, waits)
            for sn, v in waits.items():
                self.known[e][sn] = v
            if waits:
                self.streams[e].append((list(waits.items()), None, None))
        self.bufs = {}

    def emit(self):
        assert not self.pending_pe
        nc = self.nc
        hmap = {"pe": "tensor", "act": "scalar", "dve": "vector", "pool": "gpsimd", "sp": "sync"}
        semobj = self.semobj
        with nc.Block() as block:
            for e in self.ENGS:
                def body(h, stream=self.streams[e]):
                    for waits, fn, inc in stream:
                        for sn, v in waits:
                            h.wait_ge(semobj[sn], v)
                        if fn is not None:
                            ins = fn(h)
                            if inc is not None:
                                ins.then_inc(inc[0], inc[1])
                getattr(block, hmap[e])(body)


class Arena:
    def __init__(self, nc, start=16640, limit=229376):
        self.nc = nc
        self.off = start
        self.limit = limit
        self.n = 0
        self.peak = 0

    def alloc(self, shape, dtype):
        nbytes = int(np.prod(shape[1:])) * (2 if dtype == BF16 else 4)
        nbytes = (nbytes + 63) // 64 * 64
        assert self.off + nbytes <= self.limit, ("SBUF overflow", self.off, nbytes)
        self.n += 1
        t = self.nc.alloc_sbuf_tensor_at(f"sb{self.n}", list(shape), dtype, offset=self.off)
        self.off += nbytes
        self.peak = max(self.peak, self.off)
        return t

    def mark(self):
        return self.off

    def release(self, m):
        if not getattr(self, "no_release", False):
            self.off = m


def _t5_bucket_table():
    d = np.arange(256)
    max_exact = 16
    nf = np.maximum(d, 1).astype(np.float32)
    large = max_exact + (np.log(nf / max_exact) / math.log(128 / max_exact) * (32 - max_exact)).astype(np.int32)
    large = np.minimum(large, 31)
    return np.where(d < max_exact, d, large)


def build(dbg=None):
    dbg = dbg or {}
    nc = bass.Bass("TRN2", target_bir_lowering=False)
    P = Prog(nc)
    A = Arena(nc)
    A.no_release = bool(dbg.get("no_release"))
    dbg_outs = []

    def din(name, shape):
        return nc.dram_tensor(name, list(shape), F32, kind="ExternalInput").ap()

    xT_d = din("xT", [D, S])
    memT_d = din("memT", [D, 256])
    w_in_d = din("w_in", [D, D_IN])
    w_uv_d = din("w_uv", [128, 8, 64])
    w_glu_d = din("w_glu", [512, 512])
    w_kv_d = din("w_mem_kv", [D, D])
    w_brd_d = din("w_br_dsa", [512, D])
    w_brs_d = din("w_br_s5", [512, D])
    w_brx_d = din("w_br_cross", [512, D])
    w_out_d = din("w_out", [D, D])
    w_g_d = din("w_ffn_gate", [D, D_FF])
    w_u_d = din("w_ffn_up", [D, D_FF])
    w_d_d = din("w_ffn_down", [D_FF, D])
    gvec_d = din("gvec", [128, 32])
    gkv_bc_d = din("gkv_bc", [128, 128])
    t5_d = din("t5", [128, 2, 8, 128])
    rb31_d = din("rb31", [128, 8])
    cm_d = din("cmats", [128, 5, 128])
    s5a_d = din("s5a", [128, 3, 32])
    s5b_d = din("s5b", [128, 4, 32, 16])
    dsk_d = din("dsk", [128, 32])
    outT_d = nc.dram_tensor("outT", [D, S], F32, kind="ExternalOutput").ap()
    scr_u = nc.dram_tensor("scr_u", [512, S], BF16).ap()
    scr_y = nc.dram_tensor("scr_y", [512, S], BF16).ap()

    ps = [nc.alloc_psum_tensor(f"ps{i}", [128, 512], F32) for i in range(6)]
    psTs = [nc.alloc_psum_tensor(f"psT{i}", [128, 512], F32) for i in range(2)]

    w_in_v = w_in_d.rearrange("(kc p) n -> p kc n", p=128)

    def dump(name, ap, shape, dt=F32):
        t = nc.dram_tensor("dbg_" + name, list(shape), dt, kind="ExternalOutput").ap()
        dbg_outs.append(("dbg_" + name))
        return t

    cm_f = A.alloc([128, 5, 128], F32)
    ident_f = cm_f[:, 0, :]
    causal_f = cm_f[:, 1, :]
    tmask_f = cm_f[:, 2, :]
    swap_f = cm_f[:, 3, :]
    cm_b = A.alloc([128, 5, 128], BF16)
    ident_b = cm_b[:, 0, :]
    ones_b = cm_b[:, 4, :]
    gvec = A.alloc([128, 32], F32)
    rx_bc = A.alloc([128, S], F32)
    rx_tok = A.alloc([128, NB], F32)
    off_o = A.mark()
    o_dsa = A.alloc([128, 4, S], BF16)

    P.dma("sp", lambda h: h.dma_start(out=cm_f[:], in_=cm_d), writes=["cm_f"])
    P.dma("pool", lambda h: h.dma_start(out=cm_b[:], in_=cm_d), writes=["cm_b"])
    P.dma("sp", lambda h: h.dma_start(out=gvec[:], in_=gvec_d), writes=["gvec"])
    CONST = ["cm_f", "cm_b", "gvec"]

    bank_rr = [0]

    def nb(lo=0, hi=6):
        b = lo + bank_rr[0] % (hi - lo)
        bank_rr[0] += 1
        return b

    def load_w(dst, src3, key, cols, q="pool"):
        kc_n = dst.shape[1]
        c0, c1 = cols
        for kc in range(kc_n):
            P.dma(q, (lambda h, kc=kc: h.dma_start(out=dst[:, kc, 0:c1 - c0], in_=src3[:, kc, c0:c1])),
                  writes=[(key, kc)])

    def build_xg(xg, with_stats, release=False):
        m = A.mark()
        xst = [A.alloc([128, S], F32) for _ in range(2)]
        sq = [A.alloc([128, S], BF16) for _ in range(2)]
        for kc in range(8):
            b = kc % 2
            P.dma("sp", (lambda h, kc=kc, b=b: h.dma_start(out=xst[b][:], in_=xT_d[kc * 128:(kc + 1) * 128, :])),
                  writes=[("xst", b)])
            if with_stats:
                P.op("act", (lambda h, b=b: h.activation(out=sq[b][:], in_=xst[b][:], func=AF.Square)),
                     reads=[("xst", b)], writes=[("sq", b)])
                for tc in range(4):
                    P.op("pe", (lambda h, b=b, tc=tc, kc=kc: h.matmul(ps[tc][:], lhsT=ones_b, rhs=sq[b][:, tc * 512:(tc + 1) * 512],
                                                                       start=(kc == 0), stop=(kc == 7))),
                         reads=[("sq", b), "cm_b"], writes=[("ps", tc)], sig=(tc == 3), accum=(kc > 0))
            P.op("dve", (lambda h, kc=kc, b=b: h.tensor_scalar(out=xg[:, kc, :], in0=xst[b][:], scalar1=gvec[:, kc:kc + 1],
                                                                scalar2=None, op0=ALU.mult)),
                 reads=[("xst", b), "gvec"], writes=[("xg", kc)])
        if with_stats:
            for tc in range(4):
                P.op("act", (lambda h, tc=tc: h.activation(out=rx_bc[:, tc * 512:(tc + 1) * 512], in_=ps[tc][:], func=AF.Sqrt,
                                                           scale=1.0 / D, bias=EPS)),
                     reads=[("ps", tc)], writes=[("rxs", tc)])
                P.op("dve", (lambda h, tc=tc: h.reciprocal(out=rx_bc[:, tc * 512:(tc + 1) * 512], in_=rx_bc[:, tc * 512:(tc + 1) * 512])),
                     reads=[("rxs", tc)], writes=[("rx_bc", tc)])
            for tt in range(NB):
                P.op("pe", (lambda h, tt=tt: h.matmul(ps[4][:, tt:tt + 1], lhsT=rx_bc[:, tt * 128:(tt + 1) * 128], rhs=ident_f[:, 0:1],
                                                      start=True, stop=True)),
                     reads=[("rx_bc", tt // 4), "cm_f"], writes=[("ps", 4)], sig=(tt == NB - 1))
            P.op("act", lambda h: h.copy(out=rx_tok[:], in_=ps[4][:, 0:NB]), reads=[("ps", 4)], writes=["rx_tok"])
        if release:
            A.release(m)

    XG = [("xg", kc) for kc in range(8)]
    RX = [("rx_bc", tc) for tc in range(4)]

    def proj_fm(xg, wb, wkey, c0, ncol, tc, bank, rhs_view=None):
        for kc in range(8):
            rhs = xg[:, kc, tc * 512:(tc + 1) * 512] if rhs_view is None else rhs_view(kc, tc)
            P.op("pe", (lambda h, kc=kc, rhs=rhs: h.matmul(ps[bank][0:ncol, :], lhsT=wb[:, kc, c0:c0 + ncol], rhs=rhs,
                                                            start=(kc == 0), stop=(kc == 7))),
                 reads=[(wkey, kc), ("xg", kc)], writes=[("ps", bank)], sig=(kc == 7), accum=(kc > 0))

    def head_norm(bank, bank2, gcol, dst, tmp_y, tmp_sq, tmp_sd, rx_ap, extra_reads, dst_key):
        n = dst.shape[-1]
        if rx_ap is None:
            P.op("act", lambda h: h.copy(out=tmp_y, in_=ps[bank][:, 0:n]), reads=[("ps", bank)] + extra_reads, writes=["hn_y"])
        else:
            P.op("dve", lambda h: h.tensor_tensor(out=tmp_y, in0=ps[bank][:, 0:n], in1=rx_ap, op=ALU.mult),
                 reads=[("ps", bank)] + extra_reads, writes=["hn_y"])
        P.op("act", lambda h: h.activation(out=tmp_sq, in_=tmp_y, func=AF.Square), reads=["hn_y"], writes=["hn_sq"])
        P.op("pe", lambda h: h.matmul(ps[bank2][:, 0:n], lhsT=ones_b, rhs=tmp_sq, start=True, stop=True),
             reads=["hn_sq", "cm_b"], writes=[("ps", bank2)])
        P.op("act", lambda h: h.activation(out=tmp_sd, in_=ps[bank2][:, 0:n], func=AF.Sqrt, scale=1.0 / 128, bias=EPS),
             reads=[("ps", bank2)], writes=["hn_sd"])
        P.op("dve", lambda h: h.reciprocal(out=tmp_sd, in_=tmp_sd), reads=["hn_sd"], writes=["hn_sd"])
        P.op("dve", lambda h: h.scalar_tensor_tensor(out=dst, in0=tmp_y, scalar=gvec[:, gcol:gcol + 1], in1=tmp_sd,
                                                     op0=ALU.mult, op1=ALU.mult),
             reads=["hn_y", "hn_sd", "gvec"], writes=[dst_key])

    base_mark = A.mark()

    xg = A.alloc([128, 8, S], BF16)
    build_xg(xg, True, release=True)
    P.op("dve", lambda h: h.tensor_scalar(out=gvec[:, 27:28], in0=gvec[:, 24:25], scalar1=128 ** -0.5, scalar2=None, op0=ALU.mult),
         reads=["gvec"], writes=["gvec"])
    P.op("dve", lambda h: h.tensor_scalar(out=gvec[:, 28:29], in0=gvec[:, 26:27], scalar1=128 ** -0.5, scalar2=None, op0=ALU.mult),
         reads=["gvec"], writes=["gvec"])

    def early(tag, tensors):
        if dbg.get("stop") != tag:
            return False
        for i, (t, shape, dt, keys) in enumerate(tensors):
            d = dump(f"{tag}{i}", None, shape, dt)
            P.dma("sp", (lambda h, d=d, t=t: h.dma_start(out=d, in_=t)), reads=keys, writes=[f"dbg_{tag}{i}"])
        P.barrier()
        P.emit()
        return True

    if early("xg", [(xg[:], [128, 8, S], BF16, XG), (rx_bc[:], [128, S], F32, RX), (rx_tok[:], [128, NB], F32, ["rx_tok"])]):
        return nc, dbg_outs

    def mb_base(m):
        return 128 * (m * (m + 1) // 2)

    def mk_off(c, j):
        return 512 * (2 * c * c + 2 * c + j)

    MBT = A.alloc([128, 512 * 40], BF16)
    mA = A.mark()

    wbi = A.alloc([128, 8, 584], BF16)
    wki = A.alloc([128, 8, 128], BF16)
    qiT = A.alloc([128, 4, S], BF16)
    kiT = A.alloc([128, S], BF16)
    widx = A.alloc([128, NB, 8], F32)
    load_w(wbi, w_in_v, "wbi", (1152, 1736))
    for kc in range(8):
        P.dma("pool", (lambda h, kc=kc: h.dma_start(out=wki[:, kc, 0:64], in_=w_in_v[:, kc, 1664:1728])), writes=[("wki", kc)])
        P.dma("pool", (lambda h, kc=kc: h.dma_start(out=wki[:, kc, 64:128], in_=w_in_v[:, kc, 1664:1728])), writes=[("wki", kc)])
    for j in range(4):
        for tc in range(4):
            b = nb()
            proj_fm(xg, wbi, "wbi", j * 128, 128, tc, b)
            P.op("dve", (lambda h, b=b, j=j, tc=tc: h.tensor_tensor(out=qiT[:, j, tc * 512:(tc + 1) * 512], in0=ps[b][:],
                                                                      in1=rx_bc[:, tc * 512:(tc + 1) * 512], op=ALU.mult)),
                 reads=[("ps", b), ("rx_bc", tc)], writes=[("qiT", j, tc)])
    for tc in range(4):
        b = nb()
        proj_fm(xg, wki, "wki", 0, 128, tc, b)
        P.op("dve", (lambda h, b=b, tc=tc: h.tensor_tensor(out=kiT[:, tc * 512:(tc + 1) * 512], in0=ps[b][:],
                                                            in1=rx_bc[:, tc * 512:(tc + 1) * 512], op=ALU.mult)),
             reads=[("ps", b), ("rx_bc", tc)], writes=[("kiT", tc)])
    bw = nb()
    for tt in range(NB):
        for kc in range(8):
            P.op("pe", (lambda h, tt=tt, kc=kc: h.matmul(ps[bw][:, tt * 8:(tt + 1) * 8], lhsT=xg[:, kc, tt * 128:(tt + 1) * 128],
                                                         rhs=wbi[:, kc, 576:584], start=(kc == 0), stop=(kc == 7))),
                 reads=[("wbi", kc), ("xg", kc)], writes=[("ps", bw)], sig=(kc == 7 and tt == NB - 1), accum=not (kc == 0 and tt == 0))
    P.op("dve", lambda h: h.tensor_tensor(out=widx[:], in0=ps[bw][:, 0:NB * 8].rearrange("p (a b) -> p a b", b=8),
                                          in1=rx_tok[:].unsqueeze(2).to_broadcast([128, NB, 8]), op=ALU.mult),
         reads=[("ps", bw), "rx_tok"], writes=["widx"])

    if early("proj", [(qiT[:], [128, 4, S], BF16, [("qiT", j, tc) for j in range(4) for tc in range(4)]),
                      (kiT[:], [128, S], BF16, [("kiT", tc) for tc in range(4)]), (widx[:], [128, NB, 8], F32, ["widx"])]):
        return nc, dbg_outs
    acc = [A.alloc([128, S], F32) for _ in range(4)]
    work = A.alloc([128, S], F32)
    rl = [A.alloc([128, 512], F32) for _ in range(3)]
    mbt = [A.alloc([128, S], BF16) for _ in range(2)]
    m8 = A.alloc([128, 8], F32)
    thr0 = A.alloc([128, 1], F32)
    bs_lo = A.alloc([128, 1], F32)
    bs_w = A.alloc([128, 1], F32)
    bs_mid = A.alloc([128, 1], F32)
    bs_cnt = A.alloc([128, 1], F32)
    bs_t = A.alloc([128, 1], F32)
    bs_ck = A.alloc([128, 24], F32)
    bs_hk = A.alloc([128, 24], F32)
    for k_ in range(24):
        P.op("dve", (lambda h, k_=k_: h.memset(bs_ck[:, k_:k_ + 1], 2.0 ** (-(k_ + 1)))), writes=["bs_ck"])
    P.op("dve", lambda h: h.memset(thr0[:], -1.0e29), writes=["thr0"])
    a_lo = A.alloc([128, 1], F32)
    a_w = A.alloc([128, 1], F32)
    a_hk = A.alloc([128, 24], F32)
    a_nh2 = A.alloc([128, 24], F32)
    a_nm = A.alloc([128, 2], F32)
    a_cnt = A.alloc([128, 1], F32)
    a_t = A.alloc([128, 1], F32)
    a_m8 = A.alloc([128, 8], F32)
    a_thr = A.alloc([128, 1], F32)
    workA = A.alloc([128, S], BF16)
    KB = 20
    rl_i = [0]

    def score_units(m):
        ab = m % 4
        n = 128 * (m + 1)
        nsc = (n + 511) // 512
        units = []
        for sc in range(nsc):
            w = min(512, n - 512 * sc)
            for hh in range(8):
                def unit(sc=sc, w=w, hh=hh):
                    j, half = hh // 2, hh % 2
                    b = nb()
                    r = rl_i[0] % 3
                    rl_i[0] += 1
                    P.op("pe", (lambda h: h.matmul(ps[b][:, 0:w], lhsT=qiT[64 * half:64 * half + 64, j, m * 128:(m + 1) * 128],
                                                   rhs=kiT[64 * half:64 * half + 64, sc * 512:sc * 512 + w], start=True, stop=True)),
                         reads=[("qiT", j, m // 4), ("kiT", sc)], writes=[("ps", b)])
                    P.op("act", (lambda h: h.activation(out=rl[r][:, 0:w], in_=ps[b][:, 0:w], func=AF.Relu)),
                         reads=[("ps", b)], writes=[("rl", r)])
                    if hh == 0:
                        P.op("dve", (lambda h: h.tensor_scalar(out=acc[ab][:, sc * 512:sc * 512 + w], in0=rl[r][:, 0:w], scalar1=widx[:, m, 0:1],
                                                               scalar2=None, op0=ALU.mult)),
                             reads=[("rl", r), "widx"], writes=[("acc", ab)])
                    else:
                        P.op("dve", (lambda h: h.scalar_tensor_tensor(out=acc[ab][:, sc * 512:sc * 512 + w], in0=rl[r][:, 0:w], scalar=widx[:, m, hh:hh + 1],
                                                                      in1=acc[ab][:, sc * 512:sc * 512 + w], op0=ALU.mult, op1=ALU.add)),
                             reads=[("rl", r), "widx", ("acc", ab)], writes=[("acc", ab)])
                    if sc == nsc - 1 and hh == 7:
                        P.op("dve", (lambda h: h.tensor_tensor(out=acc[ab][:, m * 128:(m + 1) * 128], in0=acc[ab][:, m * 128:(m + 1) * 128],
                                                               in1=causal_f, op=ALU.add)),
                             reads=[("acc", ab), "cm_f"], writes=[("acc", ab)])
                units.append(unit)
        return units

    def bisect_init(m, mx, lo, w_, hk, keyp, on_act):
        ab = m % 4
        n = 128 * (m + 1)
        nv = m * 128
        P.op("dve", (lambda h: h.max(out=mx[:], in_=acc[ab][:, 0:n])), reads=[("acc", ab)], writes=[keyp + "m8"])
        P.op("dve", (lambda h: h.tensor_reduce(out=lo[:], in_=acc[ab][:, 0:nv], axis=mybir.AxisListType.X, op=ALU.min)),
             reads=[("acc", ab)], writes=[keyp + "lo"])
        P.op("dve", lambda h: h.tensor_tensor(out=w_[:], in0=mx[:, 0:1], in1=lo[:], op=ALU.subtract), reads=[keyp + "m8", keyp + "lo"], writes=[keyp + "w"])
        P.op("dve", lambda h: h.tensor_scalar(out=hk[:], in0=bs_ck[:], scalar1=w_[:], scalar2=None, op0=ALU.mult),
             reads=[keyp + "w", "bs_ck"], writes=[keyp + "hk"])
        if on_act:
            P.op("dve", lambda h: h.tensor_scalar(out=a_nh2[:], in0=hk[:], scalar1=-0.5, scalar2=None, op0=ALU.mult), reads=[keyp + "hk"], writes=["a_nh2"])
            P.op("dve", lambda h: h.scalar_tensor_tensor(out=a_nm[:, 0:1], in0=lo[:], scalar=-1.0, in1=hk[:, 0:1], op0=ALU.mult, op1=ALU.subtract),
                 reads=[keyp + "lo", keyp + "hk"], writes=[("a_nm", 0)])
        else:
            P.op("dve", lambda h: h.tensor_tensor(out=bs_mid[:], in0=lo[:], in1=hk[:, 0:1], op=ALU.add), reads=[keyp + "lo", keyp + "hk"], writes=["bs_mid"])

    def dve_iter(m, k_):
        ab = m % 4
        n = 128 * (m + 1)
        P.op("dve", (lambda h: h.tensor_scalar(out=work[:, 0:n], in0=acc[ab][:, 0:n], scalar1=bs_mid[:], scalar2=None,
                                               op0=ALU.is_ge, op1=ALU.add, accum_out=bs_cnt[:])),
             reads=[("acc", ab), "bs_mid"], writes=["work", "bs_cnt"])
        P.op("dve", lambda h: h.tensor_scalar(out=bs_t[:], in0=bs_cnt[:], scalar1=255.5, scalar2=-0.5, op0=ALU.is_ge, op1=ALU.add),
             reads=["bs_cnt"], writes=["bs_t"])
        P.op("dve", (lambda h: h.scalar_tensor_tensor(out=bs_mid[:], in0=bs_t[:], scalar=bs_hk[:, k_:k_ + 1], in1=bs_mid[:],
                                                      op0=ALU.mult, op1=ALU.add)),
             reads=["bs_t", "d_hk", "bs_mid"], writes=["bs_mid"])

    def dve_final(m):
        P.op("dve", (lambda h: h.tensor_tensor(out=m8[:, 7:8], in0=bs_mid[:], in1=bs_hk[:, KB:KB + 1], op=ALU.subtract)),
             reads=["bs_mid", "d_hk", "d_m8"], writes=["d_m8"])

    def act_iter(m, k_):
        ab = m % 4
        n = 128 * (m + 1)
        cur, nxt = k_ % 2, (k_ + 1) % 2
        P.op("act", (lambda h: h.activation(out=workA[:, 0:n], in_=acc[ab][:, 0:n], func=AF.Sign, bias=a_nm[:, cur:cur + 1], scale=1.0,
                                            accum_out=a_cnt[:])),
             reads=[("acc", ab), ("a_nm", cur)], writes=["workA", "a_cnt"])
        P.op("act", (lambda h: h.activation(out=a_t[:], in_=a_cnt[:], func=AF.Sign, bias=float(n) - 511.5, scale=1.0)),
             reads=["a_cnt"], writes=["a_t"])
        P.op("act", (lambda h: h.activation(out=a_nm[:, nxt:nxt + 1], in_=a_t[:], func=AF.Identity,
                                            scale=a_nh2[:, k_:k_ + 1], bias=a_nm[:, cur:cur + 1])),
             reads=["a_t", "a_nh2", ("a_nm", cur)], writes=[("a_nm", nxt)])

    def act_final(m):
        fin = KB % 2
        P.op("dve", (lambda h: h.scalar_tensor_tensor(out=a_thr[:], in0=a_nm[:, fin:fin + 1], scalar=-1.0, in1=a_hk[:, KB:KB + 1],
                                                      op0=ALU.mult, op1=ALU.subtract)),
             reads=[("a_nm", fin), "a_hk"], writes=["a_thr"])

    def finish(m, thr, thrk):
        ab = m % 2
        a4 = m % 4
        n = 128 * (m + 1)
        P.op("dve", (lambda h: h.tensor_scalar(out=mbt[ab][:, 0:n], in0=acc[a4][:, 0:n], scalar1=thr, scalar2=None, op0=ALU.is_ge)),
             reads=[("acc", a4), thrk], writes=[("mbt", ab)])
        for j0 in range(0, m + 1, 4):
            jn = min(4, m + 1 - j0)
            tb = (j0 // 4) % 2
            for jj in range(jn):
                j = j0 + jj
                P.op("pe", (lambda h, j=j, jj=jj, tb=tb: h.matmul(psTs[tb][:, jj * 128:(jj + 1) * 128], lhsT=mbt[ab][:, j * 128:(j + 1) * 128], rhs=ident_b, start=True, stop=True)),
                     reads=[("mbt", ab), "cm_b"], writes=[("psT", tb)], sig=(jj == jn - 1), accum=(jj > 0))
            P.op("act", (lambda h, j0=j0, jn=jn, tb=tb: h.copy(
                out=MBT[:, mk_off(m // 4, j0): mk_off(m // 4, j0) + jn * 512].rearrange("p (a b) -> p a b", b=512)[:, :, (m % 4) * 128:(m % 4) * 128 + 128],
                in_=psTs[tb][:, 0:jn * 128].rearrange("p (a b) -> p a b", b=128))),
                 reads=[("psT", tb)], writes=[("MBT", m)])

    for u in score_units(0) + score_units(1):
        u()
    for pr_ in range(NB // 2):
        m0, m1 = 2 * pr_, 2 * pr_ + 1
        nxt_units = (score_units(m0 + 2) + score_units(m1 + 2)) if pr_ < NB // 2 - 1 else []
        if m0 >= 2:
            bisect_init(m1, a_m8, a_lo, a_w, a_hk, "a_", True)
            bisect_init(m0, m8, bs_lo, bs_w, bs_hk, "d_", False)
            per = (len(nxt_units) + KB - 1) // KB
            for k_ in range(KB):
                act_iter(m1, k_)
                dve_iter(m0, k_)
                for u in nxt_units[k_ * per:(k_ + 1) * per]:
                    u()
            act_final(m1)
            dve_final(m0)
            finish(m0, m8[:, 7:8], "d_m8")
            finish(m1, a_thr[:], "a_thr")
        else:
            finish(m0, thr0[:], "thr0")
            finish(m1, thr0[:], "thr0")
            for u in nxt_units:
                u()
    if early("scores", [(MBT[:], [128, 512 * 40], BF16, [("MBT", m) for m in dbg.get("m_list", range(dbg.get("m_max", NB)))]),
                        (acc[0][:], [128, S], F32, []), (acc[1][:], [128, S], F32, []), (m8[:], [128, 8], F32, ["d_m8"])]):
        return nc, dbg_outs
    if "mbt" in dbg:
        t = dump("mbt", None, [128, 512 * 40], BF16)
        P.dma("sp", (lambda h, t=t: h.dma_start(out=t, in_=MBT[:])), reads=[("MBT", m) for m in range(NB)], writes=["dbg_mbt"])

    P.barrier()
    A.release(mA)

    wbq = [A.alloc([128, 8, 512], BF16) for _ in range(2)]
    wbc = A.alloc([128, 8, 128], BF16)
    wuv = A.alloc([128, 8, 64], BF16)
    qT = A.alloc([128, 8, S], BF16)
    c_tok = A.alloc([128, NB, 128], BF16)
    cT = A.alloc([128, S], BF16)
    t5f = A.alloc([128, 2, 8, 128], F32)
    t5b = A.alloc([128, 2, 8, 128], BF16)
    rb31 = A.alloc([128, 8], F32)
    gkv_bc = A.alloc([128, 128], F32)
    tmp_y = A.alloc([128, 512], F32)
    tmp_sq = A.alloc([128, 512], BF16)
    tmp_sd = A.alloc([128, 512], F32)
    ss1 = A.alloc([128, 2], F32)
    for i in range(2):
        load_w(wbq[i], w_in_v, ("wbq", i), (512 * i, 512 * i + 512))
    load_w(wbc, w_in_v, "wbc", (1024, 1152))
    P.dma("pool", lambda h: h.dma_start(out=wuv[:], in_=w_uv_d), writes=["wuv"])
    P.dma("sp", lambda h: h.dma_start(out=t5f[:], in_=t5_d), writes=["t5f"])
    P.dma("sp", lambda h: h.dma_start(out=rb31[:], in_=rb31_d), writes=["rb31"])
    P.dma("sp", lambda h: h.dma_start(out=gkv_bc[:], in_=gkv_bc_d), writes=["gkv_bc"])
    P.op("dve", lambda h: h.tensor_tensor(out=t5b[:], in0=t5f[:], in1=rb31[:].unsqueeze(1).unsqueeze(3).to_broadcast([128, 2, 8, 128]),
                                          op=ALU.subtract),
         reads=["t5f", "rb31"], writes=["t5b"])
    for hh in range(8):
        for tc in range(4):
            b = nb(0, 3)
            proj_fm(xg, wbq[hh // 4], ("wbq", hh // 4), (hh % 4) * 128, 128, tc, b)
            head_norm(b, 3 + (tc % 2), 27, qT[:, hh, tc * 512:(tc + 1) * 512], tmp_y[:], tmp_sq[:], tmp_sd[:],
                      rx_bc[:, tc * 512:(tc + 1) * 512], [("rx_bc", tc)], ("qT", hh, tc))
    for tt in range(NB):
        b = nb(0, 3)
        for kc in range(8):
            P.op("pe", (lambda h, b=b, tt=tt, kc=kc: h.matmul(ps[b][:, 0:128], lhsT=xg[:, kc, tt * 128:(tt + 1) * 128], rhs=wbc[:, kc, :],
                                                              start=(kc == 0), stop=(kc == 7))),
                 reads=[("wbc", kc), ("xg", kc)], writes=[("ps", b)], sig=(kc == 7), accum=(kc > 0))
        P.op("act", (lambda h, b=b, tt=tt: h.activation(out=tmp_y[:, 0:128], in_=ps[b][:, 0:128], func=AF.Copy, scale=rx_tok[:, tt:tt + 1])),
             reads=[("ps", b), "rx_tok"], writes=["hn_y"])
        P.op("act", (lambda h: h.activation(out=tmp_y[:, 128:256], in_=tmp_y[:, 0:128], func=AF.Square, accum_out=ss1[:, 0:1])),
             reads=["hn_y"], writes=["c_ss", "hn_y"])
        P.op("act", (lambda h: h.activation(out=ss1[:, 1:2], in_=ss1[:, 0:1], func=AF.Sqrt, scale=1.0 / 128, bias=EPS)),
             reads=["c_ss"], writes=["c_sd"])
        P.op("dve", (lambda h: h.reciprocal(out=ss1[:, 1:2], in_=ss1[:, 1:2])), reads=["c_sd"], writes=["c_sd"])
        P.op("dve", (lambda h, tt=tt: h.scalar_tensor_tensor(out=c_tok[:, tt, :], in0=tmp_y[:, 0:128], scalar=ss1[:, 1:2], in1=gkv_bc[:],
                                                             op0=ALU.mult, op1=ALU.mult)),
             reads=["hn_y", "c_sd", "gkv_bc"], writes=[("c_tok", tt)])
    for t0 in range(0, NB, 4):
        tb = (t0 // 4) % 2
        for jj in range(4):
            tt = t0 + jj
            P.op("pe", (lambda h, tt=tt, jj=jj, tb=tb: h.matmul(psTs[tb][:, jj * 128:(jj + 1) * 128], lhsT=c_tok[:, tt, :], rhs=ident_b, start=True, stop=True)),
                 reads=[("c_tok", tt), "cm_b"], writes=[("psT", tb)], sig=(jj == 3), accum=(jj > 0))
        P.op("act", (lambda h, t0=t0, tb=tb: h.copy(out=cT[:, t0 * 128:(t0 + 4) * 128], in_=psTs[tb][:, 0:512])),
             reads=[("psT", tb)], writes=[("cT", t0 // 4)])
    if "qT" in dbg:
        t = dump("qT", None, [128, 8, S], BF16)
        P.dma("sp", (lambda h, t=t: h.dma_start(out=t, in_=qT[:])), reads=[("qT", a, b_) for a in range(8) for b_ in range(4)], writes=["dbg_qT"])
        t2 = dump("cT", None, [128, S], BF16)
        P.dma("sp", (lambda h, t2=t2: h.dma_start(out=t2, in_=cT[:])), reads=[("cT", i) for i in range(4)], writes=["dbg_cT"])

    NPT = 6
    PT = [A.alloc([128, 512], BF16) for _ in range(NPT)]
    PE_ = [A.alloc([128, 512], BF16) for _ in range(NPT)]
    SB = [(ps[0], ("ps", 0)), (ps[1], ("ps", 1)), (ps[2], ("ps", 2)), (psTs[0], ("psT", 0)), (psTs[1], ("psT", 1))]
    rden = A.alloc([128, 512], F32)
    onT = [A.alloc([128, 512], BF16) for _ in range(2)]
    BU = 5
    accO = [ps[3][:], ps[3][:]]
    accD = [ps[4][:], ps[4][:]]
    accK = [(("ps", 3), ("ps", 4)), (("ps", 3), ("ps", 4))]
    its = []
    for c in range(4):
        for hh in range(8):
            nj = 4 * c + 4
            for j in range(nj):
                its.append((c, hh, j, nj))
    LAG = 4
    deferred = []

    def front(i):
        c, hh, j, nj = its[i]
        t_lo = max(512 * c, 128 * j)
        co = t_lo - 512 * c
        sbt, sbk = SB[i % 5]
        pb = i % NPT
        adds = []
        for diff in (0, 1):
            m = j + diff
            if 4 * c <= m <= 4 * c + 3:
                adds.append(((m - 4 * c) * 128, t5b[:, diff, hh, :], "t5b"))
        P.op("pe", (lambda h: h.matmul(sbt[:, co:512], lhsT=cT[:, j * 128:(j + 1) * 128], rhs=qT[:, hh, t_lo:512 * c + 512],
                                       start=True, stop=(len(adds) == 0))),
             reads=[("cT", j // 4), ("qT", hh, c)], writes=[sbk], sig=(len(adds) == 0))
        for ai, (col, rhs, key) in enumerate(adds):
            last = ai == len(adds) - 1
            P.op("pe", (lambda h, col=col, rhs=rhs, last=last: h.matmul(sbt[:, col:col + 128], lhsT=ident_b, rhs=rhs, start=False, stop=last)),
                 reads=[key, "cm_b"], writes=[sbk], sig=last, accum=True)
        P.op("act", (lambda h: h.activation(out=PE_[pb][:, 0:512 - co], in_=sbt[:, co:512], func=AF.Exp, bias=rb31[:, hh:hh + 1], scale=1.0)),
             reads=[sbk, "rb31"], writes=[("PE_", pb)])
        P.op("dve", (lambda h: h.tensor_tensor(out=PT[pb][:, 0:512 - co], in0=PE_[pb][:, 0:512 - co],
                                               in1=MBT[:, mk_off(c, j) + co: mk_off(c, j) + 512], op=ALU.mult)),
             reads=[("PE_", pb)] + [("MBT", m) for m in range(4 * c, 4 * c + 4)], writes=[("PT", pb)])

    def back(i):
        c, hh, j, nj = its[i]
        t_lo = max(512 * c, 128 * j)
        co = t_lo - 512 * c
        pb = i % NPT
        hidx = c * 8 + hh
        ab_ = hidx % 2
        aO, aD = accO[ab_], accD[ab_]
        kO, kD = accK[ab_]
        P.op("pe", (lambda h: h.matmul(aO[:, co:512], lhsT=c_tok[:, j, :], rhs=PT[pb][:, 0:512 - co], start=(j == 0), stop=(j == nj - 1))),
             reads=[("c_tok", j), ("PT", pb)], writes=[kO], sig=False, accum=(j > 0))
        P.op("pe", (lambda h: h.matmul(aD[:, co:512], lhsT=ones_b, rhs=PT[pb][:, 0:512 - co], start=(j == 0), stop=(j == nj - 1))),
             reads=["cm_b", ("PT", pb)], writes=[kD], sig=True, accum=(j > 0))
        if j == nj - 1:
            ob = hh % 2
            P.op("dve", lambda h: h.reciprocal(out=rden[:], in_=aD), reads=[kD], writes=["rden"])
            P.op("dve", (lambda h: h.tensor_tensor(out=onT[ob][:], in0=aO, in1=rden[:], op=ALU.mult)),
                 reads=[kO, "rden"], writes=[("onT", ob)])

            def epilogue():
                P.op("pe", (lambda h: h.matmul(ps[BU][64 * (hh % 2):64 * (hh % 2) + 64, :], lhsT=wuv[:, hh, :], rhs=onT[ob][:], start=True, stop=True)),
                     reads=["wuv", ("onT", ob)], writes=[("ps", BU, hh % 2)])
                if hh % 2 == 1:
                    P.op("act", (lambda h: h.copy(out=o_dsa[:, hh // 2, c * 512:(c + 1) * 512], in_=ps[BU][:])),
                         reads=[("ps", BU, 0), ("ps", BU, 1)], writes=[("o_dsa", hh // 2, c)])
            deferred.append([2, epilogue])

    for i in range(len(its) + LAG):
        if i < len(its):
            front(i)
        if i - LAG >= 0:
            back(i - LAG)
            for dct in list(deferred):
                dct[0] -= 1
                if dct[0] <= 0:
                    dct[1]()
                    deferred.remove(dct)
    for dct in deferred:
        dct[1]()
    if "o_dsa" in dbg:
        t = dump("o_dsa", None, [128, 4, S], BF16)
        P.dma("sp", (lambda h, t=t: h.dma_start(out=t, in_=o_dsa[:])), reads=[("o_dsa", a, c) for a in range(4) for c in range(4)],
              writes=["dbg_o_dsa"])

    P.barrier()
    A.release(base_mark)
    o_s5 = A.alloc([128, 4, S], BF16)
    o_x = A.alloc([128, 4, S], BF16)
    base_mark = A.mark()
    if dbg.get("od_early"):
        t = dump("od0", None, [128, 4, S], BF16)
        P.dma("sp", (lambda h, t=t: h.dma_start(out=t, in_=o_dsa[:])), writes=["dbg_od0"])
    if dbg.get("stop") == "dsa":
        P.barrier()
        P.emit()
        return nc, dbg_outs


    xgX = A.alloc([128, 8, S], BF16)
    build_xg(xgX, False)
    wbx = A.alloc([128, 8, 512], BF16)
    wbu = A.alloc([128, 8, 512], BF16)
    wkv = A.alloc([128, 8, 1024], BF16)
    memf = A.alloc([128, 8, 256], F32)
    msq = A.alloc([128, 8, 256], BF16)
    memn = A.alloc([128, 8, 256], BF16)
    rm = A.alloc([128, 256], F32)
    khT = A.alloc([128, 4, 256], BF16)
    vtok = A.alloc([128, 2, 512], BF16)
    qxT = A.alloc([128, 4, S], BF16)
    ust = [A.alloc([128, 512], BF16) for _ in range(2)]
    tyX = A.alloc([128, 512], F32)
    tqX = A.alloc([128, 512], BF16)
    tdX = A.alloc([128, 512], F32)
    PTx = [A.alloc([128, 512], BF16) for _ in range(3)]
    rdx = A.alloc([128, 512], F32)
    w_kv_v = w_kv_d.rearrange("(kc p) n -> p kc n", p=128)
    if not dbg.get("no_wbx"):
        load_w(wbx, w_in_v, "wbx", (2248, 2760))
    if not dbg.get("no_wbu"):
        load_w(wbu, w_in_v, "wbu", (1736, 2248))
    for i in range(2):
        if not dbg.get("no_wkv"):
            load_w(wkv[:, :, i * 512:(i + 1) * 512], w_kv_v, ("wkv", i), (i * 512, (i + 1) * 512))
    P.dma("sp", lambda h: h.dma_start(out=memf[:], in_=memT_d.rearrange("(kc p) m -> p kc m", p=128)), writes=["memf"])
    if dbg.get("stop") == "x0":
        P.barrier()
        t = dump("od", None, [128, 4, S], BF16)
        P.dma("sp", (lambda h, t=t: h.dma_start(out=t, in_=o_dsa[:])), writes=["dbg_od"])
        P.barrier()
        P.emit()
        return nc, dbg_outs
    P.op("act", lambda h: h.activation(out=msq[:], in_=memf[:], func=AF.Square), reads=["memf"], writes=["msq"])
    bm = nb(0, 3)
    for kc in range(8):
        P.op("pe", (lambda h, kc=kc: h.matmul(ps[bm][:, 0:256], lhsT=ones_b, rhs=msq[:, kc, :], start=(kc == 0), stop=(kc == 7))),
             reads=["msq", "cm_b"], writes=[("ps", bm)], sig=(kc == 7), accum=(kc > 0))
    P.op("act", lambda h: h.activation(out=rm[:], in_=ps[bm][:, 0:256], func=AF.Sqrt, scale=1.0 / D, bias=EPS), reads=[("ps", bm)], writes=["rm"])
    P.op("dve", lambda h: h.reciprocal(out=rm[:], in_=rm[:]), reads=["rm"], writes=["rm"])
    for kc in range(8):
        P.op("dve", (lambda h, kc=kc: h.scalar_tensor_tensor(out=memn[:, kc, :], in0=memf[:, kc, :], scalar=gvec[:, 8 + kc:9 + kc], in1=rm[:],
                                                             op0=ALU.mult, op1=ALU.mult)),
             reads=["memf", "rm", "gvec"], writes=[("memn", kc)])
    for hh in range(4):
        b = nb(0, 3)
        for kc in range(8):
            P.op("pe", (lambda h, b=b, hh=hh, kc=kc: h.matmul(ps[b][:, 0:256], lhsT=wkv[:, kc, hh * 128:(hh + 1) * 128], rhs=memn[:, kc, :],
                                                              start=(kc == 0), stop=(kc == 7))),
                 reads=[(("wkv", 0), kc), ("memn", kc)], writes=[("ps", b)], sig=(kc == 7), accum=(kc > 0))
        head_norm(b, 3 + (hh % 2), 28, khT[:, hh, :], tyX[:, 0:256], tqX[:, 0:256], tdX[:, 0:256], None, [], ("khT", hh))
    for mb in range(2):
        b = nb(0, 3)
        for kc in range(8):
            P.op("pe", (lambda h, b=b, mb=mb, kc=kc: h.matmul(ps[b][:], lhsT=memn[:, kc, mb * 128:(mb + 1) * 128], rhs=wkv[:, kc, 512:1024],
                                                              start=(kc == 0), stop=(kc == 7))),
                 reads=[(("wkv", 1), kc), ("memn", kc)], writes=[("ps", b)], sig=(kc == 7), accum=(kc > 0))
        P.op("act", (lambda h, b=b, mb=mb: h.copy(out=vtok[:, mb, :], in_=ps[b][:])), reads=[("ps", b)], writes=[("vtok", mb)])
    for hh in range(4):
        for tc in range(4):
            b = nb(0, 3)
            proj_fm(xgX, wbx, "wbx", hh * 128, 128, tc, b)
            head_norm(b, 3 + (tc % 2), 25, qxT[:, hh, tc * 512:(tc + 1) * 512], tyX[:], tqX[:], tdX[:],
                      rx_bc[:, tc * 512:(tc + 1) * 512], [("rx_bc", tc)], ("qxT", hh, tc))
    ui = 0
    for ch in range(4):
        for tcp in range(4):
            b = nb(0, 3)
            ub = ui % 2
            ui += 1
            proj_fm(xgX, wbu, "wbu", ch * 128, 128, tcp, b,
                    rhs_view=(lambda kc, tcp: xgX[:, kc, :].rearrange("p (b j) -> p j b", j=8)[:, 2 * tcp:2 * tcp + 2, :]))
            P.op("dve", (lambda h, b=b, ub=ub, tcp=tcp: h.tensor_tensor(
                out=ust[ub][:].rearrange("p (j b) -> p j b", j=2), in0=ps[b][:].rearrange("p (j b) -> p j b", j=2),
                in1=rx_bc[:].rearrange("p (b j) -> p j b", j=8)[:, 2 * tcp:2 * tcp + 2, :], op=ALU.mult)),
                reads=[("ps", b)] + RX, writes=[("ust", ub)])
            P.dma("sp", (lambda h, ub=ub, ch=ch, tcp=tcp: h.dma_start(out=scr_u[ch * 128:(ch + 1) * 128, tcp * 512:(tcp + 1) * 512], in_=ust[ub][:])),
                  reads=[("ust", ub)], writes=[("scr_u", ch, tcp)])
    ptiX = 0
    BO, BD = 3, 4
    for c in range(4):
        for hh in range(4):
            for mb in range(2):
                bs = nb(0, 3)
                pb = ptiX % 3
                ptiX += 1
                P.op("pe", (lambda h, bs=bs, hh=hh, mb=mb, c=c: h.matmul(ps[bs][:], lhsT=khT[:, hh, mb * 128:(mb + 1) * 128],
                                                                         rhs=qxT[:, hh, c * 512:(c + 1) * 512], start=True, stop=True)),
                     reads=[("khT", hh), ("qxT", hh, c)], writes=[("ps", bs)])
                P.op("act", (lambda h, bs=bs, pb=pb: h.activation(out=PTx[pb][:], in_=ps[bs][:], func=AF.Exp)),
                     reads=[("ps", bs)], writes=[("PT", pb)])
                P.op("pe", (lambda h, pb=pb, mb=mb, hh=hh: h.matmul(ps[BO][:], lhsT=vtok[:, mb, hh * 128:(hh + 1) * 128], rhs=PTx[pb][:],
                                                                    start=(mb == 0), stop=(mb == 1))),
                     reads=[("vtok", mb), ("PT", pb)], writes=[("ps", BO)], sig=False, accum=(mb > 0))
                P.op("pe", (lambda h, pb=pb, mb=mb: h.matmul(ps[BD][:], lhsT=ones_b, rhs=PTx[pb][:], start=(mb == 0), stop=(mb == 1))),
                     reads=["cm_b", ("PT", pb)], writes=[("ps", BD)], sig=True, accum=(mb > 0))
            P.op("dve", lambda h: h.reciprocal(out=rdx[:], in_=ps[BD][:]), reads=[("ps", BD)], writes=["rden"])
            P.op("dve", (lambda h, hh=hh, c=c: h.tensor_tensor(out=o_x[:, hh, c * 512:(c + 1) * 512], in0=ps[BO][:], in1=rdx[:], op=ALU.mult)),
                 reads=[("ps", BO), "rden"], writes=[("o_x", hh, c)])
    if dbg.get("stop") == "cross":
        t = dump("od", None, [128, 4, S], BF16)
        P.dma("sp", (lambda h, t=t: h.dma_start(out=t, in_=o_dsa[:])), writes=["dbg_od"])
        t = dump("o_x", None, [128, 4, S], BF16)
        P.dma("sp", (lambda h, t=t: h.dma_start(out=t, in_=o_x[:])), reads=[("o_x", a, c) for a in range(4) for c in range(4)], writes=["dbg_o_x"])
        t2 = dump("scr_u", None, [512, S], BF16)
        P.dma("sp", (lambda h, t2=t2: h.dma_start(out=t2, in_=scr_u)), reads=[("scr_u", a, c) for a in range(4) for c in range(4)], writes=["dbg_scr_u"])
        P.barrier()
        P.emit()
        return nc, dbg_outs
    P.barrier()
    A.release(base_mark)

    s5a = A.alloc([128, 3, 32], F32)
    s5b = A.alloc([128, 4, 32, 16], F32)
    dsk = A.alloc([128, 32], F32)
    TB = [A.alloc([128, 2, 8, 32], F32) for _ in range(4)]
    TK = A.alloc([128, 8, 2, 32], F32)
    cw = A.alloc([128, 16, 2, 32], F32)
    bb = A.alloc([128, 2, 32, 16], F32)
    W1 = A.alloc([128, 32, 128], BF16)
    W2 = A.alloc([128, 32, 128], BF16)
    Tm = A.alloc([128, 32, 128], BF16)
    U8 = A.alloc([128, 32, 256], BF16)
    X = A.alloc([128, 32, 256], BF16)
    mS = A.alloc([128, 1], F32)
    mS1 = A.mark()
    W1T = A.alloc([128, 32, 128], BF16)
    Lm = A.alloc([128, 32, 128], BF16)
    Rm = A.alloc([128, 32, 128], BF16)
    tA = A.alloc([128, 32, 128], F32)
    tB = A.alloc([128, 32, 128], F32)
    P.dma("sp", lambda h: h.dma_start(out=s5a[:], in_=s5a_d), writes=["s5a"])
    P.dma("sp", lambda h: h.dma_start(out=s5b[:], in_=s5b_d), writes=["s5b"])
    P.dma("sp", lambda h: h.dma_start(out=dsk[:], in_=dsk_d), writes=["dsk"])
    for jl in range(8):
        P.dma("sp", (lambda h, jl=jl: h.dma_start(out=U8[jl * 16:(jl + 1) * 16, :, :],
                                                  in_=scr_u.rearrange("(g c) (j b) -> j c g b", c=16, j=8)[jl])),
              reads=[("scr_u", a, c) for a in range(4) for c in range(4)], writes=[("U8", jl)])
    U8K = [("U8", jl) for jl in range(8)]

    def V(i):
        return cw[:, i, :, :]

    def tt(out, a, b_, op, rk, wk):
        P.op("dve", lambda h: h.tensor_tensor(out=out, in0=a, in1=b_, op=op), reads=rk, writes=wk)

    def cmul(dst, a, b_, ka, kb, kd):
        t1, t2 = V(14), V(15)
        tt(t1[:, 0, :], a[:, 0, :], b_[:, 0, :], ALU.mult, [ka, kb], ["cw_t1a"])
        tt(t1[:, 1, :], a[:, 1, :], b_[:, 1, :], ALU.mult, [ka, kb], ["cw_t1b"])
        tt(t2[:, 0, :], a[:, 0, :], b_[:, 1, :], ALU.mult, [ka, kb], ["cw_t2a"])
        tt(t2[:, 1, :], a[:, 1, :], b_[:, 0, :], ALU.mult, [ka, kb], ["cw_t2b"])
        tt(dst[:, 0, :], t1[:, 0, :], t1[:, 1, :], ALU.subtract, ["cw_t1a", "cw_t1b"], [kd])
        tt(dst[:, 1, :], t2[:, 0, :], t2[:, 1, :], ALU.add, ["cw_t2a", "cw_t2b", kd], [kd])

    a_re, a_im, ldt = s5a[:, 0, :], s5a[:, 1, :], s5a[:, 2, :]
    dtv = V(0)[:, 0, :]
    adr = V(0)[:, 1, :]
    adi = V(1)[:, 0, :]
    P.op("act", lambda h: h.activation(out=dtv, in_=ldt, func=AF.Exp), reads=["s5a"], writes=["dtv"])
    tt(adr, a_re, dtv, ALU.mult, ["s5a", "dtv"], ["adr"])
    tt(adi, a_im, dtv, ALU.mult, ["s5a", "dtv"], ["adi"])
    mag, magn, cs, sn = V(2)[:, 0, :], V(2)[:, 1, :], V(3)[:, 0, :], V(3)[:, 1, :]
    P.op("dve", lambda h: h.memset(mS[:], math.pi / 2), writes=["mS"])
    P.op("act", lambda h: h.activation(out=mag, in_=adr, func=AF.Exp, scale=1.0 / 16), reads=["adr"], writes=["mag"])
    P.op("act", lambda h: h.activation(out=magn, in_=adr, func=AF.Exp, scale=-1.0 / 16), reads=["adr"], writes=["magn"])
    P.op("act", lambda h: h.activation(out=cs, in_=adi, func=AF.Sin, scale=1.0 / 16, bias=mS[:]), reads=["adi", "mS"], writes=["cs"])
    P.op("act", lambda h: h.activation(out=sn, in_=adi, func=AF.Sin, scale=1.0 / 16), reads=["adi"], writes=["sn"])
    mu, nu = V(4), V(5)
    tt(mu[:, 0, :], mag, cs, ALU.mult, ["mag", "cs"], ["mu"])
    tt(mu[:, 1, :], mag, sn, ALU.mult, ["mag", "sn", "mu"], ["mu"])
    tt(nu[:, 0, :], magn, cs, ALU.mult, ["magn", "cs"], ["nu"])
    P.op("dve", lambda h: h.scalar_tensor_tensor(out=nu[:, 1, :], in0=magn, scalar=-1.0, in1=sn, op0=ALU.mult, op1=ALU.mult),
         reads=["magn", "sn", "nu"], writes=["nu"])
    def pw_slot(t, slot):
        return TB[t][:, :, slot, :]
    TW1, TW2, TL, TR = 0, 1, 2, 3
    cur, ck = mu, "mu"
    for i in range(4):
        dst = V(6 + (i % 2)) if i < 3 else pw_slot(TR, 1)
        kd = f"sqp{i}" if i < 3 else ("P", 1)
        cmul(dst, cur, cur, ck, ck, kd)
        cur, ck = dst, kd
    cur, ck = nu, "nu"
    for i in range(4):
        dst = V(8 + (i % 2)) if i < 3 else pw_slot(TL, 1)
        kd = f"sqn{i}" if i < 3 else ("N", 1)
        cmul(dst, cur, cur, ck, ck, kd)
        cur, ck = dst, kd
    Pp = {1: pw_slot(TR, 1)}
    Np = {1: pw_slot(TL, 1)}
    for t_ in (TR, TL, TW1):
        sl = 7 if t_ == TW1 else 0
        P.op("dve", (lambda h, t_=t_, sl=sl: h.memset(TB[t_][:, 0, sl, :], 1.0)), writes=[("one", t_, 0)])
        P.op("dve", (lambda h, t_=t_, sl=sl: h.memset(TB[t_][:, 1, sl, :], 0.0)), writes=[("one", t_, 1)])
    for k, (a, b_) in ((2, (1, 1)), (3, (2, 1)), (4, (2, 2)), (5, (4, 1)), (6, (4, 2)), (7, (4, 3))):
        Pp[k] = pw_slot(TR, k)
        cmul(Pp[k], Pp[a], Pp[b_], ("P", a), ("P", b_), ("P", k))
        Np[k] = pw_slot(TL, k)
        cmul(Np[k], Np[a], Np[b_], ("N", a), ("N", b_), ("N", k))
    Pp[8] = pw_slot(TW2, 7)
    cmul(Pp[8], Pp[4], Pp[4], ("P", 4), ("P", 4), ("P", 8))
    PK = [("P", k) for k in range(1, 9)]
    NK = [("N", k) for k in range(1, 8)]
    P.op("dve", lambda h: h.tensor_copy(out=TB[TW2][:, :, 0:7, :], in_=TB[TR][:, :, 1:8, :]), reads=PK, writes=["TW2"])
    for jl in range(7):
        P.op("dve", (lambda h, jl=jl: h.tensor_copy(out=TB[TW1][:, :, jl, :], in_=TB[TR][:, :, 7 - jl, :])), reads=PK, writes=[("TW1", jl)])
    TW1K = [("TW1", jl) for jl in range(7)] + [("one", TW1, 0), ("one", TW1, 1)]
    TRK = PK + [("one", TR, 0), ("one", TR, 1)]
    TLK = NK + [("one", TL, 0), ("one", TL, 1)]
    TW2K = ["TW2", ("P", 8)]
    P.op("dve", lambda h: h.tensor_copy(out=TK[:, 0, :, :], in_=Pp[8]), reads=[("P", 8)], writes=[("TK", 0)])
    for l in range(1, 8):
        cmul(TK[:, l, :, :], TK[:, l - 1, :, :], TK[:, l - 1, :, :], ("TK", l - 1), ("TK", l - 1), ("TK", l))
    P.op("dve", lambda h: h.tensor_scalar(out=TK[64:128, :, 1, :], in0=TK[64:128, :, 1, :], scalar1=-1.0, scalar2=None, op0=ALU.mult),
         reads=[("TK", l) for l in range(8)], writes=["TKs"])
    num, qv = V(10), V(11)
    den = V(12)[:, 0, :]
    P.op("dve", lambda h: h.tensor_scalar(out=num[:, 0, :], in0=Pp[1][:, 0, :], scalar1=-1.0, scalar2=None, op0=ALU.add),
         reads=[("P", 1)], writes=["num"])
    P.op("dve", lambda h: h.tensor_copy(out=num[:, 1, :], in_=Pp[1][:, 1, :]), reads=[("P", 1), "num"], writes=["num"])
    tt(den, a_re, a_re, ALU.mult, ["s5a"], ["den"])
    tt(V(12)[:, 1, :], a_im, a_im, ALU.mult, ["s5a"], ["den2"])
    tt(den, den, V(12)[:, 1, :], ALU.add, ["den", "den2"], ["den"])
    P.op("dve", lambda h: h.reciprocal(out=den, in_=den), reads=["den"], writes=["den"])
    t13 = V(13)
    tt(t13[:, 0, :], num[:, 0, :], a_re, ALU.mult, ["num", "s5a"], ["t13a"])
    tt(t13[:, 1, :], num[:, 1, :], a_im, ALU.mult, ["num", "s5a"], ["t13b"])
    tt(qv[:, 0, :], t13[:, 0, :], t13[:, 1, :], ALU.add, ["t13a", "t13b"], ["qv0"])
    tt(qv[:, 0, :], qv[:, 0, :], den, ALU.mult, ["qv0", "den"], ["qv0"])
    tt(t13[:, 0, :], num[:, 1, :], a_re, ALU.mult, ["num", "s5a", "qv0"], ["t13a"])
    tt(t13[:, 1, :], num[:, 0, :], a_im, ALU.mult, ["num", "s5a", "qv0"], ["t13b"])
    tt(qv[:, 1, :], t13[:, 0, :], t13[:, 1, :], ALU.subtract, ["t13a", "t13b"], ["qv1"])
    tt(qv[:, 1, :], qv[:, 1, :], den, ALU.mult, ["qv1", "den"], ["qv1"])
    q_re = qv[:, 0, :].unsqueeze(2).to_broadcast([128, 32, 16])
    q_im = qv[:, 1, :].unsqueeze(2).to_broadcast([128, 32, 16])
    B_re, B_im, C_re, C_im = s5b[:, 0], s5b[:, 1], s5b[:, 2], s5b[:, 3]
    tAv = tA[:].rearrange("p g (j c) -> p g j c", c=16)
    tBv = tB[:].rearrange("p g (j c) -> p g j c", c=16)
    tt(tAv[:, :, 0, :], q_re, B_re, ALU.mult, ["qv0", "s5b"], ["tA"])
    tt(tBv[:, :, 0, :], q_im, B_im, ALU.mult, ["qv1", "s5b"], ["tB"])
    tt(bb[:, 0], tAv[:, :, 0, :], tBv[:, :, 0, :], ALU.subtract, ["tA", "tB"], ["bb0"])
    tt(tAv[:, :, 0, :], q_re, B_im, ALU.mult, ["qv0", "s5b", "bb0"], ["tA"])
    tt(tBv[:, :, 0, :], q_im, B_re, ALU.mult, ["qv1", "s5b", "bb0"], ["tB"])
    tt(bb[:, 1], tAv[:, :, 0, :], tBv[:, :, 0, :], ALU.add, ["tA", "tB"], ["bb1"])

    def build_mat(dst, tbl, tkeys, v_re, v_im, vkeys, mode, dkey):
        dv = dst[:].rearrange("p g (j c) -> p g j c", c=16)
        for half in range(2):
            pr = slice(64 * half, 64 * half + 64)
            Tre = TB[tbl][pr, 0, :, :].rearrange("p s g -> p g s").unsqueeze(3).to_broadcast([64, 32, 8, 16])
            Tim = TB[tbl][pr, 1, :, :].rearrange("p s g -> p g s").unsqueeze(3).to_broadcast([64, 32, 8, 16])
            va, vb = (v_re, v_im) if half == 0 else (v_im, v_re)
            Va = va[pr].unsqueeze(2).to_broadcast([64, 32, 8, 16])
            Vb = vb[pr].unsqueeze(2).to_broadcast([64, 32, 8, 16])
            tt(tAv[pr], Tre, Va, ALU.mult, tkeys + vkeys + [dkey], [("tA", half)])
            tt(tBv[pr], Tim, Vb, ALU.mult, tkeys + vkeys + [dkey], [("tB", half)])
            if half == 0:
                tt(dv[pr], tAv[pr], tBv[pr], ALU.subtract, [("tA", 0), ("tB", 0)], [(dkey, 0)])
            elif mode == "B":
                tt(dv[pr], tAv[pr], tBv[pr], ALU.add, [("tA", 1), ("tB", 1)], [(dkey, 1)])
            else:
                P.op("dve", (lambda h, pr=pr: h.scalar_tensor_tensor(out=dv[pr], in0=tAv[pr], scalar=-1.0, in1=tBv[pr],
                                                                      op0=ALU.mult, op1=ALU.subtract)),
                     reads=[("tA", 1), ("tB", 1)], writes=[(dkey, 1)])

    P.op("dve", lambda h: h.memset(mS[:], 0.0), reads=["tA", "tB", "mS"], writes=[("tA", 0), ("tA", 1), ("tB", 0), ("tB", 1), "mS"])
    build_mat(W1T, TW1, TW1K, bb[:, 0], bb[:, 1], ["bb0", "bb1"], "B", "W1T")
    build_mat(W2, TW2, TW2K, C_re, C_im, ["s5b"], "C", "W2")
    build_mat(Lm, TL, TLK, bb[:, 0], bb[:, 1], ["bb0", "bb1"], "B", "Lm")
    build_mat(Rm, TR, TRK, C_re, C_im, ["s5b"], "C", "Rm")
    for g0 in range(0, 32, 4):
        tb = (g0 // 4) % 2
        for jj in range(4):
            P.op("pe", (lambda h, g0=g0, jj=jj, tb=tb: h.matmul(psTs[tb][:, jj * 128:(jj + 1) * 128], lhsT=W1T[:, g0 + jj, :], rhs=ident_b, start=True, stop=True)),
                 reads=[("W1T", 0), ("W1T", 1), "cm_b"], writes=[("psT", tb)], sig=(jj == 3), accum=(jj > 0))
        P.op("act", (lambda h, g0=g0, tb=tb: h.copy(out=W1[:, g0:g0 + 4, :], in_=psTs[tb][:, 0:512].rearrange("p (a b) -> p a b", b=128))),
             reads=[("psT", tb)], writes=[("W1", g0 // 4)])
    for g0 in range(0, 32, 4):
        b = nb(0, 3)
        for jj in range(4):
            P.op("pe", (lambda h, g0=g0, jj=jj, b=b: h.matmul(ps[b][:, jj * 128:(jj + 1) * 128], lhsT=Lm[:, g0 + jj, :], rhs=Rm[:, g0 + jj, :],
                                                              start=True, stop=True)),
                 reads=[("Lm", 0), ("Lm", 1), ("Rm", 0), ("Rm", 1)], writes=[("ps", b)], sig=(jj == 3), accum=(jj > 0))
        P.op("dve", (lambda h, b=b, g0=g0: h.tensor_tensor(out=tA[:, g0:g0 + 4, :], in0=ps[b][:].rearrange("p (a b) -> p a b", b=128),
                                                           in1=tmask_f.unsqueeze(1).to_broadcast([128, 4, 128]), op=ALU.mult)),
             reads=[("ps", b), "cm_f", ("tA", 0), ("tA", 1)], writes=[("tAm", g0 // 4)])
        for jj in range(4):
            g = g0 + jj
            P.op("dve", (lambda h, g=g: h.scalar_tensor_tensor(out=Tm[:, g, :], in0=ident_f, scalar=dsk[:, g:g + 1], in1=tA[:, g, :],
                                                               op0=ALU.mult, op1=ALU.add)),
                 reads=[("tAm", g0 // 4), "dsk", "cm_f"], writes=[("Tm", g0 // 4)])
    if dbg.get("stop") == "s5pre":
        for nm, tns, keys in (("W1", W1, [("W1", i) for i in range(8)]), ("W2", W2, [("W2", 0), ("W2", 1)]), ("Tm", Tm, [("Tm", i) for i in range(8)])):
            t = dump(nm, None, [128, 32, 128], BF16)
            P.dma("sp", (lambda h, t=t, tns=tns: h.dma_start(out=t, in_=tns[:])), reads=keys, writes=["dbg_" + nm])
        t = dump("TK", None, [128, 8, 2, 32])
        P.dma("sp", (lambda h, t=t: h.dma_start(out=t, in_=TK[:])), reads=["TKs"], writes=["dbg_TK"])
        t = dump("TB", None, [128, 2, 8, 32])
        P.dma("sp", (lambda h, t=t: h.dma_start(out=t, in_=TB[TR][:])), reads=TRK, writes=["dbg_TB"])
        P.barrier()
        P.emit()
        return nc, dbg_outs
    P.barrier()
    A.release(mS1)

    Yst = A.alloc([128, 32, 256], BF16)
    gq = A.alloc([128, 512], F32)
    gz = A.alloc([128, 512], F32)
    gs = A.alloc([128, 512], F32)
    mS2 = A.mark()
    Rl = [A.alloc([128, 32, 128], BF16) for _ in range(2)]
    rt1 = A.alloc([128, 32, 128], BF16)
    rt2 = A.alloc([128, 32, 128], BF16)
    XK = lambda gp: ("X", gp)
    for gp in range(16):
        b = nb(0, 3)
        for gi in range(2):
            g = 2 * gp + gi
            P.op("pe", (lambda h, b=b, gi=gi, g=g: h.matmul(ps[b][:, gi * 256:(gi + 1) * 256], lhsT=W1[:, g, :], rhs=U8[:, g, :], start=True, stop=True)),
                 reads=[("W1", g // 4)] + U8K, writes=[("ps", b)], sig=(gi == 1), accum=(gi > 0))
        P.op("act", (lambda h, b=b, gp=gp: h.copy(out=X[:, 2 * gp:2 * gp + 2, :], in_=ps[b][:].rearrange("p (a b) -> p a b", b=256))),
             reads=[("ps", b)], writes=[XK(gp)])
    for l in range(8):
        d = 1 << l
        R = Rl[l % 2]
        P.op("dve", (lambda h, l=l: h.tensor_tensor(out=rt1[:], in0=ident_f.unsqueeze(1).to_broadcast([128, 32, 128]),
                                                    in1=TK[:, l, 0, :].unsqueeze(2).to_broadcast([128, 32, 128]), op=ALU.mult)),
             reads=["cm_f", "TKs"], writes=["rt1"])
        P.op("dve", (lambda h, l=l: h.tensor_tensor(out=rt2[:], in0=swap_f.unsqueeze(1).to_broadcast([128, 32, 128]),
                                                    in1=TK[:, l, 1, :].unsqueeze(2).to_broadcast([128, 32, 128]), op=ALU.mult)),
             reads=["cm_f", "TKs"], writes=["rt2"])
        P.op("dve", (lambda h, R=R: h.tensor_tensor(out=R[:], in0=rt1[:], in1=rt2[:], op=ALU.add)),
             reads=["rt1", "rt2"], writes=[("R", l % 2)])
        for gp in range(16):
            b = nb(0, 3)
            for gi in range(2):
                g = 2 * gp + gi
                P.op("pe", (lambda h, b=b, gi=gi, g=g, d=d, R=R: h.matmul(ps[b][:, gi * 256 + d:(gi + 1) * 256], lhsT=R[:, g, :], rhs=X[:, g, 0:256 - d],
                                                                          start=True, stop=True)),
                     reads=[("R", l % 2), XK(gp)], writes=[("ps", b)], sig=(gi == 1), accum=(gi > 0))
            P.op("dve", (lambda h, b=b, gp=gp, d=d: h.tensor_tensor(out=X[:, 2 * gp:2 * gp + 2, d:256], in0=X[:, 2 * gp:2 * gp + 2, d:256],
                                                                   in1=ps[b][:].rearrange("p (a b) -> p a b", b=256)[:, :, d:256], op=ALU.add)),
                 reads=[("ps", b), XK(gp)], writes=[XK(gp)])
    for gp in range(16):
        b = nb(0, 3)
        for gi in range(2):
            g = 2 * gp + gi
            P.op("pe", (lambda h, b=b, gi=gi, g=g: h.matmul(ps[b][:, gi * 256:(gi + 1) * 256], lhsT=Tm[:, g, :], rhs=U8[:, g, :], start=True, stop=False)),
                 reads=[("Tm", g // 4)] + U8K, writes=[("ps", b)], sig=False, accum=(gi > 0))
            P.op("pe", (lambda h, b=b, gi=gi, g=g: h.matmul(ps[b][:, gi * 256 + 1:(gi + 1) * 256], lhsT=W2[:, g, :], rhs=X[:, g, 0:255], start=False, stop=True)),
                 reads=[("W2", 0), ("W2", 1), XK(gp)], writes=[("ps", b)], sig=(gi == 1), accum=True)
        P.op("act", (lambda h, b=b: h.activation(out=gq[:], in_=ps[b][:], func=AF.Square)), reads=[("ps", b)], writes=["gq"])
        P.op("dve", lambda h: h.tensor_scalar(out=gz[:], in0=gq[:], scalar1=0.044715, scalar2=1.0, op0=ALU.mult, op1=ALU.add),
             reads=["gq"], writes=["gz"])
        P.op("dve", (lambda h, b=b: h.tensor_tensor(out=gz[:], in0=gz[:], in1=ps[b][:], op=ALU.mult)), reads=["gz", ("ps", b)], writes=["gz"])
        P.op("act", lambda h: h.activation(out=gs[:], in_=gz[:], func=AF.Sigmoid, scale=2.0 * math.sqrt(2.0 / math.pi)), reads=["gz"], writes=["gs"])
        P.op("dve", (lambda h, b=b, gp=gp: h.tensor_tensor(out=Yst[:, 2 * gp:2 * gp + 2, :], in0=gs[:].rearrange("p (a b) -> p a b", b=256),
                                                          in1=ps[b][:].rearrange("p (a b) -> p a b", b=256), op=ALU.mult)),
             reads=["gs", ("ps", b)], writes=[("Yst", gp)])
    for il in range(8):
        P.dma("sp", (lambda h, il=il: h.dma_start(out=scr_y.rearrange("(g c) (i b) -> i c g b", c=16, i=8)[il], in_=Yst[il * 16:(il + 1) * 16, :, :])),
              reads=[("Yst", gp) for gp in range(16)], writes=[("scr_y", il)])
    P.barrier()
    A.release(mS2)
    YT = A.alloc([128, 4, S], BF16)
    wglu = A.alloc([128, 4, 512], BF16)
    P.dma("pool", lambda h: h.dma_start(out=wglu[:], in_=w_glu_d.rearrange("(kc p) n -> p kc n", p=128)), writes=["wglu"])
    P.dma("sp", lambda h: h.dma_start(out=YT[:], in_=scr_y.rearrange("(ch p) t -> p ch t", p=128)), writes=["YT"])
    for chp in range(4):
        for tcp in range(4):
            b = nb(0, 3)
            for k in range(4):
                P.op("pe", (lambda h, b=b, k=k, chp=chp, tcp=tcp: h.matmul(ps[b][:], lhsT=wglu[:, k, chp * 128:(chp + 1) * 128],
                                                                           rhs=YT[:, k, tcp * 512:(tcp + 1) * 512], start=(k == 0), stop=(k == 3))),
                     reads=["wglu", "YT"], writes=[("ps", b)], sig=(k == 3), accum=(k > 0))
            P.op("act", (lambda h, b=b: h.activation(out=gs[:], in_=ps[b][:], func=AF.Sigmoid)), reads=[("ps", b)], writes=["gs"])
            P.op("dve", (lambda h, chp=chp, tcp=tcp: h.tensor_tensor(out=o_s5[:, chp, tcp * 512:(tcp + 1) * 512], in0=gs[:],
                                                                    in1=YT[:, chp, tcp * 512:(tcp + 1) * 512], op=ALU.mult)),
                 reads=["gs", "YT"], writes=[("o_s5", chp, tcp)])
    if dbg.get("stop") == "s5":
        t = dump("od", None, [128, 4, S], BF16)
        P.dma("sp", (lambda h, t=t: h.dma_start(out=t, in_=o_dsa[:])), writes=["dbg_od"])
        t = dump("o_s5", None, [128, 4, S], BF16)
        P.dma("sp", (lambda h, t=t: h.dma_start(out=t, in_=o_s5[:])), reads=[("o_s5", a, c) for a in range(4) for c in range(4)], writes=["dbg_o_s5"])
        t = dump("YT", None, [128, 4, S], BF16)
        P.dma("sp", (lambda h, t=t: h.dma_start(out=t, in_=YT[:])), reads=["YT"], writes=["dbg_YT"])
        P.barrier()
        P.emit()
        return nc, dbg_outs
    P.barrier()
    A.release(base_mark)

    off_merged = A.mark()
    merged = A.alloc([128, 8, S], BF16)
    mM1 = A.mark()
    xgM = A.alloc([128, 8, S], BF16)
    build_xg(xgM, False)
    wg = [A.alloc([128, 8, 384], BF16) for _ in range(2)]
    wbr = [A.alloc([128, 4, 384], BF16) for _ in range(2)]
    gt = A.alloc([128, 512], F32)
    sg = A.alloc([128, 512], F32)
    macc = A.alloc([128, 512], F32)
    mtmp = A.alloc([128, 512], F32)
    w_br_v = [w.rearrange("(kc p) n -> p kc n", p=128) for w in (w_brd_d, w_brs_d, w_brx_d)]
    obr = [o_dsa, o_s5, o_x]
    for fc in range(8):
        wb_ = fc % 2
        for br in range(3):
            c0 = 2760 + br * 1024 + fc * 128
            P.dma("pool", (lambda h, wb_=wb_, br=br, c0=c0: h.dma_start(out=wg[wb_][:, :, br * 128:(br + 1) * 128], in_=w_in_v[:, :, c0:c0 + 128])),
                  writes=[("wg", wb_, br)])
            P.dma("pool", (lambda h, wb_=wb_, br=br, fc=fc: h.dma_start(out=wbr[wb_][:, :, br * 128:(br + 1) * 128],
                                                                       in_=w_br_v[br][:, :, fc * 128:(fc + 1) * 128])),
                  writes=[("wbr", wb_, br)])
        for tc in range(4):
            for br in range(3):
                bG = nb(0, 6)
                for kc in range(8):
                    P.op("pe", (lambda h, bG=bG, kc=kc, wb_=wb_, br=br, tc=tc: h.matmul(ps[bG][:], lhsT=wg[wb_][:, kc, br * 128:(br + 1) * 128],
                                                                                        rhs=xgM[:, kc, tc * 512:(tc + 1) * 512], start=(kc == 0), stop=(kc == 7))),
                         reads=[("wg", wb_, br), ("xg", kc)], writes=[("ps", bG)], sig=(kc == 7), accum=(kc > 0))
                P.op("dve", (lambda h, bG=bG, tc=tc: h.tensor_tensor(out=gt[:], in0=ps[bG][:], in1=rx_bc[:, tc * 512:(tc + 1) * 512], op=ALU.mult)),
                     reads=[("ps", bG)], writes=["gt"])
                P.op("act", lambda h: h.activation(out=sg[:], in_=gt[:], func=AF.Sigmoid), reads=["gt"], writes=["sg"])
                bB = nb(0, 6)
                for k in range(4):
                    if br == 1:
                        rhs = o_s5[:, k, :].rearrange("p (j b) -> p b j", j=8)[:, 64 * tc:64 * tc + 64, :]
                    else:
                        rhs = obr[br][:, k, tc * 512:(tc + 1) * 512]
                    P.op("pe", (lambda h, bB=bB, k=k, wb_=wb_, br=br, rhs=rhs: h.matmul(ps[bB][:], lhsT=wbr[wb_][:, k, br * 128:(br + 1) * 128], rhs=rhs,
                                                                                        start=(k == 0), stop=(k == 3))),
                         reads=[("wbr", wb_, br)], writes=[("ps", bB)], sig=(k == 3), accum=(k > 0))
                if br == 0:
                    P.op("dve", (lambda h, bB=bB: h.tensor_tensor(out=macc[:], in0=ps[bB][:], in1=sg[:], op=ALU.mult)),
                         reads=[("ps", bB), "sg"], writes=["macc"])
                else:
                    P.op("dve", (lambda h, bB=bB: h.tensor_tensor(out=mtmp[:], in0=ps[bB][:], in1=sg[:], op=ALU.mult)),
                         reads=[("ps", bB), "sg"], writes=["mtmp"])
                    if br == 1:
                        P.op("dve", lambda h: h.tensor_tensor(out=macc[:], in0=macc[:], in1=mtmp[:], op=ALU.add), reads=["macc", "mtmp"], writes=["macc"])
                    else:
                        P.op("dve", (lambda h, fc=fc, tc=tc: h.tensor_tensor(out=merged[:, fc, tc * 512:(tc + 1) * 512], in0=macc[:], in1=mtmp[:], op=ALU.add)),
                             reads=["macc", "mtmp"], writes=[("merged", fc, tc)])
                if dbg.get("stop") == "merge1" and br == dbg.get("br", 0):
                    for nm, tns, shp, dt_, keys in (("wg", wg[0][:], [128, 8, 384], BF16, [("wg", 0, i) for i in range(3)]), ("gt", gt[:], [128, 512], F32, ["gt"]),
                                                    ("sg", sg[:], [128, 512], F32, ["sg"]), ("macc", macc[:], [128, 512], F32, ["macc"]),
                                                    ("mtmp", mtmp[:], [128, 512], F32, ["mtmp"]),
                                                    ("xg", xgM[:], [128, 8, S], BF16, XG), ("od", o_dsa[:], [128, 4, S], BF16, []), ("os", o_s5[:], [128, 4, S], BF16, []), ("ox", o_x[:], [128, 4, S], BF16, []), ("wbr", wbr[0][:], [128, 4, 384], BF16, [("wbr", 0, i) for i in range(3)])):
                        t = dump(nm, None, shp, dt_)
                        P.dma("sp", (lambda h, t=t, tns=tns: h.dma_start(out=t, in_=tns)), reads=keys, writes=["dbg_" + nm])
                    P.barrier()
                    P.emit()
                    return nc, dbg_outs
    if dbg.get("stop") == "merge":
        t = dump("merged", None, [128, 8, S], BF16)
        P.dma("sp", (lambda h, t=t: h.dma_start(out=t, in_=merged[:])), reads=[("merged", a, c) for a in range(8) for c in range(4)], writes=["dbg_merged"])
        P.barrier()
        P.emit()
        return nc, dbg_outs
    P.barrier()
    A.release(mM1)
    off_x1 = A.mark()
    x1T = A.alloc([128, 8, S], F32)
    mM0 = A.mark()
    wout = A.alloc([128, 8, 1024], BF16)
    xres = [A.alloc([128, 512], F32) for _ in range(2)]
    w_out_v = w_out_d.rearrange("(kc p) n -> p kc n", p=128)
    for i in range(2):
        load_w(wout[:, :, i * 512:(i + 1) * 512], w_out_v, ("wout", i), (i * 512, (i + 1) * 512))
    xi = 0
    for fc in range(8):
        for tc in range(4):
            xb = xi % 2
            xi += 1
            P.dma("sp", (lambda h, xb=xb, fc=fc, tc=tc: h.dma_start(out=xres[xb][:], in_=xT_d[fc * 128:(fc + 1) * 128, tc * 512:(tc + 1) * 512])),
                  writes=[("xres", xb)])
            b = nb(0, 6)
            for kc in range(8):
                P.op("pe", (lambda h, b=b, kc=kc, fc=fc, tc=tc: h.matmul(ps[b][:], lhsT=wout[:, kc, fc * 128:(fc + 1) * 128],
                                                                         rhs=merged[:, kc, tc * 512:(tc + 1) * 512], start=(kc == 0), stop=(kc == 7))),
                     reads=[(("wout", fc // 4), kc), ("merged", kc, tc)], writes=[("ps", b)], sig=(kc == 7), accum=(kc > 0))
            P.op("dve", (lambda h, b=b, xb=xb, fc=fc, tc=tc: h.tensor_tensor(out=x1T[:, fc, tc * 512:(tc + 1) * 512], in0=ps[b][:], in1=xres[xb][:], op=ALU.add)),
                 reads=[("ps", b), ("xres", xb)], writes=[("x1T", fc, tc)])
    if dbg.get("stop") == "x1":
        t = dump("x1T", None, [128, 8, S])
        P.dma("sp", (lambda h, t=t: h.dma_start(out=t, in_=x1T[:])), reads=[("x1T", a, c) for a in range(8) for c in range(4)], writes=["dbg_x1T"])
        P.barrier()
        P.emit()
        return nc, dbg_outs
    P.barrier()
    A.release(mM0)
    top = A.mark()
    A.off = off_o
    fsq = [A.alloc([128, 512], BF16) for _ in range(2)]
    rf = A.alloc([128, 512], F32)
    wgu = [[A.alloc([128, 8, 256], BF16) for _ in range(2)] for _ in range(2)]
    wdn = [A.alloc([128, NFF, 256], BF16) for _ in range(2)]
    fs = A.alloc([128, 512], F32)
    assert A.off <= off_merged
    A.off = off_merged
    hfT = A.alloc([128, 8, 512], BF16)
    hT = A.alloc([128, NFF, 512], BF16)
    assert A.off <= off_x1
    A.off = top
    fo = [A.alloc([128, 512], F32) for _ in range(2)]
    w_gu_v = [w.rearrange("(kc p) n -> p kc n", p=128) for w in (w_g_d, w_u_d)]
    w_d_v = w_d_d.rearrange("(fk p) n -> p fk n", p=128)
    wi = 0
    di = 0
    oi = 0
    for tq in range(4):
        tsl = slice(tq * 512, (tq + 1) * 512)
        bS = nb(0, 6)
        for kc in range(8):
            sb_ = kc % 2
            P.op("act", (lambda h, kc=kc, sb_=sb_, tsl=tsl: h.activation(out=fsq[sb_][:], in_=x1T[:, kc, tsl], func=AF.Square)),
                 reads=[("x1T", kc, tq)], writes=[("fsq", sb_)])
            P.op("pe", (lambda h, kc=kc, sb_=sb_, bS=bS: h.matmul(ps[bS][:], lhsT=ones_b, rhs=fsq[sb_][:], start=(kc == 0), stop=(kc == 7))),
                 reads=[("fsq", sb_), "cm_b"], writes=[("ps", bS)], sig=True, accum=(kc > 0))
        P.op("act", (lambda h, bS=bS: h.activation(out=rf[:], in_=ps[bS][:], func=AF.Sqrt, scale=1.0 / D, bias=EPS)), reads=[("ps", bS)], writes=["rf"])
        P.op("dve", lambda h: h.reciprocal(out=rf[:], in_=rf[:]), reads=["rf"], writes=["rf"])
        for kc in range(8):
            P.op("dve", (lambda h, kc=kc, tsl=tsl: h.scalar_tensor_tensor(out=hfT[:, kc, :], in0=x1T[:, kc, tsl], scalar=gvec[:, 16 + kc:17 + kc], in1=rf[:],
                                                                          op0=ALU.mult, op1=ALU.mult)),
                 reads=[("x1T", kc, tq), "rf", "gvec"], writes=[("hfT", kc)])
        for f0 in range(0, NFF, 2):
            wb_ = wi % 2
            wi += 1
            for gu in range(2):
                P.dma("pool", (lambda h, gu=gu, wb_=wb_, f0=f0: h.dma_start(out=wgu[gu][wb_][:], in_=w_gu_v[gu][:, :, f0 * 128:(f0 + 2) * 128])),
                      writes=[("wgu", gu, wb_)])
            for ff in range(f0, f0 + 2):
                bg, bu = nb(0, 6), nb(0, 6)
                for gu, bb_ in ((0, bg), (1, bu)):
                    for kc in range(8):
                        P.op("pe", (lambda h, gu=gu, bb_=bb_, kc=kc, wb_=wb_, ff=ff, f0=f0: h.matmul(
                            ps[bb_][:], lhsT=wgu[gu][wb_][:, kc, (ff - f0) * 128:(ff - f0 + 1) * 128], rhs=hfT[:, kc, :], start=(kc == 0), stop=(kc == 7))),
                            reads=[("wgu", gu, wb_), ("hfT", kc)], writes=[("ps", bb_)], sig=(kc == 7), accum=(kc > 0))
                P.op("act", (lambda h, bg=bg: h.activation(out=fs[:], in_=ps[bg][:], func=AF.Silu)), reads=[("ps", bg)], writes=["fs"])
                P.op("dve", (lambda h, bu=bu, ff=ff: h.tensor_tensor(out=hT[:, ff, :], in0=ps[bu][:], in1=fs[:], op=ALU.mult)),
                     reads=[("ps", bu), "fs"], writes=[("hT", ff)])
        for c0 in range(0, 8, 2):
            db = di % 2
            di += 1
            P.dma("pool", (lambda h, db=db, c0=c0: h.dma_start(out=wdn[db][:], in_=w_d_v[:, :, c0 * 128:(c0 + 2) * 128])), writes=[("wdn", db)])
            for fc in range(c0, c0 + 2):
                b = nb(0, 6)
                for ff in range(NFF):
                    P.op("pe", (lambda h, b=b, ff=ff, db=db, fc=fc, c0=c0: h.matmul(ps[b][:], lhsT=wdn[db][:, ff, (fc - c0) * 128:(fc - c0 + 1) * 128], rhs=hT[:, ff, :],
                                                                                    start=(ff == 0), stop=(ff == NFF - 1))),
                         reads=[("wdn", db), ("hT", ff)], writes=[("ps", b)], sig=(ff == NFF - 1), accum=(ff > 0))
                ob = oi % 2
                oi += 1
                P.op("dve", (lambda h, b=b, ob=ob, fc=fc, tsl=tsl: h.tensor_tensor(out=fo[ob][:], in0=ps[b][:], in1=x1T[:, fc, tsl], op=ALU.add)),
                     reads=[("ps", b), ("x1T", fc, tq)], writes=[("fo", ob)])
                P.dma("sp", (lambda h, ob=ob, fc=fc, tsl=tsl: h.dma_start(out=outT_d[fc * 128:(fc + 1) * 128, tsl], in_=fo[ob][:])),
                      reads=[("fo", ob)], writes=[("out", fc, tq)])
    P.barrier()
    P.emit()
    return nc, dbg_outs


def _host_inputs(inputs):
    f = lambda a: np.ascontiguousarray(np.asarray(a, dtype=np.float32))
    x = f(inputs["x"])
    mem = f(inputs["mem"])
    rel_bias = f(inputs["rel_bias"])
    shared = {}
    shared["w_in"] = f(inputs["w_in"][0])
    shared["w_uv"] = f(np.transpose(inputs["w_uv_dsa"][0], (1, 0, 2)))
    shared["w_glu"] = f(inputs["w_glu"][0])
    shared["w_mem_kv"] = f(inputs["w_mem_kv"][0])
    shared["w_br_dsa"] = f(inputs["w_br_dsa"][0])
    shared["w_br_s5"] = f(inputs["w_br_s5"][0])
    shared["w_br_cross"] = f(inputs["w_br_cross"][0])
    shared["w_out"] = f(inputs["w_out"][0])
    shared["w_ffn_gate"] = f(inputs["w_ffn_gate"][0])
    shared["w_ffn_up"] = f(inputs["w_ffn_up"][0])
    shared["w_ffn_down"] = f(inputs["w_ffn_down"][0])
    gvec = np.zeros((128, 32), np.float32)
    gvec[:, 0:8] = np.asarray(inputs["g_mix_norm"][0]).reshape(8, 128).T
    gvec[:, 8:16] = np.asarray(inputs["g_mem_norm"][0]).reshape(8, 128).T
    gvec[:, 16:24] = np.asarray(inputs["g_ffn_norm"][0]).reshape(8, 128).T
    gvec[:, 24] = np.asarray(inputs["g_q_dsa"][0])
    gvec[:, 25] = np.asarray(inputs["g_q_cross"][0])
    gvec[:, 26] = np.asarray(inputs["g_k_cross"][0])
    shared["gvec"] = gvec
    shared["gkv_bc"] = f(np.broadcast_to(np.asarray(inputs["g_kv_dsa"][0])[None, :], (128, 128)))
    bt = _t5_bucket_table()
    s_i = np.arange(128)[:, None]
    t_i = np.arange(128)[None, :]
    t5 = np.zeros((128, 2, 8, 128), np.float32)
    for diff in (0, 1):
        dist = np.maximum(t_i - s_i + 128 * diff, 0)
        t5[:, diff, :, :] = np.transpose(rel_bias[bt[dist]], (0, 2, 1))
    shared["t5"] = t5
    shared["rb31"] = f(np.broadcast_to(rel_bias[31][None, :], (128, 8)))
    cm = np.zeros((128, 5, 128), np.float32)
    cm[:, 0, :] = np.eye(128)
    cm[:, 1, :] = np.where(t_i.T >= s_i.T, 0.0, NEG)
    il = np.arange(128)[None, :] // 16
    jl = np.arange(128)[:, None] // 16
    cm[:, 2, :] = (il >= jl).astype(np.float32)
    sw = np.zeros((128, 128), np.float32)
    sw[np.arange(64), np.arange(64) + 64] = 1.0
    sw[np.arange(64) + 64, np.arange(64)] = 1.0
    cm[:, 3, :] = sw
    cm[:, 4, :] = 1.0
    shared["cmats"] = cm
    tile2 = lambda a: np.concatenate([a, a], axis=0)
    s5a = np.zeros((128, 3, 32), np.float32)
    s5a[:, 0, :] = tile2(np.asarray(inputs["a_re"][0]).T)
    s5a[:, 1, :] = tile2(np.asarray(inputs["a_im"][0]).T)
    s5a[:, 2, :] = np.broadcast_to(np.asarray(inputs["log_dt"][0])[None, :], (128, 32))
    shared["s5a"] = s5a
    s5b = np.zeros((128, 4, 32, 16), np.float32)
    s5b[:, 0] = tile2(np.transpose(inputs["b_re"][0], (1, 0, 2)))
    s5b[:, 1] = tile2(np.transpose(inputs["b_im"][0], (1, 0, 2)))
    s5b[:, 2] = tile2(np.transpose(inputs["c_re"][0], (2, 0, 1)))
    s5b[:, 3] = tile2(np.transpose(inputs["c_im"][0], (2, 0, 1)))
    shared["s5b"] = s5b
    shared["dsk"] = f(np.tile(np.asarray(inputs["d_skip"][0]).T, (8, 1)))
    in_maps = []
    for b in range(8):
        d = dict(shared)
        d["xT"] = f(x[b].T)
        d["memT"] = f(mem[b].T)
        in_maps.append(d)
    return in_maps


_CACHE = {}


def kernel(**inputs):
    in_maps = _host_inputs(inputs)
    if "nc" not in _CACHE:
        _CACHE["nc"] = build()[0]
    res = run_bass_kernel_spmd(_CACHE["nc"], in_maps, core_ids=list(range(8)))
    out = np.stack([np.ascontiguousarray(res.results[b]["outT"].T) for b in range(8)], axis=0)
    return out.astype(np.float32)
```

```python
import math
import numpy as np
import concourse.bass as bass
import concourse.mybir as mybir
from concourse.bass_utils import run_bass_kernel_spmd

F32 = mybir.dt.float32
BF16 = mybir.dt.bfloat16
AF = mybir.ActivationFunctionType
ALU = mybir.AluOpType

S = 2048
D = 1024
NB = 16
EPS = 1e-6
D_IN = 5832
D_FF = 2816
NFF = 22
NEG = -1.0e30
MBIAS = -30000.0


class Prog:
    ENGS = ("pe", "act", "dve", "pool", "sp")

    def __init__(self, nc, n_dma_sems=8):
        self.nc = nc
        self.streams = {e: [] for e in self.ENGS}
        self.sems = {e: nc.alloc_semaphore("s_" + e) for e in self.ENGS}
        self.cnt = {e: 0 for e in self.ENGS}
        self.known = {e: {} for e in self.ENGS}
        self.bufs = {}
        self.dma_sems = {}
        self.dma_rr = {}
        self.dma_val = {}
        self.semobj = {}
        for q in ("sp", "pool", "act"):
            self.dma_sems[q] = [nc.alloc_semaphore(f"d_{q}{i}") for i in range(n_dma_sems)]
            self.dma_rr[q] = 0
            for s in self.dma_sems[q]:
                self.semobj[s.name] = s
                self.dma_val[s.name] = 0
        for e in self.ENGS:
            self.semobj[self.sems[e].name] = self.sems[e]
        self.pending_pe = False

    def _need(self, eng, tok, waits):
        if tok is None:
            return
        sname, val = tok
        if self.known[eng].get(sname, 0) >= val:
            return
        if waits.get(sname, 0) < val:
            waits[sname] = val

    def _deps(self, eng, reads, writes, skip_writer=False):
        waits = {}
        for k in reads:
            st = self.bufs.get(k)
            if st is not None:
                self._need(eng, st[0], waits)
        for k in writes:
            st = self.bufs.get(k)
            if st is not None:
                if not skip_writer:
                    self._need(eng, st[0], waits)
                for tok in st[1].values():
                    self._need(eng, tok, waits)
        return waits

    def _commit(self, who, tok, reads, writes):
        for k in reads:
            st = self.bufs.setdefault(k, [None, {}])
            st[1][who + ":" + tok[0]] = tok
        for k in writes:
            self.bufs[k] = [tok, {}]

    def op(self, eng, fn, reads=(), writes=(), sig=True, accum=False):
        waits = self._deps(eng, reads, writes, skip_writer=(accum and eng == "pe"))
        if eng == "pe":
            waits.pop(self.sems["pe"].name, None)
        for s, v in waits.items():
            self.known[eng][s] = v
        if sig:
            self.cnt[eng] += 1
            tok = (self.sems[eng].name, self.cnt[eng])
            if eng == "pe":
                self.pending_pe = False
        else:
            assert eng == "pe"
            tok = (self.sems[eng].name, self.cnt[eng] + 1)
            self.pending_pe = True
        self.streams[eng].append((list(waits.items()), fn, (self.sems[eng], 1) if sig else None))
        self._commit(eng, tok, reads, writes)
        return tok

    def dma(self, q, fn, reads=(), writes=()):
        waits = self._deps(q, reads, writes)
        pool = self.dma_sems[q]
        s = pool[self.dma_rr[q] % len(pool)]
        self.dma_rr[q] += 1
        prev = self.dma_val[s.name]
        if prev > 0:
            self._need(q, (s.name, prev), waits)
        for sn, v in waits.items():
            self.known[q][sn] = v
        self.dma_val[s.name] = prev + 16
        tok = (s.name, prev + 16)
        self.streams[q].append((list(waits.items()), fn, (s, 16)))
        self._commit(q, tok, reads, writes)
        return tok

    def barrier(self):
        assert not self.pending_pe
        targets = [(self.sems[e].name, self.cnt[e]) for e in self.ENGS if self.cnt[e] > 0]
        targets += [(sn, v) for sn, v in self.dma_val.items() if v > 0]
        for e in self.ENGS:
            waits = {}
            for tok in targets:
                if e == "pe" and tok[0] == self.sems["pe"].name:
                    continue
                self._need(e, tok, waits)
            for sn, v in waits.items():
                self.known[e][sn] = v
            if waits:
                self.streams[e].append((list(waits.items()), None, None))
        self.bufs = {}

    def emit(self):
        assert not self.pending_pe
        nc = self.nc
        hmap = {"pe": "tensor", "act": "scalar", "dve": "vector", "pool": "gpsimd", "sp": "sync"}
        semobj = self.semobj
        with nc.Block() as block:
            for e in self.ENGS:
                def body(h, stream=self.streams[e]):
                    for waits, fn, inc in stream:
                        for sn, v in waits:
                            h.wait_ge(semobj[sn], v)
                        if fn is not None:
                            ins = fn(h)
                            if inc is not None:
                                ins.then_inc(inc[0], inc[1])
                getattr(block, hmap[e])(body)


class Arena:
    def __init__(self, nc, start=16640, limit=229376):
        self.nc = nc
        self.off = start
        self.limit = limit
        self.n = 0
        self.peak = 0

    def alloc(self, shape, dtype):
        nbytes = int(np.prod(shape[1:])) * (2 if dtype == BF16 else 4)
        nbytes = (nbytes + 63) // 64 * 64
        assert self.off + nbytes <= self.limit, ("SBUF overflow", self.off, nbytes)
        self.n += 1
        t = self.nc.alloc_sbuf_tensor_at(f"sb{self.n}", list(shape), dtype, offset=self.off)
        self.off += nbytes
        self.peak = max(self.peak, self.off)
        return t

    def mark(self):
        return self.off

    def release(self, m):
        if not getattr(self, "no_release", False):
            self.off = m


def _t5_bucket_table():
    d = np.arange(256)
    max_exact = 16
    nf = np.maximum(d, 1).astype(np.float32)
    large = max_exact + (np.log(nf / max_exact) / math.log(128 / max_exact) * (32 - max_exact)).astype(np.int32)
    large = np.minimum(large, 31)
    return np.where(d < max_exact, d, large)


def build(dbg=None):
    dbg = dbg or {}
    nc = bass.Bass("TRN2", target_bir_lowering=False)
    P = Prog(nc)
    A = Arena(nc)
    A.no_release = bool(dbg.get("no_release"))
    dbg_outs = []

    def din(name, shape):
        return nc.dram_tensor(name, list(shape), F32, kind="ExternalInput").ap()

    xT_d = din("xT", [D, S])
    memT_d = din("memT", [D, 256])
    w_in_d = din("w_in", [D, D_IN])
    w_uv_d = din("w_uv", [128, 8, 64])
    w_glu_d = din("w_glu", [512, 512])
    w_kv_d = din("w_mem_kv", [D, D])
    w_brd_d = din("w_br_dsa", [512, D])
    w_brs_d = din("w_br_s5", [512, D])
    w_brx_d = din("w_br_cross", [512, D])
    w_out_d = din("w_out", [D, D])
    w_g_d = din("w_ffn_gate", [D, D_FF])
    w_u_d = din("w_ffn_up", [D, D_FF])
    w_d_d = din("w_ffn_down", [D_FF, D])
    gvec_d = din("gvec", [128, 32])
    gkv_bc_d = din("gkv_bc", [128, 128])
    t5_d = din("t5", [128, 2, 8, 128])
    rb31_d = din("rb31", [128, 8])
    cm_d = din("cmats", [128, 5, 128])
    s5a_d = din("s5a", [128, 3, 32])
    s5b_d = din("s5b", [128, 4, 32, 16])
    dsk_d = din("dsk", [128, 32])
    outT_d = nc.dram_tensor("outT", [D, S], F32, kind="ExternalOutput").ap()
    scr_u = nc.dram_tensor("scr_u", [512, S], BF16).ap()
    scr_y = nc.dram_tensor("scr_y", [512, S], BF16).ap()

    ps = [nc.alloc_psum_tensor(f"ps{i}", [128, 512], F32) for i in range(6)]
    psTs = [nc.alloc_psum_tensor(f"psT{i}", [128, 512], F32) for i in range(2)]

    w_in_v = w_in_d.rearrange("(kc p) n -> p kc n", p=128)

    def dump(name, ap, shape, dt=F32):
        t = nc.dram_tensor("dbg_" + name, list(shape), dt, kind="ExternalOutput").ap()
        dbg_outs.append(("dbg_" + name))
        return t

    cm_f = A.alloc([128, 5, 128], F32)
    ident_f = cm_f[:, 0, :]
    causal_f = cm_f[:, 1, :]
    tmask_f = cm_f[:, 2, :]
    swap_f = cm_f[:, 3, :]
    cm_b = A.alloc([128, 5, 128], BF16)
    ident_b = cm_b[:, 0, :]
    ones_b = cm_b[:, 4, :]
    gvec = A.alloc([128, 32], F32)
    rx_bc = A.alloc([128, S], F32)
    rx_tok = A.alloc([128, NB], F32)
    off_o = A.mark()
    o_dsa = A.alloc([128, 4, S], BF16)

    P.dma("sp", lambda h: h.dma_start(out=cm_f[:], in_=cm_d), writes=["cm_f"])
    P.dma("pool", lambda h: h.dma_start(out=cm_b[:], in_=cm_d), writes=["cm_b"])
    P.dma("sp", lambda h: h.dma_start(out=gvec[:], in_=gvec_d), writes=["gvec"])
    CONST = ["cm_f", "cm_b", "gvec"]

    bank_rr = [0]

    def nb(lo=0, hi=6):
        b = lo + bank_rr[0] % (hi - lo)
        bank_rr[0] += 1
        return b

    def load_w(dst, src3, key, cols, q="pool"):
        kc_n = dst.shape[1]
        c0, c1 = cols
        for kc in range(kc_n):
            P.dma(q, (lambda h, kc=kc: h.dma_start(out=dst[:, kc, 0:c1 - c0], in_=src3[:, kc, c0:c1])),
                  writes=[(key, kc)])

    def build_xg(xg, with_stats, release=False):
        m = A.mark()
        xst = [A.alloc([128, S], F32) for _ in range(2)]
        sq = [A.alloc([128, S], BF16) for _ in range(2)]
        for kc in range(8):
            b = kc % 2
            P.dma("sp", (lambda h, kc=kc, b=b: h.dma_start(out=xst[b][:], in_=xT_d[kc * 128:(kc + 1) * 128, :])),
                  writes=[("xst", b)])
            if with_stats:
                P.op("act", (lambda h, b=b: h.activation(out=sq[b][:], in_=xst[b][:], func=AF.Square)),
                     reads=[("xst", b)], writes=[("sq", b)])
                for tc in range(4):
                    P.op("pe", (lambda h, b=b, tc=tc, kc=kc: h.matmul(ps[tc][:], lhsT=ones_b, rhs=sq[b][:, tc * 512:(tc + 1) * 512],
                                                                       start=(kc == 0), stop=(kc == 7))),
                         reads=[("sq", b), "cm_b"], writes=[("ps", tc)], sig=(tc == 3), accum=(kc > 0))
            P.op("dve", (lambda h, kc=kc, b=b: h.tensor_scalar(out=xg[:, kc, :], in0=xst[b][:], scalar1=gvec[:, kc:kc + 1],
                                                                scalar2=None, op0=ALU.mult)),
                 reads=[("xst", b), "gvec"], writes=[("xg", kc)])
        if with_stats:
            for tc in range(4):
                P.op("act", (lambda h, tc=tc: h.activation(out=rx_bc[:, tc * 512:(tc + 1) * 512], in_=ps[tc][:], func=AF.Ln,
                                                           scale=1.0 / D, bias=EPS)),
                     reads=[("ps", tc)], writes=[("rxs", tc)])
                P.op("act", (lambda h, tc=tc: h.activation(out=rx_bc[:, tc * 512:(tc + 1) * 512], in_=rx_bc[:, tc * 512:(tc + 1) * 512],
                                                           func=AF.Exp, scale=-0.5)),
                     reads=[("rxs", tc)], writes=[("rx_bc", tc)])
            for tt in range(NB):
                P.op("pe", (lambda h, tt=tt: h.matmul(ps[4][:, tt:tt + 1], lhsT=rx_bc[:, tt * 128:(tt + 1) * 128], rhs=ident_f[:, 0:1],
                                                      start=True, stop=True)),
                     reads=[("rx_bc", tt // 4), "cm_f"], writes=[("ps", 4)], sig=(tt == NB - 1))
            P.op("act", lambda h: h.copy(out=rx_tok[:], in_=ps[4][:, 0:NB]), reads=[("ps", 4)], writes=["rx_tok"])
        if release:
            A.release(m)

    XG = [("xg", kc) for kc in range(8)]
    RX = [("rx_bc", tc) for tc in range(4)]

    def proj_fm(xg, wb, wkey, c0, ncol, tc, bank, rhs_view=None):
        for kc in range(8):
            rhs = xg[:, kc, tc * 512:(tc + 1) * 512] if rhs_view is None else rhs_view(kc, tc)
            P.op("pe", (lambda h, kc=kc, rhs=rhs: h.matmul(ps[bank][0:ncol, :], lhsT=wb[:, kc, c0:c0 + ncol], rhs=rhs,
                                                            start=(kc == 0), stop=(kc == 7))),
                 reads=[(wkey, kc), ("xg", kc)], writes=[("ps", bank)], sig=(kc == 7), accum=(kc > 0))

    def head_norm(bank, bank2, gcol, dst, tmp_y, tmp_sq, tmp_sd, rx_ap, extra_reads, dst_key):
        n = dst.shape[-1]
        if rx_ap is None:
            P.op("act", lambda h: h.copy(out=tmp_y, in_=ps[bank][:, 0:n]), reads=[("ps", bank)] + extra_reads, writes=["hn_y"])
        else:
            P.op("dve", lambda h: h.tensor_tensor(out=tmp_y, in0=ps[bank][:, 0:n], in1=rx_ap, op=ALU.mult),
                 reads=[("ps", bank)] + extra_reads, writes=["hn_y"])
        P.op("act", lambda h: h.activation(out=tmp_sq, in_=tmp_y, func=AF.Square), reads=["hn_y"], writes=["hn_sq"])
        P.op("pe", lambda h: h.matmul(ps[bank2][:, 0:n], lhsT=ones_b, rhs=tmp_sq, start=True, stop=True),
             reads=["hn_sq", "cm_b"], writes=[("ps", bank2)])
        P.op("act", lambda h: h.activation(out=tmp_sd, in_=ps[bank2][:, 0:n], func=AF.Ln, scale=1.0 / 128, bias=EPS),
             reads=[("ps", bank2)], writes=["hn_sd"])
        P.op("act", lambda h: h.activation(out=tmp_sd, in_=tmp_sd, func=AF.Exp, scale=-0.5), reads=["hn_sd"], writes=["hn_sd"])
        P.op("dve", lambda h: h.scalar_tensor_tensor(out=dst, in0=tmp_y, scalar=gvec[:, gcol:gcol + 1], in1=tmp_sd,
                                                     op0=ALU.mult, op1=ALU.mult),
             reads=["hn_y", "hn_sd", "gvec"], writes=[dst_key])

    base_mark = A.mark()

    xg = A.alloc([128, 8, S], BF16)
    build_xg(xg, True, release=True)
    P.op("dve", lambda h: h.tensor_scalar(out=gvec[:, 27:28], in0=gvec[:, 24:25], scalar1=128 ** -0.5, scalar2=None, op0=ALU.mult),
         reads=["gvec"], writes=["gvec"])
    P.op("dve", lambda h: h.tensor_scalar(out=gvec[:, 28:29], in0=gvec[:, 26:27], scalar1=128 ** -0.5, scalar2=None, op0=ALU.mult),
         reads=["gvec"], writes=["gvec"])

    def early(tag, tensors):
        if dbg.get("stop") != tag:
            return False
        for i, (t, shape, dt, keys) in enumerate(tensors):
            d = dump(f"{tag}{i}", None, shape, dt)
            P.dma("sp", (lambda h, d=d, t=t: h.dma_start(out=d, in_=t)), reads=keys, writes=[f"dbg_{tag}{i}"])
        P.barrier()
        P.emit()
        return True

    if early("xg", [(xg[:], [128, 8, S], BF16, XG), (rx_bc[:], [128, S], F32, RX), (rx_tok[:], [128, NB], F32, ["rx_tok"])]):
        return nc, dbg_outs

    def mb_base(m):
        return 128 * (m * (m + 1) // 2)

    def mk_off(c, j):
        return 512 * (2 * c * c + 2 * c + j)

    MBT = A.alloc([128, 512 * 40], BF16)
    mA = A.mark()

    wbi = A.alloc([128, 8, 584], BF16)
    wki = A.alloc([128, 8, 128], BF16)
    qiT = A.alloc([128, 4, S], BF16)
    kiT = A.alloc([128, S], BF16)
    widx = A.alloc([128, NB, 8], F32)
    load_w(wbi, w_in_v, "wbi", (1152, 1736))
    for kc in range(8):
        P.dma("pool", (lambda h, kc=kc: h.dma_start(out=wki[:, kc, 0:64], in_=w_in_v[:, kc, 1664:1728])), writes=[("wki", kc)])
        P.dma("pool", (lambda h, kc=kc: h.dma_start(out=wki[:, kc, 64:128], in_=w_in_v[:, kc, 1664:1728])), writes=[("wki", kc)])
    for j in range(4):
        for tc in range(4):
            b = nb()
            proj_fm(xg, wbi, "wbi", j * 128, 128, tc, b)
            P.op("dve", (lambda h, b=b, j=j, tc=tc: h.tensor_tensor(out=qiT[:, j, tc * 512:(tc + 1) * 512], in0=ps[b][:],
                                                                      in1=rx_bc[:, tc * 512:(tc + 1) * 512], op=ALU.mult)),
                 reads=[("ps", b), ("rx_bc", tc)], writes=[("qiT", j, tc)])
    for tc in range(4):
        b = nb()
        proj_fm(xg, wki, "wki", 0, 128, tc, b)
        P.op("dve", (lambda h, b=b, tc=tc: h.tensor_tensor(out=kiT[:, tc * 512:(tc + 1) * 512], in0=ps[b][:],
                                                            in1=rx_bc[:, tc * 512:(tc + 1) * 512], op=ALU.mult)),
             reads=[("ps", b), ("rx_bc", tc)], writes=[("kiT", tc)])
    bw = nb()
    for tt in range(NB):
        for kc in range(8):
            P.op("pe", (lambda h, tt=tt, kc=kc: h.matmul(ps[bw][:, tt * 8:(tt + 1) * 8], lhsT=xg[:, kc, tt * 128:(tt + 1) * 128],
                                                         rhs=wbi[:, kc, 576:584], start=(kc == 0), stop=(kc == 7))),
                 reads=[("wbi", kc), ("xg", kc)], writes=[("ps", bw)], sig=(kc == 7 and tt == NB - 1), accum=not (kc == 0 and tt == 0))
    P.op("dve", lambda h: h.tensor_tensor(out=widx[:], in0=ps[bw][:, 0:NB * 8].rearrange("p (a b) -> p a b", b=8),
                                          in1=rx_tok[:].unsqueeze(2).to_broadcast([128, NB, 8]), op=ALU.mult),
         reads=[("ps", bw), "rx_tok"], writes=["widx"])

    if early("proj", [(qiT[:], [128, 4, S], BF16, [("qiT", j, tc) for j in range(4) for tc in range(4)]),
                      (kiT[:], [128, S], BF16, [("kiT", tc) for tc in range(4)]), (widx[:], [128, NB, 8], F32, ["widx"])]):
        return nc, dbg_outs
    acc = [A.alloc([128, S], F32) for _ in range(4)]
    work = A.alloc([128, S], F32)
    rl = [A.alloc([128, 512], F32) for _ in range(3)]
    mbt = [A.alloc([128, S], BF16) for _ in range(2)]
    m8 = A.alloc([128, 8], F32)
    thr0 = A.alloc([128, 1], F32)
    bs_lo = A.alloc([128, 1], F32)
    bs_w = A.alloc([128, 1], F32)
    bs_mid = A.alloc([128, 1], F32)
    bs_cnt = A.alloc([128, 1], F32)
    bs_t = A.alloc([128, 1], F32)
    bs_ck = A.alloc([128, 24], F32)
    bs_hk = A.alloc([128, 24], F32)
    for k_ in range(24):
        P.op("dve", (lambda h, k_=k_: h.memset(bs_ck[:, k_:k_ + 1], 2.0 ** (-(k_ + 1)))), writes=["bs_ck"])
    P.op("dve", lambda h: h.memset(thr0[:], -1.0e29), writes=["thr0"])
    a_lo = A.alloc([128, 1], F32)
    a_w = A.alloc([128, 1], F32)
    a_hk = A.alloc([128, 24], F32)
    a_nh2 = A.alloc([128, 24], F32)
    a_nm = A.alloc([128, 2], F32)
    a_cnt = A.alloc([128, 1], F32)
    a_t = A.alloc([128, 1], F32)
    a_m8 = A.alloc([128, 8], F32)
    a_thr = A.alloc([128, 1], F32)
    workA = A.alloc([128, S], BF16)
    KB = 20
    rl_i = [0]

    def score_units(m):
        ab = m % 4
        n = 128 * (m + 1)
        nsc = (n + 511) // 512
        units = []
        for sc in range(nsc):
            w = min(512, n - 512 * sc)
            for hh in range(8):
                def unit(sc=sc, w=w, hh=hh):
                    j, half = hh // 2, hh % 2
                    b = nb()
                    r = rl_i[0] % 3
                    rl_i[0] += 1
                    P.op("pe", (lambda h: h.matmul(ps[b][:, 0:w], lhsT=qiT[64 * half:64 * half + 64, j, m * 128:(m + 1) * 128],
                                                   rhs=kiT[64 * half:64 * half + 64, sc * 512:sc * 512 + w], start=True, stop=True)),
                         reads=[("qiT", j, m // 4), ("kiT", sc)], writes=[("ps", b)])
                    P.op("act", (lambda h: h.activation(out=rl[r][:, 0:w], in_=ps[b][:, 0:w], func=AF.Relu)),
                         reads=[("ps", b)], writes=[("rl", r)])
                    if hh == 0:
                        P.op("dve", (lambda h: h.tensor_scalar(out=acc[ab][:, sc * 512:sc * 512 + w], in0=rl[r][:, 0:w], scalar1=widx[:, m, 0:1],
                                                               scalar2=None, op0=ALU.mult)),
                             reads=[("rl", r), "widx"], writes=[("acc", ab)])
                    else:
                        P.op("dve", (lambda h: h.scalar_tensor_tensor(out=acc[ab][:, sc * 512:sc * 512 + w], in0=rl[r][:, 0:w], scalar=widx[:, m, hh:hh + 1],
                                                                      in1=acc[ab][:, sc * 512:sc * 512 + w], op0=ALU.mult, op1=ALU.add)),
                             reads=[("rl", r), "widx", ("acc", ab)], writes=[("acc", ab)])
                    if sc == nsc - 1 and hh == 7:
                        P.op("dve", (lambda h: h.tensor_tensor(out=acc[ab][:, m * 128:(m + 1) * 128], in0=acc[ab][:, m * 128:(m + 1) * 128],
                                                               in1=causal_f, op=ALU.add)),
                             reads=[("acc", ab), "cm_f"], writes=[("acc", ab)])
                units.append(unit)
        return units

    def bisect_init(m, mx, lo, w_, hk, keyp, on_act):
        ab = m % 4
        n = 128 * (m + 1)
        nv = m * 128
        P.op("dve", (lambda h: h.max(out=mx[:], in_=acc[ab][:, 0:n])), reads=[("acc", ab)], writes=[keyp + "m8"])
        P.op("dve", (lambda h: h.tensor_reduce(out=lo[:], in_=acc[ab][:, 0:nv], axis=mybir.AxisListType.X, op=ALU.min)),
             reads=[("acc", ab)], writes=[keyp + "lo"])
        P.op("dve", lambda h: h.tensor_tensor(out=w_[:], in0=mx[:, 0:1], in1=lo[:], op=ALU.subtract), reads=[keyp + "m8", keyp + "lo"], writes=[keyp + "w"])
        P.op("dve", lambda h: h.tensor_scalar(out=hk[:], in0=bs_ck[:], scalar1=w_[:], scalar2=None, op0=ALU.mult),
             reads=[keyp + "w", "bs_ck"], writes=[keyp + "hk"])
        if on_act:
            P.op("dve", lambda h: h.tensor_scalar(out=a_nh2[:], in0=hk[:], scalar1=-0.5, scalar2=None, op0=ALU.mult), reads=[keyp + "hk"], writes=["a_nh2"])
            P.op("dve", lambda h: h.scalar_tensor_tensor(out=a_nm[:, 0:1], in0=lo[:], scalar=-1.0, in1=hk[:, 0:1], op0=ALU.mult, op1=ALU.subtract),
                 reads=[keyp + "lo", keyp + "hk"], writes=[("a_nm", 0)])
        else:
            P.op("dve", lambda h: h.tensor_tensor(out=bs_mid[:], in0=lo[:], in1=hk[:, 0:1], op=ALU.add), reads=[keyp + "lo", keyp + "hk"], writes=["bs_mid"])

    def dve_iter(m, k_):
        ab = m % 4
        n = 128 * (m + 1)
        P.op("dve", (lambda h: h.tensor_scalar(out=work[:, 0:n], in0=acc[ab][:, 0:n], scalar1=bs_mid[:], scalar2=None,
                                               op0=ALU.is_ge, op1=ALU.add, accum_out=bs_cnt[:])),
             reads=[("acc", ab), "bs_mid"], writes=["work", "bs_cnt"])
        P.op("dve", lambda h: h.tensor_scalar(out=bs_t[:], in0=bs_cnt[:], scalar1=255.5, scalar2=-0.5, op0=ALU.is_ge, op1=ALU.add),
             reads=["bs_cnt"], writes=["bs_t"])
        P.op("dve", (lambda h: h.scalar_tensor_tensor(out=bs_mid[:], in0=bs_t[:], scalar=bs_hk[:, k_:k_ + 1], in1=bs_mid[:],
                                                      op0=ALU.mult, op1=ALU.add)),
             reads=["bs_t", "d_hk", "bs_mid"], writes=["bs_mid"])

    def dve_final(m):
        P.op("dve", (lambda h: h.tensor_tensor(out=m8[:, 7:8], in0=bs_mid[:], in1=bs_hk[:, KB:KB + 1], op=ALU.subtract)),
             reads=["bs_mid", "d_hk", "d_m8"], writes=["d_m8"])

    def act_iter(m, k_):
        ab = m % 4
        n = 128 * (m + 1)
        cur, nxt = k_ % 2, (k_ + 1) % 2
        P.op("act", (lambda h: h.activation(out=workA[:, 0:n], in_=acc[ab][:, 0:n], func=AF.Sign, bias=a_nm[:, cur:cur + 1], scale=1.0,
                                            accum_out=a_cnt[:])),
             reads=[("acc", ab), ("a_nm", cur)], writes=["workA", "a_cnt"])
        P.op("act", (lambda h: h.activation(out=a_t[:], in_=a_cnt[:], func=AF.Sign, bias=float(n) - 511.5, scale=1.0)),
             reads=["a_cnt"], writes=["a_t"])
        P.op("act", (lambda h: h.activation(out=a_nm[:, nxt:nxt + 1], in_=a_t[:], func=AF.Identity,
                                            scale=a_nh2[:, k_:k_ + 1], bias=a_nm[:, cur:cur + 1])),
             reads=["a_t", "a_nh2", ("a_nm", cur)], writes=[("a_nm", nxt)])

    def act_final(m):
        fin = KB % 2
        P.op("dve", (lambda h: h.scalar_tensor_tensor(out=a_thr[:], in0=a_nm[:, fin:fin + 1], scalar=-1.0, in1=a_hk[:, KB:KB + 1],
                                                      op0=ALU.mult, op1=ALU.subtract)),
             reads=[("a_nm", fin), "a_hk"], writes=["a_thr"])

    def finish(m, thr, thrk):
        ab = m % 2
        a4 = m % 4
        n = 128 * (m + 1)
        P.op("dve", (lambda h: h.tensor_scalar(out=mbt[ab][:, 0:n], in0=acc[a4][:, 0:n], scalar1=thr, scalar2=None, op0=ALU.is_ge)),
             reads=[("acc", a4), thrk], writes=[("mbt", ab)])
        for j0 in range(0, m + 1, 4):
            jn = min(4, m + 1 - j0)
            tb = (j0 // 4) % 2
            for jj in range(jn):
                j = j0 + jj
                P.op("pe", (lambda h, j=j, jj=jj, tb=tb: h.matmul(psTs[tb][:, jj * 128:(jj + 1) * 128], lhsT=mbt[ab][:, j * 128:(j + 1) * 128], rhs=ident_b, start=True, stop=True)),
                     reads=[("mbt", ab), "cm_b"], writes=[("psT", tb)], sig=(jj == jn - 1), accum=(jj > 0))
            P.op("act", (lambda h, j0=j0, jn=jn, tb=tb: h.copy(
                out=MBT[:, mk_off(m // 4, j0): mk_off(m // 4, j0) + jn * 512].rearrange("p (a b) -> p a b", b=512)[:, :, (m % 4) * 128:(m % 4) * 128 + 128],
                in_=psTs[tb][:, 0:jn * 128].rearrange("p (a b) -> p a b", b=128))),
                 reads=[("psT", tb)], writes=[("MBT", m)])

    for u in score_units(0) + score_units(1):
        u()
    for pr_ in range(NB // 2):
        m0, m1 = 2 * pr_, 2 * pr_ + 1
        nxt_units = (score_units(m0 + 2) + score_units(m1 + 2)) if pr_ < NB // 2 - 1 else []
        if m0 >= 2:
            bisect_init(m1, a_m8, a_lo, a_w, a_hk, "a_", True)
            bisect_init(m0, m8, bs_lo, bs_w, bs_hk, "d_", False)
            per = (len(nxt_units) + KB - 1) // KB
            for k_ in range(KB):
                act_iter(m1, k_)
                dve_iter(m0, k_)
                for u in nxt_units[k_ * per:(k_ + 1) * per]:
                    u()
            act_final(m1)
            dve_final(m0)
            finish(m0, m8[:, 7:8], "d_m8")
            finish(m1, a_thr[:], "a_thr")
        else:
            finish(m0, thr0[:], "thr0")
            finish(m1, thr0[:], "thr0")
            for u in nxt_units:
                u()
    if early("scores", [(MBT[:], [128, 512 * 40], BF16, [("MBT", m) for m in dbg.get("m_list", range(dbg.get("m_max", NB)))]),
                        (acc[0][:], [128, S], F32, []), (acc[1][:], [128, S], F32, []), (m8[:], [128, 8], F32, ["d_m8"])]):
        return nc, dbg_outs
    if "mbt" in dbg:
        t = dump("mbt", None, [128, 512 * 40], BF16)
        P.dma("sp", (lambda h, t=t: h.dma_start(out=t, in_=MBT[:])), reads=[("MBT", m) for m in range(NB)], writes=["dbg_mbt"])

    P.barrier()
    A.release(mA)

    wbq = [A.alloc([128, 8, 512], BF16) for _ in range(2)]
    wbc = A.alloc([128, 8, 128], BF16)
    wuv = A.alloc([128, 8, 64], BF16)
    qT = A.alloc([128, 8, S], BF16)
    c_tok = A.alloc([128, NB, 128], BF16)
    cT = A.alloc([128, S], BF16)
    t5f = A.alloc([128, 2, 8, 128], F32)
    t5b = A.alloc([128, 2, 8, 128], BF16)
    rb31 = A.alloc([128, 8], F32)
    gkv_bc = A.alloc([128, 128], F32)
    tmp_y = A.alloc([128, 512], F32)
    tmp_sq = A.alloc([128, 512], BF16)
    tmp_sd = A.alloc([128, 512], F32)
    ss1 = A.alloc([128, 2], F32)
    for i in range(2):
        load_w(wbq[i], w_in_v, ("wbq", i), (512 * i, 512 * i + 512))
    load_w(wbc, w_in_v, "wbc", (1024, 1152))
    P.dma("pool", lambda h: h.dma_start(out=wuv[:], in_=w_uv_d), writes=["wuv"])
    P.dma("sp", lambda h: h.dma_start(out=t5f[:], in_=t5_d), writes=["t5f"])
    P.dma("sp", lambda h: h.dma_start(out=rb31[:], in_=rb31_d), writes=["rb31"])
    P.dma("sp", lambda h: h.dma_start(out=gkv_bc[:], in_=gkv_bc_d), writes=["gkv_bc"])
    P.op("dve", lambda h: h.tensor_tensor(out=t5b[:], in0=t5f[:], in1=rb31[:].unsqueeze(1).unsqueeze(3).to_broadcast([128, 2, 8, 128]),
                                          op=ALU.subtract),
         reads=["t5f", "rb31"], writes=["t5b"])
    for hh in range(8):
        for tc in range(4):
            b = nb(0, 3)
            proj_fm(xg, wbq[hh // 4], ("wbq", hh // 4), (hh % 4) * 128, 128, tc, b)
            head_norm(b, 3 + (tc % 2), 27, qT[:, hh, tc * 512:(tc + 1) * 512], tmp_y[:], tmp_sq[:], tmp_sd[:],
                      rx_bc[:, tc * 512:(tc + 1) * 512], [("rx_bc", tc)], ("qT", hh, tc))
    for tt in range(NB):
        b = nb(0, 3)
        for kc in range(8):
            P.op("pe", (lambda h, b=b, tt=tt, kc=kc: h.matmul(ps[b][:, 0:128], lhsT=xg[:, kc, tt * 128:(tt + 1) * 128], rhs=wbc[:, kc, :],
                                                              start=(kc == 0), stop=(kc == 7))),
                 reads=[("wbc", kc), ("xg", kc)], writes=[("ps", b)], sig=(kc == 7), accum=(kc > 0))
        P.op("act", (lambda h, b=b, tt=tt: h.activation(out=tmp_y[:, 0:128], in_=ps[b][:, 0:128], func=AF.Copy, scale=rx_tok[:, tt:tt + 1])),
             reads=[("ps", b), "rx_tok"], writes=["hn_y"])
        P.op("act", (lambda h: h.activation(out=tmp_y[:, 128:256], in_=tmp_y[:, 0:128], func=AF.Square, accum_out=ss1[:, 0:1])),
             reads=["hn_y"], writes=["c_ss", "hn_y"])
        P.op("act", (lambda h: h.activation(out=ss1[:, 1:2], in_=ss1[:, 0:1], func=AF.Sqrt, scale=1.0 / 128, bias=EPS)),
             reads=["c_ss"], writes=["c_sd"])
        P.op("dve", (lambda h: h.reciprocal(out=ss1[:, 1:2], in_=ss1[:, 1:2])), reads=["c_sd"], writes=["c_sd"])
        P.op("dve", (lambda h, tt=tt: h.scalar_tensor_tensor(out=c_tok[:, tt, :], in0=tmp_y[:, 0:128], scalar=ss1[:, 1:2], in1=gkv_bc[:],
                                                             op0=ALU.mult, op1=ALU.mult)),
             reads=["hn_y", "c_sd", "gkv_bc"], writes=[("c_tok", tt)])
    for t0 in range(0, NB, 4):
        tb = (t0 // 4) % 2
        for jj in range(4):
            tt = t0 + jj
            P.op("pe", (lambda h, tt=tt, jj=jj, tb=tb: h.matmul(psTs[tb][:, jj * 128:(jj + 1) * 128], lhsT=c_tok[:, tt, :], rhs=ident_b, start=True, stop=True)),
                 reads=[("c_tok", tt), "cm_b"], writes=[("psT", tb)], sig=(jj == 3), accum=(jj > 0))
        P.op("act", (lambda h, t0=t0, tb=tb: h.copy(out=cT[:, t0 * 128:(t0 + 4) * 128], in_=psTs[tb][:, 0:512])),
             reads=[("psT", tb)], writes=[("cT", t0 // 4)])
    if "qT" in dbg:
        t = dump("qT", None, [128, 8, S], BF16)
        P.dma("sp", (lambda h, t=t: h.dma_start(out=t, in_=qT[:])), reads=[("qT", a, b_) for a in range(8) for b_ in range(4)], writes=["dbg_qT"])
        t2 = dump("cT", None, [128, S], BF16)
        P.dma("sp", (lambda h, t2=t2: h.dma_start(out=t2, in_=cT[:])), reads=[("cT", i) for i in range(4)], writes=["dbg_cT"])

    NPT = 6
    PT = [A.alloc([128, 512], BF16) for _ in range(NPT)]
    PE_ = [A.alloc([128, 512], BF16) for _ in range(NPT)]
    SB = [(ps[0], ("ps", 0)), (ps[1], ("ps", 1)), (ps[2], ("ps", 2)), (psTs[0], ("psT", 0)), (psTs[1], ("psT", 1))]
    rden = A.alloc([128, 512], F32)
    ocp = A.alloc([128, 512], F32)
    onT = [A.alloc([128, 512], BF16) for _ in range(2)]
    BU = 5
    accO = [ps[3][:], ps[3][:]]
    accD = [ps[4][:], ps[4][:]]
    accK = [(("ps", 3), ("ps", 4)), (("ps", 3), ("ps", 4))]
    its = []
    for c in range(4):
        for hh in range(8):
            nj = 4 * c + 4
            for j in range(nj):
                its.append((c, hh, j, nj))
    LAG = 4
    deferred = []

    def front(i):
        c, hh, j, nj = its[i]
        t_lo = max(512 * c, 128 * j)
        co = t_lo - 512 * c
        sbt, sbk = SB[i % 5]
        pb = i % NPT
        adds = []
        for diff in (0, 1):
            m = j + diff
            if 4 * c <= m <= 4 * c + 3:
                adds.append(((m - 4 * c) * 128, t5b[:, diff, hh, :], "t5b"))
        P.op("pe", (lambda h: h.matmul(sbt[:, co:512], lhsT=cT[:, j * 128:(j + 1) * 128], rhs=qT[:, hh, t_lo:512 * c + 512],
                                       start=True, stop=(len(adds) == 0))),
             reads=[("cT", j // 4), ("qT", hh, c)], writes=[sbk], sig=(len(adds) == 0))
        for ai, (col, rhs, key) in enumerate(adds):
            last = ai == len(adds) - 1
            P.op("pe", (lambda h, col=col, rhs=rhs, last=last: h.matmul(sbt[:, col:col + 128], lhsT=ident_b, rhs=rhs, start=False, stop=last)),
                 reads=[key, "cm_b"], writes=[sbk], sig=last, accum=True)
        P.op("act", (lambda h: h.activation(out=PE_[pb][:, 0:512 - co], in_=sbt[:, co:512], func=AF.Exp, bias=rb31[:, hh:hh + 1], scale=1.0)),
             reads=[sbk, "rb31"], writes=[("PE_", pb)])
        P.op("dve", (lambda h: h.tensor_tensor(out=PT[pb][:, 0:512 - co], in0=PE_[pb][:, 0:512 - co],
                                               in1=MBT[:, mk_off(c, j) + co: mk_off(c, j) + 512], op=ALU.mult)),
             reads=[("PE_", pb)] + [("MBT", m) for m in range(4 * c, 4 * c + 4)], writes=[("PT", pb)])

    def back(i):
        c, hh, j, nj = its[i]
        t_lo = max(512 * c, 128 * j)
        co = t_lo - 512 * c
        pb = i % NPT
        hidx = c * 8 + hh
        ab_ = hidx % 2
        aO, aD = accO[ab_], accD[ab_]
        kO, kD = accK[ab_]
        P.op("pe", (lambda h: h.matmul(aO[:, co:512], lhsT=c_tok[:, j, :], rhs=PT[pb][:, 0:512 - co], start=(j == 0), stop=(j == nj - 1))),
             reads=[("c_tok", j), ("PT", pb)], writes=[kO], sig=False, accum=(j > 0))
        P.op("pe", (lambda h: h.matmul(aD[:, co:512], lhsT=ones_b, rhs=PT[pb][:, 0:512 - co], start=(j == 0), stop=(j == nj - 1))),
             reads=["cm_b", ("PT", pb)], writes=[kD], sig=True, accum=(j > 0))
        if j == nj - 1:
            ob = hh % 2
            P.op("act", lambda h: h.activation(out=rden[:], in_=aD, func=AF.Ln), reads=[kD], writes=["rden"])
            P.op("dve", lambda h: h.tensor_copy(out=ocp[:], in_=aO), reads=[kO], writes=["ocp"])
            P.op("act", lambda h: h.activation(out=rden[:], in_=rden[:], func=AF.Exp, scale=-1.0), reads=["rden"], writes=["rden"])
            P.op("dve", (lambda h: h.tensor_tensor(out=onT[ob][:], in0=ocp[:], in1=rden[:], op=ALU.mult)),
                 reads=["ocp", "rden"], writes=[("onT", ob)])

            def epilogue():
                P.op("pe", (lambda h: h.matmul(ps[BU][64 * (hh % 2):64 * (hh % 2) + 64, :], lhsT=wuv[:, hh, :], rhs=onT[ob][:], start=True, stop=True)),
                     reads=["wuv", ("onT", ob)], writes=[("ps", BU, hh % 2)])
                if hh % 2 == 1:
                    P.op("act", (lambda h: h.copy(out=o_dsa[:, hh // 2, c * 512:(c + 1) * 512], in_=ps[BU][:])),
                         reads=[("ps", BU, 0), ("ps", BU, 1)], writes=[("o_dsa", hh // 2, c)])
            deferred.append([2, epilogue])

    for i in range(len(its) + LAG):
        if i < len(its):
            front(i)
        if i - LAG >= 0:
            back(i - LAG)
            for dct in list(deferred):
                dct[0] -= 1
                if dct[0] <= 0:
                    dct[1]()
                    deferred.remove(dct)
    for dct in deferred:
        dct[1]()
    if "o_dsa" in dbg:
        t = dump("o_dsa", None, [128, 4, S], BF16)
        P.dma("sp", (lambda h, t=t: h.dma_start(out=t, in_=o_dsa[:])), reads=[("o_dsa", a, c) for a in range(4) for c in range(4)],
              writes=["dbg_o_dsa"])

    P.barrier()
    A.release(base_mark)
    o_s5 = A.alloc([128, 4, S], BF16)
    o_x = A.alloc([128, 4, S], BF16)
    base_mark = A.mark()
    if dbg.get("od_early"):
        t = dump("od0", None, [128, 4, S], BF16)
        P.dma("sp", (lambda h, t=t: h.dma_start(out=t, in_=o_dsa[:])), writes=["dbg_od0"])
    if dbg.get("stop") == "dsa":
        P.barrier()
        P.emit()
        return nc, dbg_outs


    xgX = A.alloc([128, 8, S], BF16)
    build_xg(xgX, False)
    wbx = A.alloc([128, 8, 512], BF16)
    wbu = A.alloc([128, 8, 512], BF16)
    wkv = A.alloc([128, 8, 1024], BF16)
    memf = A.alloc([128, 8, 256], F32)
    msq = A.alloc([128, 8, 256], BF16)
    memn = A.alloc([128, 8, 256], BF16)
    rm = A.alloc([128, 256], F32)
    khT = A.alloc([128, 4, 256], BF16)
    vtok = A.alloc([128, 2, 512], BF16)
    qxT = A.alloc([128, 4, S], BF16)
    ust = [A.alloc([128, 512], BF16) for _ in range(2)]
    tyX = A.alloc([128, 512], F32)
    tqX = A.alloc([128, 512], BF16)
    tdX = A.alloc([128, 512], F32)
    PTx = [A.alloc([128, 512], BF16) for _ in range(3)]
    rdx = A.alloc([128, 512], F32)
    w_kv_v = w_kv_d.rearrange("(kc p) n -> p kc n", p=128)
    if not dbg.get("no_wbx"):
        load_w(wbx, w_in_v, "wbx", (2248, 2760))
    if not dbg.get("no_wbu"):
        load_w(wbu, w_in_v, "wbu", (1736, 2248))
    for i in range(2):
        if not dbg.get("no_wkv"):
            load_w(wkv[:, :, i * 512:(i + 1) * 512], w_kv_v, ("wkv", i), (i * 512, (i + 1) * 512))
    P.dma("sp", lambda h: h.dma_start(out=memf[:], in_=memT_d.rearrange("(kc p) m -> p kc m", p=128)), writes=["memf"])
    if dbg.get("stop") == "x0":
        P.barrier()
        t = dump("od", None, [128, 4, S], BF16)
        P.dma("sp", (lambda h, t=t: h.dma_start(out=t, in_=o_dsa[:])), writes=["dbg_od"])
        P.barrier()
        P.emit()
        return nc, dbg_outs
    P.op("act", lambda h: h.activation(out=msq[:], in_=memf[:], func=AF.Square), reads=["memf"], writes=["msq"])
    bm = nb(0, 3)
    for kc in range(8):
        P.op("pe", (lambda h, kc=kc: h.matmul(ps[bm][:, 0:256], lhsT=ones_b, rhs=msq[:, kc, :], start=(kc == 0), stop=(kc == 7))),
             reads=["msq", "cm_b"], writes=[("ps", bm)], sig=(kc == 7), accum=(kc > 0))
    P.op("act", lambda h: h.activation(out=rm[:], in_=ps[bm][:, 0:256], func=AF.Ln, scale=1.0 / D, bias=EPS), reads=[("ps", bm)], writes=["rm"])
    P.op("act", lambda h: h.activation(out=rm[:], in_=rm[:], func=AF.Exp, scale=-0.5), reads=["rm"], writes=["rm"])
    for kc in range(8):
        P.op("dve", (lambda h, kc=kc: h.scalar_tensor_tensor(out=memn[:, kc, :], in0=memf[:, kc, :], scalar=gvec[:, 8 + kc:9 + kc], in1=rm[:],
                                                             op0=ALU.mult, op1=ALU.mult)),
             reads=["memf", "rm", "gvec"], writes=[("memn", kc)])
    for hh in range(4):
        b = nb(0, 3)
        for kc in range(8):
            P.op("pe", (lambda h, b=b, hh=hh, kc=kc: h.matmul(ps[b][:, 0:256], lhsT=wkv[:, kc, hh * 128:(hh + 1) * 128], rhs=memn[:, kc, :],
                                                              start=(kc == 0), stop=(kc == 7))),
                 reads=[(("wkv", 0), kc), ("memn", kc)], writes=[("ps", b)], sig=(kc == 7), accum=(kc > 0))
        head_norm(b, 3 + (hh % 2), 28, khT[:, hh, :], tyX[:, 0:256], tqX[:, 0:256], tdX[:, 0:256], None, [], ("khT", hh))
    for mb in range(2):
        b = nb(0, 3)
        for kc in range(8):
            P.op("pe", (lambda h, b=b, mb=mb, kc=kc: h.matmul(ps[b][:], lhsT=memn[:, kc, mb * 128:(mb + 1) * 128], rhs=wkv[:, kc, 512:1024],
                                                              start=(kc == 0), stop=(kc == 7))),
                 reads=[(("wkv", 1), kc), ("memn", kc)], writes=[("ps", b)], sig=(kc == 7), accum=(kc > 0))
        P.op("act", (lambda h, b=b, mb=mb: h.copy(out=vtok[:, mb, :], in_=ps[b][:])), reads=[("ps", b)], writes=[("vtok", mb)])
    for hh in range(4):
        for tc in range(4):
            b = nb(0, 3)
            proj_fm(xgX, wbx, "wbx", hh * 128, 128, tc, b)
            head_norm(b, 3 + (tc % 2), 25, qxT[:, hh, tc * 512:(tc + 1) * 512], tyX[:], tqX[:], tdX[:],
                      rx_bc[:, tc * 512:(tc + 1) * 512], [("rx_bc", tc)], ("qxT", hh, tc))
    ui = 0
    for ch in range(4):
        for tcp in range(4):
            b = nb(0, 3)
            ub = ui % 2
            ui += 1
            proj_fm(xgX, wbu, "wbu", ch * 128, 128, tcp, b,
                    rhs_view=(lambda kc, tcp: xgX[:, kc, :].rearrange("p (b j) -> p j b", j=8)[:, 2 * tcp:2 * tcp + 2, :]))
            P.op("dve", (lambda h, b=b, ub=ub, tcp=tcp: h.tensor_tensor(
                out=ust[ub][:].rearrange("p (j b) -> p j b", j=2), in0=ps[b][:].rearrange("p (j b) -> p j b", j=2),
                in1=rx_bc[:].rearrange("p (b j) -> p j b", j=8)[:, 2 * tcp:2 * tcp + 2, :], op=ALU.mult)),
                reads=[("ps", b)] + RX, writes=[("ust", ub)])
            P.dma("sp", (lambda h, ub=ub, ch=ch, tcp=tcp: h.dma_start(out=scr_u[ch * 128:(ch + 1) * 128, tcp * 512:(tcp + 1) * 512], in_=ust[ub][:])),
                  reads=[("ust", ub)], writes=[("scr_u", ch, tcp)])
    ptiX = 0
    BO, BD = 3, 4
    for c in range(4):
        for hh in range(4):
            for mb in range(2):
                bs = nb(0, 3)
                pb = ptiX % 3
                ptiX += 1
                P.op("pe", (lambda h, bs=bs, hh=hh, mb=mb, c=c: h.matmul(ps[bs][:], lhsT=khT[:, hh, mb * 128:(mb + 1) * 128],
                                                                         rhs=qxT[:, hh, c * 512:(c + 1) * 512], start=True, stop=True)),
                     reads=[("khT", hh), ("qxT", hh, c)], writes=[("ps", bs)])
                P.op("act", (lambda h, bs=bs, pb=pb: h.activation(out=PTx[pb][:], in_=ps[bs][:], func=AF.Exp)),
                     reads=[("ps", bs)], writes=[("PT", pb)])
                P.op("pe", (lambda h, pb=pb, mb=mb, hh=hh: h.matmul(ps[BO][:], lhsT=vtok[:, mb, hh * 128:(hh + 1) * 128], rhs=PTx[pb][:],
                                                                    start=(mb == 0), stop=(mb == 1))),
                     reads=[("vtok", mb), ("PT", pb)], writes=[("ps", BO)], sig=False, accum=(mb > 0))
                P.op("pe", (lambda h, pb=pb, mb=mb: h.matmul(ps[BD][:], lhsT=ones_b, rhs=PTx[pb][:], start=(mb == 0), stop=(mb == 1))),
                     reads=["cm_b", ("PT", pb)], writes=[("ps", BD)], sig=True, accum=(mb > 0))
            P.op("act", lambda h: h.activation(out=rdx[:], in_=ps[BD][:], func=AF.Ln), reads=[("ps", BD)], writes=["rden"])
            P.op("act", lambda h: h.activation(out=rdx[:], in_=rdx[:], func=AF.Exp, scale=-1.0), reads=["rden"], writes=["rden"])
            P.op("dve", (lambda h, hh=hh, c=c: h.tensor_tensor(out=o_x[:, hh, c * 512:(c + 1) * 512], in0=ps[BO][:], in1=rdx[:], op=ALU.mult)),
                 reads=[("ps", BO), "rden"], writes=[("o_x", hh, c)])
    if dbg.get("stop") == "cross":
        t = dump("od", None, [128, 4, S], BF16)
        P.dma("sp", (lambda h, t=t: h.dma_start(out=t, in_=o_dsa[:])), writes=["dbg_od"])
        t = dump("o_x", None, [128, 4, S], BF16)
        P.dma("sp", (lambda h, t=t: h.dma_start(out=t, in_=o_x[:])), reads=[("o_x", a, c) for a in range(4) for c in range(4)], writes=["dbg_o_x"])
        t2 = dump("scr_u", None, [512, S], BF16)
        P.dma("sp", (lambda h, t2=t2: h.dma_start(out=t2, in_=scr_u)), reads=[("scr_u", a, c) for a in range(4) for c in range(4)], writes=["dbg_scr_u"])
        P.barrier()
        P.emit()
        return nc, dbg_outs
    P.barrier()
    A.release(base_mark)

    s5a = A.alloc([128, 3, 32], F32)
    s5b = A.alloc([128, 4, 32, 16], F32)
    dsk = A.alloc([128, 32], F32)
    TB = [A.alloc([128, 2, 8, 32], F32) for _ in range(4)]
    TK = A.alloc([128, 8, 2, 32], F32)
    cw = A.alloc([128, 16, 2, 32], F32)
    bb = A.alloc([128, 2, 32, 16], F32)
    W1 = A.alloc([128, 32, 128], BF16)
    W2 = A.alloc([128, 32, 128], BF16)
    Tm = A.alloc([128, 32, 128], BF16)
    U8 = A.alloc([128, 32, 256], BF16)
    X = A.alloc([128, 32, 256], BF16)
    mS = A.alloc([128, 1], F32)
    mS1 = A.mark()
    W1T = A.alloc([128, 32, 128], BF16)
    Lm = A.alloc([128, 32, 128], BF16)
    Rm = A.alloc([128, 32, 128], BF16)
    tA = A.alloc([128, 32, 128], F32)
    tB = A.alloc([128, 32, 128], F32)
    P.dma("sp", lambda h: h.dma_start(out=s5a[:], in_=s5a_d), writes=["s5a"])
    P.dma("sp", lambda h: h.dma_start(out=s5b[:], in_=s5b_d), writes=["s5b"])
    P.dma("sp", lambda h: h.dma_start(out=dsk[:], in_=dsk_d), writes=["dsk"])
    for jl in range(8):
        P.dma("sp", (lambda h, jl=jl: h.dma_start(out=U8[jl * 16:(jl + 1) * 16, :, :],
                                                  in_=scr_u.rearrange("(g c) (j b) -> j c g b", c=16, j=8)[jl])),
              reads=[("scr_u", a, c) for a in range(4) for c in range(4)], writes=[("U8", jl)])
    U8K = [("U8", jl) for jl in range(8)]

    def V(i):
        return cw[:, i, :, :]

    def tt(out, a, b_, op, rk, wk):
        P.op("dve", lambda h: h.tensor_tensor(out=out, in0=a, in1=b_, op=op), reads=rk, writes=wk)

    def cmul(dst, a, b_, ka, kb, kd):
        t1, t2 = V(14), V(15)
        tt(t1[:, 0, :], a[:, 0, :], b_[:, 0, :], ALU.mult, [ka, kb], ["cw_t1a"])
        tt(t1[:, 1, :], a[:, 1, :], b_[:, 1, :], ALU.mult, [ka, kb], ["cw_t1b"])
        tt(t2[:, 0, :], a[:, 0, :], b_[:, 1, :], ALU.mult, [ka, kb], ["cw_t2a"])
        tt(t2[:, 1, :], a[:, 1, :], b_[:, 0, :], ALU.mult, [ka, kb], ["cw_t2b"])
        tt(dst[:, 0, :], t1[:, 0, :], t1[:, 1, :], ALU.subtract, ["cw_t1a", "cw_t1b"], [kd])
        tt(dst[:, 1, :], t2[:, 0, :], t2[:, 1, :], ALU.add, ["cw_t2a", "cw_t2b", kd], [kd])

    a_re, a_im, ldt = s5a[:, 0, :], s5a[:, 1, :], s5a[:, 2, :]
    dtv = V(0)[:, 0, :]
    adr = V(0)[:, 1, :]
    adi = V(1)[:, 0, :]
    P.op("act", lambda h: h.activation(out=dtv, in_=ldt, func=AF.Exp), reads=["s5a"], writes=["dtv"])
    tt(adr, a_re, dtv, ALU.mult, ["s5a", "dtv"], ["adr"])
    tt(adi, a_im, dtv, ALU.mult, ["s5a", "dtv"], ["adi"])
    mag, magn, cs, sn = V(2)[:, 0, :], V(2)[:, 1, :], V(3)[:, 0, :], V(3)[:, 1, :]
    P.op("dve", lambda h: h.memset(mS[:], math.pi / 2), writes=["mS"])
    P.op("act", lambda h: h.activation(out=mag, in_=adr, func=AF.Exp, scale=1.0 / 16), reads=["adr"], writes=["mag"])
    P.op("act", lambda h: h.activation(out=magn, in_=adr, func=AF.Exp, scale=-1.0 / 16), reads=["adr"], writes=["magn"])
    P.op("act", lambda h: h.activation(out=cs, in_=adi, func=AF.Sin, scale=1.0 / 16, bias=mS[:]), reads=["adi", "mS"], writes=["cs"])
    P.op("act", lambda h: h.activation(out=sn, in_=adi, func=AF.Sin, scale=1.0 / 16), reads=["adi"], writes=["sn"])
    mu, nu = V(4), V(5)
    tt(mu[:, 0, :], mag, cs, ALU.mult, ["mag", "cs"], ["mu"])
    tt(mu[:, 1, :], mag, sn, ALU.mult, ["mag", "sn", "mu"], ["mu"])
    tt(nu[:, 0, :], magn, cs, ALU.mult, ["magn", "cs"], ["nu"])
    P.op("dve", lambda h: h.scalar_tensor_tensor(out=nu[:, 1, :], in0=magn, scalar=-1.0, in1=sn, op0=ALU.mult, op1=ALU.mult),
         reads=["magn", "sn", "nu"], writes=["nu"])
    def pw_slot(t, slot):
        return TB[t][:, :, slot, :]
    TW1, TW2, TL, TR = 0, 1, 2, 3
    cur, ck = mu, "mu"
    for i in range(4):
        dst = V(6 + (i % 2)) if i < 3 else pw_slot(TR, 1)
        kd = f"sqp{i}" if i < 3 else ("P", 1)
        cmul(dst, cur, cur, ck, ck, kd)
        cur, ck = dst, kd
    cur, ck = nu, "nu"
    for i in range(4):
        dst = V(8 + (i % 2)) if i < 3 else pw_slot(TL, 1)
        kd = f"sqn{i}" if i < 3 else ("N", 1)
        cmul(dst, cur, cur, ck, ck, kd)
        cur, ck = dst, kd
    Pp = {1: pw_slot(TR, 1)}
    Np = {1: pw_slot(TL, 1)}
    for t_ in (TR, TL, TW1):
        sl = 7 if t_ == TW1 else 0
        P.op("dve", (lambda h, t_=t_, sl=sl: h.memset(TB[t_][:, 0, sl, :], 1.0)), writes=[("one", t_, 0)])
        P.op("dve", (lambda h, t_=t_, sl=sl: h.memset(TB[t_][:, 1, sl, :], 0.0)), writes=[("one", t_, 1)])
    for k, (a, b_) in ((2, (1, 1)), (3, (2, 1)), (4, (2, 2)), (5, (4, 1)), (6, (4, 2)), (7, (4, 3))):
        Pp[k] = pw_slot(TR, k)
        cmul(Pp[k], Pp[a], Pp[b_], ("P", a), ("P", b_), ("P", k))
        Np[k] = pw_slot(TL, k)
        cmul(Np[k], Np[a], Np[b_], ("N", a), ("N", b_), ("N", k))
    Pp[8] = pw_slot(TW2, 7)
    cmul(Pp[8], Pp[4], Pp[4], ("P", 4), ("P", 4), ("P", 8))
    PK = [("P", k) for k in range(1, 9)]
    NK = [("N", k) for k in range(1, 8)]
    P.op("dve", lambda h: h.tensor_copy(out=TB[TW2][:, :, 0:7, :], in_=TB[TR][:, :, 1:8, :]), reads=PK, writes=["TW2"])
    for jl in range(7):
        P.op("dve", (lambda h, jl=jl: h.tensor_copy(out=TB[TW1][:, :, jl, :], in_=TB[TR][:, :, 7 - jl, :])), reads=PK, writes=[("TW1", jl)])
    TW1K = [("TW1", jl) for jl in range(7)] + [("one", TW1, 0), ("one", TW1, 1)]
    TRK = PK + [("one", TR, 0), ("one", TR, 1)]
    TLK = NK + [("one", TL, 0), ("one", TL, 1)]
    TW2K = ["TW2", ("P", 8)]
    P.op("dve", lambda h: h.tensor_copy(out=TK[:, 0, :, :], in_=Pp[8]), reads=[("P", 8)], writes=[("TK", 0)])
    for l in range(1, 8):
        cmul(TK[:, l, :, :], TK[:, l - 1, :, :], TK[:, l - 1, :, :], ("TK", l - 1), ("TK", l - 1), ("TK", l))
    P.op("dve", lambda h: h.tensor_scalar(out=TK[64:128, :, 1, :], in0=TK[64:128, :, 1, :], scalar1=-1.0, scalar2=None, op0=ALU.mult),
         reads=[("TK", l) for l in range(8)], writes=["TKs"])
    num, qv = V(10), V(11)
    den = V(12)[:, 0, :]
    P.op("dve", lambda h: h.tensor_scalar(out=num[:, 0, :], in0=Pp[1][:, 0, :], scalar1=-1.0, scalar2=None, op0=ALU.add),
         reads=[("P", 1)], writes=["num"])
    P.op("dve", lambda h: h.tensor_copy(out=num[:, 1, :], in_=Pp[1][:, 1, :]), reads=[("P", 1), "num"], writes=["num"])
    tt(den, a_re, a_re, ALU.mult, ["s5a"], ["den"])
    tt(V(12)[:, 1, :], a_im, a_im, ALU.mult, ["s5a"], ["den2"])
    tt(den, den, V(12)[:, 1, :], ALU.add, ["den", "den2"], ["den"])
    P.op("dve", lambda h: h.reciprocal(out=den, in_=den), reads=["den"], writes=["den"])
    t13 = V(13)
    tt(t13[:, 0, :], num[:, 0, :], a_re, ALU.mult, ["num", "s5a"], ["t13a"])
    tt(t13[:, 1, :], num[:, 1, :], a_im, ALU.mult, ["num", "s5a"], ["t13b"])
    tt(qv[:, 0, :], t13[:, 0, :], t13[:, 1, :], ALU.add, ["t13a", "t13b"], ["qv0"])
    tt(qv[:, 0, :], qv[:, 0, :], den, ALU.mult, ["qv0", "den"], ["qv0"])
    tt(t13[:, 0, :], num[:, 1, :], a_re, ALU.mult, ["num", "s5a", "qv0"], ["t13a"])
    tt(t13[:, 1, :], num[:, 0, :], a_im, ALU.mult, ["num", "s5a", "qv0"], ["t13b"])
    tt(qv[:, 1, :], t13[:, 0, :], t13[:, 1, :], ALU.subtract, ["t13a", "t13b"], ["qv1"])
    tt(qv[:, 1, :], qv[:, 1, :], den, ALU.mult, ["qv1", "den"], ["qv1"])
    q_re = qv[:, 0, :].unsqueeze(2).to_broadcast([128, 32, 16])
    q_im = qv[:, 1, :].unsqueeze(2).to_broadcast([128, 32, 16])
    B_re, B_im, C_re, C_im = s5b[:, 0], s5b[:, 1], s5b[:, 2], s5b[:, 3]
    tAv = tA[:].rearrange("p g (j c) -> p g j c", c=16)
    tBv = tB[:].rearrange("p g (j c) -> p g j c", c=16)
    tt(tAv[:, :, 0, :], q_re, B_re, ALU.mult, ["qv0", "s5b"], ["tA"])
    tt(tBv[:, :, 0, :], q_im, B_im, ALU.mult, ["qv1", "s5b"], ["tB"])
    tt(bb[:, 0], tAv[:, :, 0, :], tBv[:, :, 0, :], ALU.subtract, ["tA", "tB"], ["bb0"])
    tt(tAv[:, :, 0, :], q_re, B_im, ALU.mult, ["qv0", "s5b", "bb0"], ["tA"])
    tt(tBv[:, :, 0, :], q_im, B_re, ALU.mult, ["qv1", "s5b", "bb0"], ["tB"])
    tt(bb[:, 1], tAv[:, :, 0, :], tBv[:, :, 0, :], ALU.add, ["tA", "tB"], ["bb1"])

    def build_mat(dst, tbl, tkeys, v_re, v_im, vkeys, mode, dkey):
        dv = dst[:].rearrange("p g (j c) -> p g j c", c=16)
        for half in range(2):
            pr = slice(64 * half, 64 * half + 64)
            Tre = TB[tbl][pr, 0, :, :].rearrange("p s g -> p g s").unsqueeze(3).to_broadcast([64, 32, 8, 16])
            Tim = TB[tbl][pr, 1, :, :].rearrange("p s g -> p g s").unsqueeze(3).to_broadcast([64, 32, 8, 16])
            va, vb = (v_re, v_im) if half == 0 else (v_im, v_re)
            Va = va[pr].unsqueeze(2).to_broadcast([64, 32, 8, 16])
            Vb = vb[pr].unsqueeze(2).to_broadcast([64, 32, 8, 16])
            tt(tAv[pr], Tre, Va, ALU.mult, tkeys + vkeys + [dkey], [("tA", half)])
            tt(tBv[pr], Tim, Vb, ALU.mult, tkeys + vkeys + [dkey], [("tB", half)])
            if half == 0:
                tt(dv[pr], tAv[pr], tBv[pr], ALU.subtract, [("tA", 0), ("tB", 0)], [(dkey, 0)])
            elif mode == "B":
                tt(dv[pr], tAv[pr], tBv[pr], ALU.add, [("tA", 1), ("tB", 1)], [(dkey, 1)])
            else:
                P.op("dve", (lambda h, pr=pr: h.scalar_tensor_tensor(out=dv[pr], in0=tAv[pr], scalar=-1.0, in1=tBv[pr],
                                                                      op0=ALU.mult, op1=ALU.subtract)),
                     reads=[("tA", 1), ("tB", 1)], writes=[(dkey, 1)])

    P.op("dve", lambda h: h.memset(mS[:], 0.0), reads=["tA", "tB", "mS"], writes=[("tA", 0), ("tA", 1), ("tB", 0), ("tB", 1), "mS"])
    build_mat(W1T, TW1, TW1K, bb[:, 0], bb[:, 1], ["bb0", "bb1"], "B", "W1T")
    build_mat(W2, TW2, TW2K, C_re, C_im, ["s5b"], "C", "W2")
    build_mat(Lm, TL, TLK, bb[:, 0], bb[:, 1], ["bb0", "bb1"], "B", "Lm")
    build_mat(Rm, TR, TRK, C_re, C_im, ["s5b"], "C", "Rm")
    for g0 in range(0, 32, 4):
        tb = (g0 // 4) % 2
        for jj in range(4):
            P.op("pe", (lambda h, g0=g0, jj=jj, tb=tb: h.matmul(psTs[tb][:, jj * 128:(jj + 1) * 128], lhsT=W1T[:, g0 + jj, :], rhs=ident_b, start=True, stop=True)),
                 reads=[("W1T", 0), ("W1T", 1), "cm_b"], writes=[("psT", tb)], sig=(jj == 3), accum=(jj > 0))
        P.op("act", (lambda h, g0=g0, tb=tb: h.copy(out=W1[:, g0:g0 + 4, :], in_=psTs[tb][:, 0:512].rearrange("p (a b) -> p a b", b=128))),
             reads=[("psT", tb)], writes=[("W1", g0 // 4)])
    for g0 in range(0, 32, 4):
        b = nb(0, 3)
        for jj in range(4):
            P.op("pe", (lambda h, g0=g0, jj=jj, b=b: h.matmul(ps[b][:, jj * 128:(jj + 1) * 128], lhsT=Lm[:, g0 + jj, :], rhs=Rm[:, g0 + jj, :],
                                                              start=True, stop=True)),
                 reads=[("Lm", 0), ("Lm", 1), ("Rm", 0), ("Rm", 1)], writes=[("ps", b)], sig=(jj == 3), accum=(jj > 0))
        P.op("dve", (lambda h, b=b, g0=g0: h.tensor_tensor(out=tA[:, g0:g0 + 4, :], in0=ps[b][:].rearrange("p (a b) -> p a b", b=128),
                                                           in1=tmask_f.unsqueeze(1).to_broadcast([128, 4, 128]), op=ALU.mult)),
             reads=[("ps", b), "cm_f", ("tA", 0), ("tA", 1)], writes=[("tAm", g0 // 4)])
        for jj in range(4):
            g = g0 + jj
            P.op("dve", (lambda h, g=g: h.scalar_tensor_tensor(out=Tm[:, g, :], in0=ident_f, scalar=dsk[:, g:g + 1], in1=tA[:, g, :],
                                                               op0=ALU.mult, op1=ALU.add)),
                 reads=[("tAm", g0 // 4), "dsk", "cm_f"], writes=[("Tm", g0 // 4)])
    if dbg.get("stop") == "s5pre":
        for nm, tns, keys in (("W1", W1, [("W1", i) for i in range(8)]), ("W2", W2, [("W2", 0), ("W2", 1)]), ("Tm", Tm, [("Tm", i) for i in range(8)])):
            t = dump(nm, None, [128, 32, 128], BF16)
            P.dma("sp", (lambda h, t=t, tns=tns: h.dma_start(out=t, in_=tns[:])), reads=keys, writes=["dbg_" + nm])
        t = dump("TK", None, [128, 8, 2, 32])
        P.dma("sp", (lambda h, t=t: h.dma_start(out=t, in_=TK[:])), reads=["TKs"], writes=["dbg_TK"])
        t = dump("TB", None, [128, 2, 8, 32])
        P.dma("sp", (lambda h, t=t: h.dma_start(out=t, in_=TB[TR][:])), reads=TRK, writes=["dbg_TB"])
        P.barrier()
        P.emit()
        return nc, dbg_outs
    P.barrier()
    A.release(mS1)

    Yst = A.alloc([128, 32, 256], BF16)
    gq = A.alloc([128, 512], F32)
    gz = A.alloc([128, 512], F32)
    gs = A.alloc([128, 512], F32)
    mS2 = A.mark()
    Rl = [A.alloc([128, 32, 128], BF16) for _ in range(2)]
    rt1 = A.alloc([128, 32, 128], BF16)
    rt2 = A.alloc([128, 32, 128], BF16)
    XK = lambda gp: ("X", gp)
    for gp in range(16):
        b = nb(0, 3)
        for gi in range(2):
            g = 2 * gp + gi
            P.op("pe", (lambda h, b=b, gi=gi, g=g: h.matmul(ps[b][:, gi * 256:(gi + 1) * 256], lhsT=W1[:, g, :], rhs=U8[:, g, :], start=True, stop=True)),
                 reads=[("W1", g // 4)] + U8K, writes=[("ps", b)], sig=(gi == 1), accum=(gi > 0))
        P.op("act", (lambda h, b=b, gp=gp: h.copy(out=X[:, 2 * gp:2 * gp + 2, :], in_=ps[b][:].rearrange("p (a b) -> p a b", b=256))),
             reads=[("ps", b)], writes=[XK(gp)])
    for l in range(8):
        d = 1 << l
        R = Rl[l % 2]
        P.op("dve", (lambda h, l=l: h.tensor_tensor(out=rt1[:], in0=ident_f.unsqueeze(1).to_broadcast([128, 32, 128]),
                                                    in1=TK[:, l, 0, :].unsqueeze(2).to_broadcast([128, 32, 128]), op=ALU.mult)),
             reads=["cm_f", "TKs"], writes=["rt1"])
        P.op("dve", (lambda h, l=l: h.tensor_tensor(out=rt2[:], in0=swap_f.unsqueeze(1).to_broadcast([128, 32, 128]),
                                                    in1=TK[:, l, 1, :].unsqueeze(2).to_broadcast([128, 32, 128]), op=ALU.mult)),
             reads=["cm_f", "TKs"], writes=["rt2"])
        P.op("dve", (lambda h, R=R: h.tensor_tensor(out=R[:], in0=rt1[:], in1=rt2[:], op=ALU.add)),
             reads=["rt1", "rt2"], writes=[("R", l % 2)])
        for gp in range(16):
            b = nb(0, 3)
            for gi in range(2):
                g = 2 * gp + gi
                P.op("pe", (lambda h, b=b, gi=gi, g=g, d=d, R=R: h.matmul(ps[b][:, gi * 256 + d:(gi + 1) * 256], lhsT=R[:, g, :], rhs=X[:, g, 0:256 - d],
                                                                          start=True, stop=True)),
                     reads=[("R", l % 2), XK(gp)], writes=[("ps", b)], sig=(gi == 1), accum=(gi > 0))
            P.op("dve", (lambda h, b=b, gp=gp, d=d: h.tensor_tensor(out=X[:, 2 * gp:2 * gp + 2, d:256], in0=X[:, 2 * gp:2 * gp + 2, d:256],
                                                                   in1=ps[b][:].rearrange("p (a b) -> p a b", b=256)[:, :, d:256], op=ALU.add)),
                 reads=[("ps", b), XK(gp)], writes=[XK(gp)])
    for gp in range(16):
        b = nb(0, 3)
        for gi in range(2):
            g = 2 * gp + gi
            P.op("pe", (lambda h, b=b, gi=gi, g=g: h.matmul(ps[b][:, gi * 256:(gi + 1) * 256], lhsT=Tm[:, g, :], rhs=U8[:, g, :], start=True, stop=False)),
                 reads=[("Tm", g // 4)] + U8K, writes=[("ps", b)], sig=False, accum=(gi > 0))
            P.op("pe", (lambda h, b=b, gi=gi, g=g: h.matmul(ps[b][:, gi * 256 + 1:(gi + 1) * 256], lhsT=W2[:, g, :], rhs=X[:, g, 0:255], start=False, stop=True)),
                 reads=[("W2", 0), ("W2", 1), XK(gp)], writes=[("ps", b)], sig=(gi == 1), accum=True)
        P.op("act", (lambda h, b=b: h.activation(out=gq[:], in_=ps[b][:], func=AF.Square)), reads=[("ps", b)], writes=["gq"])
        P.op("dve", lambda h: h.tensor_scalar(out=gz[:], in0=gq[:], scalar1=0.044715, scalar2=1.0, op0=ALU.mult, op1=ALU.add),
             reads=["gq"], writes=["gz"])
        P.op("dve", (lambda h, b=b: h.tensor_tensor(out=gz[:], in0=gz[:], in1=ps[b][:], op=ALU.mult)), reads=["gz", ("ps", b)], writes=["gz"])
        P.op("act", lambda h: h.activation(out=gs[:], in_=gz[:], func=AF.Sigmoid, scale=2.0 * math.sqrt(2.0 / math.pi)), reads=["gz"], writes=["gs"])
        P.op("dve", (lambda h, b=b, gp=gp: h.tensor_tensor(out=Yst[:, 2 * gp:2 * gp + 2, :], in0=gs[:].rearrange("p (a b) -> p a b", b=256),
                                                          in1=ps[b][:].rearrange("p (a b) -> p a b", b=256), op=ALU.mult)),
             reads=["gs", ("ps", b)], writes=[("Yst", gp)])
    for il in range(8):
        P.dma("sp", (lambda h, il=il: h.dma_start(out=scr_y.rearrange("(g c) (i b) -> i c g b", c=16, i=8)[il], in_=Yst[il * 16:(il + 1) * 16, :, :])),
              reads=[("Yst", gp) for gp in range(16)], writes=[("scr_y", il)])
    P.barrier()
    A.release(mS2)
    YT = A.alloc([128, 4, S], BF16)
    wglu = A.alloc([128, 4, 512], BF16)
    P.dma("pool", lambda h: h.dma_start(out=wglu[:], in_=w_glu_d.rearrange("(kc p) n -> p kc n", p=128)), writes=["wglu"])
    P.dma("sp", lambda h: h.dma_start(out=YT[:], in_=scr_y.rearrange("(ch p) t -> p ch t", p=128)), writes=["YT"])
    for chp in range(4):
        for tcp in range(4):
            b = nb(0, 3)
            for k in range(4):
                P.op("pe", (lambda h, b=b, k=k, chp=chp, tcp=tcp: h.matmul(ps[b][:], lhsT=wglu[:, k, chp * 128:(chp + 1) * 128],
                                                                           rhs=YT[:, k, tcp * 512:(tcp + 1) * 512], start=(k == 0), stop=(k == 3))),
                     reads=["wglu", "YT"], writes=[("ps", b)], sig=(k == 3), accum=(k > 0))
            P.op("act", (lambda h, b=b: h.activation(out=gs[:], in_=ps[b][:], func=AF.Sigmoid)), reads=[("ps", b)], writes=["gs"])
            P.op("dve", (lambda h, chp=chp, tcp=tcp: h.tensor_tensor(out=o_s5[:, chp, tcp * 512:(tcp + 1) * 512], in0=gs[:],
                                                                    in1=YT[:, chp, tcp * 512:(tcp + 1) * 512], op=ALU.mult)),
                 reads=["gs", "YT"], writes=[("o_s5", chp, tcp)])
    if dbg.get("stop") == "s5":
        t = dump("od", None, [128, 4, S], BF16)
        P.dma("sp", (lambda h, t=t: h.dma_start(out=t, in_=o_dsa[:])), writes=["dbg_od"])
        t = dump("o_s5", None, [128, 4, S], BF16)
        P.dma("sp", (lambda h, t=t: h.dma_start(out=t, in_=o_s5[:])), reads=[("o_s5", a, c) for a in range(4) for c in range(4)], writes=["dbg_o_s5"])
        t = dump("YT", None, [128, 4, S], BF16)
        P.dma("sp", (lambda h, t=t: h.dma_start(out=t, in_=YT[:])), reads=["YT"], writes=["dbg_YT"])
        P.barrier()
        P.emit()
        return nc, dbg_outs
    P.barrier()
    A.release(base_mark)

    off_merged = A.mark()
    merged = A.alloc([128, 8, S], BF16)
    mM1 = A.mark()
    xgM = A.alloc([128, 8, S], BF16)
    build_xg(xgM, False)
    wg = [A.alloc([128, 8, 384], BF16) for _ in range(2)]
    wbr = [A.alloc([128, 4, 384], BF16) for _ in range(2)]
    gt = A.alloc([128, 512], F32)
    sg = A.alloc([128, 512], F32)
    macc = A.alloc([128, 512], F32)
    mtmp = A.alloc([128, 512], F32)
    w_br_v = [w.rearrange("(kc p) n -> p kc n", p=128) for w in (w_brd_d, w_brs_d, w_brx_d)]
    obr = [o_dsa, o_s5, o_x]
    for fc in range(8):
        wb_ = fc % 2
        for br in range(3):
            c0 = 2760 + br * 1024 + fc * 128
            P.dma("pool", (lambda h, wb_=wb_, br=br, c0=c0: h.dma_start(out=wg[wb_][:, :, br * 128:(br + 1) * 128], in_=w_in_v[:, :, c0:c0 + 128])),
                  writes=[("wg", wb_, br)])
            P.dma("pool", (lambda h, wb_=wb_, br=br, fc=fc: h.dma_start(out=wbr[wb_][:, :, br * 128:(br + 1) * 128],
                                                                       in_=w_br_v[br][:, :, fc * 128:(fc + 1) * 128])),
                  writes=[("wbr", wb_, br)])
        for tc in range(4):
            for br in range(3):
                bG = nb(0, 6)
                for kc in range(8):
                    P.op("pe", (lambda h, bG=bG, kc=kc, wb_=wb_, br=br, tc=tc: h.matmul(ps[bG][:], lhsT=wg[wb_][:, kc, br * 128:(br + 1) * 128],
                                                                                        rhs=xgM[:, kc, tc * 512:(tc + 1) * 512], start=(kc == 0), stop=(kc == 7))),
                         reads=[("wg", wb_, br), ("xg", kc)], writes=[("ps", bG)], sig=(kc == 7), accum=(kc > 0))
                P.op("dve", (lambda h, bG=bG, tc=tc: h.tensor_tensor(out=gt[:], in0=ps[bG][:], in1=rx_bc[:, tc * 512:(tc + 1) * 512], op=ALU.mult)),
                     reads=[("ps", bG)], writes=["gt"])
                P.op("act", lambda h: h.activation(out=sg[:], in_=gt[:], func=AF.Sigmoid), reads=["gt"], writes=["sg"])
                bB = nb(0, 6)
                for k in range(4):
                    if br == 1:
                        rhs = o_s5[:, k, :].rearrange("p (j b) -> p b j", j=8)[:, 64 * tc:64 * tc + 64, :]
                    else:
                        rhs = obr[br][:, k, tc * 512:(tc + 1) * 512]
                    P.op("pe", (lambda h, bB=bB, k=k, wb_=wb_, br=br, rhs=rhs: h.matmul(ps[bB][:], lhsT=wbr[wb_][:, k, br * 128:(br + 1) * 128], rhs=rhs,
                                                                                        start=(k == 0), stop=(k == 3))),
                         reads=[("wbr", wb_, br)], writes=[("ps", bB)], sig=(k == 3), accum=(k > 0))
                if br == 0:
                    P.op("dve", (lambda h, bB=bB: h.tensor_tensor(out=macc[:], in0=ps[bB][:], in1=sg[:], op=ALU.mult)),
                         reads=[("ps", bB), "sg"], writes=["macc"])
                else:
                    P.op("dve", (lambda h, bB=bB: h.tensor_tensor(out=mtmp[:], in0=ps[bB][:], in1=sg[:], op=ALU.mult)),
                         reads=[("ps", bB), "sg"], writes=["mtmp"])
                    if br == 1:
                        P.op("dve", lambda h: h.tensor_tensor(out=macc[:], in0=macc[:], in1=mtmp[:], op=ALU.add), reads=["macc", "mtmp"], writes=["macc"])
                    else:
                        P.op("dve", (lambda h, fc=fc, tc=tc: h.tensor_tensor(out=merged[:, fc, tc * 512:(tc + 1) * 512], in0=macc[:], in1=mtmp[:], op=ALU.add)),
                             reads=["macc", "mtmp"], writes=[("merged", fc, tc)])
                if dbg.get("stop") == "merge1" and br == dbg.get("br", 0):
                    for nm, tns, shp, dt_, keys in (("wg", wg[0][:], [128, 8, 384], BF16, [("wg", 0, i) for i in range(3)]), ("gt", gt[:], [128, 512], F32, ["gt"]),
                                                    ("sg", sg[:], [128, 512], F32, ["sg"]), ("macc", macc[:], [128, 512], F32, ["macc"]),
                                                    ("mtmp", mtmp[:], [128, 512], F32, ["mtmp"]),
                                                    ("xg", xgM[:], [128, 8, S], BF16, XG), ("od", o_dsa[:], [128, 4, S], BF16, []), ("os", o_s5[:], [128, 4, S], BF16, []), ("ox", o_x[:], [128, 4, S], BF16, []), ("wbr", wbr[0][:], [128, 4, 384], BF16, [("wbr", 0, i) for i in range(3)])):
                        t = dump(nm, None, shp, dt_)
                        P.dma("sp", (lambda h, t=t, tns=tns: h.dma_start(out=t, in_=tns)), reads=keys, writes=["dbg_" + nm])
                    P.barrier()
                    P.emit()
                    return nc, dbg_outs
    if dbg.get("stop") == "merge":
        t = dump("merged", None, [128, 8, S], BF16)
        P.dma("sp", (lambda h, t=t: h.dma_start(out=t, in_=merged[:])), reads=[("merged", a, c) for a in range(8) for c in range(4)], writes=["dbg_merged"])
        P.barrier()
        P.emit()
        return nc, dbg_outs
    P.barrier()
    A.release(mM1)
    off_x1 = A.mark()
    x1T = A.alloc([128, 8, S], F32)
    mM0 = A.mark()
    wout = A.alloc([128, 8, 1024], BF16)
    xres = [A.alloc([128, 512], F32) for _ in range(2)]
    w_out_v = w_out_d.rearrange("(kc p) n -> p kc n", p=128)
    for i in range(2):
        load_w(wout[:, :, i * 512:(i + 1) * 512], w_out_v, ("wout", i), (i * 512, (i + 1) * 512))
    xi = 0
    for fc in range(8):
        for tc in range(4):
            xb = xi % 2
            xi += 1
            P.dma("sp", (lambda h, xb=xb, fc=fc, tc=tc: h.dma_start(out=xres[xb][:], in_=xT_d[fc * 128:(fc + 1) * 128, tc * 512:(tc + 1) * 512])),
                  writes=[("xres", xb)])
            b = nb(0, 6)
            for kc in range(8):
                P.op("pe", (lambda h, b=b, kc=kc, fc=fc, tc=tc: h.matmul(ps[b][:], lhsT=wout[:, kc, fc * 128:(fc + 1) * 128],
                                                                         rhs=merged[:, kc, tc * 512:(tc + 1) * 512], start=(kc == 0), stop=(kc == 7))),
                     reads=[(("wout", fc // 4), kc), ("merged", kc, tc)], writes=[("ps", b)], sig=(kc == 7), accum=(kc > 0))
            P.op("dve", (lambda h, b=b, xb=xb, fc=fc, tc=tc: h.tensor_tensor(out=x1T[:, fc, tc * 512:(tc + 1) * 512], in0=ps[b][:], in1=xres[xb][:], op=ALU.add)),
                 reads=[("ps", b), ("xres", xb)], writes=[("x1T", fc, tc)])
    if dbg.get("stop") == "x1":
        t = dump("x1T", None, [128, 8, S])
        P.dma("sp", (lambda h, t=t: h.dma_start(out=t, in_=x1T[:])), reads=[("x1T", a, c) for a in range(8) for c in range(4)], writes=["dbg_x1T"])
        P.barrier()
        P.emit()
        return nc, dbg_outs
    P.barrier()
    A.release(mM0)
    top = A.mark()
    A.off = off_o
    fsq = [A.alloc([128, 512], BF16) for _ in range(2)]
    rf = A.alloc([128, 512], F32)
    wgu = [[A.alloc([128, 8, 256], BF16) for _ in range(2)] for _ in range(2)]
    wdn = [A.alloc([128, NFF, 256], BF16) for _ in range(2)]
    fs = A.alloc([128, 512], F32)
    assert A.off <= off_merged
    A.off = off_merged
    hfT = A.alloc([128, 8, 512], BF16)
    hT = A.alloc([128, NFF, 512], BF16)
    assert A.off <= off_x1
    A.off = top
    fo = [A.alloc([128, 512], F32) for _ in range(2)]
    w_gu_v = [w.rearrange("(kc p) n -> p kc n", p=128) for w in (w_g_d, w_u_d)]
    w_d_v = w_d_d.rearrange("(fk p) n -> p fk n", p=128)
    wi = 0
    di = 0
    oi = 0
    for tq in range(4):
        tsl = slice(tq * 512, (tq + 1) * 512)
        bS = nb(0, 6)
        for kc in range(8):
            sb_ = kc % 2
            P.op("act", (lambda h, kc=kc, sb_=sb_, tsl=tsl: h.activation(out=fsq[sb_][:], in_=x1T[:, kc, tsl], func=AF.Square)),
                 reads=[("x1T", kc, tq)], writes=[("fsq", sb_)])
            P.op("pe", (lambda h, kc=kc, sb_=sb_, bS=bS: h.matmul(ps[bS][:], lhsT=ones_b, rhs=fsq[sb_][:], start=(kc == 0), stop=(kc == 7))),
                 reads=[("fsq", sb_), "cm_b"], writes=[("ps", bS)], sig=True, accum=(kc > 0))
        P.op("act", (lambda h, bS=bS: h.activation(out=rf[:], in_=ps[bS][:], func=AF.Ln, scale=1.0 / D, bias=EPS)), reads=[("ps", bS)], writes=["rf"])
        P.op("act", lambda h: h.activation(out=rf[:], in_=rf[:], func=AF.Exp, scale=-0.5), reads=["rf"], writes=["rf"])
        for kc in range(8):
            P.op("dve", (lambda h, kc=kc, tsl=tsl: h.scalar_tensor_tensor(out=hfT[:, kc, :], in0=x1T[:, kc, tsl], scalar=gvec[:, 16 + kc:17 + kc], in1=rf[:],
                                                                          op0=ALU.mult, op1=ALU.mult)),
                 reads=[("x1T", kc, tq), "rf", "gvec"], writes=[("hfT", kc)])
        for f0 in range(0, NFF, 2):
            wb_ = wi % 2
            wi += 1
            for gu in range(2):
                P.dma("pool", (lambda h, gu=gu, wb_=wb_, f0=f0: h.dma_start(out=wgu[gu][wb_][:], in_=w_gu_v[gu][:, :, f0 * 128:(f0 + 2) * 128])),
                      writes=[("wgu", gu, wb_)])
            for ff in range(f0, f0 + 2):
                bg, bu = nb(0, 6), nb(0, 6)
                for gu, bb_ in ((0, bg), (1, bu)):
                    for kc in range(8):
                        P.op("pe", (lambda h, gu=gu, bb_=bb_, kc=kc, wb_=wb_, ff=ff, f0=f0: h.matmul(
                            ps[bb_][:], lhsT=wgu[gu][wb_][:, kc, (ff - f0) * 128:(ff - f0 + 1) * 128], rhs=hfT[:, kc, :], start=(kc == 0), stop=(kc == 7))),
                            reads=[("wgu", gu, wb_), ("hfT", kc)], writes=[("ps", bb_)], sig=(kc == 7), accum=(kc > 0))
                P.op("act", (lambda h, bg=bg: h.activation(out=fs[:], in_=ps[bg][:], func=AF.Silu)), reads=[("ps", bg)], writes=["fs"])
                P.op("dve", (lambda h, bu=bu, ff=ff: h.tensor_tensor(out=hT[:, ff, :], in0=ps[bu][:], in1=fs[:], op=ALU.mult)),
                     reads=[("ps", bu), "fs"], writes=[("hT", ff)])
        for c0 in range(0, 8, 2):
            db = di % 2
            di += 1
            P.dma("pool", (lambda h, db=db, c0=c0: h.dma_start(out=wdn[db][:], in_=w_d_v[:, :, c0 * 128:(c0 + 2) * 128])), writes=[("wdn", db)])
            for fc in range(c0, c0 + 2):
                b = nb(0, 6)
                for ff in range(NFF):
                    P.op("pe", (lambda h, b=b, ff=ff, db=db, fc=fc, c0=c0: h.matmul(ps[b][:], lhsT=wdn[db][:, ff, (fc - c0) * 128:(fc - c0 + 1) * 128], rhs=hT[:, ff, :],
                                                                                    start=(ff == 0), stop=(ff == NFF - 1))),
                         reads=[("wdn", db), ("hT", ff)], writes=[("ps", b)], sig=(ff == NFF - 1), accum=(ff > 0))
                ob = oi % 2
                oi += 1
                P.op("dve", (lambda h, b=b, ob=ob, fc=fc, tsl=tsl: h.tensor_tensor(out=fo[ob][:], in0=ps[b][:], in1=x1T[:, fc, tsl], op=ALU.add)),
                     reads=[("ps", b), ("x1T", fc, tq)], writes=[("fo", ob)])
                P.dma("sp", (lambda h, ob=ob, fc=fc, tsl=tsl: h.dma_start(out=outT_d[fc * 128:(fc + 1) * 128, tsl], in_=fo[ob][:])),
                      reads=[("fo", ob)], writes=[("out", fc, tq)])
    P.barrier()
    P.emit()
    return nc, dbg_outs


def _host_inputs(inputs):
    f = lambda a: np.ascontiguousarray(np.asarray(a, dtype=np.float32))
    x = f(inputs["x"])
    mem = f(inputs["mem"])
    rel_bias = f(inputs["rel_bias"])
    shared = {}
    shared["w_in"] = f(inputs["w_in"][0])
    shared["w_uv"] = f(np.transpose(inputs["w_uv_dsa"][0], (1, 0, 2)))
    shared["w_glu"] = f(inputs["w_glu"][0])
    shared["w_mem_kv"] = f(inputs["w_mem_kv"][0])
    shared["w_br_dsa"] = f(inputs["w_br_dsa"][0])
    shared["w_br_s5"] = f(inputs["w_br_s5"][0])
    shared["w_br_cross"] = f(inputs["w_br_cross"][0])
    shared["w_out"] = f(inputs["w_out"][0])
    shared["w_ffn_gate"] = f(inputs["w_ffn_gate"][0])
    shared["w_ffn_up"] = f(inputs["w_ffn_up"][0])
    shared["w_ffn_down"] = f(inputs["w_ffn_down"][0])
    gvec = np.zeros((128, 32), np.float32)
    gvec[:, 0:8] = np.asarray(inputs["g_mix_norm"][0]).reshape(8, 128).T
    gvec[:, 8:16] = np.asarray(inputs["g_mem_norm"][0]).reshape(8, 128).T
    gvec[:, 16:24] = np.asarray(inputs["g_ffn_norm"][0]).reshape(8, 128).T
    gvec[:, 24] = np.asarray(inputs["g_q_dsa"][0])
    gvec[:, 25] = np.asarray(inputs["g_q_cross"][0])
    gvec[:, 26] = np.asarray(inputs["g_k_cross"][0])
    shared["gvec"] = gvec
    shared["gkv_bc"] = f(np.broadcast_to(np.asarray(inputs["g_kv_dsa"][0])[None, :], (128, 128)))
    bt = _t5_bucket_table()
    s_i = np.arange(128)[:, None]
    t_i = np.arange(128)[None, :]
    t5 = np.zeros((128, 2, 8, 128), np.float32)
    for diff in (0, 1):
        dist = np.maximum(t_i - s_i + 128 * diff, 0)
        t5[:, diff, :, :] = np.transpose(rel_bias[bt[dist]], (0, 2, 1))
    shared["t5"] = t5
    shared["rb31"] = f(np.broadcast_to(rel_bias[31][None, :], (128, 8)))
    cm = np.zeros((128, 5, 128), np.float32)
    cm[:, 0, :] = np.eye(128)
    cm[:, 1, :] = np.where(t_i.T >= s_i.T, 0.0, NEG)
    il = np.arange(128)[None, :] // 16
    jl = np.arange(128)[:, None] // 16
    cm[:, 2, :] = (il >= jl).astype(np.float32)
    sw = np.zeros((128, 128), np.float32)
    sw[np.arange(64), np.arange(64) + 64] = 1.0
    sw[np.arange(64) + 64, np.arange(64)] = 1.0
    cm[:, 3, :] = sw
    cm[:, 4, :] = 1.0
    shared["cmats"] = cm
    tile2 = lambda a: np.concatenate([a, a], axis=0)
    s5a = np.zeros((128, 3, 32), np.float32)
    s5a[:, 0, :] = tile2(np.asarray(inputs["a_re"][0]).T)
    s5a[:, 1, :] = tile2(np.asarray(inputs["a_im"][0]).T)
    s5a[:, 2, :] = np.broadcast_to(np.asarray(inputs["log_dt"][0])[None, :], (128, 32))
    shared["s5a"] = s5a
    s5b = np.zeros((128, 4, 32, 16), np.float32)
    s5b[:, 0] = tile2(np.transpose(inputs["b_re"][0], (1, 0, 2)))
    s5b[:, 1] = tile2(np.transpose(inputs["b_im"][0], (1, 0, 2)))
    s5b[:, 2] = tile2(np.transpose(inputs["c_re"][0], (2, 0, 1)))
    s5b[:, 3] = tile2(np.transpose(inputs["c_im"][0], (2, 0, 1)))
    shared["s5b"] = s5b
    shared["dsk"] = f(np.tile(np.asarray(inputs["d_skip"][0]).T, (8, 1)))
    in_maps = []
    for b in range(8):
        d = dict(shared)
        d["xT"] = f(x[b].T)
        d["memT"] = f(mem[b].T)
        in_maps.append(d)
    return in_maps


_CACHE = {}


def kernel(**inputs):
    in_maps = _host_inputs(inputs)
    if "nc" not in _CACHE:
        _CACHE["nc"] = build()[0]
    res = run_bass_kernel_spmd(_CACHE["nc"], in_maps, core_ids=list(range(8)))
    out = np.stack([np.ascontiguousarray(res.results[b]["outT"].T) for b in range(8)], axis=0)
    return out.astype(np.float32)
```

```python
import math
import numpy as np
import concourse.bass as bass
import concourse.mybir as mybir
from concourse.bass_utils import run_bass_kernel_spmd

F32 = mybir.dt.float32
BF16 = mybir.dt.bfloat16
AF = mybir.ActivationFunctionType
ALU = mybir.AluOpType

S = 2048
D = 1024
NB = 16
EPS = 1e-6
D_IN = 5832
D_FF = 2816
NFF = 22
NEG = -1.0e30
MBIAS = -30000.0


class Prog:
    ENGS = ("pe", "act", "dve", "pool", "sp")

    def __init__(self, nc, n_dma_sems=8):
        self.nc = nc
        self.streams = {e: [] for e in self.ENGS}
        self.sems = {e: nc.alloc_semaphore("s_" + e) for e in self.ENGS}
        self.cnt = {e: 0 for e in self.ENGS}
        self.known = {e: {} for e in self.ENGS}
        self.bufs = {}
        self.dma_sems = {}
        self.dma_rr = {}
        self.dma_val = {}
        self.semobj = {}
        for q in ("sp", "pool", "act"):
            self.dma_sems[q] = [nc.alloc_semaphore(f"d_{q}{i}") for i in range(n_dma_sems)]
            self.dma_rr[q] = 0
            for s in self.dma_sems[q]:
                self.semobj[s.name] = s
                self.dma_val[s.name] = 0
        for e in self.ENGS:
            self.semobj[self.sems[e].name] = self.sems[e]
        self.pending_pe = False

    def _need(self, eng, tok, waits):
        if tok is None:
            return
        sname, val = tok
        if self.known[eng].get(sname, 0) >= val:
            return
        if waits.get(sname, 0) < val:
            waits[sname] = val

    def _deps(self, eng, reads, writes, skip_writer=False):
        waits = {}
        for k in reads:
            st = self.bufs.get(k)
            if st is not None:
                self._need(eng, st[0], waits)
        for k in writes:
            st = self.bufs.get(k)
            if st is not None:
                if not skip_writer:
                    self._need(eng, st[0], waits)
                for tok in st[1].values():
                    self._need(eng, tok, waits)
        return waits

    def _commit(self, who, tok, reads, writes):
        for k in reads:
            st = self.bufs.setdefault(k, [None, {}])
            st[1][who + ":" + tok[0]] = tok
        for k in writes:
            self.bufs[k] = [tok, {}]

    def op(self, eng, fn, reads=(), writes=(), sig=True, accum=False):
        waits = self._deps(eng, reads, writes, skip_writer=(accum and eng == "pe"))
        if eng == "pe":
            waits.pop(self.sems["pe"].name, None)
        for s, v in waits.items():
            self.known[eng][s] = v
        if sig:
            self.cnt[eng] += 1
            tok = (self.sems[eng].name, self.cnt[eng])
            if eng == "pe":
                self.pending_pe = False
        else:
            assert eng == "pe"
            tok = (self.sems[eng].name, self.cnt[eng] + 1)
            self.pending_pe = True
        self.streams[eng].append((list(waits.items()), fn, (self.sems[eng], 1) if sig else None))
        self._commit(eng, tok, reads, writes)
        return tok

    def dma(self, q, fn, reads=(), writes=()):
        waits = self._deps(q, reads, writes)
        pool = self.dma_sems[q]
        s = pool[self.dma_rr[q] % len(pool)]
        self.dma_rr[q] += 1
        prev = self.dma_val[s.name]
        if prev > 0:
            self._need(q, (s.name, prev), waits)
        for sn, v in waits.items():
            self.known[q][sn] = v
        self.dma_val[s.name] = prev + 16
        tok = (s.name, prev + 16)
        self.streams[q].append((list(waits.items()), fn, (s, 16)))
        self._commit(q, tok, reads, writes)
        return tok

    def barrier(self):
        assert not self.pending_pe
        targets = [(self.sems[e].name, self.cnt[e]) for e in self.ENGS if self.cnt[e] > 0]
        targets += [(sn, v) for sn, v in self.dma_val.items() if v > 0]
        for e in self.ENGS:
            waits = {}
            for tok in targets:
                if e == "pe" and tok[0] == self.sems["pe"].name:
                    continue
                self._need(e, tok, waits)
            for sn, v in waits.items():
                self.known[e][sn] = v
            if waits:
                self.streams[e].append((list(waits.items()), None, None))
        self.bufs = {}

    def emit(self):
        assert not self.pending_pe
        nc = self.nc
        hmap = {"pe": "tensor", "act": "scalar", "dve": "vector", "pool": "gpsimd", "sp": "sync"}
        semobj = self.semobj
        with nc.Block() as block:
            for e in self.ENGS:
                def body(h, stream=self.streams[e]):
                    for waits, fn, inc in stream:
                        for sn, v in waits:
                            h.wait_ge(semobj[sn], v)
                        if fn is not None:
                            ins = fn(h)
                            if inc is not None:
                                ins.then_inc(inc[0], inc[1])
                getattr(block, hmap[e])(body)


class Arena:
    def __init__(self, nc, start=16640, limit=229376):
        self.nc = nc
        self.off = start
        self.limit = limit
        self.n = 0
        self.peak = 0

    def alloc(self, shape, dtype):
        nbytes = int(np.prod(shape[1:])) * (2 if dtype == BF16 else 4)
        nbytes = (nbytes + 63) // 64 * 64
        assert self.off + nbytes <= self.limit, ("SBUF overflow", self.off, nbytes)
        self.n += 1
        t = self.nc.alloc_sbuf_tensor_at(f"sb{self.n}", list(shape), dtype, offset=self.off)
        self.off += nbytes
        self.peak = max(self.peak, self.off)
        return t

    def mark(self):
        return self.off

    def release(self, m):
        if not getattr(self, "no_release", False):
            self.off = m


def _t5_bucket_table():
    d = np.arange(256)
    max_exact = 16
    nf = np.maximum(d, 1).astype(np.float32)
    large = max_exact + (np.log(nf / max_exact) / math.log(128 / max_exact) * (32 - max_exact)).astype(np.int32)
    large = np.minimum(large, 31)
    return np.where(d < max_exact, d, large)


def build(dbg=None):
    dbg = dbg or {}
    nc = bass.Bass("TRN2", target_bir_lowering=False)
    P = Prog(nc)
    A = Arena(nc)
    A.no_release = bool(dbg.get("no_release"))
    dbg_outs = []

    def din(name, shape):
        return nc.dram_tensor(name, list(shape), F32, kind="ExternalInput").ap()

    xT_d = din("xT", [D, S])
    memT_d = din("memT", [D, 256])
    w_in_d = din("w_in", [D, D_IN])
    w_uv_d = din("w_uv", [128, 8, 64])
    w_glu_d = din("w_glu", [512, 512])
    w_kv_d = din("w_mem_kv", [D, D])
    w_brd_d = din("w_br_dsa", [512, D])
    w_brs_d = din("w_br_s5", [512, D])
    w_brx_d = din("w_br_cross", [512, D])
    w_out_d = din("w_out", [D, D])
    w_g_d = din("w_ffn_gate", [D, D_FF])
    w_u_d = din("w_ffn_up", [D, D_FF])
    w_d_d = din("w_ffn_down", [D_FF, D])
    gvec_d = din("gvec", [128, 32])
    gkv_bc_d = din("gkv_bc", [128, 128])
    t5_d = din("t5", [128, 2, 8, 128])
    rb31_d = din("rb31", [128, 8])
    cm_d = din("cmats", [128, 5, 128])
    s5a_d = din("s5a", [128, 3, 32])
    s5b_d = din("s5b", [128, 4, 32, 16])
    dsk_d = din("dsk", [128, 32])
    outT_d = nc.dram_tensor("outT", [D, S], F32, kind="ExternalOutput").ap()
    scr_u = nc.dram_tensor("scr_u", [512, S], BF16).ap()
    scr_y = nc.dram_tensor("scr_y", [512, S], BF16).ap()

    ps = [nc.alloc_psum_tensor(f"ps{i}", [128, 512], F32) for i in range(6)]
    psTs = [nc.alloc_psum_tensor(f"psT{i}", [128, 512], F32) for i in range(2)]

    w_in_v = w_in_d.rearrange("(kc p) n -> p kc n", p=128)

    def dump(name, ap, shape, dt=F32):
        t = nc.dram_tensor("dbg_" + name, list(shape), dt, kind="ExternalOutput").ap()
        dbg_outs.append(("dbg_" + name))
        return t

    cm_f = A.alloc([128, 5, 128], F32)
    ident_f = cm_f[:, 0, :]
    causal_f = cm_f[:, 1, :]
    tmask_f = cm_f[:, 2, :]
    swap_f = cm_f[:, 3, :]
    cm_b = A.alloc([128, 5, 128], BF16)
    ident_b = cm_b[:, 0, :]
    ones_b = cm_b[:, 4, :]
    gvec = A.alloc([128, 32], F32)
    rx_bc = A.alloc([128, S], F32)
    rx_tok = A.alloc([128, NB], F32)
    off_o = A.mark()
    o_dsa = A.alloc([128, 4, S], BF16)

    P.dma("sp", lambda h: h.dma_start(out=cm_f[:], in_=cm_d), writes=["cm_f"])
    P.dma("pool", lambda h: h.dma_start(out=cm_b[:], in_=cm_d), writes=["cm_b"])
    P.dma("sp", lambda h: h.dma_start(out=gvec[:], in_=gvec_d), writes=["gvec"])
    CONST = ["cm_f", "cm_b", "gvec"]

    bank_rr = [0]

    def nb(lo=0, hi=6):
        b = lo + bank_rr[0] % (hi - lo)
        bank_rr[0] += 1
        return b

    def load_w(dst, src3, key, cols, q="pool"):
        kc_n = dst.shape[1]
        c0, c1 = cols
        for kc in range(kc_n):
            P.dma(q, (lambda h, kc=kc: h.dma_start(out=dst[:, kc, 0:c1 - c0], in_=src3[:, kc, c0:c1])),
                  writes=[(key, kc)])

    def build_xg(xg, with_stats, release=False):
        m = A.mark()
        xst = [A.alloc([128, S], F32) for _ in range(2)]
        sq = [A.alloc([128, S], BF16) for _ in range(2)]
        for kc in range(8):
            b = kc % 2
            P.dma("sp", (lambda h, kc=kc, b=b: h.dma_start(out=xst[b][:], in_=xT_d[kc * 128:(kc + 1) * 128, :])),
                  writes=[("xst", b)])
            if with_stats:
                P.op("act", (lambda h, b=b: h.activation(out=sq[b][:], in_=xst[b][:], func=AF.Square)),
                     reads=[("xst", b)], writes=[("sq", b)])
                for tc in range(4):
                    P.op("pe", (lambda h, b=b, tc=tc, kc=kc: h.matmul(ps[tc][:], lhsT=ones_b, rhs=sq[b][:, tc * 512:(tc + 1) * 512],
                                                                       start=(kc == 0), stop=(kc == 7))),
                         reads=[("sq", b), "cm_b"], writes=[("ps", tc)], sig=(tc == 3), accum=(kc > 0))
            P.op("dve", (lambda h, kc=kc, b=b: h.tensor_scalar(out=xg[:, kc, :], in0=xst[b][:], scalar1=gvec[:, kc:kc + 1],
                                                                scalar2=None, op0=ALU.mult)),
                 reads=[("xst", b), "gvec"], writes=[("xg", kc)])
        if with_stats:
            for tc in range(4):
                P.op("act", (lambda h, tc=tc: h.activation(out=rx_bc[:, tc * 512:(tc + 1) * 512], in_=ps[tc][:], func=AF.Ln,
                                                           scale=1.0 / D, bias=EPS)),
                     reads=[("ps", tc)], writes=[("rxs", tc)])
                P.op("act", (lambda h, tc=tc: h.activation(out=rx_bc[:, tc * 512:(tc + 1) * 512], in_=rx_bc[:, tc * 512:(tc + 1) * 512],
                                                           func=AF.Exp, scale=-0.5)),
                     reads=[("rxs", tc)], writes=[("rx_bc", tc)])
            for tt in range(NB):
                P.op("pe", (lambda h, tt=tt: h.matmul(ps[4][:, tt:tt + 1], lhsT=rx_bc[:, tt * 128:(tt + 1) * 128], rhs=ident_f[:, 0:1],
                                                      start=True, stop=True)),
                     reads=[("rx_bc", tt // 4), "cm_f"], writes=[("ps", 4)], sig=(tt == NB - 1))
            P.op("act", lambda h: h.copy(out=rx_tok[:], in_=ps[4][:, 0:NB]), reads=[("ps", 4)], writes=["rx_tok"])
        if release:
            A.release(m)

    XG = [("xg", kc) for kc in range(8)]
    RX = [("rx_bc", tc) for tc in range(4)]

    def proj_fm(xg, wb, wkey, c0, ncol, tc, bank, rhs_view=None):
        for kc in range(8):
            rhs = xg[:, kc, tc * 512:(tc + 1) * 512] if rhs_view is None else rhs_view(kc, tc)
            P.op("pe", (lambda h, kc=kc, rhs=rhs: h.matmul(ps[bank][0:ncol, :], lhsT=wb[:, kc, c0:c0 + ncol], rhs=rhs,
                                                            start=(kc == 0), stop=(kc == 7))),
                 reads=[(wkey, kc), ("xg", kc)], writes=[("ps", bank)], sig=(kc == 7), accum=(kc > 0))

    hn_slot = [0]
    hn_queue = []

    def head_norm(bank, gcol, dst, tmps, rx_ap, extra_reads, dst_key):
        n = dst.shape[-1]
        sl = hn_slot[0] % 3
        hn_slot[0] += 1
        bank2 = 3 + sl
        tmp_y, tmp_sq, tmp_sd = tmps[0][sl][:, 0:n], tmps[1][sl][:, 0:n], tmps[2][sl][:, 0:n]
        ky, kq, kd = (("hn_y", 0), sl), ("hn_sq", sl), ("hn_sd", sl)
        if rx_ap is None:
            P.op("act", lambda h: h.copy(out=tmp_y, in_=ps[bank][:, 0:n]), reads=[("ps", bank)] + extra_reads, writes=[ky])
        else:
            P.op("dve", lambda h: h.tensor_tensor(out=tmp_y, in0=ps[bank][:, 0:n], in1=rx_ap, op=ALU.mult),
                 reads=[("ps", bank)] + extra_reads, writes=[ky])
        P.op("act", lambda h: h.activation(out=tmp_sq, in_=tmp_y, func=AF.Square), reads=[ky], writes=[kq])

        def part2():
            P.op("pe", lambda h: h.matmul(ps[bank2][:, 0:n], lhsT=ones_b, rhs=tmp_sq, start=True, stop=True),
                 reads=[kq, "cm_b"], writes=[("ps", bank2)])
            P.op("act", lambda h: h.activation(out=tmp_sd, in_=ps[bank2][:, 0:n], func=AF.Ln, scale=1.0 / 128, bias=EPS),
                 reads=[("ps", bank2)], writes=[kd])
            P.op("act", lambda h: h.activation(out=tmp_sd, in_=tmp_sd, func=AF.Exp, scale=-0.5), reads=[kd], writes=[kd])
            P.op("dve", lambda h: h.scalar_tensor_tensor(out=dst, in0=tmp_y, scalar=gvec[:, gcol:gcol + 1], in1=tmp_sd,
                                                         op0=ALU.mult, op1=ALU.mult),
                 reads=[ky, kd, "gvec"], writes=[dst_key])
        hn_queue.append(part2)
        if len(hn_queue) > 2:
            hn_queue.pop(0)()

    def hn_flush():
        while hn_queue:
            hn_queue.pop(0)()

    base_mark = A.mark()

    xg = A.alloc([128, 8, S], BF16)
    build_xg(xg, True, release=True)
    P.op("dve", lambda h: h.tensor_scalar(out=gvec[:, 27:28], in0=gvec[:, 24:25], scalar1=128 ** -0.5, scalar2=None, op0=ALU.mult),
         reads=["gvec"], writes=["gvec"])
    P.op("dve", lambda h: h.tensor_scalar(out=gvec[:, 28:29], in0=gvec[:, 26:27], scalar1=128 ** -0.5, scalar2=None, op0=ALU.mult),
         reads=["gvec"], writes=["gvec"])

    def early(tag, tensors):
        if dbg.get("stop") != tag:
            return False
        for i, (t, shape, dt, keys) in enumerate(tensors):
            d = dump(f"{tag}{i}", None, shape, dt)
            P.dma("sp", (lambda h, d=d, t=t: h.dma_start(out=d, in_=t)), reads=keys, writes=[f"dbg_{tag}{i}"])
        P.barrier()
        P.emit()
        return True

    if early("xg", [(xg[:], [128, 8, S], BF16, XG), (rx_bc[:], [128, S], F32, RX), (rx_tok[:], [128, NB], F32, ["rx_tok"])]):
        return nc, dbg_outs

    def mb_base(m):
        return 128 * (m * (m + 1) // 2)

    def mk_off(c, j):
        return 512 * (2 * c * c + 2 * c + j)

    MBT = A.alloc([128, 512 * 40], BF16)
    mA = A.mark()

    wbi = A.alloc([128, 8, 584], BF16)
    wki = A.alloc([128, 8, 128], BF16)
    qiT = A.alloc([128, 4, S], BF16)
    kiT = A.alloc([128, S], BF16)
    widx = A.alloc([128, NB, 8], F32)
    load_w(wbi, w_in_v, "wbi", (1152, 1736))
    for kc in range(8):
        P.dma("pool", (lambda h, kc=kc: h.dma_start(out=wki[:, kc, 0:64], in_=w_in_v[:, kc, 1664:1728])), writes=[("wki", kc)])
        P.dma("pool", (lambda h, kc=kc: h.dma_start(out=wki[:, kc, 64:128], in_=w_in_v[:, kc, 1664:1728])), writes=[("wki", kc)])
    for j in range(4):
        for tc in range(4):
            b = nb()
            proj_fm(xg, wbi, "wbi", j * 128, 128, tc, b)
            P.op("dve", (lambda h, b=b, j=j, tc=tc: h.tensor_tensor(out=qiT[:, j, tc * 512:(tc + 1) * 512], in0=ps[b][:],
                                                                      in1=rx_bc[:, tc * 512:(tc + 1) * 512], op=ALU.mult)),
                 reads=[("ps", b), ("rx_bc", tc)], writes=[("qiT", j, tc)])
    for tc in range(4):
        b = nb()
        proj_fm(xg, wki, "wki", 0, 128, tc, b)
        P.op("dve", (lambda h, b=b, tc=tc: h.tensor_tensor(out=kiT[:, tc * 512:(tc + 1) * 512], in0=ps[b][:],
                                                            in1=rx_bc[:, tc * 512:(tc + 1) * 512], op=ALU.mult)),
             reads=[("ps", b), ("rx_bc", tc)], writes=[("kiT", tc)])
    bw = nb()
    for tt in range(NB):
        for kc in range(8):
            P.op("pe", (lambda h, tt=tt, kc=kc: h.matmul(ps[bw][:, tt * 8:(tt + 1) * 8], lhsT=xg[:, kc, tt * 128:(tt + 1) * 128],
                                                         rhs=wbi[:, kc, 576:584], start=(kc == 0), stop=(kc == 7))),
                 reads=[("wbi", kc), ("xg", kc)], writes=[("ps", bw)], sig=(kc == 7 and tt == NB - 1), accum=not (kc == 0 and tt == 0))
    P.op("dve", lambda h: h.tensor_tensor(out=widx[:], in0=ps[bw][:, 0:NB * 8].rearrange("p (a b) -> p a b", b=8),
                                          in1=rx_tok[:].unsqueeze(2).to_broadcast([128, NB, 8]), op=ALU.mult),
         reads=[("ps", bw), "rx_tok"], writes=["widx"])

    if early("proj", [(qiT[:], [128, 4, S], BF16, [("qiT", j, tc) for j in range(4) for tc in range(4)]),
                      (kiT[:], [128, S], BF16, [("kiT", tc) for tc in range(4)]), (widx[:], [128, NB, 8], F32, ["widx"])]):
        return nc, dbg_outs
    acc = [A.alloc([128, S], F32) for _ in range(4)]
    work = A.alloc([128, S], F32)
    rl = [A.alloc([128, 512], F32) for _ in range(3)]
    mbt = [A.alloc([128, S], BF16) for _ in range(2)]
    m8 = A.alloc([128, 8], F32)
    thr0 = A.alloc([128, 1], F32)
    bs_lo = A.alloc([128, 1], F32)
    bs_w = A.alloc([128, 1], F32)
    bs_mid = A.alloc([128, 1], F32)
    bs_cnt = A.alloc([128, 1], F32)
    bs_t = A.alloc([128, 1], F32)
    bs_ck = A.alloc([128, 24], F32)
    bs_hk = A.alloc([128, 24], F32)
    for k_ in range(24):
        P.op("dve", (lambda h, k_=k_: h.memset(bs_ck[:, k_:k_ + 1], 2.0 ** (-(k_ + 1)))), writes=["bs_ck"])
    P.op("dve", lambda h: h.memset(thr0[:], -1.0e29), writes=["thr0"])
    a_lo = A.alloc([128, 1], F32)
    a_w = A.alloc([128, 1], F32)
    a_hk = A.alloc([128, 24], F32)
    a_nh2 = A.alloc([128, 24], F32)
    a_nm = A.alloc([128, 2], F32)
    a_cnt = A.alloc([128, 1], F32)
    a_t = A.alloc([128, 1], F32)
    a_m8 = A.alloc([128, 8], F32)
    a_thr = A.alloc([128, 1], F32)
    workA = A.alloc([128, S], BF16)
    KB = 20
    rl_i = [0]

    def score_units(m):
        ab = m % 4
        n = 128 * (m + 1)
        nsc = (n + 511) // 512
        units = []
        for sc in range(nsc):
            w = min(512, n - 512 * sc)
            for hh in range(8):
                def unit(sc=sc, w=w, hh=hh):
                    j, half = hh // 2, hh % 2
                    b = nb()
                    r = rl_i[0] % 3
                    rl_i[0] += 1
                    P.op("pe", (lambda h: h.matmul(ps[b][:, 0:w], lhsT=qiT[64 * half:64 * half + 64, j, m * 128:(m + 1) * 128],
                                                   rhs=kiT[64 * half:64 * half + 64, sc * 512:sc * 512 + w], start=True, stop=True)),
                         reads=[("qiT", j, m // 4), ("kiT", sc)], writes=[("ps", b)])
                    P.op("act", (lambda h: h.activation(out=rl[r][:, 0:w], in_=ps[b][:, 0:w], func=AF.Relu)),
                         reads=[("ps", b)], writes=[("rl", r)])
                    if hh == 0:
                        P.op("dve", (lambda h: h.tensor_scalar(out=acc[ab][:, sc * 512:sc * 512 + w], in0=rl[r][:, 0:w], scalar1=widx[:, m, 0:1],
                                                               scalar2=None, op0=ALU.mult)),
                             reads=[("rl", r), "widx"], writes=[("acc", ab)])
                    else:
                        P.op("dve", (lambda h: h.scalar_tensor_tensor(out=acc[ab][:, sc * 512:sc * 512 + w], in0=rl[r][:, 0:w], scalar=widx[:, m, hh:hh + 1],
                                                                      in1=acc[ab][:, sc * 512:sc * 512 + w], op0=ALU.mult, op1=ALU.add)),
                             reads=[("rl", r), "widx", ("acc", ab)], writes=[("acc", ab)])
                    if sc == nsc - 1 and hh == 7:
                        P.op("dve", (lambda h: h.tensor_tensor(out=acc[ab][:, m * 128:(m + 1) * 128], in0=acc[ab][:, m * 128:(m + 1) * 128],
                                                               in1=causal_f, op=ALU.add)),
                             reads=[("acc", ab), "cm_f"], writes=[("acc", ab)])
                units.append(unit)
        return units

    def bisect_init(m, mx, lo, w_, hk, keyp, on_act):
        ab = m % 4
        n = 128 * (m + 1)
        nv = m * 128
        P.op("dve", (lambda h: h.max(out=mx[:], in_=acc[ab][:, 0:n])), reads=[("acc", ab)], writes=[keyp + "m8"])
        P.op("dve", (lambda h: h.tensor_reduce(out=lo[:], in_=acc[ab][:, 0:nv], axis=mybir.AxisListType.X, op=ALU.min)),
             reads=[("acc", ab)], writes=[keyp + "lo"])
        P.op("dve", lambda h: h.tensor_tensor(out=w_[:], in0=mx[:, 0:1], in1=lo[:], op=ALU.subtract), reads=[keyp + "m8", keyp + "lo"], writes=[keyp + "w"])
        P.op("dve", lambda h: h.tensor_scalar(out=hk[:], in0=bs_ck[:], scalar1=w_[:], scalar2=None, op0=ALU.mult),
             reads=[keyp + "w", "bs_ck"], writes=[keyp + "hk"])
        if on_act:
            P.op("dve", lambda h: h.tensor_scalar(out=a_nh2[:], in0=hk[:], scalar1=-0.5, scalar2=None, op0=ALU.mult), reads=[keyp + "hk"], writes=["a_nh2"])
            P.op("dve", lambda h: h.scalar_tensor_tensor(out=a_nm[:, 0:1], in0=lo[:], scalar=-1.0, in1=hk[:, 0:1], op0=ALU.mult, op1=ALU.subtract),
                 reads=[keyp + "lo", keyp + "hk"], writes=[("a_nm", 0)])
        else:
            P.op("dve", lambda h: h.tensor_tensor(out=bs_mid[:], in0=lo[:], in1=hk[:, 0:1], op=ALU.add), reads=[keyp + "lo", keyp + "hk"], writes=["bs_mid"])

    def dve_iter(m, k_):
        ab = m % 4
        n = 128 * (m + 1)
        P.op("dve", (lambda h: h.tensor_scalar(out=work[:, 0:n], in0=acc[ab][:, 0:n], scalar1=bs_mid[:], scalar2=None,
                                               op0=ALU.is_ge, op1=ALU.add, accum_out=bs_cnt[:])),
             reads=[("acc", ab), "bs_mid"], writes=["work", "bs_cnt"])
        P.op("dve", lambda h: h.tensor_scalar(out=bs_t[:], in0=bs_cnt[:], scalar1=255.5, scalar2=-0.5, op0=ALU.is_ge, op1=ALU.add),
             reads=["bs_cnt"], writes=["bs_t"])
        P.op("dve", (lambda h: h.scalar_tensor_tensor(out=bs_mid[:], in0=bs_t[:], scalar=bs_hk[:, k_:k_ + 1], in1=bs_mid[:],
                                                      op0=ALU.mult, op1=ALU.add)),
             reads=["bs_t", "d_hk", "bs_mid"], writes=["bs_mid"])

    def dve_final(m):
        P.op("dve", (lambda h: h.tensor_tensor(out=m8[:, 7:8], in0=bs_mid[:], in1=bs_hk[:, KB:KB + 1], op=ALU.subtract)),
             reads=["bs_mid", "d_hk", "d_m8"], writes=["d_m8"])

    def act_iter(m, k_):
        ab = m % 4
        n = 128 * (m + 1)
        cur, nxt = k_ % 2, (k_ + 1) % 2
        P.op("act", (lambda h: h.activation(out=workA[:, 0:n], in_=acc[ab][:, 0:n], func=AF.Sign, bias=a_nm[:, cur:cur + 1], scale=1.0,
                                            accum_out=a_cnt[:])),
             reads=[("acc", ab), ("a_nm", cur)], writes=["workA", "a_cnt"])
        P.op("act", (lambda h: h.activation(out=a_t[:], in_=a_cnt[:], func=AF.Sign, bias=float(n) - 511.5, scale=1.0)),
             reads=["a_cnt"], writes=["a_t"])
        P.op("act", (lambda h: h.activation(out=a_nm[:, nxt:nxt + 1], in_=a_t[:], func=AF.Identity,
                                            scale=a_nh2[:, k_:k_ + 1], bias=a_nm[:, cur:cur + 1])),
             reads=["a_t", "a_nh2", ("a_nm", cur)], writes=[("a_nm", nxt)])

    def act_final(m):
        fin = KB % 2
        P.op("dve", (lambda h: h.scalar_tensor_tensor(out=a_thr[:], in0=a_nm[:, fin:fin + 1], scalar=-1.0, in1=a_hk[:, KB:KB + 1],
                                                      op0=ALU.mult, op1=ALU.subtract)),
             reads=[("a_nm", fin), "a_hk"], writes=["a_thr"])

    def finish(m, thr, thrk):
        ab = m % 2
        a4 = m % 4
        n = 128 * (m + 1)
        P.op("dve", (lambda h: h.tensor_scalar(out=mbt[ab][:, 0:n], in0=acc[a4][:, 0:n], scalar1=thr, scalar2=None, op0=ALU.is_ge)),
             reads=[("acc", a4), thrk], writes=[("mbt", ab)])
        for j0 in range(0, m + 1, 4):
            jn = min(4, m + 1 - j0)
            tb = (j0 // 4) % 2
            for jj in range(jn):
                j = j0 + jj
                P.op("pe", (lambda h, j=j, jj=jj, tb=tb: h.matmul(psTs[tb][:, jj * 128:(jj + 1) * 128], lhsT=mbt[ab][:, j * 128:(j + 1) * 128], rhs=ident_b, start=True, stop=True)),
                     reads=[("mbt", ab), "cm_b"], writes=[("psT", tb)], sig=(jj == jn - 1), accum=(jj > 0))
            P.op("act", (lambda h, j0=j0, jn=jn, tb=tb: h.copy(
                out=MBT[:, mk_off(m // 4, j0): mk_off(m // 4, j0) + jn * 512].rearrange("p (a b) -> p a b", b=512)[:, :, (m % 4) * 128:(m % 4) * 128 + 128],
                in_=psTs[tb][:, 0:jn * 128].rearrange("p (a b) -> p a b", b=128))),
                 reads=[("psT", tb)], writes=[("MBT", m)])

    for u in score_units(0) + score_units(1):
        u()
    for pr_ in range(NB // 2):
        m0, m1 = 2 * pr_, 2 * pr_ + 1
        nxt_units = (score_units(m0 + 2) + score_units(m1 + 2)) if pr_ < NB // 2 - 1 else []
        if m0 >= 2:
            bisect_init(m1, a_m8, a_lo, a_w, a_hk, "a_", True)
            bisect_init(m0, m8, bs_lo, bs_w, bs_hk, "d_", False)
            per = (len(nxt_units) + KB - 1) // KB
            for k_ in range(KB):
                act_iter(m1, k_)
                dve_iter(m0, k_)
                for u in nxt_units[k_ * per:(k_ + 1) * per]:
                    u()
            act_final(m1)
            dve_final(m0)
            finish(m0, m8[:, 7:8], "d_m8")
            finish(m1, a_thr[:], "a_thr")
        else:
            finish(m0, thr0[:], "thr0")
            finish(m1, thr0[:], "thr0")
            for u in nxt_units:
                u()
    if early("scores", [(MBT[:], [128, 512 * 40], BF16, [("MBT", m) for m in dbg.get("m_list", range(dbg.get("m_max", NB)))]),
                        (acc[0][:], [128, S], F32, []), (acc[1][:], [128, S], F32, []), (m8[:], [128, 8], F32, ["d_m8"])]):
        return nc, dbg_outs
    if "mbt" in dbg:
        t = dump("mbt", None, [128, 512 * 40], BF16)
        P.dma("sp", (lambda h, t=t: h.dma_start(out=t, in_=MBT[:])), reads=[("MBT", m) for m in range(NB)], writes=["dbg_mbt"])

    P.barrier()
    A.release(mA)

    wbq = [A.alloc([128, 8, 512], BF16) for _ in range(2)]
    wbc = A.alloc([128, 8, 128], BF16)
    wuv = A.alloc([128, 8, 64], BF16)
    qT = A.alloc([128, 8, S], BF16)
    c_tok = A.alloc([128, NB, 128], BF16)
    cT = A.alloc([128, S], BF16)
    t5f = A.alloc([128, 2, 8, 128], F32)
    t5b = A.alloc([128, 2, 8, 128], BF16)
    rb31 = A.alloc([128, 8], F32)
    gkv_bc = A.alloc([128, 128], F32)
    hnY = [A.alloc([128, 512], F32) for _ in range(3)]
    hnQ = [A.alloc([128, 512], BF16) for _ in range(3)]
    hnD = [A.alloc([128, 512], F32) for _ in range(3)]
    tmp_y = hnY[0]
    ss1 = A.alloc([128, 2], F32)
    for i in range(2):
        load_w(wbq[i], w_in_v, ("wbq", i), (512 * i, 512 * i + 512))
    load_w(wbc, w_in_v, "wbc", (1024, 1152))
    P.dma("pool", lambda h: h.dma_start(out=wuv[:], in_=w_uv_d), writes=["wuv"])
    P.dma("sp", lambda h: h.dma_start(out=t5f[:], in_=t5_d), writes=["t5f"])
    P.dma("sp", lambda h: h.dma_start(out=rb31[:], in_=rb31_d), writes=["rb31"])
    P.dma("sp", lambda h: h.dma_start(out=gkv_bc[:], in_=gkv_bc_d), writes=["gkv_bc"])
    P.op("dve", lambda h: h.tensor_tensor(out=t5b[:], in0=t5f[:], in1=rb31[:].unsqueeze(1).unsqueeze(3).to_broadcast([128, 2, 8, 128]),
                                          op=ALU.subtract),
         reads=["t5f", "rb31"], writes=["t5b"])
    for hh in range(8):
        for tc in range(4):
            b = nb(0, 3)
            proj_fm(xg, wbq[hh // 4], ("wbq", hh // 4), (hh % 4) * 128, 128, tc, b)
            head_norm(b, 27, qT[:, hh, tc * 512:(tc + 1) * 512], (hnY, hnQ, hnD),
                      rx_bc[:, tc * 512:(tc + 1) * 512], [("rx_bc", tc)], ("qT", hh, tc))
    hn_flush()
    for tt in range(NB):
        b = nb(0, 3)
        for kc in range(8):
            P.op("pe", (lambda h, b=b, tt=tt, kc=kc: h.matmul(ps[b][:, 0:128], lhsT=xg[:, kc, tt * 128:(tt + 1) * 128], rhs=wbc[:, kc, :],
                                                              start=(kc == 0), stop=(kc == 7))),
                 reads=[("wbc", kc), ("xg", kc)], writes=[("ps", b)], sig=(kc == 7), accum=(kc > 0))
        P.op("act", (lambda h, b=b, tt=tt: h.activation(out=tmp_y[:, 0:128], in_=ps[b][:, 0:128], func=AF.Copy, scale=rx_tok[:, tt:tt + 1])),
             reads=[("ps", b), "rx_tok"], writes=[("hn_y", 0)])
        P.op("act", (lambda h: h.activation(out=tmp_y[:, 128:256], in_=tmp_y[:, 0:128], func=AF.Square, accum_out=ss1[:, 0:1])),
             reads=[("hn_y", 0)], writes=["c_ss", ("hn_y", 0)])
        P.op("act", (lambda h: h.activation(out=ss1[:, 1:2], in_=ss1[:, 0:1], func=AF.Sqrt, scale=1.0 / 128, bias=EPS)),
             reads=["c_ss"], writes=["c_sd"])
        P.op("dve", (lambda h: h.reciprocal(out=ss1[:, 1:2], in_=ss1[:, 1:2])), reads=["c_sd"], writes=["c_sd"])
        P.op("dve", (lambda h, tt=tt: h.scalar_tensor_tensor(out=c_tok[:, tt, :], in0=tmp_y[:, 0:128], scalar=ss1[:, 1:2], in1=gkv_bc[:],
                                                             op0=ALU.mult, op1=ALU.mult)),
             reads=[("hn_y", 0), "c_sd", "gkv_bc"], writes=[("c_tok", tt)])
    for t0 in range(0, NB, 4):
        tb = (t0 // 4) % 2
        for jj in range(4):
            tt = t0 + jj
            P.op("pe", (lambda h, tt=tt, jj=jj, tb=tb: h.matmul(psTs[tb][:, jj * 128:(jj + 1) * 128], lhsT=c_tok[:, tt, :], rhs=ident_b, start=True, stop=True)),
                 reads=[("c_tok", tt), "cm_b"], writes=[("psT", tb)], sig=(jj == 3), accum=(jj > 0))
        P.op("act", (lambda h, t0=t0, tb=tb: h.copy(out=cT[:, t0 * 128:(t0 + 4) * 128], in_=psTs[tb][:, 0:512])),
             reads=[("psT", tb)], writes=[("cT", t0 // 4)])
    if "qT" in dbg:
        t = dump("qT", None, [128, 8, S], BF16)
        P.dma("sp", (lambda h, t=t: h.dma_start(out=t, in_=qT[:])), reads=[("qT", a, b_) for a in range(8) for b_ in range(4)], writes=["dbg_qT"])
        t2 = dump("cT", None, [128, S], BF16)
        P.dma("sp", (lambda h, t2=t2: h.dma_start(out=t2, in_=cT[:])), reads=[("cT", i) for i in range(4)], writes=["dbg_cT"])

    NPT = 6
    PT = [A.alloc([128, 512], BF16) for _ in range(NPT)]
    PE_ = [A.alloc([128, 512], BF16) for _ in range(NPT)]
    SB = [(ps[0], ("ps", 0)), (ps[1], ("ps", 1)), (ps[2], ("ps", 2)), (psTs[0], ("psT", 0)), (psTs[1], ("psT", 1))]
    rden = A.alloc([128, 512], F32)
    ocp = A.alloc([128, 512], F32)
    onT = [A.alloc([128, 512], BF16) for _ in range(2)]
    BU = 5
    accO = [ps[3][:], ps[3][:]]
    accD = [ps[4][:], ps[4][:]]
    accK = [(("ps", 3), ("ps", 4)), (("ps", 3), ("ps", 4))]
    its = []
    for c in range(4):
        for hh in range(8):
            nj = 4 * c + 4
            for j in range(nj):
                its.append((c, hh, j, nj))
    LAG = 4
    deferred = []

    def front(i):
        c, hh, j, nj = its[i]
        t_lo = max(512 * c, 128 * j)
        co = t_lo - 512 * c
        sbt, sbk = SB[i % 5]
        pb = i % NPT
        adds = []
        for diff in (0, 1):
            m = j + diff
            if 4 * c <= m <= 4 * c + 3:
                adds.append(((m - 4 * c) * 128, t5b[:, diff, hh, :], "t5b"))
        P.op("pe", (lambda h: h.matmul(sbt[:, co:512], lhsT=cT[:, j * 128:(j + 1) * 128], rhs=qT[:, hh, t_lo:512 * c + 512],
                                       start=True, stop=(len(adds) == 0))),
             reads=[("cT", j // 4), ("qT", hh, c)], writes=[sbk], sig=(len(adds) == 0))
        for ai, (col, rhs, key) in enumerate(adds):
            last = ai == len(adds) - 1
            P.op("pe", (lambda h, col=col, rhs=rhs, last=last: h.matmul(sbt[:, col:col + 128], lhsT=ident_b, rhs=rhs, start=False, stop=last)),
                 reads=[key, "cm_b"], writes=[sbk], sig=last, accum=True)
        P.op("act", (lambda h: h.activation(out=PE_[pb][:, 0:512 - co], in_=sbt[:, co:512], func=AF.Exp, bias=rb31[:, hh:hh + 1], scale=1.0)),
             reads=[sbk, "rb31"], writes=[("PE_", pb)])
        P.op("dve", (lambda h: h.tensor_tensor(out=PT[pb][:, 0:512 - co], in0=PE_[pb][:, 0:512 - co],
                                               in1=MBT[:, mk_off(c, j) + co: mk_off(c, j) + 512], op=ALU.mult)),
             reads=[("PE_", pb)] + [("MBT", m) for m in range(4 * c, 4 * c + 4)], writes=[("PT", pb)])

    def back(i):
        c, hh, j, nj = its[i]
        t_lo = max(512 * c, 128 * j)
        co = t_lo - 512 * c
        pb = i % NPT
        hidx = c * 8 + hh
        ab_ = hidx % 2
        aO, aD = accO[ab_], accD[ab_]
        kO, kD = accK[ab_]
        P.op("pe", (lambda h: h.matmul(aO[:, co:512], lhsT=c_tok[:, j, :], rhs=PT[pb][:, 0:512 - co], start=(j == 0), stop=(j == nj - 1))),
             reads=[("c_tok", j), ("PT", pb)], writes=[kO], sig=False, accum=(j > 0))
        P.op("pe", (lambda h: h.matmul(aD[:, co:512], lhsT=ones_b, rhs=PT[pb][:, 0:512 - co], start=(j == 0), stop=(j == nj - 1))),
             reads=["cm_b", ("PT", pb)], writes=[kD], sig=True, accum=(j > 0))
        if j == nj - 1:
            ob = hh % 2
            P.op("act", lambda h: h.activation(out=rden[:], in_=aD, func=AF.Ln), reads=[kD], writes=["rden"])
            P.op("dve", lambda h: h.tensor_copy(out=ocp[:], in_=aO), reads=[kO], writes=["ocp"])
            P.op("act", lambda h: h.activation(out=rden[:], in_=rden[:], func=AF.Exp, scale=-1.0), reads=["rden"], writes=["rden"])
            P.op("dve", (lambda h: h.tensor_tensor(out=onT[ob][:], in0=ocp[:], in1=rden[:], op=ALU.mult)),
                 reads=["ocp", "rden"], writes=[("onT", ob)])

            def epilogue():
                P.op("pe", (lambda h: h.matmul(ps[BU][64 * (hh % 2):64 * (hh % 2) + 64, :], lhsT=wuv[:, hh, :], rhs=onT[ob][:], start=True, stop=True)),
                     reads=["wuv", ("onT", ob)], writes=[("ps", BU, hh % 2)])
                if hh % 2 == 1:
                    P.op("act", (lambda h: h.copy(out=o_dsa[:, hh // 2, c * 512:(c + 1) * 512], in_=ps[BU][:])),
                         reads=[("ps", BU, 0), ("ps", BU, 1)], writes=[("o_dsa", hh // 2, c)])
            deferred.append([2, epilogue])

    for i in range(len(its) + LAG):
        if i < len(its):
            front(i)
        if i - LAG >= 0:
            back(i - LAG)
            for dct in list(deferred):
                dct[0] -= 1
                if dct[0] <= 0:
                    dct[1]()
                    deferred.remove(dct)
    for dct in deferred:
        dct[1]()
    if "o_dsa" in dbg:
        t = dump("o_dsa", None, [128, 4, S], BF16)
        P.dma("sp", (lambda h, t=t: h.dma_start(out=t, in_=o_dsa[:])), reads=[("o_dsa", a, c) for a in range(4) for c in range(4)],
              writes=["dbg_o_dsa"])

    P.barrier()
    A.release(base_mark)
    o_s5 = A.alloc([128, 4, S], BF16)
    o_x = A.alloc([128, 4, S], BF16)
    base_mark = A.mark()
    if dbg.get("od_early"):
        t = dump("od0", None, [128, 4, S], BF16)
        P.dma("sp", (lambda h, t=t: h.dma_start(out=t, in_=o_dsa[:])), writes=["dbg_od0"])
    if dbg.get("stop") == "dsa":
        P.barrier()
        P.emit()
        return nc, dbg_outs


    xgX = A.alloc([128, 8, S], BF16)
    build_xg(xgX, False)
    wbx = A.alloc([128, 8, 512], BF16)
    wbu = A.alloc([128, 8, 512], BF16)
    wkv = A.alloc([128, 8, 1024], BF16)
    memf = A.alloc([128, 8, 256], F32)
    msq = A.alloc([128, 8, 256], BF16)
    memn = A.alloc([128, 8, 256], BF16)
    rm = A.alloc([128, 256], F32)
    khT = A.alloc([128, 4, 256], BF16)
    vtok = A.alloc([128, 2, 512], BF16)
    qxT = A.alloc([128, 4, S], BF16)
    ust = [A.alloc([128, 512], BF16) for _ in range(2)]
    tyX = [A.alloc([128, 512], F32) for _ in range(3)]
    tqX = [A.alloc([128, 512], BF16) for _ in range(3)]
    tdX = [A.alloc([128, 512], F32) for _ in range(3)]
    PTx = [A.alloc([128, 512], BF16) for _ in range(3)]
    rdx = A.alloc([128, 512], F32)
    w_kv_v = w_kv_d.rearrange("(kc p) n -> p kc n", p=128)
    if not dbg.get("no_wbx"):
        load_w(wbx, w_in_v, "wbx", (2248, 2760))
    if not dbg.get("no_wbu"):
        load_w(wbu, w_in_v, "wbu", (1736, 2248))
    for i in range(2):
        if not dbg.get("no_wkv"):
            load_w(wkv[:, :, i * 512:(i + 1) * 512], w_kv_v, ("wkv", i), (i * 512, (i + 1) * 512))
    P.dma("sp", lambda h: h.dma_start(out=memf[:], in_=memT_d.rearrange("(kc p) m -> p kc m", p=128)), writes=["memf"])
    if dbg.get("stop") == "x0":
        P.barrier()
        t = dump("od", None, [128, 4, S], BF16)
        P.dma("sp", (lambda h, t=t: h.dma_start(out=t, in_=o_dsa[:])), writes=["dbg_od"])
        P.barrier()
        P.emit()
        return nc, dbg_outs
    P.op("act", lambda h: h.activation(out=msq[:], in_=memf[:], func=AF.Square), reads=["memf"], writes=["msq"])
    bm = nb(0, 3)
    for kc in range(8):
        P.op("pe", (lambda h, kc=kc: h.matmul(ps[bm][:, 0:256], lhsT=ones_b, rhs=msq[:, kc, :], start=(kc == 0), stop=(kc == 7))),
             reads=["msq", "cm_b"], writes=[("ps", bm)], sig=(kc == 7), accum=(kc > 0))
    P.op("act", lambda h: h.activation(out=rm[:], in_=ps[bm][:, 0:256], func=AF.Ln, scale=1.0 / D, bias=EPS), reads=[("ps", bm)], writes=["rm"])
    P.op("act", lambda h: h.activation(out=rm[:], in_=rm[:], func=AF.Exp, scale=-0.5), reads=["rm"], writes=["rm"])
    for kc in range(8):
        P.op("dve", (lambda h, kc=kc: h.scalar_tensor_tensor(out=memn[:, kc, :], in0=memf[:, kc, :], scalar=gvec[:, 8 + kc:9 + kc], in1=rm[:],
                                                             op0=ALU.mult, op1=ALU.mult)),
             reads=["memf", "rm", "gvec"], writes=[("memn", kc)])
    for hh in range(4):
        b = nb(0, 3)
        for kc in range(8):
            P.op("pe", (lambda h, b=b, hh=hh, kc=kc: h.matmul(ps[b][:, 0:256], lhsT=wkv[:, kc, hh * 128:(hh + 1) * 128], rhs=memn[:, kc, :],
                                                              start=(kc == 0), stop=(kc == 7))),
                 reads=[(("wkv", 0), kc), ("memn", kc)], writes=[("ps", b)], sig=(kc == 7), accum=(kc > 0))
        head_norm(b, 28, khT[:, hh, :], (tyX, tqX, tdX), None, [], ("khT", hh))
    hn_flush()
    for mb in range(2):
        b = nb(0, 3)
        for kc in range(8):
            P.op("pe", (lambda h, b=b, mb=mb, kc=kc: h.matmul(ps[b][:], lhsT=memn[:, kc, mb * 128:(mb + 1) * 128], rhs=wkv[:, kc, 512:1024],
                                                              start=(kc == 0), stop=(kc == 7))),
                 reads=[(("wkv", 1), kc), ("memn", kc)], writes=[("ps", b)], sig=(kc == 7), accum=(kc > 0))
        P.op("act", (lambda h, b=b, mb=mb: h.copy(out=vtok[:, mb, :], in_=ps[b][:])), reads=[("ps", b)], writes=[("vtok", mb)])
    for hh in range(4):
        for tc in range(4):
            b = nb(0, 3)
            proj_fm(xgX, wbx, "wbx", hh * 128, 128, tc, b)
            head_norm(b, 25, qxT[:, hh, tc * 512:(tc + 1) * 512], (tyX, tqX, tdX),
                      rx_bc[:, tc * 512:(tc + 1) * 512], [("rx_bc", tc)], ("qxT", hh, tc))
    hn_flush()
    ui = 0
    for ch in range(4):
        for tcp in range(4):
            b = nb(0, 3)
            ub = ui % 2
            ui += 1
            proj_fm(xgX, wbu, "wbu", ch * 128, 128, tcp, b,
                    rhs_view=(lambda kc, tcp: xgX[:, kc, :].rearrange("p (b j) -> p j b", j=8)[:, 2 * tcp:2 * tcp + 2, :]))
            P.op("dve", (lambda h, b=b, ub=ub, tcp=tcp: h.tensor_tensor(
                out=ust[ub][:].rearrange("p (j b) -> p j b", j=2), in0=ps[b][:].rearrange("p (j b) -> p j b", j=2),
                in1=rx_bc[:].rearrange("p (b j) -> p j b", j=8)[:, 2 * tcp:2 * tcp + 2, :], op=ALU.mult)),
                reads=[("ps", b)] + RX, writes=[("ust", ub)])
            P.dma("sp", (lambda h, ub=ub, ch=ch, tcp=tcp: h.dma_start(out=scr_u[ch * 128:(ch + 1) * 128, tcp * 512:(tcp + 1) * 512], in_=ust[ub][:])),
                  reads=[("ust", ub)], writes=[("scr_u", ch, tcp)])
    ptiX = 0
    BO, BD = 3, 4
    for c in range(4):
        for hh in range(4):
            for mb in range(2):
                bs = nb(0, 3)
                pb = ptiX % 3
                ptiX += 1
                P.op("pe", (lambda h, bs=bs, hh=hh, mb=mb, c=c: h.matmul(ps[bs][:], lhsT=khT[:, hh, mb * 128:(mb + 1) * 128],
                                                                         rhs=qxT[:, hh, c * 512:(c + 1) * 512], start=True, stop=True)),
                     reads=[("khT", hh), ("qxT", hh, c)], writes=[("ps", bs)])
                P.op("act", (lambda h, bs=bs, pb=pb: h.activation(out=PTx[pb][:], in_=ps[bs][:], func=AF.Exp)),
                     reads=[("ps", bs)], writes=[("PT", pb)])
                P.op("pe", (lambda h, pb=pb, mb=mb, hh=hh: h.matmul(ps[BO][:], lhsT=vtok[:, mb, hh * 128:(hh + 1) * 128], rhs=PTx[pb][:],
                                                                    start=(mb == 0), stop=(mb == 1))),
                     reads=[("vtok", mb), ("PT", pb)], writes=[("ps", BO)], sig=False, accum=(mb > 0))
                P.op("pe", (lambda h, pb=pb, mb=mb: h.matmul(ps[BD][:], lhsT=ones_b, rhs=PTx[pb][:], start=(mb == 0), stop=(mb == 1))),
                     reads=["cm_b", ("PT", pb)], writes=[("ps", BD)], sig=True, accum=(mb > 0))
            P.op("act", lambda h: h.activation(out=rdx[:], in_=ps[BD][:], func=AF.Ln), reads=[("ps", BD)], writes=["rden"])
            P.op("act", lambda h: h.activation(out=rdx[:], in_=rdx[:], func=AF.Exp, scale=-1.0), reads=["rden"], writes=["rden"])
            P.op("dve", (lambda h, hh=hh, c=c: h.tensor_tensor(out=o_x[:, hh, c * 512:(c + 1) * 512], in0=ps[BO][:], in1=rdx[:], op=ALU.mult)),
                 reads=[("ps", BO), "rden"], writes=[("o_x", hh, c)])
    if dbg.get("stop") == "cross":
        t = dump("od", None, [128, 4, S], BF16)
        P.dma("sp", (lambda h, t=t: h.dma_start(out=t, in_=o_dsa[:])), writes=["dbg_od"])
        t = dump("o_x", None, [128, 4, S], BF16)
        P.dma("sp", (lambda h, t=t: h.dma_start(out=t, in_=o_x[:])), reads=[("o_x", a, c) for a in range(4) for c in range(4)], writes=["dbg_o_x"])
        t2 = dump("scr_u", None, [512, S], BF16)
        P.dma("sp", (lambda h, t2=t2: h.dma_start(out=t2, in_=scr_u)), reads=[("scr_u", a, c) for a in range(4) for c in range(4)], writes=["dbg_scr_u"])
        P.barrier()
        P.emit()
        return nc, dbg_outs
    P.barrier()
    A.release(base_mark)

    s5a = A.alloc([128, 3, 32], F32)
    s5b = A.alloc([128, 4, 32, 16], F32)
    dsk = A.alloc([128, 32], F32)
    TB = [A.alloc([128, 2, 8, 32], F32) for _ in range(4)]
    TK = A.alloc([128, 8, 2, 32], F32)
    cw = A.alloc([128, 16, 2, 32], F32)
    bb = A.alloc([128, 2, 32, 16], F32)
    W1 = A.alloc([128, 32, 128], BF16)
    W2 = A.alloc([128, 32, 128], BF16)
    Tm = A.alloc([128, 32, 128], BF16)
    U8 = A.alloc([128, 32, 256], BF16)
    X = A.alloc([128, 32, 256], BF16)
    mS = A.alloc([128, 1], F32)
    mS1 = A.mark()
    W1T = A.alloc([128, 32, 128], BF16)
    Lm = A.alloc([128, 32, 128], BF16)
    Rm = A.alloc([128, 32, 128], BF16)
    tA = A.alloc([128, 32, 128], F32)
    tB = A.alloc([128, 32, 128], F32)
    P.dma("sp", lambda h: h.dma_start(out=s5a[:], in_=s5a_d), writes=["s5a"])
    P.dma("sp", lambda h: h.dma_start(out=s5b[:], in_=s5b_d), writes=["s5b"])
    P.dma("sp", lambda h: h.dma_start(out=dsk[:], in_=dsk_d), writes=["dsk"])
    for jl in range(8):
        P.dma("sp", (lambda h, jl=jl: h.dma_start(out=U8[jl * 16:(jl + 1) * 16, :, :],
                                                  in_=scr_u.rearrange("(g c) (j b) -> j c g b", c=16, j=8)[jl])),
              reads=[("scr_u", a, c) for a in range(4) for c in range(4)], writes=[("U8", jl)])
    U8K = [("U8", jl) for jl in range(8)]

    def V(i):
        return cw[:, i, :, :]

    def tt(out, a, b_, op, rk, wk):
        P.op("dve", lambda h: h.tensor_tensor(out=out, in0=a, in1=b_, op=op), reads=rk, writes=wk)

    def cmul(dst, a, b_, ka, kb, kd):
        t1, t2 = V(14), V(15)
        tt(t1[:, 0, :], a[:, 0, :], b_[:, 0, :], ALU.mult, [ka, kb], ["cw_t1a"])
        tt(t1[:, 1, :], a[:, 1, :], b_[:, 1, :], ALU.mult, [ka, kb], ["cw_t1b"])
        tt(t2[:, 0, :], a[:, 0, :], b_[:, 1, :], ALU.mult, [ka, kb], ["cw_t2a"])
        tt(t2[:, 1, :], a[:, 1, :], b_[:, 0, :], ALU.mult, [ka, kb], ["cw_t2b"])
        tt(dst[:, 0, :], t1[:, 0, :], t1[:, 1, :], ALU.subtract, ["cw_t1a", "cw_t1b"], [kd])
        tt(dst[:, 1, :], t2[:, 0, :], t2[:, 1, :], ALU.add, ["cw_t2a", "cw_t2b", kd], [kd])

    a_re, a_im, ldt = s5a[:, 0, :], s5a[:, 1, :], s5a[:, 2, :]
    dtv = V(0)[:, 0, :]
    adr = V(0)[:, 1, :]
    adi = V(1)[:, 0, :]
    P.op("act", lambda h: h.activation(out=dtv, in_=ldt, func=AF.Exp), reads=["s5a"], writes=["dtv"])
    tt(adr, a_re, dtv, ALU.mult, ["s5a", "dtv"], ["adr"])
    tt(adi, a_im, dtv, ALU.mult, ["s5a", "dtv"], ["adi"])
    mag, magn, cs, sn = V(2)[:, 0, :], V(2)[:, 1, :], V(3)[:, 0, :], V(3)[:, 1, :]
    P.op("dve", lambda h: h.memset(mS[:], math.pi / 2), writes=["mS"])
    P.op("act", lambda h: h.activation(out=mag, in_=adr, func=AF.Exp, scale=1.0 / 16), reads=["adr"], writes=["mag"])
    P.op("act", lambda h: h.activation(out=magn, in_=adr, func=AF.Exp, scale=-1.0 / 16), reads=["adr"], writes=["magn"])
    P.op("act", lambda h: h.activation(out=cs, in_=adi, func=AF.Sin, scale=1.0 / 16, bias=mS[:]), reads=["adi", "mS"], writes=["cs"])
    P.op("act", lambda h: h.activation(out=sn, in_=adi, func=AF.Sin, scale=1.0 / 16), reads=["adi"], writes=["sn"])
    mu, nu = V(4), V(5)
    tt(mu[:, 0, :], mag, cs, ALU.mult, ["mag", "cs"], ["mu"])
    tt(mu[:, 1, :], mag, sn, ALU.mult, ["mag", "sn", "mu"], ["mu"])
    tt(nu[:, 0, :], magn, cs, ALU.mult, ["magn", "cs"], ["nu"])
    P.op("dve", lambda h: h.scalar_tensor_tensor(out=nu[:, 1, :], in0=magn, scalar=-1.0, in1=sn, op0=ALU.mult, op1=ALU.mult),
         reads=["magn", "sn", "nu"], writes=["nu"])
    def pw_slot(t, slot):
        return TB[t][:, :, slot, :]
    TW1, TW2, TL, TR = 0, 1, 2, 3
    cur, ck = mu, "mu"
    for i in range(4):
        dst = V(6 + (i % 2)) if i < 3 else pw_slot(TR, 1)
        kd = f"sqp{i}" if i < 3 else ("P", 1)
        cmul(dst, cur, cur, ck, ck, kd)
        cur, ck = dst, kd
    cur, ck = nu, "nu"
    for i in range(4):
        dst = V(8 + (i % 2)) if i < 3 else pw_slot(TL, 1)
        kd = f"sqn{i}" if i < 3 else ("N", 1)
        cmul(dst, cur, cur, ck, ck, kd)
        cur, ck = dst, kd
    Pp = {1: pw_slot(TR, 1)}
    Np = {1: pw_slot(TL, 1)}
    for t_ in (TR, TL, TW1):
        sl = 7 if t_ == TW1 else 0
        P.op("dve", (lambda h, t_=t_, sl=sl: h.memset(TB[t_][:, 0, sl, :], 1.0)), writes=[("one", t_, 0)])
        P.op("dve", (lambda h, t_=t_, sl=sl: h.memset(TB[t_][:, 1, sl, :], 0.0)), writes=[("one", t_, 1)])
    for k, (a, b_) in ((2, (1, 1)), (3, (2, 1)), (4, (2, 2)), (5, (4, 1)), (6, (4, 2)), (7, (4, 3))):
        Pp[k] = pw_slot(TR, k)
        cmul(Pp[k], Pp[a], Pp[b_], ("P", a), ("P", b_), ("P", k))
        Np[k] = pw_slot(TL, k)
        cmul(Np[k], Np[a], Np[b_], ("N", a), ("N", b_), ("N", k))
    Pp[8] = pw_slot(TW2, 7)
    cmul(Pp[8], Pp[4], Pp[4], ("P", 4), ("P", 4), ("P", 8))
    PK = [("P", k) for k in range(1, 9)]
    NK = [("N", k) for k in range(1, 8)]
    P.op("dve", lambda h: h.tensor_copy(out=TB[TW2][:, :, 0:7, :], in_=TB[TR][:, :, 1:8, :]), reads=PK, writes=["TW2"])
    for jl in range(7):
        P.op("dve", (lambda h, jl=jl: h.tensor_copy(out=TB[TW1][:, :, jl, :], in_=TB[TR][:, :, 7 - jl, :])), reads=PK, writes=[("TW1", jl)])
    TW1K = [("TW1", jl) for jl in range(7)] + [("one", TW1, 0), ("one", TW1, 1)]
    TRK = PK + [("one", TR, 0), ("one", TR, 1)]
    TLK = NK + [("one", TL, 0), ("one", TL, 1)]
    TW2K = ["TW2", ("P", 8)]
    P.op("dve", lambda h: h.tensor_copy(out=TK[:, 0, :, :], in_=Pp[8]), reads=[("P", 8)], writes=[("TK", 0)])
    for l in range(1, 8):
        cmul(TK[:, l, :, :], TK[:, l - 1, :, :], TK[:, l - 1, :, :], ("TK", l - 1), ("TK", l - 1), ("TK", l))
    P.op("dve", lambda h: h.tensor_scalar(out=TK[64:128, :, 1, :], in0=TK[64:128, :, 1, :], scalar1=-1.0, scalar2=None, op0=ALU.mult),
         reads=[("TK", l) for l in range(8)], writes=["TKs"])
    num, qv = V(10), V(11)
    den = V(12)[:, 0, :]
    P.op("dve", lambda h: h.tensor_scalar(out=num[:, 0, :], in0=Pp[1][:, 0, :], scalar1=-1.0, scalar2=None, op0=ALU.add),
         reads=[("P", 1)], writes=["num"])
    P.op("dve", lambda h: h.tensor_copy(out=num[:, 1, :], in_=Pp[1][:, 1, :]), reads=[("P", 1), "num"], writes=["num"])
    tt(den, a_re, a_re, ALU.mult, ["s5a"], ["den"])
    tt(V(12)[:, 1, :], a_im, a_im, ALU.mult, ["s5a"], ["den2"])
    tt(den, den, V(12)[:, 1, :], ALU.add, ["den", "den2"], ["den"])
    P.op("dve", lambda h: h.reciprocal(out=den, in_=den), reads=["den"], writes=["den"])
    t13 = V(13)
    tt(t13[:, 0, :], num[:, 0, :], a_re, ALU.mult, ["num", "s5a"], ["t13a"])
    tt(t13[:, 1, :], num[:, 1, :], a_im, ALU.mult, ["num", "s5a"], ["t13b"])
    tt(qv[:, 0, :], t13[:, 0, :], t13[:, 1, :], ALU.add, ["t13a", "t13b"], ["qv0"])
    tt(qv[:, 0, :], qv[:, 0, :], den, ALU.mult, ["qv0", "den"], ["qv0"])
    tt(t13[:, 0, :], num[:, 1, :], a_re, ALU.mult, ["num", "s5a", "qv0"], ["t13a"])
    tt(t13[:, 1, :], num[:, 0, :], a_im, ALU.mult, ["num", "s5a", "qv0"], ["t13b"])
    tt(qv[:, 1, :], t13[:, 0, :], t13[:, 1, :], ALU.subtract, ["t13a", "t13b"], ["qv1"])
    tt(qv[:, 1, :], qv[:, 1, :], den, ALU.mult, ["qv1", "den"], ["qv1"])
    q_re = qv[:, 0, :].unsqueeze(2).to_broadcast([128, 32, 16])
    q_im = qv[:, 1, :].unsqueeze(2).to_broadcast([128, 32, 16])
    B_re, B_im, C_re, C_im = s5b[:, 0], s5b[:, 1], s5b[:, 2], s5b[:, 3]
    tAv = tA[:].rearrange("p g (j c) -> p g j c", c=16)
    tBv = tB[:].rearrange("p g (j c) -> p g j c", c=16)
    tt(tAv[:, :, 0, :], q_re, B_re, ALU.mult, ["qv0", "s5b"], ["tA"])
    tt(tBv[:, :, 0, :], q_im, B_im, ALU.mult, ["qv1", "s5b"], ["tB"])
    tt(bb[:, 0], tAv[:, :, 0, :], tBv[:, :, 0, :], ALU.subtract, ["tA", "tB"], ["bb0"])
    tt(tAv[:, :, 0, :], q_re, B_im, ALU.mult, ["qv0", "s5b", "bb0"], ["tA"])
    tt(tBv[:, :, 0, :], q_im, B_re, ALU.mult, ["qv1", "s5b", "bb0"], ["tB"])
    tt(bb[:, 1], tAv[:, :, 0, :], tBv[:, :, 0, :], ALU.add, ["tA", "tB"], ["bb1"])

    def build_mat(dst, tbl, tkeys, v_re, v_im, vkeys, mode, dkey):
        dv = dst[:].rearrange("p g (j c) -> p g j c", c=16)
        for half in range(2):
            pr = slice(64 * half, 64 * half + 64)
            Tre = TB[tbl][pr, 0, :, :].rearrange("p s g -> p g s").unsqueeze(3).to_broadcast([64, 32, 8, 16])
            Tim = TB[tbl][pr, 1, :, :].rearrange("p s g -> p g s").unsqueeze(3).to_broadcast([64, 32, 8, 16])
            va, vb = (v_re, v_im) if half == 0 else (v_im, v_re)
            Va = va[pr].unsqueeze(2).to_broadcast([64, 32, 8, 16])
            Vb = vb[pr].unsqueeze(2).to_broadcast([64, 32, 8, 16])
            tt(tAv[pr], Tre, Va, ALU.mult, tkeys + vkeys + [dkey], [("tA", half)])
            tt(tBv[pr], Tim, Vb, ALU.mult, tkeys + vkeys + [dkey], [("tB", half)])
            if half == 0:
                tt(dv[pr], tAv[pr], tBv[pr], ALU.subtract, [("tA", 0), ("tB", 0)], [(dkey, 0)])
            elif mode == "B":
                tt(dv[pr], tAv[pr], tBv[pr], ALU.add, [("tA", 1), ("tB", 1)], [(dkey, 1)])
            else:
                P.op("dve", (lambda h, pr=pr: h.scalar_tensor_tensor(out=dv[pr], in0=tAv[pr], scalar=-1.0, in1=tBv[pr],
                                                                      op0=ALU.mult, op1=ALU.subtract)),
                     reads=[("tA", 1), ("tB", 1)], writes=[(dkey, 1)])

    P.op("dve", lambda h: h.memset(mS[:], 0.0), reads=["tA", "tB", "mS"], writes=[("tA", 0), ("tA", 1), ("tB", 0), ("tB", 1), "mS"])
    build_mat(W1T, TW1, TW1K, bb[:, 0], bb[:, 1], ["bb0", "bb1"], "B", "W1T")
    build_mat(W2, TW2, TW2K, C_re, C_im, ["s5b"], "C", "W2")
    build_mat(Lm, TL, TLK, bb[:, 0], bb[:, 1], ["bb0", "bb1"], "B", "Lm")
    build_mat(Rm, TR, TRK, C_re, C_im, ["s5b"], "C", "Rm")
    for g0 in range(0, 32, 4):
        tb = (g0 // 4) % 2
        for jj in range(4):
            P.op("pe", (lambda h, g0=g0, jj=jj, tb=tb: h.matmul(psTs[tb][:, jj * 128:(jj + 1) * 128], lhsT=W1T[:, g0 + jj, :], rhs=ident_b, start=True, stop=True)),
                 reads=[("W1T", 0), ("W1T", 1), "cm_b"], writes=[("psT", tb)], sig=(jj == 3), accum=(jj > 0))
        P.op("act", (lambda h, g0=g0, tb=tb: h.copy(out=W1[:, g0:g0 + 4, :], in_=psTs[tb][:, 0:512].rearrange("p (a b) -> p a b", b=128))),
             reads=[("psT", tb)], writes=[("W1", g0 // 4)])
    for g0 in range(0, 32, 4):
        b = nb(0, 3)
        for jj in range(4):
            P.op("pe", (lambda h, g0=g0, jj=jj, b=b: h.matmul(ps[b][:, jj * 128:(jj + 1) * 128], lhsT=Lm[:, g0 + jj, :], rhs=Rm[:, g0 + jj, :],
                                                              start=True, stop=True)),
                 reads=[("Lm", 0), ("Lm", 1), ("Rm", 0), ("Rm", 1)], writes=[("ps", b)], sig=(jj == 3), accum=(jj > 0))
        P.op("dve", (lambda h, b=b, g0=g0: h.tensor_tensor(out=tA[:, g0:g0 + 4, :], in0=ps[b][:].rearrange("p (a b) -> p a b", b=128),
                                                           in1=tmask_f.unsqueeze(1).to_broadcast([128, 4, 128]), op=ALU.mult)),
             reads=[("ps", b), "cm_f", ("tA", 0), ("tA", 1)], writes=[("tAm", g0 // 4)])
        for jj in range(4):
            g = g0 + jj
            P.op("dve", (lambda h, g=g: h.scalar_tensor_tensor(out=Tm[:, g, :], in0=ident_f, scalar=dsk[:, g:g + 1], in1=tA[:, g, :],
                                                               op0=ALU.mult, op1=ALU.add)),
                 reads=[("tAm", g0 // 4), "dsk", "cm_f"], writes=[("Tm", g0 // 4)])
    if dbg.get("stop") == "s5pre":
        for nm, tns, keys in (("W1", W1, [("W1", i) for i in range(8)]), ("W2", W2, [("W2", 0), ("W2", 1)]), ("Tm", Tm, [("Tm", i) for i in range(8)])):
            t = dump(nm, None, [128, 32, 128], BF16)
            P.dma("sp", (lambda h, t=t, tns=tns: h.dma_start(out=t, in_=tns[:])), reads=keys, writes=["dbg_" + nm])
        t = dump("TK", None, [128, 8, 2, 32])
        P.dma("sp", (lambda h, t=t: h.dma_start(out=t, in_=TK[:])), reads=["TKs"], writes=["dbg_TK"])
        t = dump("TB", None, [128, 2, 8, 32])
        P.dma("sp", (lambda h, t=t: h.dma_start(out=t, in_=TB[TR][:])), reads=TRK, writes=["dbg_TB"])
        P.barrier()
        P.emit()
        return nc, dbg_outs
    P.barrier()
    A.release(mS1)

    Yst = A.alloc([128, 32, 256], BF16)
    gq = A.alloc([128, 512], F32)
    gz = A.alloc([128, 512], F32)
    gs = A.alloc([128, 512], F32)
    mS2 = A.mark()
    Rl = [A.alloc([128, 32, 128], BF16) for _ in range(2)]
    rt1 = A.alloc([128, 32, 128], BF16)
    rt2 = A.alloc([128, 32, 128], BF16)
    XK = lambda gp: ("X", gp)
    for gp in range(16):
        b = nb(0, 3)
        for gi in range(2):
            g = 2 * gp + gi
            P.op("pe", (lambda h, b=b, gi=gi, g=g: h.matmul(ps[b][:, gi * 256:(gi + 1) * 256], lhsT=W1[:, g, :], rhs=U8[:, g, :], start=True, stop=True)),
                 reads=[("W1", g // 4)] + U8K, writes=[("ps", b)], sig=(gi == 1), accum=(gi > 0))
        P.op("act", (lambda h, b=b, gp=gp: h.copy(out=X[:, 2 * gp:2 * gp + 2, :], in_=ps[b][:].rearrange("p (a b) -> p a b", b=256))),
             reads=[("ps", b)], writes=[XK(gp)])
    for l in range(8):
        d = 1 << l
        R = Rl[l % 2]
        P.op("dve", (lambda h, l=l: h.tensor_tensor(out=rt1[:], in0=ident_f.unsqueeze(1).to_broadcast([128, 32, 128]),
                                                    in1=TK[:, l, 0, :].unsqueeze(2).to_broadcast([128, 32, 128]), op=ALU.mult)),
             reads=["cm_f", "TKs"], writes=["rt1"])
        P.op("dve", (lambda h, l=l: h.tensor_tensor(out=rt2[:], in0=swap_f.unsqueeze(1).to_broadcast([128, 32, 128]),
                                                    in1=TK[:, l, 1, :].unsqueeze(2).to_broadcast([128, 32, 128]), op=ALU.mult)),
             reads=["cm_f", "TKs"], writes=["rt2"])
        P.op("dve", (lambda h, R=R: h.tensor_tensor(out=R[:], in0=rt1[:], in1=rt2[:], op=ALU.add)),
             reads=["rt1", "rt2"], writes=[("R", l % 2)])
        for gp in range(16):
            b = nb(0, 3)
            for gi in range(2):
                g = 2 * gp + gi
                P.op("pe", (lambda h, b=b, gi=gi, g=g, d=d, R=R: h.matmul(ps[b][:, gi * 256 + d:(gi + 1) * 256], lhsT=R[:, g, :], rhs=X[:, g, 0:256 - d],
                                                                          start=True, stop=True)),
                     reads=[("R", l % 2), XK(gp)], writes=[("ps", b)], sig=(gi == 1), accum=(gi > 0))
            P.op("dve", (lambda h, b=b, gp=gp, d=d: h.tensor_tensor(out=X[:, 2 * gp:2 * gp + 2, d:256], in0=X[:, 2 * gp:2 * gp + 2, d:256],
                                                                   in1=ps[b][:].rearrange("p (a b) -> p a b", b=256)[:, :, d:256], op=ALU.add)),
                 reads=[("ps", b), XK(gp)], writes=[XK(gp)])
    for gp in range(16):
        b = nb(0, 3)
        for gi in range(2):
            g = 2 * gp + gi
            P.op("pe", (lambda h, b=b, gi=gi, g=g: h.matmul(ps[b][:, gi * 256:(gi + 1) * 256], lhsT=Tm[:, g, :], rhs=U8[:, g, :], start=True, stop=False)),
                 reads=[("Tm", g // 4)] + U8K, writes=[("ps", b)], sig=False, accum=(gi > 0))
            P.op("pe", (lambda h, b=b, gi=gi, g=g: h.matmul(ps[b][:, gi * 256 + 1:(gi + 1) * 256], lhsT=W2[:, g, :], rhs=X[:, g, 0:255], start=False, stop=True)),
                 reads=[("W2", 0), ("W2", 1), XK(gp)], writes=[("ps", b)], sig=(gi == 1), accum=True)
        P.op("act", (lambda h, b=b: h.activation(out=gq[:], in_=ps[b][:], func=AF.Square)), reads=[("ps", b)], writes=["gq"])
        P.op("dve", lambda h: h.tensor_scalar(out=gz[:], in0=gq[:], scalar1=0.044715, scalar2=1.0, op0=ALU.mult, op1=ALU.add),
             reads=["gq"], writes=["gz"])
        P.op("dve", (lambda h, b=b: h.tensor_tensor(out=gz[:], in0=gz[:], in1=ps[b][:], op=ALU.mult)), reads=["gz", ("ps", b)], writes=["gz"])
        P.op("act", lambda h: h.activation(out=gs[:], in_=gz[:], func=AF.Sigmoid, scale=2.0 * math.sqrt(2.0 / math.pi)), reads=["gz"], writes=["gs"])
        P.op("dve", (lambda h, b=b, gp=gp: h.tensor_tensor(out=Yst[:, 2 * gp:2 * gp + 2, :], in0=gs[:].rearrange("p (a b) -> p a b", b=256),
                                                          in1=ps[b][:].rearrange("p (a b) -> p a b", b=256), op=ALU.mult)),
             reads=["gs", ("ps", b)], writes=[("Yst", gp)])
    for il in range(8):
        P.dma("sp", (lambda h, il=il: h.dma_start(out=scr_y.rearrange("(g c) (i b) -> i c g b", c=16, i=8)[il], in_=Yst[il * 16:(il + 1) * 16, :, :])),
              reads=[("Yst", gp) for gp in range(16)], writes=[("scr_y", il)])
    P.barrier()
    A.release(mS2)
    YT = A.alloc([128, 4, S], BF16)
    wglu = A.alloc([128, 4, 512], BF16)
    P.dma("pool", lambda h: h.dma_start(out=wglu[:], in_=w_glu_d.rearrange("(kc p) n -> p kc n", p=128)), writes=["wglu"])
    P.dma("sp", lambda h: h.dma_start(out=YT[:], in_=scr_y.rearrange("(ch p) t -> p ch t", p=128)), writes=["YT"])
    for chp in range(4):
        for tcp in range(4):
            b = nb(0, 3)
            for k in range(4):
                P.op("pe", (lambda h, b=b, k=k, chp=chp, tcp=tcp: h.matmul(ps[b][:], lhsT=wglu[:, k, chp * 128:(chp + 1) * 128],
                                                                           rhs=YT[:, k, tcp * 512:(tcp + 1) * 512], start=(k == 0), stop=(k == 3))),
                     reads=["wglu", "YT"], writes=[("ps", b)], sig=(k == 3), accum=(k > 0))
            P.op("act", (lambda h, b=b: h.activation(out=gs[:], in_=ps[b][:], func=AF.Sigmoid)), reads=[("ps", b)], writes=["gs"])
            P.op("dve", (lambda h, chp=chp, tcp=tcp: h.tensor_tensor(out=o_s5[:, chp, tcp * 512:(tcp + 1) * 512], in0=gs[:],
                                                                    in1=YT[:, chp, tcp * 512:(tcp + 1) * 512], op=ALU.mult)),
                 reads=["gs", "YT"], writes=[("o_s5", chp, tcp)])
    if dbg.get("stop") == "s5":
        t = dump("od", None, [128, 4, S], BF16)
        P.dma("sp", (lambda h, t=t: h.dma_start(out=t, in_=o_dsa[:])), writes=["dbg_od"])
        t = dump("o_s5", None, [128, 4, S], BF16)
        P.dma("sp", (lambda h, t=t: h.dma_start(out=t, in_=o_s5[:])), reads=[("o_s5", a, c) for a in range(4) for c in range(4)], writes=["dbg_o_s5"])
        t = dump("YT", None, [128, 4, S], BF16)
        P.dma("sp", (lambda h, t=t: h.dma_start(out=t, in_=YT[:])), reads=["YT"], writes=["dbg_YT"])
        P.barrier()
        P.emit()
        return nc, dbg_outs
    P.barrier()
    A.release(base_mark)

    off_merged = A.mark()
    merged = A.alloc([128, 8, S], BF16)
    mM1 = A.mark()
    xgM = A.alloc([128, 8, S], BF16)
    build_xg(xgM, False)
    wg = [A.alloc([128, 8, 384], BF16) for _ in range(2)]
    wbr = [A.alloc([128, 4, 384], BF16) for _ in range(2)]
    gt = A.alloc([128, 512], F32)
    sg = A.alloc([128, 512], F32)
    macc = A.alloc([128, 512], F32)
    mtmp = A.alloc([128, 512], F32)
    w_br_v = [w.rearrange("(kc p) n -> p kc n", p=128) for w in (w_brd_d, w_brs_d, w_brx_d)]
    obr = [o_dsa, o_s5, o_x]
    for fc in range(8):
        wb_ = fc % 2
        for br in range(3):
            c0 = 2760 + br * 1024 + fc * 128
            P.dma("pool", (lambda h, wb_=wb_, br=br, c0=c0: h.dma_start(out=wg[wb_][:, :, br * 128:(br + 1) * 128], in_=w_in_v[:, :, c0:c0 + 128])),
                  writes=[("wg", wb_, br)])
            P.dma("pool", (lambda h, wb_=wb_, br=br, fc=fc: h.dma_start(out=wbr[wb_][:, :, br * 128:(br + 1) * 128],
                                                                       in_=w_br_v[br][:, :, fc * 128:(fc + 1) * 128])),
                  writes=[("wbr", wb_, br)])
        for tc in range(4):
            for br in range(3):
                bG = nb(0, 6)
                for kc in range(8):
                    P.op("pe", (lambda h, bG=bG, kc=kc, wb_=wb_, br=br, tc=tc: h.matmul(ps[bG][:], lhsT=wg[wb_][:, kc, br * 128:(br + 1) * 128],
                                                                                        rhs=xgM[:, kc, tc * 512:(tc + 1) * 512], start=(kc == 0), stop=(kc == 7))),
                         reads=[("wg", wb_, br), ("xg", kc)], writes=[("ps", bG)], sig=(kc == 7), accum=(kc > 0))
                P.op("dve", (lambda h, bG=bG, tc=tc: h.tensor_tensor(out=gt[:], in0=ps[bG][:], in1=rx_bc[:, tc * 512:(tc + 1) * 512], op=ALU.mult)),
                     reads=[("ps", bG)], writes=["gt"])
                P.op("act", lambda h: h.activation(out=sg[:], in_=gt[:], func=AF.Sigmoid), reads=["gt"], writes=["sg"])
                bB = nb(0, 6)
                for k in range(4):
                    if br == 1:
                        rhs = o_s5[:, k, :].rearrange("p (j b) -> p b j", j=8)[:, 64 * tc:64 * tc + 64, :]
                    else:
                        rhs = obr[br][:, k, tc * 512:(tc + 1) * 512]
                    P.op("pe", (lambda h, bB=bB, k=k, wb_=wb_, br=br, rhs=rhs: h.matmul(ps[bB][:], lhsT=wbr[wb_][:, k, br * 128:(br + 1) * 128], rhs=rhs,
                                                                                        start=(k == 0), stop=(k == 3))),
                         reads=[("wbr", wb_, br)], writes=[("ps", bB)], sig=(k == 3), accum=(k > 0))
                if br == 0:
                    P.op("dve", (lambda h, bB=bB: h.tensor_tensor(out=macc[:], in0=ps[bB][:], in1=sg[:], op=ALU.mult)),
                         reads=[("ps", bB), "sg"], writes=["macc"])
                else:
                    P.op("dve", (lambda h, bB=bB: h.tensor_tensor(out=mtmp[:], in0=ps[bB][:], in1=sg[:], op=ALU.mult)),
                         reads=[("ps", bB), "sg"], writes=["mtmp"])
                    if br == 1:
                        P.op("dve", lambda h: h.tensor_tensor(out=macc[:], in0=macc[:], in1=mtmp[:], op=ALU.add), reads=["macc", "mtmp"], writes=["macc"])
                    else:
                        P.op("dve", (lambda h, fc=fc, tc=tc: h.tensor_tensor(out=merged[:, fc, tc * 512:(tc + 1) * 512], in0=macc[:], in1=mtmp[:], op=ALU.add)),
                             reads=["macc", "mtmp"], writes=[("merged", fc, tc)])
                if dbg.get("stop") == "merge1" and br == dbg.get("br", 0):
                    for nm, tns, shp, dt_, keys in (("wg", wg[0][:], [128, 8, 384], BF16, [("wg", 0, i) for i in range(3)]), ("gt", gt[:], [128, 512], F32, ["gt"]),
                                                    ("sg", sg[:], [128, 512], F32, ["sg"]), ("macc", macc[:], [128, 512], F32, ["macc"]),
                                                    ("mtmp", mtmp[:], [128, 512], F32, ["mtmp"]),
                                                    ("xg", xgM[:], [128, 8, S], BF16, XG), ("od", o_dsa[:], [128, 4, S], BF16, []), ("os", o_s5[:], [128, 4, S], BF16, []), ("ox", o_x[:], [128, 4, S], BF16, []), ("wbr", wbr[0][:], [128, 4, 384], BF16, [("wbr", 0, i) for i in range(3)])):
                        t = dump(nm, None, shp, dt_)
                        P.dma("sp", (lambda h, t=t, tns=tns: h.dma_start(out=t, in_=tns)), reads=keys, writes=["dbg_" + nm])
                    P.barrier()
                    P.emit()
                    return nc, dbg_outs
    if dbg.get("stop") == "merge":
        t = dump("merged", None, [128, 8, S], BF16)
        P.dma("sp", (lambda h, t=t: h.dma_start(out=t, in_=merged[:])), reads=[("merged", a, c) for a in range(8) for c in range(4)], writes=["dbg_merged"])
        P.barrier()
        P.emit()
        return nc, dbg_outs
    P.barrier()
    A.release(mM1)
    off_x1 = A.mark()
    x1T = A.alloc([128, 8, S], F32)
    mM0 = A.mark()
    wout = A.alloc([128, 8, 1024], BF16)
    xres = [A.alloc([128, 512], F32) for _ in range(2)]
    w_out_v = w_out_d.rearrange("(kc p) n -> p kc n", p=128)
    for i in range(2):
        load_w(wout[:, :, i * 512:(i + 1) * 512], w_out_v, ("wout", i), (i * 512, (i + 1) * 512))
    xi = 0
    for fc in range(8):
        for tc in range(4):
            xb = xi % 2
            xi += 1
            P.dma("sp", (lambda h, xb=xb, fc=fc, tc=tc: h.dma_start(out=xres[xb][:], in_=xT_d[fc * 128:(fc + 1) * 128, tc * 512:(tc + 1) * 512])),
                  writes=[("xres", xb)])
            b = nb(0, 6)
            for kc in range(8):
                P.op("pe", (lambda h, b=b, kc=kc, fc=fc, tc=tc: h.matmul(ps[b][:], lhsT=wout[:, kc, fc * 128:(fc + 1) * 128],
                                                                         rhs=merged[:, kc, tc * 512:(tc + 1) * 512], start=(kc == 0), stop=(kc == 7))),
                     reads=[(("wout", fc // 4), kc), ("merged", kc, tc)], writes=[("ps", b)], sig=(kc == 7), accum=(kc > 0))
            P.op("dve", (lambda h, b=b, xb=xb, fc=fc, tc=tc: h.tensor_tensor(out=x1T[:, fc, tc * 512:(tc + 1) * 512], in0=ps[b][:], in1=xres[xb][:], op=ALU.add)),
                 reads=[("ps", b), ("xres", xb)], writes=[("x1T", fc, tc)])
    if dbg.get("stop") == "x1":
        t = dump("x1T", None, [128, 8, S])
        P.dma("sp", (lambda h, t=t: h.dma_start(out=t, in_=x1T[:])), reads=[("x1T", a, c) for a in range(8) for c in range(4)], writes=["dbg_x1T"])
        P.barrier()
        P.emit()
        return nc, dbg_outs
    P.barrier()
    A.release(mM0)
    top = A.mark()
    A.off = off_o
    fsq = [A.alloc([128, 512], BF16) for _ in range(2)]
    rf = A.alloc([128, 512], F32)
    wgu = [[A.alloc([128, 8, 256], BF16) for _ in range(2)] for _ in range(2)]
    wdn = [A.alloc([128, NFF, 256], BF16) for _ in range(2)]
    fs = A.alloc([128, 512], F32)
    assert A.off <= off_merged
    A.off = off_merged
    hfT = A.alloc([128, 8, 512], BF16)
    hT = A.alloc([128, NFF, 512], BF16)
    assert A.off <= off_x1
    A.off = top
    fo = [A.alloc([128, 512], F32) for _ in range(2)]
    w_gu_v = [w.rearrange("(kc p) n -> p kc n", p=128) for w in (w_g_d, w_u_d)]
    w_d_v = w_d_d.rearrange("(fk p) n -> p fk n", p=128)
    wi = 0
    di = 0
    oi = 0
    for tq in range(4):
        tsl = slice(tq * 512, (tq + 1) * 512)
        bS = nb(0, 6)
        for kc in range(8):
            sb_ = kc % 2
            P.op("act", (lambda h, kc=kc, sb_=sb_, tsl=tsl: h.activation(out=fsq[sb_][:], in_=x1T[:, kc, tsl], func=AF.Square)),
                 reads=[("x1T", kc, tq)], writes=[("fsq", sb_)])
            P.op("pe", (lambda h, kc=kc, sb_=sb_, bS=bS: h.matmul(ps[bS][:], lhsT=ones_b, rhs=fsq[sb_][:], start=(kc == 0), stop=(kc == 7))),
                 reads=[("fsq", sb_), "cm_b"], writes=[("ps", bS)], sig=True, accum=(kc > 0))
        P.op("act", (lambda h, bS=bS: h.activation(out=rf[:], in_=ps[bS][:], func=AF.Ln, scale=1.0 / D, bias=EPS)), reads=[("ps", bS)], writes=["rf"])
        P.op("act", lambda h: h.activation(out=rf[:], in_=rf[:], func=AF.Exp, scale=-0.5), reads=["rf"], writes=["rf"])
        for kc in range(8):
            P.op("dve", (lambda h, kc=kc, tsl=tsl: h.scalar_tensor_tensor(out=hfT[:, kc, :], in0=x1T[:, kc, tsl], scalar=gvec[:, 16 + kc:17 + kc], in1=rf[:],
                                                                          op0=ALU.mult, op1=ALU.mult)),
                 reads=[("x1T", kc, tq), "rf", "gvec"], writes=[("hfT", kc)])
        for f0 in range(0, NFF, 2):
            wb_ = wi % 2
            wi += 1
            for gu in range(2):
                P.dma("pool", (lambda h, gu=gu, wb_=wb_, f0=f0: h.dma_start(out=wgu[gu][wb_][:], in_=w_gu_v[gu][:, :, f0 * 128:(f0 + 2) * 128])),
                      writes=[("wgu", gu, wb_)])
            for ff in range(f0, f0 + 2):
                bg, bu = nb(0, 6), nb(0, 6)
                for gu, bb_ in ((0, bg), (1, bu)):
                    for kc in range(8):
                        P.op("pe", (lambda h, gu=gu, bb_=bb_, kc=kc, wb_=wb_, ff=ff, f0=f0: h.matmul(
                            ps[bb_][:], lhsT=wgu[gu][wb_][:, kc, (ff - f0) * 128:(ff - f0 + 1) * 128], rhs=hfT[:, kc, :], start=(kc == 0), stop=(kc == 7))),
                            reads=[("wgu", gu, wb_), ("hfT", kc)], writes=[("ps", bb_)], sig=(kc == 7), accum=(kc > 0))
                P.op("act", (lambda h, bg=bg: h.activation(out=fs[:], in_=ps[bg][:], func=AF.Silu)), reads=[("ps", bg)], writes=["fs"])
                P.op("dve", (lambda h, bu=bu, ff=ff: h.tensor_tensor(out=hT[:, ff, :], in0=ps[bu][:], in1=fs[:], op=ALU.mult)),
                     reads=[("ps", bu), "fs"], writes=[("hT", ff)])
        for c0 in range(0, 8, 2):
            db = di % 2
            di += 1
            P.dma("pool", (lambda h, db=db, c0=c0: h.dma_start(out=wdn[db][:], in_=w_d_v[:, :, c0 * 128:(c0 + 2) * 128])), writes=[("wdn", db)])
            for fc in range(c0, c0 + 2):
                b = nb(0, 6)
                for ff in range(NFF):
                    P.op("pe", (lambda h, b=b, ff=ff, db=db, fc=fc, c0=c0: h.matmul(ps[b][:], lhsT=wdn[db][:, ff, (fc - c0) * 128:(fc - c0 + 1) * 128], rhs=hT[:, ff, :],
                                                                                    start=(ff == 0), stop=(ff == NFF - 1))),
                         reads=[("wdn", db), ("hT", ff)], writes=[("ps", b)], sig=(ff == NFF - 1), accum=(ff > 0))
                ob = oi % 2
                oi += 1
                P.op("dve", (lambda h, b=b, ob=ob, fc=fc, tsl=tsl: h.tensor_tensor(out=fo[ob][:], in0=ps[b][:], in1=x1T[:, fc, tsl], op=ALU.add)),
                     reads=[("ps", b), ("x1T", fc, tq)], writes=[("fo", ob)])
                P.dma("sp", (lambda h, ob=ob, fc=fc, tsl=tsl: h.dma_start(out=outT_d[fc * 128:(fc + 1) * 128, tsl], in_=fo[ob][:])),
                      reads=[("fo", ob)], writes=[("out", fc, tq)])
    P.barrier()
    P.emit()
    return nc, dbg_outs


def _host_inputs(inputs):
    f = lambda a: np.ascontiguousarray(np.asarray(a, dtype=np.float32))
    x = f(inputs["x"])
    mem = f(inputs["mem"])
    rel_bias = f(inputs["rel_bias"])
    shared = {}
    shared["w_in"] = f(inputs["w_in"][0])
    shared["w_uv"] = f(np.transpose(inputs["w_uv_dsa"][0], (1, 0, 2)))
    shared["w_glu"] = f(inputs["w_glu"][0])
    shared["w_mem_kv"] = f(inputs["w_mem_kv"][0])
    shared["w_br_dsa"] = f(inputs["w_br_dsa"][0])
    shared["w_br_s5"] = f(inputs["w_br_s5"][0])
    shared["w_br_cross"] = f(inputs["w_br_cross"][0])
    shared["w_out"] = f(inputs["w_out"][0])
    shared["w_ffn_gate"] = f(inputs["w_ffn_gate"][0])
    shared["w_ffn_up"] = f(inputs["w_ffn_up"][0])
    shared["w_ffn_down"] = f(inputs["w_ffn_down"][0])
    gvec = np.zeros((128, 32), np.float32)
    gvec[:, 0:8] = np.asarray(inputs["g_mix_norm"][0]).reshape(8, 128).T
    gvec[:, 8:16] = np.asarray(inputs["g_mem_norm"][0]).reshape(8, 128).T
    gvec[:, 16:24] = np.asarray(inputs["g_ffn_norm"][0]).reshape(8, 128).T
    gvec[:, 24] = np.asarray(inputs["g_q_dsa"][0])
    gvec[:, 25] = np.asarray(inputs["g_q_cross"][0])
    gvec[:, 26] = np.asarray(inputs["g_k_cross"][0])
    shared["gvec"] = gvec
    shared["gkv_bc"] = f(np.broadcast_to(np.asarray(inputs["g_kv_dsa"][0])[None, :], (128, 128)))
    bt = _t5_bucket_table()
    s_i = np.arange(128)[:, None]
    t_i = np.arange(128)[None, :]
    t5 = np.zeros((128, 2, 8, 128), np.float32)
    for diff in (0, 1):
        dist = np.maximum(t_i - s_i + 128 * diff, 0)
        t5[:, diff, :, :] = np.transpose(rel_bias[bt[dist]], (0, 2, 1))
    shared["t5"] = t5
    shared["rb31"] = f(np.broadcast_to(rel_bias[31][None, :], (128, 8)))
    cm = np.zeros((128, 5, 128), np.float32)
    cm[:, 0, :] = np.eye(128)
    cm[:, 1, :] = np.where(t_i.T >= s_i.T, 0.0, NEG)
    il = np.arange(128)[None, :] // 16
    jl = np.arange(128)[:, None] // 16
    cm[:, 2, :] = (il >= jl).astype(np.float32)
    sw = np.zeros((128, 128), np.float32)
    sw[np.arange(64), np.arange(64) + 64] = 1.0
    sw[np.arange(64) + 64, np.arange(64)] = 1.0
    cm[:, 3, :] = sw
    cm[:, 4, :] = 1.0
    shared["cmats"] = cm
    tile2 = lambda a: np.concatenate([a, a], axis=0)
    s5a = np.zeros((128, 3, 32), np.float32)
    s5a[:, 0, :] = tile2(np.asarray(inputs["a_re"][0]).T)
    s5a[:, 1, :] = tile2(np.asarray(inputs["a_im"][0]).T)
    s5a[:, 2, :] = np.broadcast_to(np.asarray(inputs["log_dt"][0])[None, :], (128, 32))
    shared["s5a"] = s5a
    s5b = np.zeros((128, 4, 32, 16), np.float32)
    s5b[:, 0] = tile2(np.transpose(inputs["b_re"][0], (1, 0, 2)))
    s5b[:, 1] = tile2(np.transpose(inputs["b_im"][0], (1, 0, 2)))
    s5b[:, 2] = tile2(np.transpose(inputs["c_re"][0], (2, 0, 1)))
    s5b[:, 3] = tile2(np.transpose(inputs["c_im"][0], (2, 0, 1)))
    shared["s5b"] = s5b
    shared["dsk"] = f(np.tile(np.asarray(inputs["d_skip"][0]).T, (8, 1)))
    in_maps = []
    for b in range(8):
        d = dict(shared)
        d["xT"] = f(x[b].T)
        d["memT"] = f(mem[b].T)
        in_maps.append(d)
    return in_maps


_CACHE = {}


def kernel(**inputs):
    in_maps = _host_inputs(inputs)
    if "nc" not in _CACHE:
        _CACHE["nc"] = build()[0]
    res = run_bass_kernel_spmd(_CACHE["nc"], in_maps, core_ids=list(range(8)))
    out = np.stack([np.ascontiguousarray(res.results[b]["outT"].T) for b in range(8)], axis=0)
    return out.astype(np.float32)
```

```python
import math
import numpy as np
import concourse.bass as bass
import concourse.mybir as mybir
from concourse.bass_utils import run_bass_kernel_spmd

F32 = mybir.dt.float32
BF16 = mybir.dt.bfloat16
AF = mybir.ActivationFunctionType
ALU = mybir.AluOpType

S = 2048
D = 1024
NB = 16
EPS = 1e-6
D_IN = 5832
D_FF = 2816
NFF = 22
NEG = -1.0e30
MBIAS = -30000.0


class Prog:
    ENGS = ("pe", "act", "dve", "pool", "sp")

    def __init__(self, nc, n_dma_sems=8):
        self.nc = nc
        self.streams = {e: [] for e in self.ENGS}
        self.sems = {e: nc.alloc_semaphore("s_" + e) for e in self.ENGS}
        self.cnt = {e: 0 for e in self.ENGS}
        self.known = {e: {} for e in self.ENGS}
        self.bufs = {}
        self.dma_sems = {}
        self.dma_rr = {}
        self.dma_val = {}
        self.semobj = {}
        for q in ("sp", "pool", "act"):
            self.dma_sems[q] = [nc.alloc_semaphore(f"d_{q}{i}") for i in range(n_dma_sems)]
            self.dma_rr[q] = 0
            for s in self.dma_sems[q]:
                self.semobj[s.name] = s
                self.dma_val[s.name] = 0
        for e in self.ENGS:
            self.semobj[self.sems[e].name] = self.sems[e]
        self.pending_pe = False

    def _need(self, eng, tok, waits):
        if tok is None:
            return
        sname, val = tok
        if self.known[eng].get(sname, 0) >= val:
            return
        if waits.get(sname, 0) < val:
            waits[sname] = val

    def _deps(self, eng, reads, writes, skip_writer=False):
        waits = {}
        for k in reads:
            st = self.bufs.get(k)
            if st is not None:
                self._need(eng, st[0], waits)
        for k in writes:
            st = self.bufs.get(k)
            if st is not None:
                if not skip_writer:
                    self._need(eng, st[0], waits)
                for tok in st[1].values():
                    self._need(eng, tok, waits)
        return waits

    def _commit(self, who, tok, reads, writes):
        for k in reads:
            st = self.bufs.setdefault(k, [None, {}])
            st[1][who + ":" + tok[0]] = tok
        for k in writes:
            self.bufs[k] = [tok, {}]

    def op(self, eng, fn, reads=(), writes=(), sig=True, accum=False):
        waits = self._deps(eng, reads, writes, skip_writer=(accum and eng == "pe"))
        if eng == "pe":
            waits.pop(self.sems["pe"].name, None)
        for s, v in waits.items():
            self.known[eng][s] = v
        if sig:
            self.cnt[eng] += 1
            tok = (self.sems[eng].name, self.cnt[eng])
            if eng == "pe":
                self.pending_pe = False
        else:
            assert eng == "pe"
            tok = (self.sems[eng].name, self.cnt[eng] + 1)
            self.pending_pe = True
        self.streams[eng].append((list(waits.items()), fn, (self.sems[eng], 1) if sig else None))
        self._commit(eng, tok, reads, writes)
        return tok

    def dma(self, q, fn, reads=(), writes=()):
        waits = self._deps(q, reads, writes)
        pool = self.dma_sems[q]
        s = pool[self.dma_rr[q] % len(pool)]
        self.dma_rr[q] += 1
        prev = self.dma_val[s.name]
        if prev > 0:
            self._need(q, (s.name, prev), waits)
        for sn, v in waits.items():
            self.known[q][sn] = v
        self.dma_val[s.name] = prev + 16
        tok = (s.name, prev + 16)
        self.streams[q].append((list(waits.items()), fn, (s, 16)))
        self._commit(q, tok, reads, writes)
        return tok

    def barrier(self):
        assert not self.pending_pe
        targets = [(self.sems[e].name, self.cnt[e]) for e in self.ENGS if self.cnt[e] > 0]
        targets += [(sn, v) for sn, v in self.dma_val.items() if v > 0]
        for e in self.ENGS:
            waits = {}
            for tok in targets:
                if e == "pe" and tok[0] == self.sems["pe"].name:
                    continue
                self._need(e, tok, waits)
            for sn, v in waits.items():
                self.known[e][sn] = v
            if waits:
                self.streams[e].append((list(waits.items()), None, None))
        self.bufs = {}

    def emit(self):
        assert not self.pending_pe
        nc = self.nc
        hmap = {"pe": "tensor", "act": "scalar", "dve": "vector", "pool": "gpsimd", "sp": "sync"}
        semobj = self.semobj
        with nc.Block() as block:
            for e in self.ENGS:
                def body(h, stream=self.streams[e]):
                    for waits, fn, inc in stream:
                        for sn, v in waits:
                            h.wait_ge(semobj[sn], v)
                        if fn is not None:
                            ins = fn(h)
                            if inc is not None:
                                ins.then_inc(inc[0], inc[1])
                getattr(block, hmap[e])(body)


class Arena:
    def __init__(self, nc, start=16640, limit=229376):
        self.nc = nc
        self.off = start
        self.limit = limit
        self.n = 0
        self.peak = 0

    def alloc(self, shape, dtype):
        nbytes = int(np.prod(shape[1:])) * (2 if dtype == BF16 else 4)
        nbytes = (nbytes + 63) // 64 * 64
        assert self.off + nbytes <= self.limit, ("SBUF overflow", self.off, nbytes)
        self.n += 1
        t = self.nc.alloc_sbuf_tensor_at(f"sb{self.n}", list(shape), dtype, offset=self.off)
        self.off += nbytes
        self.peak = max(self.peak, self.off)
        return t

    def mark(self):
        return self.off

    def release(self, m):
        if not getattr(self, "no_release", False):
            self.off = m


def _t5_bucket_table():
    d = np.arange(256)
    max_exact = 16
    nf = np.maximum(d, 1).astype(np.float32)
    large = max_exact + (np.log(nf / max_exact) / math.log(128 / max_exact) * (32 - max_exact)).astype(np.int32)
    large = np.minimum(large, 31)
    return np.where(d < max_exact, d, large)


def build(dbg=None):
    dbg = dbg or {}
    nc = bass.Bass("TRN2", target_bir_lowering=False)
    P = Prog(nc)
    A = Arena(nc)
    A.no_release = bool(dbg.get("no_release"))
    dbg_outs = []

    def din(name, shape):
        return nc.dram_tensor(name, list(shape), F32, kind="ExternalInput").ap()

    xT_d = din("xT", [D, S])
    memT_d = din("memT", [D, 256])
    w_in_d = din("w_in", [D, D_IN])
    w_uv_d = din("w_uv", [128, 8, 64])
    w_glu_d = din("w_glu", [512, 512])
    w_kv_d = din("w_mem_kv", [D, D])
    w_brd_d = din("w_br_dsa", [512, D])
    w_brs_d = din("w_br_s5", [512, D])
    w_brx_d = din("w_br_cross", [512, D])
    w_out_d = din("w_out", [D, D])
    w_g_d = din("w_ffn_gate", [D, D_FF])
    w_u_d = din("w_ffn_up", [D, D_FF])
    w_d_d = din("w_ffn_down", [D_FF, D])
    gvec_d = din("gvec", [128, 32])
    gkv_bc_d = din("gkv_bc", [128, 128])
    t5_d = din("t5", [128, 2, 8, 128])
    rb31_d = din("rb31", [128, 8])
    cm_d = din("cmats", [128, 5, 128])
    s5a_d = din("s5a", [128, 3, 32])
    s5b_d = din("s5b", [128, 4, 32, 16])
    dsk_d = din("dsk", [128, 32])
    outT_d = nc.dram_tensor("outT", [D, S], F32, kind="ExternalOutput").ap()
    scr_u = nc.dram_tensor("scr_u", [512, S], BF16).ap()
    scr_y = nc.dram_tensor("scr_y", [512, S], BF16).ap()

    ps = [nc.alloc_psum_tensor(f"ps{i}", [128, 512], F32) for i in range(6)]
    psTs = [nc.alloc_psum_tensor(f"psT{i}", [128, 512], F32) for i in range(2)]

    w_in_v = w_in_d.rearrange("(kc p) n -> p kc n", p=128)

    def dump(name, ap, shape, dt=F32):
        t = nc.dram_tensor("dbg_" + name, list(shape), dt, kind="ExternalOutput").ap()
        dbg_outs.append(("dbg_" + name))
        return t

    cm_f = A.alloc([128, 5, 128], F32)
    ident_f = cm_f[:, 0, :]
    causal_f = cm_f[:, 1, :]
    tmask_f = cm_f[:, 2, :]
    swap_f = cm_f[:, 3, :]
    cm_b = A.alloc([128, 5, 128], BF16)
    ident_b = cm_b[:, 0, :]
    ones_b = cm_b[:, 4, :]
    gvec = A.alloc([128, 32], F32)
    rx_bc = A.alloc([128, S], F32)
    rx_tok = A.alloc([128, NB], F32)
    off_o = A.mark()
    o_dsa = A.alloc([128, 4, S], BF16)

    P.dma("sp", lambda h: h.dma_start(out=cm_f[:], in_=cm_d), writes=["cm_f"])
    P.dma("pool", lambda h: h.dma_start(out=cm_b[:], in_=cm_d), writes=["cm_b"])
    P.dma("sp", lambda h: h.dma_start(out=gvec[:], in_=gvec_d), writes=["gvec"])
    CONST = ["cm_f", "cm_b", "gvec"]

    bank_rr = [0]

    def nb(lo=0, hi=6):
        b = lo + bank_rr[0] % (hi - lo)
        bank_rr[0] += 1
        return b

    def load_w(dst, src3, key, cols, q="pool"):
        kc_n = dst.shape[1]
        c0, c1 = cols
        for kc in range(kc_n):
            P.dma(q, (lambda h, kc=kc: h.dma_start(out=dst[:, kc, 0:c1 - c0], in_=src3[:, kc, c0:c1])),
                  writes=[(key, kc)])

    def build_xg(xg, with_stats, release=False):
        m = A.mark()
        xst = [A.alloc([128, S], F32) for _ in range(2)]
        sq = [A.alloc([128, S], BF16) for _ in range(2)] if with_stats else None
        for kc in range(8):
            b = kc % 2
            P.dma("sp", (lambda h, kc=kc, b=b: h.dma_start(out=xst[b][:], in_=xT_d[kc * 128:(kc + 1) * 128, :])),
                  writes=[("xst", b)])
            if with_stats:
                P.op("act", (lambda h, b=b: h.activation(out=sq[b][:], in_=xst[b][:], func=AF.Square)),
                     reads=[("xst", b)], writes=[("sq", b)])
                for tc in range(4):
                    P.op("pe", (lambda h, b=b, tc=tc, kc=kc: h.matmul(ps[tc][:], lhsT=ones_b, rhs=sq[b][:, tc * 512:(tc + 1) * 512],
                                                                       start=(kc == 0), stop=(kc == 7))),
                         reads=[("sq", b), "cm_b"], writes=[("ps", tc)], sig=(tc == 3), accum=(kc > 0))
            P.op("dve", (lambda h, kc=kc, b=b: h.tensor_scalar(out=xg[:, kc, :], in0=xst[b][:], scalar1=gvec[:, kc:kc + 1],
                                                                scalar2=None, op0=ALU.mult)),
                 reads=[("xst", b), "gvec"], writes=[("xg", kc)])
        if with_stats:
            for tc in range(4):
                P.op("act", (lambda h, tc=tc: h.activation(out=rx_bc[:, tc * 512:(tc + 1) * 512], in_=ps[tc][:], func=AF.Ln,
                                                           scale=1.0 / D, bias=EPS)),
                     reads=[("ps", tc)], writes=[("rxs", tc)])
                P.op("act", (lambda h, tc=tc: h.activation(out=rx_bc[:, tc * 512:(tc + 1) * 512], in_=rx_bc[:, tc * 512:(tc + 1) * 512],
                                                           func=AF.Exp, scale=-0.5)),
                     reads=[("rxs", tc)], writes=[("rx_bc", tc)])
            for tt in range(NB):
                P.op("pe", (lambda h, tt=tt: h.matmul(ps[4][:, tt:tt + 1], lhsT=rx_bc[:, tt * 128:(tt + 1) * 128], rhs=ident_f[:, 0:1],
                                                      start=True, stop=True)),
                     reads=[("rx_bc", tt // 4), "cm_f"], writes=[("ps", 4)], sig=(tt == NB - 1))
            P.op("act", lambda h: h.copy(out=rx_tok[:], in_=ps[4][:, 0:NB]), reads=[("ps", 4)], writes=["rx_tok"])
        if release:
            A.release(m)

    XG = [("xg", kc) for kc in range(8)]
    RX = [("rx_bc", tc) for tc in range(4)]

    def proj_fm(xg, wb, wkey, c0, ncol, tc, bank, rhs_view=None):
        for kc in range(8):
            rhs = xg[:, kc, tc * 512:(tc + 1) * 512] if rhs_view is None else rhs_view(kc, tc)
            P.op("pe", (lambda h, kc=kc, rhs=rhs: h.matmul(ps[bank][0:ncol, :], lhsT=wb[:, kc, c0:c0 + ncol], rhs=rhs,
                                                            start=(kc == 0), stop=(kc == 7))),
                 reads=[(wkey, kc), ("xg", kc)], writes=[("ps", bank)], sig=(kc == 7), accum=(kc > 0))

    hn_slot = [0]
    hn_queue = []

    def head_norm(bank, gcol, dst, tmps, rx_ap, extra_reads, dst_key):
        n = dst.shape[-1]
        sl = hn_slot[0] % 3
        hn_slot[0] += 1
        bank2 = 3 + sl
        tmp_y, tmp_sq, tmp_sd = tmps[0][sl][:, 0:n], tmps[1][sl][:, 0:n], tmps[2][sl][:, 0:n]
        ky, kq, kd = (("hn_y", 0), sl), ("hn_sq", sl), ("hn_sd", sl)
        if rx_ap is None:
            P.op("act", lambda h: h.copy(out=tmp_y, in_=ps[bank][:, 0:n]), reads=[("ps", bank)] + extra_reads, writes=[ky])
        else:
            P.op("dve", lambda h: h.tensor_tensor(out=tmp_y, in0=ps[bank][:, 0:n], in1=rx_ap, op=ALU.mult),
                 reads=[("ps", bank)] + extra_reads, writes=[ky])
        P.op("act", lambda h: h.activation(out=tmp_sq, in_=tmp_y, func=AF.Square), reads=[ky], writes=[kq])

        def part2():
            P.op("pe", lambda h: h.matmul(ps[bank2][:, 0:n], lhsT=ones_b, rhs=tmp_sq, start=True, stop=True),
                 reads=[kq, "cm_b"], writes=[("ps", bank2)])
            P.op("act", lambda h: h.activation(out=tmp_sd, in_=ps[bank2][:, 0:n], func=AF.Ln, scale=1.0 / 128, bias=EPS),
                 reads=[("ps", bank2)], writes=[kd])
            P.op("act", lambda h: h.activation(out=tmp_sd, in_=tmp_sd, func=AF.Exp, scale=-0.5), reads=[kd], writes=[kd])
            P.op("dve", lambda h: h.scalar_tensor_tensor(out=dst, in0=tmp_y, scalar=gvec[:, gcol:gcol + 1], in1=tmp_sd,
                                                         op0=ALU.mult, op1=ALU.mult),
                 reads=[ky, kd, "gvec"], writes=[dst_key])
        hn_queue.append(part2)
        if len(hn_queue) > 2:
            hn_queue.pop(0)()

    def hn_flush():
        while hn_queue:
            hn_queue.pop(0)()

    base_mark = A.mark()

    xg = A.alloc([128, 8, S], BF16)
    build_xg(xg, True, release=True)
    P.op("dve", lambda h: h.tensor_scalar(out=gvec[:, 27:28], in0=gvec[:, 24:25], scalar1=128 ** -0.5, scalar2=None, op0=ALU.mult),
         reads=["gvec"], writes=["gvec"])
    P.op("dve", lambda h: h.tensor_scalar(out=gvec[:, 28:29], in0=gvec[:, 26:27], scalar1=128 ** -0.5, scalar2=None, op0=ALU.mult),
         reads=["gvec"], writes=["gvec"])

    def early(tag, tensors):
        if dbg.get("stop") != tag:
            return False
        for i, (t, shape, dt, keys) in enumerate(tensors):
            d = dump(f"{tag}{i}", None, shape, dt)
            P.dma("sp", (lambda h, d=d, t=t: h.dma_start(out=d, in_=t)), reads=keys, writes=[f"dbg_{tag}{i}"])
        P.barrier()
        P.emit()
        return True

    if early("xg", [(xg[:], [128, 8, S], BF16, XG), (rx_bc[:], [128, S], F32, RX), (rx_tok[:], [128, NB], F32, ["rx_tok"])]):
        return nc, dbg_outs

    def mb_base(m):
        return 128 * (m * (m + 1) // 2)

    def mk_off(c, j):
        return 512 * (2 * c * c + 2 * c + j)

    MBT = A.alloc([128, 512 * 40], BF16)
    mA = A.mark()

    wbi = A.alloc([128, 8, 584], BF16)
    wki = A.alloc([128, 8, 128], BF16)
    qiT = A.alloc([128, 4, S], BF16)
    kiT = A.alloc([128, S], BF16)
    widx = A.alloc([128, NB, 8], F32)
    load_w(wbi, w_in_v, "wbi", (1152, 1736))
    for kc in range(8):
        P.dma("pool", (lambda h, kc=kc: h.dma_start(out=wki[:, kc, 0:64], in_=w_in_v[:, kc, 1664:1728])), writes=[("wki", kc)])
        P.dma("pool", (lambda h, kc=kc: h.dma_start(out=wki[:, kc, 64:128], in_=w_in_v[:, kc, 1664:1728])), writes=[("wki", kc)])
    for j in range(4):
        for tc in range(4):
            b = nb()
            proj_fm(xg, wbi, "wbi", j * 128, 128, tc, b)
            P.op("dve", (lambda h, b=b, j=j, tc=tc: h.tensor_tensor(out=qiT[:, j, tc * 512:(tc + 1) * 512], in0=ps[b][:],
                                                                      in1=rx_bc[:, tc * 512:(tc + 1) * 512], op=ALU.mult)),
                 reads=[("ps", b), ("rx_bc", tc)], writes=[("qiT", j, tc)])
    for tc in range(4):
        b = nb()
        proj_fm(xg, wki, "wki", 0, 128, tc, b)
        P.op("dve", (lambda h, b=b, tc=tc: h.tensor_tensor(out=kiT[:, tc * 512:(tc + 1) * 512], in0=ps[b][:],
                                                            in1=rx_bc[:, tc * 512:(tc + 1) * 512], op=ALU.mult)),
             reads=[("ps", b), ("rx_bc", tc)], writes=[("kiT", tc)])
    bw = nb()
    for tt in range(NB):
        for kc in range(8):
            P.op("pe", (lambda h, tt=tt, kc=kc: h.matmul(ps[bw][:, tt * 8:(tt + 1) * 8], lhsT=xg[:, kc, tt * 128:(tt + 1) * 128],
                                                         rhs=wbi[:, kc, 576:584], start=(kc == 0), stop=(kc == 7))),
                 reads=[("wbi", kc), ("xg", kc)], writes=[("ps", bw)], sig=(kc == 7 and tt == NB - 1), accum=not (kc == 0 and tt == 0))
    P.op("dve", lambda h: h.tensor_tensor(out=widx[:], in0=ps[bw][:, 0:NB * 8].rearrange("p (a b) -> p a b", b=8),
                                          in1=rx_tok[:].unsqueeze(2).to_broadcast([128, NB, 8]), op=ALU.mult),
         reads=[("ps", bw), "rx_tok"], writes=["widx"])

    if early("proj", [(qiT[:], [128, 4, S], BF16, [("qiT", j, tc) for j in range(4) for tc in range(4)]),
                      (kiT[:], [128, S], BF16, [("kiT", tc) for tc in range(4)]), (widx[:], [128, NB, 8], F32, ["widx"])]):
        return nc, dbg_outs
    acc = [A.alloc([128, S], F32) for _ in range(4)]
    work = A.alloc([128, S], F32)
    rl = [A.alloc([128, 512], F32) for _ in range(3)]
    mbt = [A.alloc([128, S], BF16) for _ in range(2)]
    m8 = A.alloc([128, 8], F32)
    thr0 = A.alloc([128, 1], F32)
    bs_lo = A.alloc([128, 1], F32)
    bs_w = A.alloc([128, 1], F32)
    bs_mid = A.alloc([128, 1], F32)
    bs_cnt = A.alloc([128, 1], F32)
    bs_t = A.alloc([128, 1], F32)
    bs_ck = A.alloc([128, 24], F32)
    bs_hk = A.alloc([128, 24], F32)
    for k_ in range(24):
        P.op("dve", (lambda h, k_=k_: h.memset(bs_ck[:, k_:k_ + 1], 2.0 ** (-(k_ + 1)))), writes=["bs_ck"])
    P.op("dve", lambda h: h.memset(thr0[:], -1.0e29), writes=["thr0"])
    a_lo = A.alloc([128, 1], F32)
    a_w = A.alloc([128, 1], F32)
    a_hk = A.alloc([128, 24], F32)
    a_nh2 = A.alloc([128, 24], F32)
    a_nm = A.alloc([128, 2], F32)
    a_cnt = A.alloc([128, 1], F32)
    a_t = A.alloc([128, 1], F32)
    a_m8 = A.alloc([128, 8], F32)
    a_thr = A.alloc([128, 1], F32)
    workA = A.alloc([128, S], BF16)
    KB = 20
    rl_i = [0]

    def score_units(m):
        ab = m % 4
        n = 128 * (m + 1)
        nsc = (n + 511) // 512
        units = []
        for sc in range(nsc):
            w = min(512, n - 512 * sc)
            for hh in range(8):
                def unit(sc=sc, w=w, hh=hh):
                    j, half = hh // 2, hh % 2
                    b = nb()
                    r = rl_i[0] % 3
                    rl_i[0] += 1
                    P.op("pe", (lambda h: h.matmul(ps[b][:, 0:w], lhsT=qiT[64 * half:64 * half + 64, j, m * 128:(m + 1) * 128],
                                                   rhs=kiT[64 * half:64 * half + 64, sc * 512:sc * 512 + w], start=True, stop=True)),
                         reads=[("qiT", j, m // 4), ("kiT", sc)], writes=[("ps", b)])
                    P.op("act", (lambda h: h.activation(out=rl[r][:, 0:w], in_=ps[b][:, 0:w], func=AF.Relu)),
                         reads=[("ps", b)], writes=[("rl", r)])
                    if hh == 0:
                        P.op("dve", (lambda h: h.tensor_scalar(out=acc[ab][:, sc * 512:sc * 512 + w], in0=rl[r][:, 0:w], scalar1=widx[:, m, 0:1],
                                                               scalar2=None, op0=ALU.mult)),
                             reads=[("rl", r), "widx"], writes=[("acc", ab)])
                    else:
                        P.op("dve", (lambda h: h.scalar_tensor_tensor(out=acc[ab][:, sc * 512:sc * 512 + w], in0=rl[r][:, 0:w], scalar=widx[:, m, hh:hh + 1],
                                                                      in1=acc[ab][:, sc * 512:sc * 512 + w], op0=ALU.mult, op1=ALU.add)),
                             reads=[("rl", r), "widx", ("acc", ab)], writes=[("acc", ab)])
                    if sc == nsc - 1 and hh == 7:
                        P.op("dve", (lambda h: h.tensor_tensor(out=acc[ab][:, m * 128:(m + 1) * 128], in0=acc[ab][:, m * 128:(m + 1) * 128],
                                                               in1=causal_f, op=ALU.add)),
                             reads=[("acc", ab), "cm_f"], writes=[("acc", ab)])
                units.append(unit)
        return units

    def bisect_init(m, mx, lo, w_, hk, keyp, on_act):
        ab = m % 4
        n = 128 * (m + 1)
        nv = m * 128
        P.op("dve", (lambda h: h.max(out=mx[:], in_=acc[ab][:, 0:n])), reads=[("acc", ab)], writes=[keyp + "m8"])
        P.op("dve", (lambda h: h.tensor_reduce(out=lo[:], in_=acc[ab][:, 0:nv], axis=mybir.AxisListType.X, op=ALU.min)),
             reads=[("acc", ab)], writes=[keyp + "lo"])
        P.op("dve", lambda h: h.tensor_tensor(out=w_[:], in0=mx[:, 0:1], in1=lo[:], op=ALU.subtract), reads=[keyp + "m8", keyp + "lo"], writes=[keyp + "w"])
        P.op("dve", lambda h: h.tensor_scalar(out=hk[:], in0=bs_ck[:], scalar1=w_[:], scalar2=None, op0=ALU.mult),
             reads=[keyp + "w", "bs_ck"], writes=[keyp + "hk"])
        if on_act:
            P.op("dve", lambda h: h.tensor_scalar(out=a_nh2[:], in0=hk[:], scalar1=-0.5, scalar2=None, op0=ALU.mult), reads=[keyp + "hk"], writes=["a_nh2"])
            P.op("dve", lambda h: h.scalar_tensor_tensor(out=a_nm[:, 0:1], in0=lo[:], scalar=-1.0, in1=hk[:, 0:1], op0=ALU.mult, op1=ALU.subtract),
                 reads=[keyp + "lo", keyp + "hk"], writes=[("a_nm", 0)])
        else:
            P.op("dve", lambda h: h.tensor_tensor(out=bs_mid[:], in0=lo[:], in1=hk[:, 0:1], op=ALU.add), reads=[keyp + "lo", keyp + "hk"], writes=["bs_mid"])

    def dve_iter(m, k_):
        ab = m % 4
        n = 128 * (m + 1)
        P.op("dve", (lambda h: h.tensor_scalar(out=work[:, 0:n], in0=acc[ab][:, 0:n], scalar1=bs_mid[:], scalar2=None,
                                               op0=ALU.is_ge, op1=ALU.add, accum_out=bs_cnt[:])),
             reads=[("acc", ab), "bs_mid"], writes=["work", "bs_cnt"])
        P.op("dve", lambda h: h.tensor_scalar(out=bs_t[:], in0=bs_cnt[:], scalar1=255.5, scalar2=-0.5, op0=ALU.is_ge, op1=ALU.add),
             reads=["bs_cnt"], writes=["bs_t"])
        P.op("dve", (lambda h: h.scalar_tensor_tensor(out=bs_mid[:], in0=bs_t[:], scalar=bs_hk[:, k_:k_ + 1], in1=bs_mid[:],
                                                      op0=ALU.mult, op1=ALU.add)),
             reads=["bs_t", "d_hk", "bs_mid"], writes=["bs_mid"])

    def dve_final(m):
        P.op("dve", (lambda h: h.tensor_tensor(out=m8[:, 7:8], in0=bs_mid[:], in1=bs_hk[:, KB:KB + 1], op=ALU.subtract)),
             reads=["bs_mid", "d_hk", "d_m8"], writes=["d_m8"])

    def act_iter(m, k_):
        ab = m % 4
        n = 128 * (m + 1)
        cur, nxt = k_ % 2, (k_ + 1) % 2
        P.op("act", (lambda h: h.activation(out=workA[:, 0:n], in_=acc[ab][:, 0:n], func=AF.Sign, bias=a_nm[:, cur:cur + 1], scale=1.0,
                                            accum_out=a_cnt[:])),
             reads=[("acc", ab), ("a_nm", cur)], writes=["workA", "a_cnt"])
        P.op("act", (lambda h: h.activation(out=a_t[:], in_=a_cnt[:], func=AF.Sign, bias=float(n) - 511.5, scale=1.0)),
             reads=["a_cnt"], writes=["a_t"])
        P.op("act", (lambda h: h.activation(out=a_nm[:, nxt:nxt + 1], in_=a_t[:], func=AF.Identity,
                                            scale=a_nh2[:, k_:k_ + 1], bias=a_nm[:, cur:cur + 1])),
             reads=["a_t", "a_nh2", ("a_nm", cur)], writes=[("a_nm", nxt)])

    def act_final(m):
        fin = KB % 2
        P.op("dve", (lambda h: h.scalar_tensor_tensor(out=a_thr[:], in0=a_nm[:, fin:fin + 1], scalar=-1.0, in1=a_hk[:, KB:KB + 1],
                                                      op0=ALU.mult, op1=ALU.subtract)),
             reads=[("a_nm", fin), "a_hk"], writes=["a_thr"])

    def finish(m, thr, thrk):
        ab = m % 2
        a4 = m % 4
        n = 128 * (m + 1)
        P.op("dve", (lambda h: h.tensor_scalar(out=mbt[ab][:, 0:n], in0=acc[a4][:, 0:n], scalar1=thr, scalar2=None, op0=ALU.is_ge)),
             reads=[("acc", a4), thrk], writes=[("mbt", ab)])
        for j0 in range(0, m + 1, 4):
            jn = min(4, m + 1 - j0)
            tb = (j0 // 4) % 2
            for jj in range(jn):
                j = j0 + jj
                P.op("pe", (lambda h, j=j, jj=jj, tb=tb: h.matmul(psTs[tb][:, jj * 128:(jj + 1) * 128], lhsT=mbt[ab][:, j * 128:(j + 1) * 128], rhs=ident_b, start=True, stop=True)),
                     reads=[("mbt", ab), "cm_b"], writes=[("psT", tb)], sig=(jj == jn - 1), accum=(jj > 0))
            P.op("act", (lambda h, j0=j0, jn=jn, tb=tb: h.copy(
                out=MBT[:, mk_off(m // 4, j0): mk_off(m // 4, j0) + jn * 512].rearrange("p (a b) -> p a b", b=512)[:, :, (m % 4) * 128:(m % 4) * 128 + 128],
                in_=psTs[tb][:, 0:jn * 128].rearrange("p (a b) -> p a b", b=128))),
                 reads=[("psT", tb)], writes=[("MBT", m)])

    for u in score_units(0) + score_units(1):
        u()
    for pr_ in range(NB // 2):
        m0, m1 = 2 * pr_, 2 * pr_ + 1
        nxt_units = (score_units(m0 + 2) + score_units(m1 + 2)) if pr_ < NB // 2 - 1 else []
        if m0 >= 2:
            bisect_init(m1, a_m8, a_lo, a_w, a_hk, "a_", True)
            bisect_init(m0, m8, bs_lo, bs_w, bs_hk, "d_", False)
            per = (len(nxt_units) + KB - 1) // KB
            for k_ in range(KB):
                act_iter(m1, k_)
                dve_iter(m0, k_)
                for u in nxt_units[k_ * per:(k_ + 1) * per]:
                    u()
            act_final(m1)
            dve_final(m0)
            finish(m0, m8[:, 7:8], "d_m8")
            finish(m1, a_thr[:], "a_thr")
        else:
            finish(m0, thr0[:], "thr0")
            finish(m1, thr0[:], "thr0")
            for u in nxt_units:
                u()
    if early("scores", [(MBT[:], [128, 512 * 40], BF16, [("MBT", m) for m in dbg.get("m_list", range(dbg.get("m_max", NB)))]),
                        (acc[0][:], [128, S], F32, []), (acc[1][:], [128, S], F32, []), (m8[:], [128, 8], F32, ["d_m8"])]):
        return nc, dbg_outs
    if "mbt" in dbg:
        t = dump("mbt", None, [128, 512 * 40], BF16)
        P.dma("sp", (lambda h, t=t: h.dma_start(out=t, in_=MBT[:])), reads=[("MBT", m) for m in range(NB)], writes=["dbg_mbt"])

    P.barrier()
    A.release(mA)

    wbq = [A.alloc([128, 8, 512], BF16) for _ in range(2)]
    wbc = A.alloc([128, 8, 128], BF16)
    wuv = A.alloc([128, 8, 64], BF16)
    qT = A.alloc([128, 8, S], BF16)
    c_tok = A.alloc([128, NB, 128], BF16)
    cT = A.alloc([128, S], BF16)
    t5f = A.alloc([128, 2, 8, 128], F32)
    t5b = A.alloc([128, 2, 8, 128], BF16)
    rb31 = A.alloc([128, 8], F32)
    gkv_bc = A.alloc([128, 128], F32)
    hnY = [A.alloc([128, 512], F32) for _ in range(3)]
    hnQ = [A.alloc([128, 512], BF16) for _ in range(3)]
    hnD = [A.alloc([128, 512], F32) for _ in range(3)]
    tmp_y = hnY[0]
    ss1 = A.alloc([128, 2], F32)
    for i in range(2):
        load_w(wbq[i], w_in_v, ("wbq", i), (512 * i, 512 * i + 512))
    load_w(wbc, w_in_v, "wbc", (1024, 1152))
    P.dma("pool", lambda h: h.dma_start(out=wuv[:], in_=w_uv_d), writes=["wuv"])
    P.dma("sp", lambda h: h.dma_start(out=t5f[:], in_=t5_d), writes=["t5f"])
    P.dma("sp", lambda h: h.dma_start(out=rb31[:], in_=rb31_d), writes=["rb31"])
    P.dma("sp", lambda h: h.dma_start(out=gkv_bc[:], in_=gkv_bc_d), writes=["gkv_bc"])
    P.op("dve", lambda h: h.tensor_tensor(out=t5b[:], in0=t5f[:], in1=rb31[:].unsqueeze(1).unsqueeze(3).to_broadcast([128, 2, 8, 128]),
                                          op=ALU.subtract),
         reads=["t5f", "rb31"], writes=["t5b"])
    for hh in range(8):
        for tc in range(4):
            b = nb(0, 3)
            proj_fm(xg, wbq[hh // 4], ("wbq", hh // 4), (hh % 4) * 128, 128, tc, b)
            head_norm(b, 27, qT[:, hh, tc * 512:(tc + 1) * 512], (hnY, hnQ, hnD),
                      rx_bc[:, tc * 512:(tc + 1) * 512], [("rx_bc", tc)], ("qT", hh, tc))
    hn_flush()
    cy = t5f[:].rearrange("p a b c -> p (a b) c")
    csq = A.alloc([128, NB, 128], BF16)
    css = A.alloc([128, NB], F32)
    for tt in range(NB):
        b = nb(0, 3)
        for kc in range(8):
            P.op("pe", (lambda h, b=b, tt=tt, kc=kc: h.matmul(ps[b][:, 0:128], lhsT=xg[:, kc, tt * 128:(tt + 1) * 128], rhs=wbc[:, kc, :],
                                                              start=(kc == 0), stop=(kc == 7))),
                 reads=[("wbc", kc), ("xg", kc)], writes=[("ps", b)], sig=(kc == 7), accum=(kc > 0))
        P.op("act", (lambda h, b=b, tt=tt: h.activation(out=cy[:, tt, :], in_=ps[b][:, 0:128], func=AF.Copy, scale=rx_tok[:, tt:tt + 1])),
             reads=[("ps", b), "rx_tok"], writes=["t5f"])
    P.op("act", lambda h: h.activation(out=csq[:], in_=cy, func=AF.Square), reads=["t5f"], writes=["csq"])
    P.op("dve", lambda h: h.tensor_reduce(out=css[:], in_=csq[:], axis=mybir.AxisListType.X, op=ALU.add), reads=["csq"], writes=["css"])
    P.op("act", lambda h: h.activation(out=css[:], in_=css[:], func=AF.Ln, scale=1.0 / 128, bias=EPS), reads=["css"], writes=["css"])
    P.op("act", lambda h: h.activation(out=css[:], in_=css[:], func=AF.Exp, scale=-0.5), reads=["css"], writes=["css"])
    P.op("dve", lambda h: h.tensor_tensor(out=cy, in0=cy, in1=gkv_bc[:].unsqueeze(1).to_broadcast([128, NB, 128]), op=ALU.mult),
         reads=["t5f", "gkv_bc"], writes=["t5f"])
    P.op("dve", lambda h: h.tensor_tensor(out=c_tok[:], in0=cy, in1=css[:].unsqueeze(2).to_broadcast([128, NB, 128]), op=ALU.mult),
         reads=["t5f", "css"], writes=[("c_tok", tt) for tt in range(NB)])
    for t0 in range(0, NB, 4):
        tb = (t0 // 4) % 2
        for jj in range(4):
            tt = t0 + jj
            P.op("pe", (lambda h, tt=tt, jj=jj, tb=tb: h.matmul(psTs[tb][:, jj * 128:(jj + 1) * 128], lhsT=c_tok[:, tt, :], rhs=ident_b, start=True, stop=True)),
                 reads=[("c_tok", tt), "cm_b"], writes=[("psT", tb)], sig=(jj == 3), accum=(jj > 0))
        P.op("act", (lambda h, t0=t0, tb=tb: h.copy(out=cT[:, t0 * 128:(t0 + 4) * 128], in_=psTs[tb][:, 0:512])),
             reads=[("psT", tb)], writes=[("cT", t0 // 4)])
    if "qT" in dbg:
        t = dump("qT", None, [128, 8, S], BF16)
        P.dma("sp", (lambda h, t=t: h.dma_start(out=t, in_=qT[:])), reads=[("qT", a, b_) for a in range(8) for b_ in range(4)], writes=["dbg_qT"])
        t2 = dump("cT", None, [128, S], BF16)
        P.dma("sp", (lambda h, t2=t2: h.dma_start(out=t2, in_=cT[:])), reads=[("cT", i) for i in range(4)], writes=["dbg_cT"])

    NPT = 5
    PT = [A.alloc([128, 512], BF16) for _ in range(NPT)]
    PE_ = [A.alloc([128, 512], BF16) for _ in range(NPT)]
    SB = [(ps[0], ("ps", 0)), (ps[1], ("ps", 1)), (ps[2], ("ps", 2)), (psTs[0], ("psT", 0)), (psTs[1], ("psT", 1))]
    rden = A.alloc([128, 512], F32)
    ocp = A.alloc([128, 512], F32)
    onT = [A.alloc([128, 512], BF16) for _ in range(2)]
    BU = 5
    accO = [ps[3][:], ps[3][:]]
    accD = [ps[4][:], ps[4][:]]
    accK = [(("ps", 3), ("ps", 4)), (("ps", 3), ("ps", 4))]
    its = []
    for c in range(4):
        for hh in range(8):
            nj = 4 * c + 4
            for j in range(nj):
                its.append((c, hh, j, nj))
    LAG = 4
    deferred = []

    def front(i):
        c, hh, j, nj = its[i]
        t_lo = max(512 * c, 128 * j)
        co = t_lo - 512 * c
        sbt, sbk = SB[i % 5]
        pb = i % NPT
        adds = []
        for diff in (0, 1):
            m = j + diff
            if 4 * c <= m <= 4 * c + 3:
                adds.append(((m - 4 * c) * 128, t5b[:, diff, hh, :], "t5b"))
        P.op("pe", (lambda h: h.matmul(sbt[:, co:512], lhsT=cT[:, j * 128:(j + 1) * 128], rhs=qT[:, hh, t_lo:512 * c + 512],
                                       start=True, stop=(len(adds) == 0))),
             reads=[("cT", j // 4), ("qT", hh, c)], writes=[sbk], sig=(len(adds) == 0))
        for ai, (col, rhs, key) in enumerate(adds):
            last = ai == len(adds) - 1
            P.op("pe", (lambda h, col=col, rhs=rhs, last=last: h.matmul(sbt[:, col:col + 128], lhsT=ident_b, rhs=rhs, start=False, stop=last)),
                 reads=[key, "cm_b"], writes=[sbk], sig=last, accum=True)
        P.op("act", (lambda h: h.activation(out=PE_[pb][:, 0:512 - co], in_=sbt[:, co:512], func=AF.Exp, bias=rb31[:, hh:hh + 1], scale=1.0)),
             reads=[sbk, "rb31"], writes=[("PE_", pb)])
        P.op("dve", (lambda h: h.tensor_tensor(out=PT[pb][:, 0:512 - co], in0=PE_[pb][:, 0:512 - co],
                                               in1=MBT[:, mk_off(c, j) + co: mk_off(c, j) + 512], op=ALU.mult)),
             reads=[("PE_", pb)] + [("MBT", m) for m in range(4 * c, 4 * c + 4)], writes=[("PT", pb)])

    def back(i):
        c, hh, j, nj = its[i]
        t_lo = max(512 * c, 128 * j)
        co = t_lo - 512 * c
        pb = i % NPT
        hidx = c * 8 + hh
        ab_ = hidx % 2
        aO, aD = accO[ab_], accD[ab_]
        kO, kD = accK[ab_]
        P.op("pe", (lambda h: h.matmul(aO[:, co:512], lhsT=c_tok[:, j, :], rhs=PT[pb][:, 0:512 - co], start=(j == 0), stop=(j == nj - 1))),
             reads=[("c_tok", j), ("PT", pb)], writes=[kO], sig=False, accum=(j > 0))
        P.op("pe", (lambda h: h.matmul(aD[:, co:512], lhsT=ones_b, rhs=PT[pb][:, 0:512 - co], start=(j == 0), stop=(j == nj - 1))),
             reads=["cm_b", ("PT", pb)], writes=[kD], sig=True, accum=(j > 0))
        if j == nj - 1:
            ob = hh % 2
            P.op("act", lambda h: h.activation(out=rden[:], in_=aD, func=AF.Ln), reads=[kD], writes=["rden"])
            P.op("dve", lambda h: h.tensor_copy(out=ocp[:], in_=aO), reads=[kO], writes=["ocp"])
            P.op("act", lambda h: h.activation(out=rden[:], in_=rden[:], func=AF.Exp, scale=-1.0), reads=["rden"], writes=["rden"])
            P.op("dve", (lambda h: h.tensor_tensor(out=onT[ob][:], in0=ocp[:], in1=rden[:], op=ALU.mult)),
                 reads=["ocp", "rden"], writes=[("onT", ob)])

            def epilogue():
                P.op("pe", (lambda h: h.matmul(ps[BU][64 * (hh % 2):64 * (hh % 2) + 64, :], lhsT=wuv[:, hh, :], rhs=onT[ob][:], start=True, stop=True)),
                     reads=["wuv", ("onT", ob)], writes=[("ps", BU, hh % 2)])
                if hh % 2 == 1:
                    P.op("act", (lambda h: h.copy(out=o_dsa[:, hh // 2, c * 512:(c + 1) * 512], in_=ps[BU][:])),
                         reads=[("ps", BU, 0), ("ps", BU, 1)], writes=[("o_dsa", hh // 2, c)])
            deferred.append([2, epilogue])

    for i in range(len(its) + LAG):
        if i < len(its):
            front(i)
        if i - LAG >= 0:
            back(i - LAG)
            for dct in list(deferred):
                dct[0] -= 1
                if dct[0] <= 0:
                    dct[1]()
                    deferred.remove(dct)
    for dct in deferred:
        dct[1]()
    if "o_dsa" in dbg:
        t = dump("o_dsa", None, [128, 4, S], BF16)
        P.dma("sp", (lambda h, t=t: h.dma_start(out=t, in_=o_dsa[:])), reads=[("o_dsa", a, c) for a in range(4) for c in range(4)],
              writes=["dbg_o_dsa"])

    P.barrier()
    A.release(base_mark)
    o_s5 = A.alloc([128, 4, S], BF16)
    o_x = A.alloc([128, 4, S], BF16)
    base_mark = A.mark()
    if dbg.get("od_early"):
        t = dump("od0", None, [128, 4, S], BF16)
        P.dma("sp", (lambda h, t=t: h.dma_start(out=t, in_=o_dsa[:])), writes=["dbg_od0"])
    if dbg.get("stop") == "dsa":
        P.barrier()
        P.emit()
        return nc, dbg_outs


    xgX = A.alloc([128, 8, S], BF16)
    build_xg(xgX, False)
    wbx = A.alloc([128, 8, 512], BF16)
    wbu = A.alloc([128, 8, 512], BF16)
    wkv = A.alloc([128, 8, 1024], BF16)
    memf = A.alloc([128, 8, 256], F32)
    msq = A.alloc([128, 8, 256], BF16)
    memn = A.alloc([128, 8, 256], BF16)
    rm = A.alloc([128, 256], F32)
    khT = A.alloc([128, 4, 256], BF16)
    vtok = A.alloc([128, 2, 512], BF16)
    qxT = A.alloc([128, 4, S], BF16)
    ust = [A.alloc([128, 512], BF16) for _ in range(2)]
    tyX = [A.alloc([128, 512], F32) for _ in range(3)]
    tqX = [A.alloc([128, 512], BF16) for _ in range(3)]
    tdX = [A.alloc([128, 512], F32) for _ in range(3)]
    PTx = [A.alloc([128, 512], BF16) for _ in range(3)]
    rdx = A.alloc([128, 512], F32)
    ocx = A.alloc([128, 512], F32)
    w_kv_v = w_kv_d.rearrange("(kc p) n -> p kc n", p=128)
    if not dbg.get("no_wbx"):
        load_w(wbx, w_in_v, "wbx", (2248, 2760))
    if not dbg.get("no_wbu"):
        load_w(wbu, w_in_v, "wbu", (1736, 2248))
    for i in range(2):
        if not dbg.get("no_wkv"):
            load_w(wkv[:, :, i * 512:(i + 1) * 512], w_kv_v, ("wkv", i), (i * 512, (i + 1) * 512))
    P.dma("sp", lambda h: h.dma_start(out=memf[:], in_=memT_d.rearrange("(kc p) m -> p kc m", p=128)), writes=["memf"])
    if dbg.get("stop") == "x0":
        P.barrier()
        t = dump("od", None, [128, 4, S], BF16)
        P.dma("sp", (lambda h, t=t: h.dma_start(out=t, in_=o_dsa[:])), writes=["dbg_od"])
        P.barrier()
        P.emit()
        return nc, dbg_outs
    P.op("act", lambda h: h.activation(out=msq[:], in_=memf[:], func=AF.Square), reads=["memf"], writes=["msq"])
    bm = nb(0, 3)
    for kc in range(8):
        P.op("pe", (lambda h, kc=kc: h.matmul(ps[bm][:, 0:256], lhsT=ones_b, rhs=msq[:, kc, :], start=(kc == 0), stop=(kc == 7))),
             reads=["msq", "cm_b"], writes=[("ps", bm)], sig=(kc == 7), accum=(kc > 0))
    P.op("act", lambda h: h.activation(out=rm[:], in_=ps[bm][:, 0:256], func=AF.Ln, scale=1.0 / D, bias=EPS), reads=[("ps", bm)], writes=["rm"])
    P.op("act", lambda h: h.activation(out=rm[:], in_=rm[:], func=AF.Exp, scale=-0.5), reads=["rm"], writes=["rm"])
    for kc in range(8):
        P.op("dve", (lambda h, kc=kc: h.scalar_tensor_tensor(out=memn[:, kc, :], in0=memf[:, kc, :], scalar=gvec[:, 8 + kc:9 + kc], in1=rm[:],
                                                             op0=ALU.mult, op1=ALU.mult)),
             reads=["memf", "rm", "gvec"], writes=[("memn", kc)])
    for hh in range(4):
        b = nb(0, 3)
        for kc in range(8):
            P.op("pe", (lambda h, b=b, hh=hh, kc=kc: h.matmul(ps[b][:, 0:256], lhsT=wkv[:, kc, hh * 128:(hh + 1) * 128], rhs=memn[:, kc, :],
                                                              start=(kc == 0), stop=(kc == 7))),
                 reads=[(("wkv", 0), kc), ("memn", kc)], writes=[("ps", b)], sig=(kc == 7), accum=(kc > 0))
        head_norm(b, 28, khT[:, hh, :], (tyX, tqX, tdX), None, [], ("khT", hh))
    hn_flush()
    for mb in range(2):
        b = nb(0, 3)
        for kc in range(8):
            P.op("pe", (lambda h, b=b, mb=mb, kc=kc: h.matmul(ps[b][:], lhsT=memn[:, kc, mb * 128:(mb + 1) * 128], rhs=wkv[:, kc, 512:1024],
                                                              start=(kc == 0), stop=(kc == 7))),
                 reads=[(("wkv", 1), kc), ("memn", kc)], writes=[("ps", b)], sig=(kc == 7), accum=(kc > 0))
        P.op("act", (lambda h, b=b, mb=mb: h.copy(out=vtok[:, mb, :], in_=ps[b][:])), reads=[("ps", b)], writes=[("vtok", mb)])
    for hh in range(4):
        for tc in range(4):
            b = nb(0, 3)
            proj_fm(xgX, wbx, "wbx", hh * 128, 128, tc, b)
            head_norm(b, 25, qxT[:, hh, tc * 512:(tc + 1) * 512], (tyX, tqX, tdX),
                      rx_bc[:, tc * 512:(tc + 1) * 512], [("rx_bc", tc)], ("qxT", hh, tc))
    hn_flush()
    ui = 0
    for ch in range(4):
        for tcp in range(4):
            b = nb(0, 3)
            ub = ui % 2
            ui += 1
            proj_fm(xgX, wbu, "wbu", ch * 128, 128, tcp, b,
                    rhs_view=(lambda kc, tcp: xgX[:, kc, :].rearrange("p (b j) -> p j b", j=8)[:, 2 * tcp:2 * tcp + 2, :]))
            P.op("dve", (lambda h, b=b, ub=ub, tcp=tcp: h.tensor_tensor(
                out=ust[ub][:].rearrange("p (j b) -> p j b", j=2), in0=ps[b][:].rearrange("p (j b) -> p j b", j=2),
                in1=rx_bc[:].rearrange("p (b j) -> p j b", j=8)[:, 2 * tcp:2 * tcp + 2, :], op=ALU.mult)),
                reads=[("ps", b)] + RX, writes=[("ust", ub)])
            P.dma("sp", (lambda h, ub=ub, ch=ch, tcp=tcp: h.dma_start(out=scr_u[ch * 128:(ch + 1) * 128, tcp * 512:(tcp + 1) * 512], in_=ust[ub][:])),
                  reads=[("ust", ub)], writes=[("scr_u", ch, tcp)])
    BO, BD = 3, 4
    itx = [(c, hh, mb) for c in range(4) for hh in range(4) for mb in range(2)]
    LAGX = 2

    def xfront(i):
        c, hh, mb = itx[i]
        bs = i % 3
        pb = i % 3
        P.op("pe", (lambda h: h.matmul(ps[bs][:], lhsT=khT[:, hh, mb * 128:(mb + 1) * 128], rhs=qxT[:, hh, c * 512:(c + 1) * 512], start=True, stop=True)),
             reads=[("khT", hh), ("qxT", hh, c)], writes=[("ps", bs)])
        P.op("act", (lambda h: h.activation(out=PTx[pb][:], in_=ps[bs][:], func=AF.Exp)), reads=[("ps", bs)], writes=[("PT", pb)])

    def xback(i):
        c, hh, mb = itx[i]
        pb = i % 3
        P.op("pe", (lambda h: h.matmul(ps[BO][:], lhsT=vtok[:, mb, hh * 128:(hh + 1) * 128], rhs=PTx[pb][:], start=(mb == 0), stop=(mb == 1))),
             reads=[("vtok", mb), ("PT", pb)], writes=[("ps", BO)], sig=False, accum=(mb > 0))
        P.op("pe", (lambda h: h.matmul(ps[BD][:], lhsT=ones_b, rhs=PTx[pb][:], start=(mb == 0), stop=(mb == 1))),
             reads=["cm_b", ("PT", pb)], writes=[("ps", BD)], sig=True, accum=(mb > 0))
        if mb == 1:
            P.op("act", lambda h: h.activation(out=rdx[:], in_=ps[BD][:], func=AF.Ln), reads=[("ps", BD)], writes=["rden"])
            P.op("dve", lambda h: h.tensor_copy(out=ocx[:], in_=ps[BO][:]), reads=[("ps", BO)], writes=["ocx"])
            P.op("act", lambda h: h.activation(out=rdx[:], in_=rdx[:], func=AF.Exp, scale=-1.0), reads=["rden"], writes=["rden"])
            P.op("dve", (lambda h: h.tensor_tensor(out=o_x[:, hh, c * 512:(c + 1) * 512], in0=ocx[:], in1=rdx[:], op=ALU.mult)),
                 reads=["ocx", "rden"], writes=[("o_x", hh, c)])

    for i in range(len(itx) + LAGX):
        if i < len(itx):
            xfront(i)
        if i - LAGX >= 0:
            xback(i - LAGX)
    if dbg.get("stop") == "cross":
        t = dump("od", None, [128, 4, S], BF16)
        P.dma("sp", (lambda h, t=t: h.dma_start(out=t, in_=o_dsa[:])), writes=["dbg_od"])
        t = dump("o_x", None, [128, 4, S], BF16)
        P.dma("sp", (lambda h, t=t: h.dma_start(out=t, in_=o_x[:])), reads=[("o_x", a, c) for a in range(4) for c in range(4)], writes=["dbg_o_x"])
        t2 = dump("scr_u", None, [512, S], BF16)
        P.dma("sp", (lambda h, t2=t2: h.dma_start(out=t2, in_=scr_u)), reads=[("scr_u", a, c) for a in range(4) for c in range(4)], writes=["dbg_scr_u"])
        P.barrier()
        P.emit()
        return nc, dbg_outs
    P.barrier()
    A.release(base_mark)

    s5a = A.alloc([128, 3, 32], F32)
    s5b = A.alloc([128, 4, 32, 16], F32)
    dsk = A.alloc([128, 32], F32)
    TB = [A.alloc([128, 2, 8, 32], F32) for _ in range(4)]
    TK = A.alloc([128, 8, 2, 32], F32)
    cw = A.alloc([128, 16, 2, 32], F32)
    bb = A.alloc([128, 2, 32, 16], F32)
    W1 = A.alloc([128, 32, 128], BF16)
    W2 = A.alloc([128, 32, 128], BF16)
    Tm = A.alloc([128, 32, 128], BF16)
    U8 = A.alloc([128, 32, 256], BF16)
    X = A.alloc([128, 32, 256], BF16)
    mS = A.alloc([128, 1], F32)
    mS1 = A.mark()
    W1T = A.alloc([128, 32, 128], BF16)
    Lm = A.alloc([128, 32, 128], BF16)
    Rm = A.alloc([128, 32, 128], BF16)
    tA = A.alloc([128, 32, 128], F32)
    tB = A.alloc([128, 32, 128], F32)
    P.dma("sp", lambda h: h.dma_start(out=s5a[:], in_=s5a_d), writes=["s5a"])
    P.dma("sp", lambda h: h.dma_start(out=s5b[:], in_=s5b_d), writes=["s5b"])
    P.dma("sp", lambda h: h.dma_start(out=dsk[:], in_=dsk_d), writes=["dsk"])
    for jl in range(8):
        P.dma("sp", (lambda h, jl=jl: h.dma_start(out=U8[jl * 16:(jl + 1) * 16, :, :],
                                                  in_=scr_u.rearrange("(g c) (j b) -> j c g b", c=16, j=8)[jl])),
              reads=[("scr_u", a, c) for a in range(4) for c in range(4)], writes=[("U8", jl)])
    U8K = [("U8", jl) for jl in range(8)]

    def V(i):
        return cw[:, i, :, :]

    def tt(out, a, b_, op, rk, wk):
        P.op("dve", lambda h: h.tensor_tensor(out=out, in0=a, in1=b_, op=op), reads=rk, writes=wk)

    def cmul(dst, a, b_, ka, kb, kd):
        t1, t2 = V(14), V(15)
        tt(t1[:, 0, :], a[:, 0, :], b_[:, 0, :], ALU.mult, [ka, kb], ["cw_t1a"])
        tt(t1[:, 1, :], a[:, 1, :], b_[:, 1, :], ALU.mult, [ka, kb], ["cw_t1b"])
        tt(t2[:, 0, :], a[:, 0, :], b_[:, 1, :], ALU.mult, [ka, kb], ["cw_t2a"])
        tt(t2[:, 1, :], a[:, 1, :], b_[:, 0, :], ALU.mult, [ka, kb], ["cw_t2b"])
        tt(dst[:, 0, :], t1[:, 0, :], t1[:, 1, :], ALU.subtract, ["cw_t1a", "cw_t1b"], [kd])
        tt(dst[:, 1, :], t2[:, 0, :], t2[:, 1, :], ALU.add, ["cw_t2a", "cw_t2b", kd], [kd])

    a_re, a_im, ldt = s5a[:, 0, :], s5a[:, 1, :], s5a[:, 2, :]
    dtv = V(0)[:, 0, :]
    adr = V(0)[:, 1, :]
    adi = V(1)[:, 0, :]
    P.op("act", lambda h: h.activation(out=dtv, in_=ldt, func=AF.Exp), reads=["s5a"], writes=["dtv"])
    tt(adr, a_re, dtv, ALU.mult, ["s5a", "dtv"], ["adr"])
    tt(adi, a_im, dtv, ALU.mult, ["s5a", "dtv"], ["adi"])
    mag, magn, cs, sn = V(2)[:, 0, :], V(2)[:, 1, :], V(3)[:, 0, :], V(3)[:, 1, :]
    P.op("dve", lambda h: h.memset(mS[:], math.pi / 2), writes=["mS"])
    P.op("act", lambda h: h.activation(out=mag, in_=adr, func=AF.Exp, scale=1.0 / 16), reads=["adr"], writes=["mag"])
    P.op("act", lambda h: h.activation(out=magn, in_=adr, func=AF.Exp, scale=-1.0 / 16), reads=["adr"], writes=["magn"])
    P.op("act", lambda h: h.activation(out=cs, in_=adi, func=AF.Sin, scale=1.0 / 16, bias=mS[:]), reads=["adi", "mS"], writes=["cs"])
    P.op("act", lambda h: h.activation(out=sn, in_=adi, func=AF.Sin, scale=1.0 / 16), reads=["adi"], writes=["sn"])
    mu, nu = V(4), V(5)
    tt(mu[:, 0, :], mag, cs, ALU.mult, ["mag", "cs"], ["mu"])
    tt(mu[:, 1, :], mag, sn, ALU.mult, ["mag", "sn", "mu"], ["mu"])
    tt(nu[:, 0, :], magn, cs, ALU.mult, ["magn", "cs"], ["nu"])
    P.op("dve", lambda h: h.scalar_tensor_tensor(out=nu[:, 1, :], in0=magn, scalar=-1.0, in1=sn, op0=ALU.mult, op1=ALU.mult),
         reads=["magn", "sn", "nu"], writes=["nu"])
    def pw_slot(t, slot):
        return TB[t][:, :, slot, :]
    TW1, TW2, TL, TR = 0, 1, 2, 3
    cur, ck = mu, "mu"
    for i in range(4):
        dst = V(6 + (i % 2)) if i < 3 else pw_slot(TR, 1)
        kd = f"sqp{i}" if i < 3 else ("P", 1)
        cmul(dst, cur, cur, ck, ck, kd)
        cur, ck = dst, kd
    cur, ck = nu, "nu"
    for i in range(4):
        dst = V(8 + (i % 2)) if i < 3 else pw_slot(TL, 1)
        kd = f"sqn{i}" if i < 3 else ("N", 1)
        cmul(dst, cur, cur, ck, ck, kd)
        cur, ck = dst, kd
    Pp = {1: pw_slot(TR, 1)}
    Np = {1: pw_slot(TL, 1)}
    for t_ in (TR, TL, TW1):
        sl = 7 if t_ == TW1 else 0
        P.op("dve", (lambda h, t_=t_, sl=sl: h.memset(TB[t_][:, 0, sl, :], 1.0)), writes=[("one", t_, 0)])
        P.op("dve", (lambda h, t_=t_, sl=sl: h.memset(TB[t_][:, 1, sl, :], 0.0)), writes=[("one", t_, 1)])
    for k, (a, b_) in ((2, (1, 1)), (3, (2, 1)), (4, (2, 2)), (5, (4, 1)), (6, (4, 2)), (7, (4, 3))):
        Pp[k] = pw_slot(TR, k)
        cmul(Pp[k], Pp[a], Pp[b_], ("P", a), ("P", b_), ("P", k))
        Np[k] = pw_slot(TL, k)
        cmul(Np[k], Np[a], Np[b_], ("N", a), ("N", b_), ("N", k))
    Pp[8] = pw_slot(TW2, 7)
    cmul(Pp[8], Pp[4], Pp[4], ("P", 4), ("P", 4), ("P", 8))
    PK = [("P", k) for k in range(1, 9)]
    NK = [("N", k) for k in range(1, 8)]
    P.op("dve", lambda h: h.tensor_copy(out=TB[TW2][:, :, 0:7, :], in_=TB[TR][:, :, 1:8, :]), reads=PK, writes=["TW2"])
    for jl in range(7):
        P.op("dve", (lambda h, jl=jl: h.tensor_copy(out=TB[TW1][:, :, jl, :], in_=TB[TR][:, :, 7 - jl, :])), reads=PK, writes=[("TW1", jl)])
    TW1K = [("TW1", jl) for jl in range(7)] + [("one", TW1, 0), ("one", TW1, 1)]
    TRK = PK + [("one", TR, 0), ("one", TR, 1)]
    TLK = NK + [("one", TL, 0), ("one", TL, 1)]
    TW2K = ["TW2", ("P", 8)]
    P.op("dve", lambda h: h.tensor_copy(out=TK[:, 0, :, :], in_=Pp[8]), reads=[("P", 8)], writes=[("TK", 0)])
    for l in range(1, 8):
        cmul(TK[:, l, :, :], TK[:, l - 1, :, :], TK[:, l - 1, :, :], ("TK", l - 1), ("TK", l - 1), ("TK", l))
    P.op("dve", lambda h: h.tensor_scalar(out=TK[64:128, :, 1, :], in0=TK[64:128, :, 1, :], scalar1=-1.0, scalar2=None, op0=ALU.mult),
         reads=[("TK", l) for l in range(8)], writes=["TKs"])
    num, qv = V(10), V(11)
    den = V(12)[:, 0, :]
    P.op("dve", lambda h: h.tensor_scalar(out=num[:, 0, :], in0=Pp[1][:, 0, :], scalar1=-1.0, scalar2=None, op0=ALU.add),
         reads=[("P", 1)], writes=["num"])
    P.op("dve", lambda h: h.tensor_copy(out=num[:, 1, :], in_=Pp[1][:, 1, :]), reads=[("P", 1), "num"], writes=["num"])
    tt(den, a_re, a_re, ALU.mult, ["s5a"], ["den"])
    tt(V(12)[:, 1, :], a_im, a_im, ALU.mult, ["s5a"], ["den2"])
    tt(den, den, V(12)[:, 1, :], ALU.add, ["den", "den2"], ["den"])
    P.op("dve", lambda h: h.reciprocal(out=den, in_=den), reads=["den"], writes=["den"])
    t13 = V(13)
    tt(t13[:, 0, :], num[:, 0, :], a_re, ALU.mult, ["num", "s5a"], ["t13a"])
    tt(t13[:, 1, :], num[:, 1, :], a_im, ALU.mult, ["num", "s5a"], ["t13b"])
    tt(qv[:, 0, :], t13[:, 0, :], t13[:, 1, :], ALU.add, ["t13a", "t13b"], ["qv0"])
    tt(qv[:, 0, :], qv[:, 0, :], den, ALU.mult, ["qv0", "den"], ["qv0"])
    tt(t13[:, 0, :], num[:, 1, :], a_re, ALU.mult, ["num", "s5a", "qv0"], ["t13a"])
    tt(t13[:, 1, :], num[:, 0, :], a_im, ALU.mult, ["num", "s5a", "qv0"], ["t13b"])
    tt(qv[:, 1, :], t13[:, 0, :], t13[:, 1, :], ALU.subtract, ["t13a", "t13b"], ["qv1"])
    tt(qv[:, 1, :], qv[:, 1, :], den, ALU.mult, ["qv1", "den"], ["qv1"])
    q_re = qv[:, 0, :].unsqueeze(2).to_broadcast([128, 32, 16])
    q_im = qv[:, 1, :].unsqueeze(2).to_broadcast([128, 32, 16])
    B_re, B_im, C_re, C_im = s5b[:, 0], s5b[:, 1], s5b[:, 2], s5b[:, 3]
    tAv = tA[:].rearrange("p g (j c) -> p g j c", c=16)
    tBv = tB[:].rearrange("p g (j c) -> p g j c", c=16)
    tt(tAv[:, :, 0, :], q_re, B_re, ALU.mult, ["qv0", "s5b"], ["tA"])
    tt(tBv[:, :, 0, :], q_im, B_im, ALU.mult, ["qv1", "s5b"], ["tB"])
    tt(bb[:, 0], tAv[:, :, 0, :], tBv[:, :, 0, :], ALU.subtract, ["tA", "tB"], ["bb0"])
    tt(tAv[:, :, 0, :], q_re, B_im, ALU.mult, ["qv0", "s5b", "bb0"], ["tA"])
    tt(tBv[:, :, 0, :], q_im, B_re, ALU.mult, ["qv1", "s5b", "bb0"], ["tB"])
    tt(bb[:, 1], tAv[:, :, 0, :], tBv[:, :, 0, :], ALU.add, ["tA", "tB"], ["bb1"])

    def build_mat(dst, tbl, tkeys, v_re, v_im, vkeys, mode, dkey):
        dv = dst[:].rearrange("p g (j c) -> p g j c", c=16)
        for half in range(2):
            pr = slice(64 * half, 64 * half + 64)
            Tre = TB[tbl][pr, 0, :, :].rearrange("p s g -> p g s").unsqueeze(3).to_broadcast([64, 32, 8, 16])
            Tim = TB[tbl][pr, 1, :, :].rearrange("p s g -> p g s").unsqueeze(3).to_broadcast([64, 32, 8, 16])
            va, vb = (v_re, v_im) if half == 0 else (v_im, v_re)
            Va = va[pr].unsqueeze(2).to_broadcast([64, 32, 8, 16])
            Vb = vb[pr].unsqueeze(2).to_broadcast([64, 32, 8, 16])
            tt(tAv[pr], Tre, Va, ALU.mult, tkeys + vkeys + [dkey], [("tA", half)])
            tt(tBv[pr], Tim, Vb, ALU.mult, tkeys + vkeys + [dkey], [("tB", half)])
            if half == 0:
                tt(dv[pr], tAv[pr], tBv[pr], ALU.subtract, [("tA", 0), ("tB", 0)], [(dkey, 0)])
            elif mode == "B":
                tt(dv[pr], tAv[pr], tBv[pr], ALU.add, [("tA", 1), ("tB", 1)], [(dkey, 1)])
            else:
                P.op("dve", (lambda h, pr=pr: h.scalar_tensor_tensor(out=dv[pr], in0=tAv[pr], scalar=-1.0, in1=tBv[pr],
                                                                      op0=ALU.mult, op1=ALU.subtract)),
                     reads=[("tA", 1), ("tB", 1)], writes=[(dkey, 1)])

    P.op("dve", lambda h: h.memset(mS[:], 0.0), reads=["tA", "tB", "mS"], writes=[("tA", 0), ("tA", 1), ("tB", 0), ("tB", 1), "mS"])
    build_mat(W1T, TW1, TW1K, bb[:, 0], bb[:, 1], ["bb0", "bb1"], "B", "W1T")
    build_mat(W2, TW2, TW2K, C_re, C_im, ["s5b"], "C", "W2")
    build_mat(Lm, TL, TLK, bb[:, 0], bb[:, 1], ["bb0", "bb1"], "B", "Lm")
    build_mat(Rm, TR, TRK, C_re, C_im, ["s5b"], "C", "Rm")
    for g0 in range(0, 32, 4):
        tb = (g0 // 4) % 2
        for jj in range(4):
            P.op("pe", (lambda h, g0=g0, jj=jj, tb=tb: h.matmul(psTs[tb][:, jj * 128:(jj + 1) * 128], lhsT=W1T[:, g0 + jj, :], rhs=ident_b, start=True, stop=True)),
                 reads=[("W1T", 0), ("W1T", 1), "cm_b"], writes=[("psT", tb)], sig=(jj == 3), accum=(jj > 0))
        P.op("act", (lambda h, g0=g0, tb=tb: h.copy(out=W1[:, g0:g0 + 4, :], in_=psTs[tb][:, 0:512].rearrange("p (a b) -> p a b", b=128))),
             reads=[("psT", tb)], writes=[("W1", g0 // 4)])
    for g0 in range(0, 32, 4):
        b = nb(0, 3)
        for jj in range(4):
            P.op("pe", (lambda h, g0=g0, jj=jj, b=b: h.matmul(ps[b][:, jj * 128:(jj + 1) * 128], lhsT=Lm[:, g0 + jj, :], rhs=Rm[:, g0 + jj, :],
                                                              start=True, stop=True)),
                 reads=[("Lm", 0), ("Lm", 1), ("Rm", 0), ("Rm", 1)], writes=[("ps", b)], sig=(jj == 3), accum=(jj > 0))
        P.op("dve", (lambda h, b=b, g0=g0: h.tensor_tensor(out=tA[:, g0:g0 + 4, :], in0=ps[b][:].rearrange("p (a b) -> p a b", b=128),
                                                           in1=tmask_f.unsqueeze(1).to_broadcast([128, 4, 128]), op=ALU.mult)),
             reads=[("ps", b), "cm_f", ("tA", 0), ("tA", 1)], writes=[("tAm", g0 // 4)])
        for jj in range(4):
            g = g0 + jj
            P.op("dve", (lambda h, g=g: h.scalar_tensor_tensor(out=Tm[:, g, :], in0=ident_f, scalar=dsk[:, g:g + 1], in1=tA[:, g, :],
                                                               op0=ALU.mult, op1=ALU.add)),
                 reads=[("tAm", g0 // 4), "dsk", "cm_f"], writes=[("Tm", g0 // 4)])
    if dbg.get("stop") == "s5pre":
        for nm, tns, keys in (("W1", W1, [("W1", i) for i in range(8)]), ("W2", W2, [("W2", 0), ("W2", 1)]), ("Tm", Tm, [("Tm", i) for i in range(8)])):
            t = dump(nm, None, [128, 32, 128], BF16)
            P.dma("sp", (lambda h, t=t, tns=tns: h.dma_start(out=t, in_=tns[:])), reads=keys, writes=["dbg_" + nm])
        t = dump("TK", None, [128, 8, 2, 32])
        P.dma("sp", (lambda h, t=t: h.dma_start(out=t, in_=TK[:])), reads=["TKs"], writes=["dbg_TK"])
        t = dump("TB", None, [128, 2, 8, 32])
        P.dma("sp", (lambda h, t=t: h.dma_start(out=t, in_=TB[TR][:])), reads=TRK, writes=["dbg_TB"])
        P.barrier()
        P.emit()
        return nc, dbg_outs
    P.barrier()
    A.release(mS1)

    Yst = A.alloc([128, 32, 256], BF16)
    gq = A.alloc([128, 512], F32)
    gz = A.alloc([128, 512], F32)
    gs = A.alloc([128, 512], F32)
    mS2 = A.mark()
    Rl = [A.alloc([128, 32, 128], BF16) for _ in range(2)]
    rt1 = A.alloc([128, 32, 128], BF16)
    rt2 = A.alloc([128, 32, 128], BF16)
    XK = lambda gp: ("X", gp)
    for gp in range(16):
        b = nb(0, 3)
        for gi in range(2):
            g = 2 * gp + gi
            P.op("pe", (lambda h, b=b, gi=gi, g=g: h.matmul(ps[b][:, gi * 256:(gi + 1) * 256], lhsT=W1[:, g, :], rhs=U8[:, g, :], start=True, stop=True)),
                 reads=[("W1", g // 4)] + U8K, writes=[("ps", b)], sig=(gi == 1), accum=(gi > 0))
        P.op("act", (lambda h, b=b, gp=gp: h.copy(out=X[:, 2 * gp:2 * gp + 2, :], in_=ps[b][:].rearrange("p (a b) -> p a b", b=256))),
             reads=[("ps", b)], writes=[XK(gp)])
    for l in range(8):
        d = 1 << l
        R = Rl[l % 2]
        P.op("dve", (lambda h, l=l: h.tensor_tensor(out=rt1[:], in0=ident_f.unsqueeze(1).to_broadcast([128, 32, 128]),
                                                    in1=TK[:, l, 0, :].unsqueeze(2).to_broadcast([128, 32, 128]), op=ALU.mult)),
             reads=["cm_f", "TKs"], writes=["rt1"])
        P.op("dve", (lambda h, l=l: h.tensor_tensor(out=rt2[:], in0=swap_f.unsqueeze(1).to_broadcast([128, 32, 128]),
                                                    in1=TK[:, l, 1, :].unsqueeze(2).to_broadcast([128, 32, 128]), op=ALU.mult)),
             reads=["cm_f", "TKs"], writes=["rt2"])
        P.op("dve", (lambda h, R=R: h.tensor_tensor(out=R[:], in0=rt1[:], in1=rt2[:], op=ALU.add)),
             reads=["rt1", "rt2"], writes=[("R", l % 2)])
        for gp in range(16):
            b = nb(0, 3)
            for gi in range(2):
                g = 2 * gp + gi
                P.op("pe", (lambda h, b=b, gi=gi, g=g, d=d, R=R: h.matmul(ps[b][:, gi * 256 + d:(gi + 1) * 256], lhsT=R[:, g, :], rhs=X[:, g, 0:256 - d],
                                                                          start=True, stop=True)),
                     reads=[("R", l % 2), XK(gp)], writes=[("ps", b)], sig=(gi == 1), accum=(gi > 0))
            P.op("dve", (lambda h, b=b, gp=gp, d=d: h.tensor_tensor(out=X[:, 2 * gp:2 * gp + 2, d:256], in0=X[:, 2 * gp:2 * gp + 2, d:256],
                                                                   in1=ps[b][:].rearrange("p (a b) -> p a b", b=256)[:, :, d:256], op=ALU.add)),
                 reads=[("ps", b), XK(gp)], writes=[XK(gp)])
    for gp in range(16):
        b = nb(0, 3)
        for gi in range(2):
            g = 2 * gp + gi
            P.op("pe", (lambda h, b=b, gi=gi, g=g: h.matmul(ps[b][:, gi * 256:(gi + 1) * 256], lhsT=Tm[:, g, :], rhs=U8[:, g, :], start=True, stop=False)),
                 reads=[("Tm", g // 4)] + U8K, writes=[("ps", b)], sig=False, accum=(gi > 0))
            P.op("pe", (lambda h, b=b, gi=gi, g=g: h.matmul(ps[b][:, gi * 256 + 1:(gi + 1) * 256], lhsT=W2[:, g, :], rhs=X[:, g, 0:255], start=False, stop=True)),
                 reads=[("W2", 0), ("W2", 1), XK(gp)], writes=[("ps", b)], sig=(gi == 1), accum=True)
        P.op("act", (lambda h, b=b: h.activation(out=gq[:], in_=ps[b][:], func=AF.Square)), reads=[("ps", b)], writes=["gq"])
        P.op("dve", lambda h: h.tensor_scalar(out=gz[:], in0=gq[:], scalar1=0.044715, scalar2=1.0, op0=ALU.mult, op1=ALU.add),
             reads=["gq"], writes=["gz"])
        P.op("dve", (lambda h, b=b: h.tensor_tensor(out=gz[:], in0=gz[:], in1=ps[b][:], op=ALU.mult)), reads=["gz", ("ps", b)], writes=["gz"])
        P.op("act", lambda h: h.activation(out=gs[:], in_=gz[:], func=AF.Sigmoid, scale=2.0 * math.sqrt(2.0 / math.pi)), reads=["gz"], writes=["gs"])
        P.op("dve", (lambda h, b=b, gp=gp: h.tensor_tensor(out=Yst[:, 2 * gp:2 * gp + 2, :], in0=gs[:].rearrange("p (a b) -> p a b", b=256),
                                                          in1=ps[b][:].rearrange("p (a b) -> p a b", b=256), op=ALU.mult)),
             reads=["gs", ("ps", b)], writes=[("Yst", gp)])
    for il in range(8):
        P.dma("sp", (lambda h, il=il: h.dma_start(out=scr_y.rearrange("(g c) (i b) -> i c g b", c=16, i=8)[il], in_=Yst[il * 16:(il + 1) * 16, :, :])),
              reads=[("Yst", gp) for gp in range(16)], writes=[("scr_y", il)])
    P.barrier()
    A.release(mS2)
    YT = A.alloc([128, 4, S], BF16)
    wglu = A.alloc([128, 4, 512], BF16)
    P.dma("pool", lambda h: h.dma_start(out=wglu[:], in_=w_glu_d.rearrange("(kc p) n -> p kc n", p=128)), writes=["wglu"])
    P.dma("sp", lambda h: h.dma_start(out=YT[:], in_=scr_y.rearrange("(ch p) t -> p ch t", p=128)), writes=["YT"])
    for chp in range(4):
        for tcp in range(4):
            b = nb(0, 3)
            for k in range(4):
                P.op("pe", (lambda h, b=b, k=k, chp=chp, tcp=tcp: h.matmul(ps[b][:], lhsT=wglu[:, k, chp * 128:(chp + 1) * 128],
                                                                           rhs=YT[:, k, tcp * 512:(tcp + 1) * 512], start=(k == 0), stop=(k == 3))),
                     reads=["wglu", "YT"], writes=[("ps", b)], sig=(k == 3), accum=(k > 0))
            P.op("act", (lambda h, b=b: h.activation(out=gs[:], in_=ps[b][:], func=AF.Sigmoid)), reads=[("ps", b)], writes=["gs"])
            P.op("dve", (lambda h, chp=chp, tcp=tcp: h.tensor_tensor(out=o_s5[:, chp, tcp * 512:(tcp + 1) * 512], in0=gs[:],
                                                                    in1=YT[:, chp, tcp * 512:(tcp + 1) * 512], op=ALU.mult)),
                 reads=["gs", "YT"], writes=[("o_s5", chp, tcp)])
    if dbg.get("stop") == "s5":
        t = dump("od", None, [128, 4, S], BF16)
        P.dma("sp", (lambda h, t=t: h.dma_start(out=t, in_=o_dsa[:])), writes=["dbg_od"])
        t = dump("o_s5", None, [128, 4, S], BF16)
        P.dma("sp", (lambda h, t=t: h.dma_start(out=t, in_=o_s5[:])), reads=[("o_s5", a, c) for a in range(4) for c in range(4)], writes=["dbg_o_s5"])
        t = dump("YT", None, [128, 4, S], BF16)
        P.dma("sp", (lambda h, t=t: h.dma_start(out=t, in_=YT[:])), reads=["YT"], writes=["dbg_YT"])
        P.barrier()
        P.emit()
        return nc, dbg_outs
    P.barrier()
    A.release(base_mark)

    off_merged = A.mark()
    merged = A.alloc([128, 8, S], BF16)
    mM1 = A.mark()
    xgM = A.alloc([128, 8, S], BF16)
    build_xg(xgM, False)
    wg = [A.alloc([128, 8, 384], BF16) for _ in range(2)]
    wbr = [A.alloc([128, 4, 384], BF16) for _ in range(2)]
    gt = A.alloc([128, 512], F32)
    sg = A.alloc([128, 512], F32)
    macc = A.alloc([128, 512], F32)
    mtmp = A.alloc([128, 512], F32)
    w_br_v = [w.rearrange("(kc p) n -> p kc n", p=128) for w in (w_brd_d, w_brs_d, w_brx_d)]
    obr = [o_dsa, o_s5, o_x]
    for fc in range(8):
        wb_ = fc % 2
        for br in range(3):
            c0 = 2760 + br * 1024 + fc * 128
            P.dma("pool", (lambda h, wb_=wb_, br=br, c0=c0: h.dma_start(out=wg[wb_][:, :, br * 128:(br + 1) * 128], in_=w_in_v[:, :, c0:c0 + 128])),
                  writes=[("wg", wb_, br)])
            P.dma("pool", (lambda h, wb_=wb_, br=br, fc=fc: h.dma_start(out=wbr[wb_][:, :, br * 128:(br + 1) * 128],
                                                                       in_=w_br_v[br][:, :, fc * 128:(fc + 1) * 128])),
                  writes=[("wbr", wb_, br)])
        for tc in range(4):
            for br in range(3):
                bG = nb(0, 6)
                for kc in range(8):
                    P.op("pe", (lambda h, bG=bG, kc=kc, wb_=wb_, br=br, tc=tc: h.matmul(ps[bG][:], lhsT=wg[wb_][:, kc, br * 128:(br + 1) * 128],
                                                                                        rhs=xgM[:, kc, tc * 512:(tc + 1) * 512], start=(kc == 0), stop=(kc == 7))),
                         reads=[("wg", wb_, br), ("xg", kc)], writes=[("ps", bG)], sig=(kc == 7), accum=(kc > 0))
                P.op("dve", (lambda h, bG=bG, tc=tc: h.tensor_tensor(out=gt[:], in0=ps[bG][:], in1=rx_bc[:, tc * 512:(tc + 1) * 512], op=ALU.mult)),
                     reads=[("ps", bG)], writes=["gt"])
                P.op("act", lambda h: h.activation(out=sg[:], in_=gt[:], func=AF.Sigmoid), reads=["gt"], writes=["sg"])
                bB = nb(0, 6)
                for k in range(4):
                    if br == 1:
                        rhs = o_s5[:, k, :].rearrange("p (j b) -> p b j", j=8)[:, 64 * tc:64 * tc + 64, :]
                    else:
                        rhs = obr[br][:, k, tc * 512:(tc + 1) * 512]
                    P.op("pe", (lambda h, bB=bB, k=k, wb_=wb_, br=br, rhs=rhs: h.matmul(ps[bB][:], lhsT=wbr[wb_][:, k, br * 128:(br + 1) * 128], rhs=rhs,
                                                                                        start=(k == 0), stop=(k == 3))),
                         reads=[("wbr", wb_, br)], writes=[("ps", bB)], sig=(k == 3), accum=(k > 0))
                if br == 0:
                    P.op("dve", (lambda h, bB=bB: h.tensor_tensor(out=macc[:], in0=ps[bB][:], in1=sg[:], op=ALU.mult)),
                         reads=[("ps", bB), "sg"], writes=["macc"])
                else:
                    P.op("dve", (lambda h, bB=bB: h.tensor_tensor(out=mtmp[:], in0=ps[bB][:], in1=sg[:], op=ALU.mult)),
                         reads=[("ps", bB), "sg"], writes=["mtmp"])
                    if br == 1:
                        P.op("dve", lambda h: h.tensor_tensor(out=macc[:], in0=macc[:], in1=mtmp[:], op=ALU.add), reads=["macc", "mtmp"], writes=["macc"])
                    else:
                        P.op("dve", (lambda h, fc=fc, tc=tc: h.tensor_tensor(out=merged[:, fc, tc * 512:(tc + 1) * 512], in0=macc[:], in1=mtmp[:], op=ALU.add)),
                             reads=["macc", "mtmp"], writes=[("merged", fc, tc)])
                if dbg.get("stop") == "merge1" and br == dbg.get("br", 0):
                    for nm, tns, shp, dt_, keys in (("wg", wg[0][:], [128, 8, 384], BF16, [("wg", 0, i) for i in range(3)]), ("gt", gt[:], [128, 512], F32, ["gt"]),
                                                    ("sg", sg[:], [128, 512], F32, ["sg"]), ("macc", macc[:], [128, 512], F32, ["macc"]),
                                                    ("mtmp", mtmp[:], [128, 512], F32, ["mtmp"]),
                                                    ("xg", xgM[:], [128, 8, S], BF16, XG), ("od", o_dsa[:], [128, 4, S], BF16, []), ("os", o_s5[:], [128, 4, S], BF16, []), ("ox", o_x[:], [128, 4, S], BF16, []), ("wbr", wbr[0][:], [128, 4, 384], BF16, [("wbr", 0, i) for i in range(3)])):
                        t = dump(nm, None, shp, dt_)
                        P.dma("sp", (lambda h, t=t, tns=tns: h.dma_start(out=t, in_=tns)), reads=keys, writes=["dbg_" + nm])
                    P.barrier()
                    P.emit()
                    return nc, dbg_outs
    if dbg.get("stop") == "merge":
        t = dump("merged", None, [128, 8, S], BF16)
        P.dma("sp", (lambda h, t=t: h.dma_start(out=t, in_=merged[:])), reads=[("merged", a, c) for a in range(8) for c in range(4)], writes=["dbg_merged"])
        P.barrier()
        P.emit()
        return nc, dbg_outs
    P.barrier()
    A.release(mM1)
    off_x1 = A.mark()
    x1T = A.alloc([128, 8, S], F32)
    mM0 = A.mark()
    wout = A.alloc([128, 8, 1024], BF16)
    xres = [A.alloc([128, 512], F32) for _ in range(2)]
    w_out_v = w_out_d.rearrange("(kc p) n -> p kc n", p=128)
    for i in range(2):
        load_w(wout[:, :, i * 512:(i + 1) * 512], w_out_v, ("wout", i), (i * 512, (i + 1) * 512))
    xi = 0
    for fc in range(8):
        for tc in range(4):
            xb = xi % 2
            xi += 1
            P.dma("sp", (lambda h, xb=xb, fc=fc, tc=tc: h.dma_start(out=xres[xb][:], in_=xT_d[fc * 128:(fc + 1) * 128, tc * 512:(tc + 1) * 512])),
                  writes=[("xres", xb)])
            b = nb(0, 6)
            for kc in range(8):
                P.op("pe", (lambda h, b=b, kc=kc, fc=fc, tc=tc: h.matmul(ps[b][:], lhsT=wout[:, kc, fc * 128:(fc + 1) * 128],
                                                                         rhs=merged[:, kc, tc * 512:(tc + 1) * 512], start=(kc == 0), stop=(kc == 7))),
                     reads=[(("wout", fc // 4), kc), ("merged", kc, tc)], writes=[("ps", b)], sig=(kc == 7), accum=(kc > 0))
            P.op("dve", (lambda h, b=b, xb=xb, fc=fc, tc=tc: h.tensor_tensor(out=x1T[:, fc, tc * 512:(tc + 1) * 512], in0=ps[b][:], in1=xres[xb][:], op=ALU.add)),
                 reads=[("ps", b), ("xres", xb)], writes=[("x1T", fc, tc)])
    if dbg.get("stop") == "x1":
        t = dump("x1T", None, [128, 8, S])
        P.dma("sp", (lambda h, t=t: h.dma_start(out=t, in_=x1T[:])), reads=[("x1T", a, c) for a in range(8) for c in range(4)], writes=["dbg_x1T"])
        P.barrier()
        P.emit()
        return nc, dbg_outs
    P.barrier()
    A.release(mM0)
    top = A.mark()
    A.off = off_o
    fsq = [A.alloc([128, 512], BF16) for _ in range(2)]
    rf = A.alloc([128, 512], F32)
    wgu = [[A.alloc([128, 8, 256], BF16) for _ in range(2)] for _ in range(2)]
    wdn = [A.alloc([128, NFF, 256], BF16) for _ in range(2)]
    fs = A.alloc([128, 512], F32)
    assert A.off <= off_merged
    A.off = off_merged
    hfT = A.alloc([128, 8, 512], BF16)
    hT = A.alloc([128, NFF, 512], BF16)
    assert A.off <= off_x1
    A.off = top
    fo = [A.alloc([128, 512], F32) for _ in range(2)]
    w_gu_v = [w.rearrange("(kc p) n -> p kc n", p=128) for w in (w_g_d, w_u_d)]
    w_d_v = w_d_d.rearrange("(fk p) n -> p fk n", p=128)
    wi = 0
    di = 0
    oi = 0
    for tq in range(4):
        tsl = slice(tq * 512, (tq + 1) * 512)
        bS = nb(0, 6)
        for kc in range(8):
            sb_ = kc % 2
            P.op("act", (lambda h, kc=kc, sb_=sb_, tsl=tsl: h.activation(out=fsq[sb_][:], in_=x1T[:, kc, tsl], func=AF.Square)),
                 reads=[("x1T", kc, tq)], writes=[("fsq", sb_)])
            P.op("pe", (lambda h, kc=kc, sb_=sb_, bS=bS: h.matmul(ps[bS][:], lhsT=ones_b, rhs=fsq[sb_][:], start=(kc == 0), stop=(kc == 7))),
                 reads=[("fsq", sb_), "cm_b"], writes=[("ps", bS)], sig=True, accum=(kc > 0))
        P.op("act", (lambda h, bS=bS: h.activation(out=rf[:], in_=ps[bS][:], func=AF.Ln, scale=1.0 / D, bias=EPS)), reads=[("ps", bS)], writes=["rf"])
        P.op("act", lambda h: h.activation(out=rf[:], in_=rf[:], func=AF.Exp, scale=-0.5), reads=["rf"], writes=["rf"])
        for kc in range(8):
            P.op("dve", (lambda h, kc=kc, tsl=tsl: h.scalar_tensor_tensor(out=hfT[:, kc, :], in0=x1T[:, kc, tsl], scalar=gvec[:, 16 + kc:17 + kc], in1=rf[:],
                                                                          op0=ALU.mult, op1=ALU.mult)),
                 reads=[("x1T", kc, tq), "rf", "gvec"], writes=[("hfT", kc)])
        for f0 in range(0, NFF, 2):
            wb_ = wi % 2
            wi += 1
            for gu in range(2):
                P.dma("pool", (lambda h, gu=gu, wb_=wb_, f0=f0: h.dma_start(out=wgu[gu][wb_][:], in_=w_gu_v[gu][:, :, f0 * 128:(f0 + 2) * 128])),
                      writes=[("wgu", gu, wb_)])
            for ff in range(f0, f0 + 2):
                bg, bu = nb(0, 6), nb(0, 6)
                for gu, bb_ in ((0, bg), (1, bu)):
                    for kc in range(8):
                        P.op("pe", (lambda h, gu=gu, bb_=bb_, kc=kc, wb_=wb_, ff=ff, f0=f0: h.matmul(
                            ps[bb_][:], lhsT=wgu[gu][wb_][:, kc, (ff - f0) * 128:(ff - f0 + 1) * 128], rhs=hfT[:, kc, :], start=(kc == 0), stop=(kc == 7))),
                            reads=[("wgu", gu, wb_), ("hfT", kc)], writes=[("ps", bb_)], sig=(kc == 7), accum=(kc > 0))
                P.op("act", (lambda h, bg=bg: h.activation(out=fs[:], in_=ps[bg][:], func=AF.Silu)), reads=[("ps", bg)], writes=["fs"])
                P.op("dve", (lambda h, bu=bu, ff=ff: h.tensor_tensor(out=hT[:, ff, :], in0=ps[bu][:], in1=fs[:], op=ALU.mult)),
                     reads=[("ps", bu), "fs"], writes=[("hT", ff)])
        for c0 in range(0, 8, 2):
            db = di % 2
            di += 1
            P.dma("pool", (lambda h, db=db, c0=c0: h.dma_start(out=wdn[db][:], in_=w_d_v[:, :, c0 * 128:(c0 + 2) * 128])), writes=[("wdn", db)])
            for fc in range(c0, c0 + 2):
                b = nb(0, 6)
                for ff in range(NFF):
                    P.op("pe", (lambda h, b=b, ff=ff, db=db, fc=fc, c0=c0: h.matmul(ps[b][:], lhsT=wdn[db][:, ff, (fc - c0) * 128:(fc - c0 + 1) * 128], rhs=hT[:, ff, :],
                                                                                    start=(ff == 0), stop=(ff == NFF - 1))),
                         reads=[("wdn", db), ("hT", ff)], writes=[("ps", b)], sig=(ff == NFF - 1), accum=(ff > 0))
                ob = oi % 2
                oi += 1
                P.op("dve", (lambda h, b=b, ob=ob, fc=fc, tsl=tsl: h.tensor_tensor(out=fo[ob][:], in0=ps[b][:], in1=x1T[:, fc, tsl], op=ALU.add)),
                     reads=[("ps", b), ("x1T", fc, tq)], writes=[("fo", ob)])
                P.dma("sp", (lambda h, ob=ob, fc=fc, tsl=tsl: h.dma_start(out=outT_d[fc * 128:(fc + 1) * 128, tsl], in_=fo[ob][:])),
                      reads=[("fo", ob)], writes=[("out", fc, tq)])
    P.barrier()
    P.emit()
    return nc, dbg_outs


def _host_inputs(inputs):
    f = lambda a: np.ascontiguousarray(np.asarray(a, dtype=np.float32))
    x = f(inputs["x"])
    mem = f(inputs["mem"])
    rel_bias = f(inputs["rel_bias"])
    shared = {}
    shared["w_in"] = f(inputs["w_in"][0])
    shared["w_uv"] = f(np.transpose(inputs["w_uv_dsa"][0], (1, 0, 2)))
    shared["w_glu"] = f(inputs["w_glu"][0])
    shared["w_mem_kv"] = f(inputs["w_mem_kv"][0])
    shared["w_br_dsa"] = f(inputs["w_br_dsa"][0])
    shared["w_br_s5"] = f(inputs["w_br_s5"][0])
    shared["w_br_cross"] = f(inputs["w_br_cross"][0])
    shared["w_out"] = f(inputs["w_out"][0])
    shared["w_ffn_gate"] = f(inputs["w_ffn_gate"][0])
    shared["w_ffn_up"] = f(inputs["w_ffn_up"][0])
    shared["w_ffn_down"] = f(inputs["w_ffn_down"][0])
    gvec = np.zeros((128, 32), np.float32)
    gvec[:, 0:8] = np.asarray(inputs["g_mix_norm"][0]).reshape(8, 128).T
    gvec[:, 8:16] = np.asarray(inputs["g_mem_norm"][0]).reshape(8, 128).T
    gvec[:, 16:24] = np.asarray(inputs["g_ffn_norm"][0]).reshape(8, 128).T
    gvec[:, 24] = np.asarray(inputs["g_q_dsa"][0])
    gvec[:, 25] = np.asarray(inputs["g_q_cross"][0])
    gvec[:, 26] = np.asarray(inputs["g_k_cross"][0])
    shared["gvec"] = gvec
    shared["gkv_bc"] = f(np.broadcast_to(np.asarray(inputs["g_kv_dsa"][0])[None, :], (128, 128)))
    bt = _t5_bucket_table()
    s_i = np.arange(128)[:, None]
    t_i = np.arange(128)[None, :]
    t5 = np.zeros((128, 2, 8, 128), np.float32)
    for diff in (0, 1):
        dist = np.maximum(t_i - s_i + 128 * diff, 0)
        t5[:, diff, :, :] = np.transpose(rel_bias[bt[dist]], (0, 2, 1))
    shared["t5"] = t5
    shared["rb31"] = f(np.broadcast_to(rel_bias[31][None, :], (128, 8)))
    cm = np.zeros((128, 5, 128), np.float32)
    cm[:, 0, :] = np.eye(128)
    cm[:, 1, :] = np.where(t_i.T >= s_i.T, 0.0, NEG)
    il = np.arange(128)[None, :] // 16
    jl = np.arange(128)[:, None] // 16
    cm[:, 2, :] = (il >= jl).astype(np.float32)
    sw = np.zeros((128, 128), np.float32)
    sw[np.arange(64), np.arange(64) + 64] = 1.0
    sw[np.arange(64) + 64, np.arange(64)] = 1.0
    cm[:, 3, :] = sw
    cm[:, 4, :] = 1.0
    shared["cmats"] = cm
    tile2 = lambda a: np.concatenate([a, a], axis=0)
    s5a = np.zeros((128, 3, 32), np.float32)
    s5a[:, 0, :] = tile2(np.asarray(inputs["a_re"][0]).T)
    s5a[:, 1, :] = tile2(np.asarray(inputs["a_im"][0]).T)
    s5a[:, 2, :] = np.broadcast_to(np.asarray(inputs["log_dt"][0])[None, :], (128, 32))
    shared["s5a"] = s5a
    s5b = np.zeros((128, 4, 32, 16), np.float32)
    s5b[:, 0] = tile2(np.transpose(inputs["b_re"][0], (1, 0, 2)))
    s5b[:, 1] = tile2(np.transpose(inputs["b_im"][0], (1, 0, 2)))
    s5b[:, 2] = tile2(np.transpose(inputs["c_re"][0], (2, 0, 1)))
    s5b[:, 3] = tile2(np.transpose(inputs["c_im"][0], (2, 0, 1)))
    shared["s5b"] = s5b
    shared["dsk"] = f(np.tile(np.asarray(inputs["d_skip"][0]).T, (8, 1)))
    in_maps = []
    for b in range(8):
        d = dict(shared)
        d["xT"] = f(x[b].T)
        d["memT"] = f(mem[b].T)
        in_maps.append(d)
    return in_maps


_CACHE = {}


def kernel(**inputs):
    in_maps = _host_inputs(inputs)
    if "nc" not in _CACHE:
        _CACHE["nc"] = build()[0]
    res = run_bass_kernel_spmd(_CACHE["nc"], in_maps, core_ids=list(range(8)))
    out = np.stack([np.ascontiguousarray(res.results[b]["outT"].T) for b in range(8)], axis=0)
    return out.astype(np.float32)
```
